# Optimizing a Trainium2 kernel written in Bass

```python
import jax, jax.numpy as jnp
from jax import lax
import numpy as np

D_MODEL = 2048
BATCH = 2
SEQ = 8192
DEPTH = 4

N_MIXERS = 2
N_REC = (DEPTH + 1) // 2
N_ATT = DEPTH // 2
D_FF = 5632
LRU_WIDTH = D_MODEL
LRU_BLOCKS = 8
LRU_BLOCK_W = LRU_WIDTH // LRU_BLOCKS
CONV_WIDTH = 4
LRU_C = 8.0
ATT_PATTERNS = ((128, 1), (512, 4), (2048, 16))
N_GROUPS = len(ATT_PATTERNS)
ATT_HEADS = 8
ATT_HEAD_DIM = 128
ATT_WIDTH = ATT_HEADS * ATT_HEAD_DIM
MEM_TOKENS = 256
MEM_HEADS = 4
MEM_HEAD_DIM = D_MODEL // 8
MEM_WIDTH = MEM_HEADS * MEM_HEAD_DIM
REC_IN = 2 * LRU_WIDTH + MEM_WIDTH
ATT_IN = N_GROUPS * 3 * ATT_WIDTH + MEM_WIDTH
EPS = 1e-6
NEG_INF = -1e30

kernel_name = "hybrid_rglru_dilated_swa_macaron_encoder"


def rms_norm(x, g):
    xf = x.astype(jnp.float32)
    y = xf * lax.rsqrt(jnp.mean(xf * xf, axis=-1, keepdims=True) + EPS)
    return (y * g.astype(jnp.float32)).astype(x.dtype)


def swiglu(x, w_in, w_out):
    gate, up = jnp.split(x @ w_in, 2, axis=-1)
    return (jax.nn.silu(gate) * up) @ w_out


def alibi_slopes():
    n = N_GROUPS * ATT_HEADS
    s = jnp.exp2(-8.0 * (jnp.arange(n, dtype=jnp.float32) + 1.0) / n)
    return s.reshape(N_GROUPS, ATT_HEADS)


def centred_depthwise_conv(x, w, b):
    left = CONV_WIDTH // 2
    y = lax.conv_general_dilated(x, w[:, None, :].astype(x.dtype), window_strides=(1,),
                                 padding=[(left, CONV_WIDTH - 1 - left)],
                                 dimension_numbers=('NWC', 'WIO', 'NWC'),
                                 feature_group_count=x.shape[-1])
    return y + b.astype(x.dtype)


def linear_scan(a, b, reverse):
    if reverse:
        a, b = jnp.flip(a, 1), jnp.flip(b, 1)
    def combine(l, r):
        return (l[0] * r[0], r[0] * l[1] + r[1])
    _, h = lax.associative_scan(combine, (a, b), axis=1)
    return jnp.flip(h, 1) if reverse else h


def bidirectional_rg_lru(xc, gate_w, gate_b, lam):
    B, S, W = xc.shape
    xb = xc.reshape(B, S, LRU_BLOCKS, LRU_BLOCK_W)
    xf = xb.astype(jnp.float32)
    hs = []
    for direction in range(2):
        r = jax.nn.sigmoid(jnp.einsum('bsnc,ncz->bsnz', xb, gate_w[direction, 0]).astype(jnp.float32)
                           + gate_b[direction, 0].astype(jnp.float32))
        i = jax.nn.sigmoid(jnp.einsum('bsnc,ncz->bsnz', xb, gate_w[direction, 1]).astype(jnp.float32)
                           + gate_b[direction, 1].astype(jnp.float32))
        log_a = -LRU_C * r * jax.nn.softplus(-lam[direction].astype(jnp.float32)).reshape(LRU_BLOCKS, LRU_BLOCK_W)
        a = jnp.exp(log_a)
        u = jnp.sqrt(-jnp.expm1(2.0 * log_a)) * (i * xf)
        hs.append(linear_scan(a, u, reverse=(direction == 1)))
    return (hs[0] + hs[1]).reshape(B, S, W)


def dilated_band_attention(q, k, v, slope, dilation, half):
    B, S, H, hd = q.shape
    L = S // dilation
    def to_classes(t):
        return t.reshape(B, L, dilation, H, hd).transpose(0, 3, 2, 1, 4)
    qc, kc, vc = to_classes(q), to_classes(k), to_classes(v)
    C = half
    nb = -(-L // C)
    Lp = nb * C
    qb = jnp.pad(qc, [(0, 0)] * 3 + [(0, Lp - L), (0, 0)]).reshape(B, H, dilation, nb, C, hd)
    def windows(t):
        tb = jnp.pad(t, [(0, 0)] * 3 + [(C, Lp - L + C), (0, 0)]).reshape(B, H, dilation, nb + 2, C, hd)
        return jnp.concatenate([tb[:, :, :, :-2], tb[:, :, :, 1:-1], tb[:, :, :, 2:]], axis=4)
    kw, vw = windows(kc), windows(vc)
    scores = jnp.einsum('bhrnqd,bhrnkd->bhrnqk', qb, kw).astype(jnp.float32) * (hd ** -0.5)
    rel = jnp.arange(3 * C)[None, :] - C - jnp.arange(C)[:, None]
    key_pos = jnp.arange(nb)[:, None] * C - C + jnp.arange(3 * C)[None, :]
    valid = (jnp.abs(rel) <= half)[None] & ((key_pos >= 0) & (key_pos < L))[:, None, :]
    bias = -slope[:, None, None] * (dilation * jnp.abs(rel)).astype(jnp.float32)[None]
    scores = jnp.where(valid[None, None, None], scores + bias[None, :, None, None], NEG_INF)
    m = jnp.max(scores, axis=-1, keepdims=True)
    e = jnp.exp(scores - m)
    den = jnp.sum(e, axis=-1, keepdims=True)
    lse = (m + jnp.log(den))[..., 0]
    out = jnp.einsum('bhrnqk,bhrnkd->bhrnqd', (e / den).astype(v.dtype), vw)
    out = out.reshape(B, H, dilation, Lp, hd)[:, :, :, :L].transpose(0, 3, 2, 1, 4).reshape(B, S, H, hd)
    lse = lse.reshape(B, H, dilation, Lp)[:, :, :, :L].transpose(0, 3, 2, 1).reshape(B, S, H)
    return out, lse


def memory_attention(q, mem_n, w_kv, qk_gain):
    B, S = q.shape[:2]
    q = rms_norm(q, qk_gain[0])
    kv = (mem_n @ w_kv).reshape(B, mem_n.shape[1], 2, MEM_HEADS, MEM_HEAD_DIM)
    k = rms_norm(kv[:, :, 0], qk_gain[1])
    v = kv[:, :, 1]
    s = jnp.einsum('bshd,bmhd->bhsm', q, k).astype(jnp.float32) * (MEM_HEAD_DIM ** -0.5)
    p = jax.nn.softmax(s, axis=-1).astype(v.dtype)
    return jnp.einsum('bhsm,bmhd->bshd', p, v).reshape(B, S, MEM_WIDTH)


def recurrent_mixer(xn, mem_n, w_in, conv_w, conv_b, gate_w, gate_b, lam, w_out, mem_w_kv, mem_qk_gain):
    B, S, _ = xn.shape
    z = xn @ w_in
    gate_br = z[..., :LRU_WIDTH]
    rec_br = z[..., LRU_WIDTH:2 * LRU_WIDTH]
    q_mem = z[..., 2 * LRU_WIDTH:].reshape(B, S, MEM_HEADS, MEM_HEAD_DIM)
    xc = centred_depthwise_conv(rec_br, conv_w, conv_b)
    h = bidirectional_rg_lru(xc, gate_w, gate_b, lam).astype(xn.dtype)
    y_rec = h * jax.nn.gelu(gate_br)
    y_mem = memory_attention(q_mem, mem_n, mem_w_kv, mem_qk_gain)
    return jnp.concatenate([y_rec, y_mem], axis=-1) @ w_out


def attention_mixer(xn, mem_n, w_in, qk_gain, w_out, mem_w_kv, mem_qk_gain):
    B, S, _ = xn.shape
    z = xn @ w_in
    zg = z[..., :N_GROUPS * 3 * ATT_WIDTH].reshape(B, S, N_GROUPS, 3, ATT_HEADS, ATT_HEAD_DIM)
    q_mem = z[..., N_GROUPS * 3 * ATT_WIDTH:].reshape(B, S, MEM_HEADS, MEM_HEAD_DIM)
    slopes = alibi_slopes()
    outs, lses = [], []
    for g, (window, dilation) in enumerate(ATT_PATTERNS):
        q = rms_norm(zg[:, :, g, 0], qk_gain[0, g])
        k = rms_norm(zg[:, :, g, 1], qk_gain[1, g])
        o, l = dilated_band_attention(q, k, zg[:, :, g, 2], slopes[g], dilation, window // (2 * dilation))
        outs.append(o)
        lses.append(l)
    wts = jax.nn.softmax(jnp.stack(lses, axis=0), axis=0)
    att = jnp.einsum('gbsh,gbshd->bshd', wts, jnp.stack(outs, axis=0).astype(jnp.float32))
    att = att.astype(xn.dtype).reshape(B, S, ATT_WIDTH)
    y_mem = memory_attention(q_mem, mem_n, mem_w_kv, mem_qk_gain)
    return jnp.concatenate([att, y_mem], axis=-1) @ w_out


def setup_inputs(seed: int = 0) -> dict:
    key = jax.random.key(seed)
    ks = jax.random.split(key, 20)
    f32 = jnp.float32
    def nrm(k, shape, fan_in, scale=1.0):
        return jax.random.normal(k, shape, f32) * (scale * fan_in ** -0.5)
    def gain(k, shape):
        return 1.0 + 0.05 * jax.random.normal(k, shape, f32)
    a0 = jax.random.uniform(ks[13], (N_REC, 2, LRU_WIDTH), f32, minval=0.9, maxval=0.999)
    p = a0 ** (1.0 / LRU_C)
    rec_lambda = jnp.log(p) - jnp.log1p(-p)
    return {
        "x": jax.random.normal(ks[0], (BATCH, SEQ, D_MODEL), f32),
        "mem": jax.random.normal(ks[1], (BATCH, MEM_TOKENS, D_MODEL), f32),
        "ffn_norm": gain(ks[2], (DEPTH, 2, D_MODEL)),
        "ffn_w_in": nrm(ks[3], (DEPTH, 2, D_MODEL, 2 * D_FF), D_MODEL),
        "ffn_w_out": nrm(ks[4], (DEPTH, 2, D_FF, D_MODEL), D_FF, 0.5),
        "mix_norm": gain(ks[5], (DEPTH, D_MODEL)),
        "mem_norm": gain(ks[6], (DEPTH, D_MODEL)),
        "mem_w_kv": nrm(ks[7], (DEPTH, D_MODEL, 2 * MEM_WIDTH), D_MODEL),
        "mem_qk_gain": gain(ks[8], (DEPTH, 2, MEM_HEAD_DIM)),
        "rec_w_in": nrm(ks[9], (N_REC, D_MODEL, REC_IN), D_MODEL),
        "rec_conv_w": nrm(ks[10], (N_REC, CONV_WIDTH, LRU_WIDTH), CONV_WIDTH),
        "rec_conv_b": 0.01 * jax.random.normal(ks[11], (N_REC, LRU_WIDTH), f32),
        "rec_gate_w": nrm(ks[12], (N_REC, 2, 2, LRU_BLOCKS, LRU_BLOCK_W, LRU_BLOCK_W), LRU_BLOCK_W),
        "rec_gate_b": 0.01 * jax.random.normal(ks[14], (N_REC, 2, 2, LRU_BLOCKS, LRU_BLOCK_W), f32),
        "rec_lambda": rec_lambda,
        "rec_w_out": nrm(ks[15], (N_REC, LRU_WIDTH + MEM_WIDTH, D_MODEL), LRU_WIDTH + MEM_WIDTH, 0.5),
        "att_w_in": nrm(ks[16], (N_ATT, D_MODEL, ATT_IN), D_MODEL),
        "att_qk_gain": gain(ks[17], (N_ATT, 2, N_GROUPS, ATT_HEAD_DIM)),
        "att_w_out": nrm(ks[18], (N_ATT, ATT_WIDTH + MEM_WIDTH, D_MODEL), ATT_WIDTH + MEM_WIDTH, 0.5),
    }


def reference(x, mem, ffn_norm, ffn_w_in, ffn_w_out, mix_norm, mem_norm, mem_w_kv, mem_qk_gain,
              rec_w_in, rec_conv_w, rec_conv_b, rec_gate_w, rec_gate_b, rec_lambda, rec_w_out,
              att_w_in, att_qk_gain, att_w_out):
    for layer in range(DEPTH):
        x = x + 0.5 * swiglu(rms_norm(x, ffn_norm[layer, 0]), ffn_w_in[layer, 0], ffn_w_out[layer, 0])
        xn = rms_norm(x, mix_norm[layer])
        mem_n = rms_norm(mem, mem_norm[layer])
        j = layer // N_MIXERS
        if layer % N_MIXERS == 0:
            y = recurrent_mixer(xn, mem_n, rec_w_in[j], rec_conv_w[j], rec_conv_b[j], rec_gate_w[j],
                                rec_gate_b[j], rec_lambda[j], rec_w_out[j], mem_w_kv[layer], mem_qk_gain[layer])
        else:
            y = attention_mixer(xn, mem_n, att_w_in[j], att_qk_gain[j], att_w_out[j],
                                mem_w_kv[layer], mem_qk_gain[layer])
        x = x + y
        x = x + 0.5 * swiglu(rms_norm(x, ffn_norm[layer, 1]), ffn_w_in[layer, 1], ffn_w_out[layer, 1])
    return x
```

```python
import numpy as np
from contextlib import ExitStack
import concourse.bass as bass
import concourse.mybir as mybir
from concourse.bass_utils import run_bass_kernel_spmd

F32 = mybir.dt.float32
AF = mybir.ActivationFunctionType
ALU = mybir.AluOpType
ENG = ['pe', 'act', 'dve', 'pool', 'sp']

FULL = dict(D=2048, S=8192, FF=5632, DEPTH=4, MH=4, AH=8, M=256, PAT=((128, 1), (512, 4), (2048, 16)))


class Buf:
    def __init__(s, name, di=None):
        s.name, s.lw, s.rd, s.di = name, None, {}, di


class Prog:
    def __init__(s, nc, ndsem=40):
        s.nc = nc
        s.es = ExitStack()
        s.eng = {'pe': nc.tensor, 'act': nc.scalar, 'dve': nc.vector, 'pool': nc.gpsimd, 'sp': nc.sync}
        s.esem = {e: s.es.enter_context(nc.semaphore("es_" + e)) for e in ENG}
        s.ecnt = {e: 0 for e in ENG}
        s.dsem = [s.es.enter_context(nc.semaphore("ds%d" % i)) for i in range(ndsem)]
        s.dcnt = [0] * ndsem
        s.q = {e: [] for e in ENG}
        s.seen = {e: {} for e in ENG}
        s.nextd = 0

    def buf(s, name, dma=False):
        di = None
        if dma:
            di = s.nextd % len(s.dsem)
            s.nextd += 1
        return Buf(name, di)

    def _sem(s, key):
        return s.esem[key[1]] if key[0] == 'e' else s.dsem[key[1]]

    def _waits(s, eng, evs):
        out = []
        for ev in evs:
            if ev is None:
                continue
            key, val = ev
            if key == ('e', eng) and eng == 'pe':
                continue
            if s.seen[eng].get(key, 0) >= val:
                continue
            s.seen[eng][key] = val
            out.append(ev)
        return out

    def _deps(s, reads, writes):
        evs = [b.lw for b in reads]
        for b in writes:
            evs.append(b.lw)
            evs += list(b.rd.items())
        return evs

    def _upd(s, ev, reads, writes):
        for b in reads:
            b.rd[ev[0]] = max(b.rd.get(ev[0], 0), ev[1])
        for b in writes:
            b.lw = ev
            b.rd = {}

    def op(s, eng, fn, reads=(), writes=()):
        w = s._waits(eng, s._deps(reads, writes))
        s.ecnt[eng] += 1
        ev = (('e', eng), s.ecnt[eng])
        s.q[eng].append((fn, w, ('e', eng), 1))
        s._upd(ev, reads, writes)
        return ev

    def dma(s, eng, fn, sb, reads=(), writes=()):
        di = sb.di
        evs = s._deps(reads, writes)
        if s.dcnt[di] > 0:
            evs.append((('d', di), s.dcnt[di]))
        w = s._waits(eng, evs)
        s.dcnt[di] += 16
        ev = (('d', di), s.dcnt[di])
        s.q[eng].append((fn, w, ('d', di), 16))
        s._upd(ev, reads, writes)
        return ev

    def barrier(s):
        for e in ENG:
            evs = [(('e', e2), s.ecnt[e2]) for e2 in ENG if e2 != e and s.ecnt[e2] > 0]
            evs += [(('d', i), c) for i, c in enumerate(s.dcnt) if c > 0]
            s.q[e].append((None, s._waits(e, evs), None, 0))

    def flush(s):
        s.barrier()
        with s.nc.Block() as block:
            def mk(e):
                def run(eo):
                    for (fn, w, key, inc) in s.q[e]:
                        for (k, v) in w:
                            eo.wait_ge(s._sem(k), v)
                        if fn is not None:
                            fn(eo).then_inc(s._sem(key), inc)
                return run
            block.tensor(mk('pe'))
            block.scalar(mk('act'))
            block.vector(mk('dve'))
            block.gpsimd(mk('pool'))
            block.sync(mk('sp'))
        s.q = {e: [] for e in ENG}


class Ctx:
    def __init__(s, P, cfg, dr):
        s.P, s.cfg, s.dr, s.nc = P, cfg, dr, P.nc
        s.es = ExitStack()
        s.wi = 0

    _uid = [0]

    def sb(s, name, shape):
        Ctx._uid[0] += 1
        return s.es.enter_context(s.nc.sbuf_tensor("%s_u%d" % (name, Ctx._uid[0]), list(shape), F32))

    def ps(s, name, shape):
        Ctx._uid[0] += 1
        return s.es.enter_context(s.nc.psum_tensor("%s_u%d" % (name, Ctx._uid[0]), list(shape), F32))

    def common(s, nring=4, smalls_layer=None):
        P = s.P
        s.ring = [s.sb("wr%d" % i, [128, 2048]) for i in range(nring)]
        s.ringb = [P.buf("wr%d" % i, dma=True) for i in range(nring)]
        s.ones = s.sb("ones", [128, 128])
        s.onesb = P.buf("ones")
        s.cst = s.sb("cst", [128, 4])
        s.cstb = P.buf("cst")
        o, c = s.ones, s.cst
        P.op('pool', lambda e: e.memset(o[:, :], 1.0), writes=[s.onesb])
        P.op('pool', lambda e: e.memset(c[:, 0:1], 1e-6), writes=[s.cstb])
        P.op('pool', lambda e: e.memset(c[:, 1:2], 1.0), writes=[s.cstb])
        if smalls_layer is not None:
            ns = s.dr['smalls'].shape[2]
            s.sm = s.sb("smalls", [128, ns])
            s.smb = P.buf("smalls", dma=True)
            sm, src = s.sm, s.dr['smalls'][smalls_layer]
            P.dma('pool', lambda e: e.dma_start(out=sm[:, :], in_=src), s.smb, writes=[s.smb])

    def wload(s, src, kcn):
        i = s.wi % len(s.ring)
        s.wi += 1
        t, b = s.ring[i], s.ringb[i]
        s.P.dma('sp', lambda e: e.dma_start(out=t[:, 0:kcn * 128], in_=src.rearrange("p k c -> p (k c)")), b, writes=[b])
        return t, b

    def close(s):
        s.P.flush()
        s.es.close()


def mm_acc(P, ps, psb, n, wt, wb, acts, actb, first=True, last=True):
    K = len(acts)
    for n0 in range(0, n, 512):
        n1 = min(n, n0 + 512)
        for k in range(K):
            a = acts[k]
            P.op('pe', lambda e, k=k, a=a, n0=n0, n1=n1: e.matmul(ps[:, n0:n1], lhsT=wt[:, k * 128:(k + 1) * 128], rhs=a(n0, n1),
                                                            start=(first and k == 0), stop=(last and k == K - 1)),
                 reads=[wb] + actb, writes=[psb])


def rms_rstd(P, C, srcs, srcb, n, dim, sqs, sqb, psn, psnb, rs, rsb):
    K = len(srcs)
    for k in range(K):
        sq, sb_ = sqs[k % 2], sqb[k % 2]
        a = srcs[k]
        P.op('act', lambda e, sq=sq, a=a: e.activation(out=sq[:, 0:n], in_=a, func=AF.Square), reads=srcb, writes=[sb_])
        for n0 in range(0, n, 512):
            n1 = min(n, n0 + 512)
            P.op('pe', lambda e, sq=sq, k=k, n0=n0, n1=n1: e.matmul(psn[:, n0:n1], lhsT=C.ones[:, :], rhs=sq[:, n0:n1],
                                                                start=(k == 0), stop=(k == K - 1)),
                 reads=[sb_, C.onesb], writes=[psnb])
    P.op('act', lambda e: e.activation(out=rs[:, 0:n], in_=psn[:, 0:n], func=AF.Sqrt, bias=C.cst[:, 0:1], scale=1.0 / dim),
         reads=[psnb, C.cstb], writes=[rsb])
    P.op('dve', lambda e: e.reciprocal(out=rs[:, 0:n], in_=rs[:, 0:n]), reads=[rsb], writes=[rsb])


class NormRes:
    def __init__(s, C, n, tag=""):
        P = C.P
        s.sq = [C.sb("nsq0" + tag, [128, n]), C.sb("nsq1" + tag, [128, n])]
        s.sqb = [P.buf("nsq0"), P.buf("nsq1")]
        s.psn = C.ps("npsn" + tag, [128, max(n, 512)])
        s.psnb = P.buf("npsn")
        s.rs = C.sb("nrs" + tag, [128, n])
        s.rsb = P.buf("nrs")


def rmsnorm_tile(P, C, NR, xt, xtb, xn, xnb, KC, n, gcol0):
    rms_rstd(P, C, [xt[:, k, 0:n] for k in range(KC)], [xtb], n, KC * 128, NR.sq, NR.sqb, NR.psn, NR.psnb, NR.rs, NR.rsb)
    for k in range(KC):
        P.op('dve', lambda e, k=k: e.scalar_tensor_tensor(out=xn[:, k, 0:n], in0=xt[:, k, 0:n], scalar=C.sm[:, gcol0 + k:gcol0 + k + 1],
                                                       in1=NR.rs[:, 0:n], op0=ALU.mult, op1=ALU.mult),
             reads=[xtb, NR.rsb, C.smb], writes=[xnb])


def smalls_layout(cfg):
    KC = cfg['D'] // 128
    off, o = {}, 0
    for name, n in [('ffn_g0', KC), ('ffn_g1', KC), ('mix_g', KC), ('mem_g', KC), ('memq_g', 2), ('memk_g', 2),
                    ('conv_w', KC * 4), ('conv_b', KC), ('gb', 4 * KC), ('lam', 2 * KC), ('qg', 3), ('kg', 3)]:
        off[name] = o
        o += n
    return off, o


def load_x(P, C, X, xt, xtb, t, N):
    src = X[:, :, t * N:(t + 1) * N].rearrange("k p n -> p k n")
    P.dma('pool', lambda e: e.dma_start(out=xt[:, :, :], in_=src), xtb, writes=[xtb])


def store_x(P, C, X, xt, xtb, t, N):
    dst = X[:, :, t * N:(t + 1) * N].rearrange("k p n -> p k n")
    P.dma('pool', lambda e: e.dma_start(out=dst, in_=xt[:, :, :]), xtb, reads=[xtb])


def phase_ffn(P, cfg, dr, Xs, Xd, l, i):
    D, S, FF = cfg['D'], cfg['S'], cfg['FF']
    KC, FC, N = D // 128, FF // 128, min(512, cfg['S'])
    SO, _ = smalls_layout(cfg)
    C = Ctx(P, cfg, dr)
    C.common(nring=4, smalls_layer=l)
    w_in, w_out = dr['ffn_w_in'][l * 2 + i], dr['ffn_w_out'][l * 2 + i]
    xt, xtb = C.sb("xt", [128, KC, N]), P.buf("xt", dma=True)
    xn, xnb = C.sb("xn", [128, KC, N]), P.buf("xn")
    h, hb = C.sb("h", [128, FC, N]), P.buf("h")
    NR = NormRes(C, N)
    psg = [C.ps("psg%d" % j, [128, 512]) for j in range(2)]
    psu = [C.ps("psu%d" % j, [128, 512]) for j in range(2)]
    pso = [C.ps("pso%d" % j, [128, 512]) for j in range(2)]
    psgb, psub, psob = [P.buf("g") for _ in range(2)], [P.buf("u") for _ in range(2)], [P.buf("o") for _ in range(2)]
    sg = [C.sb("sg%d" % j, [128, N]) for j in range(2)]
    sgb = [P.buf("sg") for _ in range(2)]
    kcn = max(k for k in range(1, 17) if FC % k == 0)
    for t in range(S // N):
        load_x(P, C, Xs, xt, xtb, t, N)
        rmsnorm_tile(P, C, NR, xt, xtb, xn, xnb, KC, N, SO['ffn_g%d' % i])
        acts = [(lambda n0, n1, k=k: xn[:, k, n0:n1]) for k in range(KC)]
        for j in range(FC):
            b = j % 2
            wt, wb = C.wload(w_in[j], KC)
            mm_acc(P, psg[b], psgb[b], N, wt, wb, acts, [xnb])
            wt, wb = C.wload(w_in[FC + j], KC)
            mm_acc(P, psu[b], psub[b], N, wt, wb, acts, [xnb])
            P.op('act', lambda e, b=b: e.activation(out=sg[b][:, 0:N], in_=psg[b][:, 0:N], func=AF.Silu), reads=[psgb[b]], writes=[sgb[b]])
            P.op('dve', lambda e, b=b, j=j: e.tensor_tensor(out=h[:, j, 0:N], in0=psu[b][:, 0:N], in1=sg[b][:, 0:N], op=ALU.mult),
                 reads=[psub[b], sgb[b]], writes=[hb])
        for m in range(KC):
            b = m % 2
            npc = FC // kcn
            for pc in range(npc):
                wt, wb = C.wload(w_out[m][:, pc * kcn:(pc + 1) * kcn, :], kcn)
                acts2 = [(lambda n0, n1, k=k: h[:, k, n0:n1]) for k in range(pc * kcn, (pc + 1) * kcn)]
                mm_acc(P, pso[b], psob[b], N, wt, wb, acts2, [hb], first=(pc == 0), last=(pc == npc - 1))
            P.op('dve', lambda e, b=b, m=m: e.scalar_tensor_tensor(out=xt[:, m, 0:N], in0=pso[b][:, 0:N], scalar=0.5, in1=xt[:, m, 0:N],
                                                               op0=ALU.mult, op1=ALU.add), reads=[psob[b], xtb], writes=[xtb])
        store_x(P, C, Xd, xt, xtb, t, N)
    C.close()


def phase_memkv(P, cfg, dr, l):
    D, M, MH = cfg['D'], cfg['M'], cfg['MH']
    KC, MW = D // 128, cfg['MH'] * 256
    MC = 2 * MW // 128
    SO, _ = smalls_layout(cfg)
    C = Ctx(P, cfg, dr)
    C.common(nring=4, smalls_layer=l)
    ident, identb = C.sb("ident", [128, 128]), P.buf("ident", dma=True)
    P.dma('pool', lambda e: e.dma_start(out=ident[:, :], in_=dr['ident']), identb, writes=[identb])
    mt, mtb = C.sb("mt", [128, KC, M]), P.buf("mt", dma=True)
    mn, mnb = C.sb("mn", [128, KC, M]), P.buf("mn")
    kv, kvb = C.sb("kv", [128, MC, M]), P.buf("kv")
    kn, knb = C.sb("kn", [128, MC // 2, M]), P.buf("kn", dma=True)
    vm, vmb = C.sb("vm", [128, M // 128, MW]), P.buf("vm", dma=True)
    NR = NormRes(C, M)
    ps = [C.ps("ps%d" % j, [128, 512]) for j in range(2)]
    psb = [P.buf("ps") for _ in range(2)]
    src = dr['memT'].rearrange("k p n -> p k n")
    P.dma('pool', lambda e: e.dma_start(out=mt[:, :, :], in_=src), mtb, writes=[mtb])
    rmsnorm_tile(P, C, NR, mt, mtb, mn, mnb, KC, M, SO['mem_g'])
    acts = [(lambda n0, n1, k=k: mn[:, k, n0:n1]) for k in range(KC)]
    for m in range(MC):
        b = m % 2
        wt, wb = C.wload(dr['mem_w_kv'][l][m], KC)
        mm_acc(P, ps[b], psb[b], M, wt, wb, acts, [mnb])
        P.op('act', lambda e, b=b, m=m: e.activation(out=kv[:, m, 0:M], in_=ps[b][:, 0:M], func=AF.Copy), reads=[psb[b]], writes=[kvb])
    for hh in range(MH):
        rms_rstd(P, C, [kv[:, 2 * hh + f, 0:M] for f in range(2)], [kvb], M, 256, NR.sq, NR.sqb, NR.psn, NR.psnb, NR.rs, NR.rsb)
        for f in range(2):
            c = SO['memk_g'] + f
            P.op('dve', lambda e, hh=hh, f=f, c=c: e.scalar_tensor_tensor(out=kn[:, 2 * hh + f, 0:M], in0=kv[:, 2 * hh + f, 0:M],
                                                                      scalar=C.sm[:, c:c + 1], in1=NR.rs[:, 0:M], op0=ALU.mult, op1=ALU.mult),
                 reads=[kvb, NR.rsb, C.smb], writes=[knb])
    for m in range(MC // 2):
        for tc in range(M // 128):
            b = (m * 2 + tc) % 2
            P.op('pe', lambda e, b=b, m=m, tc=tc: e.transpose(out=ps[b][:, 0:128], in_=kv[:, MC // 2 + m, tc * 128:(tc + 1) * 128], identity=ident[:, :]),
                 reads=[kvb, identb], writes=[psb[b]])
            P.op('act', lambda e, b=b, m=m, tc=tc: e.activation(out=vm[:, tc, m * 128:(m + 1) * 128], in_=ps[b][:, 0:128], func=AF.Copy),
                 reads=[psb[b]], writes=[vmb])
    P.dma('pool', lambda e: e.dma_start(out=dr['KN'], in_=kn[:, :, :]), knb, reads=[knb])
    P.dma('pool', lambda e: e.dma_start(out=dr['VM'], in_=vm[:, :, :]), vmb, reads=[vmb])
    C.close()


class MemAtt:
    def __init__(s, C, N):
        P, cfg, dr = C.P, C.cfg, C.dr
        s.C, s.N = C, N
        M, MH = cfg['M'], cfg['MH']
        MW = MH * 256
        s.kn, s.knb = C.sb("kn", [128, 2 * MH, M]), P.buf("kn", dma=True)
        s.vm, s.vmb = C.sb("vm", [128, M // 128, MW]), P.buf("vm", dma=True)
        P.dma('pool', lambda e: e.dma_start(out=s.kn[:, :, :], in_=dr['KN']), s.knb, writes=[s.knb])
        P.dma('pool', lambda e: e.dma_start(out=s.vm[:, :, :], in_=dr['VM']), s.vmb, writes=[s.vmb])
        s.qm, s.qmb = C.sb("qm", [128, 2 * MH, N]), P.buf("qm")
        s.qn, s.qnb = C.sb("qn", [128, 2, N]), P.buf("qn")
        s.pm = [C.sb("pm%d" % j, [128, N]) for j in range(M // 128)]
        s.pmb = [P.buf("pm") for _ in range(M // 128)]
        s.rden, s.rdenb = C.sb("rden", [128, N]), P.buf("rden")
        s.ym, s.ymb = C.sb("ym", [128, 2 * MH, N]), P.buf("ym", dma=True)

    def run(s, NR, psA, psAb, psB, psBb, t):
        C, N = s.C, s.N
        P, cfg, dr = C.P, C.cfg, C.dr
        M, MH = cfg['M'], cfg['MH']
        SO, _ = smalls_layout(cfg)
        MCk = M // 128
        for hh in range(MH):
            rms_rstd(P, C, [s.qm[:, 2 * hh + f, 0:N] for f in range(2)], [s.qmb], N, 256, NR.sq, NR.sqb, NR.psn, NR.psnb, NR.rs, NR.rsb)
            for f in range(2):
                c = SO['memq_g'] + f
                P.op('dve', lambda e, hh=hh, f=f, c=c: e.scalar_tensor_tensor(out=s.qn[:, f, 0:N], in0=s.qm[:, 2 * hh + f, 0:N], scalar=C.sm[:, c:c + 1],
                                                                          in1=NR.rs[:, 0:N], op0=ALU.mult, op1=ALU.mult),
                     reads=[s.qmb, NR.rsb, C.smb], writes=[s.qnb])
            for mc in range(MCk):
                for f in range(2):
                    P.op('pe', lambda e, hh=hh, f=f, mc=mc: e.matmul(psA[:, 0:N], lhsT=s.kn[:, 2 * hh + f, mc * 128:(mc + 1) * 128], rhs=s.qn[:, f, 0:N],
                                                                 start=(f == 0), stop=(f == 1)), reads=[s.knb, s.qnb], writes=[psAb])
                P.op('act', lambda e, mc=mc: e.activation(out=s.pm[mc][:, 0:N], in_=psA[:, 0:N], func=AF.Exp, scale=256.0 ** -0.5),
                     reads=[psAb], writes=[s.pmb[mc]])
            for mc in range(MCk):
                P.op('pe', lambda e, mc=mc: e.matmul(psB[:, 0:N], lhsT=C.ones[:, :], rhs=s.pm[mc][:, 0:N], start=(mc == 0), stop=(mc == MCk - 1)),
                     reads=[C.onesb, s.pmb[mc]], writes=[psBb])
            P.op('dve', lambda e: e.reciprocal(out=s.rden[:, 0:N], in_=psB[:, 0:N]), reads=[psBb], writes=[s.rdenb])
            for f in range(2):
                for mc in range(MCk):
                    c0 = (2 * hh + f) * 128
                    P.op('pe', lambda e, mc=mc, c0=c0: e.matmul(psA[:, 0:N], lhsT=s.vm[:, mc, c0:c0 + 128], rhs=s.pm[mc][:, 0:N],
                                                             start=(mc == 0), stop=(mc == MCk - 1)), reads=[s.vmb, s.pmb[mc]], writes=[psAb])
                P.op('dve', lambda e, hh=hh, f=f: e.tensor_tensor(out=s.ym[:, 2 * hh + f, 0:N], in0=psA[:, 0:N], in1=s.rden[:, 0:N], op=ALU.mult),
                     reads=[psAb, s.rdenb], writes=[s.ymb])
        dst = dr['YM'][:, :, t * N:(t + 1) * N].rearrange("k p n -> p k n")
        P.dma('pool', lambda e: e.dma_start(out=dst, in_=s.ym[:, :, :]), s.ymb, reads=[s.ymb])


def phase_rec_p1(P, cfg, dr, X, l, j):
    D, S, MH = cfg['D'], cfg['S'], cfg['MH']
    KC, N = D // 128, min(512, cfg['S'])
    SO, _ = smalls_layout(cfg)
    C = Ctx(P, cfg, dr)
    C.common(nring=4, smalls_layer=l)
    w_in = dr['rec_w_in'][j]
    xt, xtb = C.sb("xt", [128, KC, N]), P.buf("xt", dma=True)
    xn, xnb = C.sb("xn", [128, KC, N]), P.buf("xn")
    NR = NormRes(C, N)
    MA = MemAtt(C, N)
    ps = [C.ps("ps%d" % i, [128, 512]) for i in range(2)]
    psb = [P.buf("ps") for _ in range(2)]
    psA, psAb, psB, psBb = C.ps("psA", [128, 512]), P.buf("psA"), C.ps("psB", [128, 512]), P.buf("psB")
    stg = [C.sb("stg%d" % i, [128, N]) for i in range(3)]
    stgb = [P.buf("stg", dma=True) for _ in range(3)]
    t1, t1b = C.sb("t1", [128, N]), P.buf("t1")
    t2, t2b = C.sb("t2", [128, N]), P.buf("t2")
    zt, ztb = C.sb("zt", [128, KC, 2]), P.buf("zt", dma=True)
    P.op('pool', lambda e: e.memset(zt[:, :, :], 0.0), writes=[ztb])
    RB = dr['RB']
    nc_ = P.nc
    def _zp(e, a, b):
        with nc_.allow_non_contiguous_dma(reason="tiny zero pad"):
            return e.dma_start(out=RB[:, :, a:b].rearrange("k p n -> p k n"), in_=zt[:, :, 0:b - a])
    P.dma('pool', lambda e: _zp(e, 0, 2), ztb, reads=[ztb])
    P.dma('pool', lambda e: _zp(e, S + 2, S + 3), ztb, reads=[ztb])
    si = 0
    for t in range(S // N):
        load_x(P, C, X, xt, xtb, t, N)
        rmsnorm_tile(P, C, NR, xt, xtb, xn, xnb, KC, N, SO['mix_g'])
        acts = [(lambda n0, n1, k=k: xn[:, k, n0:n1]) for k in range(KC)]
        for m in range(2 * KC + 2 * MH):
            b = m % 2
            wt, wb = C.wload(w_in[m], KC)
            mm_acc(P, ps[b], psb[b], N, wt, wb, acts, [xnb])
            if m < KC:
                s_, sb_ = stg[si % 3], stgb[si % 3]
                si += 1
                P.op('act', lambda e, b=b: e.activation(out=t1[:, 0:N], in_=ps[b][:, 0:N], func=AF.Square), reads=[psb[b]], writes=[t1b])
                P.op('dve', lambda e: e.tensor_scalar(out=t1[:, 0:N], in0=t1[:, 0:N], scalar1=0.044715, scalar2=1.0, op0=ALU.mult, op1=ALU.add),
                     reads=[t1b], writes=[t1b])
                P.op('dve', lambda e, b=b: e.tensor_tensor(out=t1[:, 0:N], in0=ps[b][:, 0:N], in1=t1[:, 0:N], op=ALU.mult), reads=[psb[b], t1b], writes=[t1b])
                P.op('act', lambda e: e.activation(out=t2[:, 0:N], in_=t1[:, 0:N], func=AF.Sigmoid, scale=1.5957691216), reads=[t1b], writes=[t2b])
                P.op('dve', lambda e, b=b, s_=s_: e.tensor_tensor(out=s_[:, 0:N], in0=ps[b][:, 0:N], in1=t2[:, 0:N], op=ALU.mult),
                     reads=[psb[b], t2b], writes=[sb_])
                dst = dr['GG'][m][:, t * N:(t + 1) * N]
                P.dma('pool', lambda e, s_=s_, dst=dst: e.dma_start(out=dst, in_=s_[:, 0:N]), sb_, reads=[sb_])
            elif m < 2 * KC:
                s_, sb_ = stg[si % 3], stgb[si % 3]
                si += 1
                P.op('act', lambda e, b=b, s_=s_: e.activation(out=s_[:, 0:N], in_=ps[b][:, 0:N], func=AF.Copy), reads=[psb[b]], writes=[sb_])
                dst = RB[m - KC][:, 2 + t * N:2 + (t + 1) * N]
                P.dma('pool', lambda e, s_=s_, dst=dst: e.dma_start(out=dst, in_=s_[:, 0:N]), sb_, reads=[sb_])
            else:
                P.op('act', lambda e, b=b, m=m: e.activation(out=MA.qm[:, m - 2 * KC, 0:N], in_=ps[b][:, 0:N], func=AF.Copy), reads=[psb[b]], writes=[MA.qmb])
        MA.run(NR, psA, psAb, psB, psBb, t)
    C.close()


def phase_rec_p2(P, cfg, dr, l, j, d):
    D, S = cfg['D'], cfg['S']
    KC, NB = D // 128, D // 256
    T = min(2048, S)
    nseg = S // T
    SO, _ = smalls_layout(cfg)
    C = Ctx(P, cfg, dr)
    C.common(nring=4, smalls_layer=l)
    sm = C.sm
    cs, csb = C.sb("cs", [128, 4 * KC]), P.buf("cs")
    lo = SO['lam']
    P.op('act', lambda e: e.activation(out=cs[:, 0:2 * KC], in_=sm[:, lo:lo + 2 * KC], func=AF.Exp, scale=-1.0), reads=[C.smb], writes=[csb])
    P.op('act', lambda e: e.activation(out=cs[:, 0:2 * KC], in_=cs[:, 0:2 * KC], func=AF.Ln, bias=C.cst[:, 1:2], scale=1.0), reads=[csb, C.cstb], writes=[csb])
    P.op('dve', lambda e: e.tensor_scalar(out=cs[:, 2 * KC:4 * KC], in0=cs[:, 0:2 * KC], scalar1=-16.0, scalar2=None, op0=ALU.mult), reads=[csb], writes=[csb])
    P.op('dve', lambda e: e.tensor_scalar(out=cs[:, 0:2 * KC], in0=cs[:, 0:2 * KC], scalar1=-8.0, scalar2=None, op0=ALU.mult), reads=[csb], writes=[csb])
    rb, rbb = C.sb("rb", [128, 2, T + 3]), P.buf("rb", dma=True)
    xc, xcb = C.sb("xc", [128, 2, T]), P.buf("xc")
    names = ['rr', 'ii', 'aa', 'bb', 'hh']
    tl = {n: C.sb(n, [128, T]) for n in names}
    tb = {n: P.buf(n, dma=(n == 'hh')) for n in names}
    hf, hfb = C.sb("hf", [128, T]), P.buf("hf", dma=True)
    gg, ggb = C.sb("gg", [128, T]), P.buf("gg", dma=True)
    st, stb = C.sb("st", [128, 2]), P.buf("st")
    nb512 = max(1, T // 512)
    psr, psrb = C.ps("psr", [128, nb512 * 512]), P.buf("psr")
    psi, psib = C.ps("psi", [128, nb512 * 512]), P.buf("psi")
    gw = dr['gate_w'][j]
    rr, ii, aa, bb, hh = (tl[n] for n in names)
    for n in range(NB):
        for d in [d]:
            segs = list(range(nseg)) if d == 0 else list(range(nseg - 1, -1, -1))
            for si_, seg in enumerate(segs):
                src = dr['RB'][2 * n:2 * n + 2, :, seg * T:seg * T + T + 3].rearrange("k p n -> p k n")
                P.dma('pool', lambda e, src=src: e.dma_start(out=rb[:, :, :], in_=src), rbb, writes=[rbb])
                for c in range(2):
                    ch = 2 * n + c
                    w0, bcol = SO['conv_w'] + ch * 4, SO['conv_b'] + ch
                    P.op('dve', lambda e, c=c, w0=w0, bcol=bcol: e.tensor_scalar(out=xc[:, c, :], in0=rb[:, c, 0:T], scalar1=sm[:, w0:w0 + 1],
                                                                             scalar2=sm[:, bcol:bcol + 1], op0=ALU.mult, op1=ALU.add),
                         reads=[rbb, C.smb], writes=[xcb])
                    for jj in range(1, 4):
                        P.op('dve', lambda e, c=c, w0=w0, jj=jj: e.scalar_tensor_tensor(out=xc[:, c, :], in0=rb[:, c, jj:jj + T], scalar=sm[:, w0 + jj:w0 + jj + 1],
                                                                                    in1=xc[:, c, :], op0=ALU.mult, op1=ALU.add),
                             reads=[rbb, C.smb, xcb], writes=[xcb])
                for oc in range(2):
                    ch = 2 * n + oc
                    for gt, (pst, pstb, dst, dstb) in enumerate([(psr, psrb, rr, tb['rr']), (psi, psib, ii, tb['ii'])]):
                        wt, wb = C.wload(gw[d][gt][n][oc], 2)
                        mm_acc(P, pst, pstb, T, wt, wb, [(lambda n0, n1, k=k: xc[:, k, n0:n1]) for k in range(2)], [xcb])
                        bc = SO['gb'] + (d * 2 + gt) * KC + ch
                        P.op('act', lambda e, pst=pst, dst=dst, bc=bc: e.activation(out=dst[:, 0:T], in_=pst[:, 0:T], func=AF.Sigmoid, bias=sm[:, bc:bc + 1], scale=1.0),
                             reads=[pstb, C.smb], writes=[dstb])
                    c1, c2 = d * KC + ch, 2 * KC + d * KC + ch
                    P.op('act', lambda e, c1=c1: e.activation(out=aa[:, 0:T], in_=rr[:, 0:T], func=AF.Exp, scale=cs[:, c1:c1 + 1]), reads=[tb['rr'], csb], writes=[tb['aa']])
                    P.op('act', lambda e, c2=c2: e.activation(out=bb[:, 0:T], in_=rr[:, 0:T], func=AF.Exp, scale=cs[:, c2:c2 + 1]), reads=[tb['rr'], csb], writes=[tb['bb']])
                    P.op('act', lambda e: e.activation(out=bb[:, 0:T], in_=bb[:, 0:T], func=AF.Sqrt, bias=C.cst[:, 1:2], scale=-1.0), reads=[tb['bb'], C.cstb], writes=[tb['bb']])
                    P.op('dve', lambda e: e.tensor_tensor(out=bb[:, 0:T], in0=bb[:, 0:T], in1=ii[:, 0:T], op=ALU.mult), reads=[tb['bb'], tb['ii']], writes=[tb['bb']])
                    P.op('dve', lambda e, oc=oc: e.tensor_tensor(out=bb[:, 0:T], in0=bb[:, 0:T], in1=xc[:, oc, :], op=ALU.mult), reads=[tb['bb'], xcb], writes=[tb['bb']])
                    init = 0.0 if si_ == 0 else st[:, oc:oc + 1]
                    if d == 0:
                        P.op('dve', lambda e, init=init: e.tensor_tensor_scan(out=hh[:, 0:T], data0=aa[:, 0:T], data1=bb[:, 0:T], initial=init, op0=ALU.mult, op1=ALU.add),
                             reads=[tb['aa'], tb['bb'], stb], writes=[tb['hh']])
                        P.op('dve', lambda e, oc=oc: e.tensor_copy(out=st[:, oc:oc + 1], in_=hh[:, T - 1:T]), reads=[tb['hh']], writes=[stb])
                        dst = dr['HF'][ch][:, seg * T:(seg + 1) * T]
                        P.dma('pool', lambda e, dst=dst: e.dma_start(out=dst, in_=hh[:, 0:T]), tb['hh'], reads=[tb['hh']])
                    else:
                        s1 = dr['HF'][ch][:, seg * T:(seg + 1) * T]
                        s2 = dr['GG'][ch][:, seg * T:(seg + 1) * T]
                        P.dma('pool', lambda e, s1=s1: e.dma_start(out=hf[:, 0:T], in_=s1), hfb, writes=[hfb])
                        P.dma('pool', lambda e, s2=s2: e.dma_start(out=gg[:, 0:T], in_=s2), ggb, writes=[ggb])
                        P.op('dve', lambda e, init=init: e.tensor_tensor_scan(out=hh[:, ::-1], data0=aa[:, ::-1], data1=bb[:, ::-1], initial=init, op0=ALU.mult, op1=ALU.add),
                             reads=[tb['aa'], tb['bb'], stb], writes=[tb['hh']])
                        P.op('dve', lambda e, oc=oc: e.tensor_copy(out=st[:, oc:oc + 1], in_=hh[:, 0:1]), reads=[tb['hh']], writes=[stb])
                        P.op('dve', lambda e: e.tensor_tensor(out=hh[:, 0:T], in0=hh[:, 0:T], in1=hf[:, 0:T], op=ALU.add), reads=[tb['hh'], hfb], writes=[tb['hh']])
                        P.op('dve', lambda e: e.tensor_tensor(out=hh[:, 0:T], in0=hh[:, 0:T], in1=gg[:, 0:T], op=ALU.mult), reads=[tb['hh'], ggb], writes=[tb['hh']])
                        dst = dr['YR'][ch][:, seg * T:(seg + 1) * T]
                        P.dma('pool', lambda e, dst=dst: e.dma_start(out=dst, in_=hh[:, 0:T]), tb['hh'], reads=[tb['hh']])
    C.close()


def phase_wout(P, cfg, dr, X, Xd, w_out, YMAIN, nmain):
    D, S, MH = cfg['D'], cfg['S'], cfg['MH']
    KC, N = D // 128, min(512, cfg['S'])
    YC = nmain + 2 * MH
    C = Ctx(P, cfg, dr)
    C.common(nring=4)
    xt, xtb = C.sb("xt", [128, KC, N]), P.buf("xt", dma=True)
    yt, ytb = C.sb("yt", [128, YC, N]), P.buf("yt", dma=True)
    pso = [C.ps("pso%d" % j, [128, 512]) for j in range(2)]
    psob = [P.buf("o") for _ in range(2)]
    kcn = max(k for k in range(1, 17) if YC % k == 0)
    for t in range(S // N):
        load_x(P, C, X, xt, xtb, t, N)
        s1 = YMAIN[:, :, t * N:(t + 1) * N].rearrange("k p n -> p k n")
        s2 = dr['YM'][:, :, t * N:(t + 1) * N].rearrange("k p n -> p k n")
        P.dma('pool', lambda e, s1=s1: e.dma_start(out=yt[:, 0:nmain, :], in_=s1), ytb, writes=[ytb])
        P.dma('pool', lambda e, s2=s2: e.dma_start(out=yt[:, nmain:YC, :], in_=s2), ytb, writes=[ytb])
        for m in range(KC):
            b = m % 2
            npc = YC // kcn
            for pc in range(npc):
                wt, wb = C.wload(w_out[m][:, pc * kcn:(pc + 1) * kcn, :], kcn)
                acts2 = [(lambda n0, n1, k=k: yt[:, k, n0:n1]) for k in range(pc * kcn, (pc + 1) * kcn)]
                mm_acc(P, pso[b], psob[b], N, wt, wb, acts2, [ytb], first=(pc == 0), last=(pc == npc - 1))
            P.op('dve', lambda e, b=b, m=m: e.tensor_tensor(out=xt[:, m, 0:N], in0=pso[b][:, 0:N], in1=xt[:, m, 0:N], op=ALU.add),
                 reads=[psob[b], xtb], writes=[xtb])
        store_x(P, C, Xd, xt, xtb, t, N)
    C.close()


def phase_att_p1(P, cfg, dr, X, l, j):
    D, S, MH, AH = cfg['D'], cfg['S'], cfg['MH'], cfg['AH']
    KC, N = D // 128, min(512, cfg['S'])
    SO, _ = smalls_layout(cfg)
    C = Ctx(P, cfg, dr)
    C.common(nring=4, smalls_layer=l)
    ident, identb = C.sb("ident", [128, 128]), P.buf("ident", dma=True)
    P.dma('pool', lambda e: e.dma_start(out=ident[:, :], in_=dr['ident']), identb, writes=[identb])
    w_in = dr['att_w_in'][j]
    xt, xtb = C.sb("xt", [128, KC, N]), P.buf("xt", dma=True)
    xn, xnb = C.sb("xn", [128, KC, N]), P.buf("xn")
    NR = NormRes(C, N)
    NR2 = NormRes(C, N, "b")
    MA = MemAtt(C, N)
    ps = [C.ps("ps%d" % i, [128, 512]) for i in range(2)]
    psb = [P.buf("ps") for _ in range(2)]
    psA, psAb, psB, psBb = C.ps("psA", [128, 512]), P.buf("psA"), C.ps("psB", [128, 512]), P.buf("psB")
    stg = [C.sb("stg%d" % i, [128, N]) for i in range(3)]
    stgb = [P.buf("stg", dma=True) for _ in range(3)]
    vs, vsb = C.sb("vs", [128, N]), P.buf("vs")
    vt, vtb = C.sb("vt", [128, N // 128, 128]), P.buf("vt", dma=True)
    si = 0
    nqkv = 9 * AH
    for t in range(S // N):
        load_x(P, C, X, xt, xtb, t, N)
        rmsnorm_tile(P, C, NR, xt, xtb, xn, xnb, KC, N, SO['mix_g'])
        acts = [(lambda n0, n1, k=k: xn[:, k, n0:n1]) for k in range(KC)]
        for m in range(nqkv + 2 * MH):
            b = m % 2
            wt, wb = C.wload(w_in[m], KC)
            mm_acc(P, ps[b], psb[b], N, wt, wb, acts, [xnb])
            if m >= nqkv:
                P.op('act', lambda e, b=b, m=m: e.activation(out=MA.qm[:, m - nqkv, 0:N], in_=ps[b][:, 0:N], func=AF.Copy), reads=[psb[b]], writes=[MA.qmb])
                continue
            g, kind, hd_ = m // (3 * AH), (m % (3 * AH)) // AH, m % AH
            if kind < 2:
                s_, sb_ = stg[si % 3], stgb[si % 3]
                si += 1
                rms_rstd(P, C, [ps[b][:, 0:N]], [psb[b]], N, 128, NR2.sq, NR2.sqb, NR2.psn, NR2.psnb, NR2.rs, NR2.rsb)
                gc = SO['qg' if kind == 0 else 'kg'] + g
                P.op('dve', lambda e, b=b, s_=s_, gc=gc: e.scalar_tensor_tensor(out=s_[:, 0:N], in0=ps[b][:, 0:N], scalar=C.sm[:, gc:gc + 1], in1=NR2.rs[:, 0:N],
                                                                            op0=ALU.mult, op1=ALU.mult), reads=[psb[b], NR2.rsb, C.smb], writes=[sb_])
                dst = dr['QK'][(g * 2 + kind) * AH + hd_][:, t * N:(t + 1) * N]
                P.dma('pool', lambda e, s_=s_, dst=dst: e.dma_start(out=dst, in_=s_[:, 0:N]), sb_, reads=[sb_])
            else:
                P.op('act', lambda e, b=b: e.activation(out=vs[:, 0:N], in_=ps[b][:, 0:N], func=AF.Copy), reads=[psb[b]], writes=[vsb])
                for bi in range(N // 128):
                    P.op('pe', lambda e, bi=bi: e.transpose(out=psA[:, 0:128], in_=vs[:, bi * 128:(bi + 1) * 128], identity=ident[:, :]),
                         reads=[vsb, identb], writes=[psAb])
                    P.op('act', lambda e, bi=bi: e.activation(out=vt[:, bi, :], in_=psA[:, 0:128], func=AF.Copy), reads=[psAb], writes=[vtb])
                dst = dr['V'][g * AH + hd_][t * N:(t + 1) * N, :].rearrange("(c p) d -> p c d", p=128)
                P.dma('pool', lambda e, dst=dst: e.dma_start(out=dst, in_=vt[:, :, :]), vtb, reads=[vtb])
        MA.run(NR, psA, psAb, psB, psBb, t)
    C.close()


def alibi_slope(cfg, g, h):
    n = 3 * cfg['AH']
    return float(np.exp2(np.float32(-8.0) * (np.float32(g * cfg['AH'] + h) + np.float32(1.0)) / np.float32(n)))


def phase_att_p2(P, cfg, dr):
    S, AH = cfg['S'], cfg['AH']
    C = Ctx(P, cfg, dr)
    C.common(nring=0)
    rel, relb = C.sb("rel", [128, 6, 512]), P.buf("rel", dma=True)
    P.dma('pool', lambda e: e.dma_start(out=rel[:, :, :], in_=dr['c_rel'].rearrange("c p n -> p c n")), relb, writes=[relb])
    bt, btb = C.sb("bt", [128, 6, 512]), P.buf("bt")
    qT, qTb = C.sb("qT", [128, S]), P.buf("qT", dma=True)
    kT, kTb = C.sb("kT", [128, S]), P.buf("kT", dma=True)
    vc, vcb = C.sb("vc", [128, S // 128, 128]), P.buf("vc", dma=True)
    aO, aOb = C.sb("aO", [128, S]), P.buf("aO", dma=True)
    aD, aDb = C.sb("aD", [128, S]), P.buf("aD")
    tmp = [C.sb("tmp%d" % i, [128, 512]) for i in range(2)]
    tmpb = [P.buf("tmp") for _ in range(2)]
    pt = [C.sb("pt%d" % i, [128, 512]) for i in range(2)]
    ptb = [P.buf("pt") for _ in range(2)]
    psS = [C.ps("psS%d" % i, [128, 512]) for i in range(2)]
    psSb = [P.buf("psS") for _ in range(2)]
    psO = [C.ps("psO%d" % i, [128, 512]) for i in range(2)]
    psOb = [P.buf("psO") for _ in range(2)]
    psD = [C.ps("psD%d" % i, [128, 512]) for i in range(2)]
    psDb = [P.buf("psD") for _ in range(2)]
    it = 0
    qi = 0
    for h in range(AH):
        for g, (win, d) in enumerate(cfg['PAT']):
            L = S // d
            QT = min(512, L)
            nck = L // 128
            P.dma('pool', lambda e, g=g, h=h: e.dma_start(out=qT[:, :], in_=dr['QK'][(g * 2 + 0) * AH + h]), qTb, writes=[qTb])
            P.dma('pool', lambda e, g=g, h=h: e.dma_start(out=kT[:, :], in_=dr['QK'][(g * 2 + 1) * AH + h]), kTb, writes=[kTb])
            Vd = dr['V'][g * AH + h].rearrange("(c i r) e -> r i c e", i=128, r=d)
            for r in range(d):
                P.dma('pool', lambda e, r=r, Vd=Vd, nck=nck: e.dma_start(out=vc[:, r * nck:(r + 1) * nck, :], in_=Vd[r]), vcb, writes=[vcb])
            sl = -alibi_slope(cfg, g, h) * d
            for ci in range(6):
                P.op('dve', lambda e, ci=ci, sl=sl: e.tensor_scalar(out=bt[:, ci, :], in0=rel[:, ci, :], scalar1=sl, scalar2=None, op0=ALU.mult),
                     reads=[relb], writes=[btb])
            for r in range(d):
                for j0 in range(0, L, QT):
                    qb = qi % 2
                    qi += 1
                    qap = qT[:, r + d * j0: r + d * (j0 + QT - 1) + 1: d]
                    cis = [ci for ci in range(QT // 128 + 2) if 0 <= j0 - 128 + 128 * ci and j0 - 128 + 128 * ci + 128 <= L]
                    for idx, ci in enumerate(cis):
                        kj = j0 - 128 + 128 * ci
                        b = it % 2
                        it += 1
                        kap = kT[:, r + d * kj: r + d * (kj + 127) + 1: d]
                        P.op('pe', lambda e, b=b, kap=kap, qap=qap, QT=QT: e.matmul(psS[b][:, 0:QT], lhsT=kap, rhs=qap, start=True, stop=True),
                             reads=[kTb, qTb], writes=[psSb[b]])
                        P.op('dve', lambda e, b=b, ci=ci, QT=QT: e.scalar_tensor_tensor(out=tmp[b][:, 0:QT], in0=psS[b][:, 0:QT], scalar=128.0 ** -0.5, in1=bt[:, ci, 0:QT],
                                                                             op0=ALU.mult, op1=ALU.add), reads=[psSb[b], btb], writes=[tmpb[b]])
                        P.op('act', lambda e, b=b, QT=QT: e.activation(out=pt[b][:, 0:QT], in_=tmp[b][:, 0:QT], func=AF.Exp), reads=[tmpb[b]], writes=[ptb[b]])
                        vck = r * nck + kj // 128
                        P.op('pe', lambda e, b=b, qb=qb, vck=vck, idx=idx, QT=QT, nci=len(cis): e.matmul(psO[qb][:, 0:QT], lhsT=vc[:, vck, :], rhs=pt[b][:, 0:QT],
                                                                                 start=(idx == 0), stop=(idx == nci - 1)), reads=[vcb, ptb[b]], writes=[psOb[qb]])
                        P.op('pe', lambda e, b=b, qb=qb, idx=idx, QT=QT, nci=len(cis): e.matmul(psD[qb][:, 0:QT], lhsT=C.ones[:, :], rhs=pt[b][:, 0:QT],
                                                                        start=(idx == 0), stop=(idx == nci - 1)), reads=[C.onesb, ptb[b]], writes=[psDb[qb]])
                    oap = aO[:, r + d * j0: r + d * (j0 + QT - 1) + 1: d]
                    dap = aD[:, r + d * j0: r + d * (j0 + QT - 1) + 1: d]
                    if g == 0:
                        P.op('act', lambda e, qb=qb, oap=oap, QT=QT: e.activation(out=oap, in_=psO[qb][:, 0:QT], func=AF.Copy), reads=[psOb[qb]], writes=[aOb])
                        P.op('dve', lambda e, qb=qb, dap=dap, QT=QT: e.tensor_copy(out=dap, in_=psD[qb][:, 0:QT]), reads=[psDb[qb]], writes=[aDb])
                    else:
                        P.op('dve', lambda e, qb=qb, oap=oap, QT=QT: e.tensor_tensor(out=oap, in0=psO[qb][:, 0:QT], in1=oap, op=ALU.add), reads=[psOb[qb], aOb], writes=[aOb])
                        P.op('dve', lambda e, qb=qb, dap=dap, QT=QT: e.tensor_tensor(out=dap, in0=psD[qb][:, 0:QT], in1=dap, op=ALU.add), reads=[psDb[qb], aDb], writes=[aDb])
        P.op('dve', lambda e: e.reciprocal(out=aD[:, :], in_=aD[:, :]), reads=[aDb], writes=[aDb])
        P.op('dve', lambda e: e.tensor_tensor(out=aO[:, :], in0=aO[:, :], in1=aD[:, :], op=ALU.mult), reads=[aOb, aDb], writes=[aOb])
        P.dma('pool', lambda e, h=h: e.dma_start(out=dr['YA'][h], in_=aO[:, :]), aOb, reads=[aOb])
    C.close()


def build(cfg):
    D, S, FF, L, MH, AH, M = cfg['D'], cfg['S'], cfg['FF'], cfg['DEPTH'], cfg['MH'], cfg['AH'], cfg['M']
    KC, FC = D // 128, FF // 128
    NREC, NATT = (L + 1) // 2, L // 2
    MW = MH * 256
    _, NS = smalls_layout(cfg)
    nc = bass.Bass("TRN2", target_bir_lowering=False)
    dr = {}

    def din(name, shape):
        dr[name] = nc.dram_tensor(name, list(shape), F32, kind="ExternalInput").ap()

    def dsc(name, shape):
        dr[name] = nc.dram_tensor(name, list(shape), F32, kind="ExternalOutput" if cfg.get('dbg') else "Internal").ap()
    din('xT', [KC, 128, S])
    din('memT', [KC, 128, M])
    din('smalls', [L, 128, NS])
    din('ident', [128, 128])
    din('c_rel', [6, 128, 512])
    din('c_negm', [6, 128, 512])
    din('ffn_w_in', [L * 2, 2 * FC, 128, KC, 128])
    din('ffn_w_out', [L * 2, KC, 128, FC, 128])
    din('mem_w_kv', [L, 2 * MW // 128, 128, KC, 128])
    din('rec_w_in', [NREC, 2 * KC + 2 * MH, 128, KC, 128])
    din('rec_w_out', [NREC, KC, 128, KC + 2 * MH, 128])
    din('gate_w', [NREC, 2, 2, D // 256, 2, 128, 2, 128])
    if NATT:
        din('att_w_in', [NATT, 9 * AH + 2 * MH, 128, KC, 128])
        din('att_w_out', [NATT, KC, 128, AH + 2 * MH, 128])
    dr['out'] = nc.dram_tensor('out', [KC, 128, S], F32, kind="ExternalOutput").ap()
    dsc('KN', [128, 2 * MH, M])
    dsc('VM', [128, M // 128, MW])
    dsc('YM', [2 * MH, 128, S])
    dsc('GG', [KC, 128, S])
    dsc('RB', [KC, 128, S + 3])
    dsc('HF', [KC, 128, S])
    dsc('YR', [KC, 128, S])
    dsc('QK', [6 * AH, 128, S])
    dsc('V', [3 * AH, S, 128])
    dsc('YA', [AH, 128, S])
    P = Prog(nc)
    X = dr['out']
    stop = cfg.get('stop')
    for l in range(L):
        phase_ffn(P, cfg, dr, dr['xT'] if l == 0 else X, X, l, 0)
        if stop == 'ffn0':
            break
        phase_memkv(P, cfg, dr, l)
        j = l // 2
        if l % 2 == 0:
            phase_rec_p1(P, cfg, dr, X, l, j)
            phase_rec_p2(P, cfg, dr, l, j, 0)
            phase_rec_p2(P, cfg, dr, l, j, 1)
            phase_wout(P, cfg, dr, X, X, dr['rec_w_out'][j], dr['YR'], KC)
            if stop == 'mix0':
                break
        else:
            phase_att_p1(P, cfg, dr, X, l, j)
            phase_att_p2(P, cfg, dr)
            phase_wout(P, cfg, dr, X, X, dr['att_w_out'][j], dr['YA'], AH)
        phase_ffn(P, cfg, dr, X, X, l, 1)
    return nc


def pieces(W):
    K, Mo = W.shape[-2], W.shape[-1]
    lead = W.shape[:-2]
    Wr = W.reshape(*lead, K // 128, 128, Mo // 128, 128)
    nl = len(lead)
    perm = list(range(nl)) + [nl + 2, nl + 1, nl + 0, nl + 3]
    return np.ascontiguousarray(Wr.transpose(perm))


def cols(v):
    n = v.shape[-1] // 128
    return np.swapaxes(v.reshape(*v.shape[:-1], n, 128), -1, -2)


def const_tables():
    k = np.arange(128)[:, None]
    q = np.arange(512)[None, :]
    rel = np.stack([np.abs(128 * (ci - 1) + k - q) for ci in range(6)]).astype(np.float32)
    rel = np.where(rel <= 64, rel, 1.0e6).astype(np.float32)
    return rel, rel


def prep_inputs(cfg, inp):
    D, S, L = cfg['D'], cfg['S'], cfg['DEPTH']
    KC = D // 128
    SO, NS = smalls_layout(cfg)
    f = lambda a: np.asarray(a, dtype=np.float32)
    sm = np.zeros((L, 128, NS), np.float32)
    for l in range(L):
        j = l // 2
        sm[l, :, SO['ffn_g0']:SO['ffn_g0'] + KC] = cols(f(inp['ffn_norm'])[l, 0])
        sm[l, :, SO['ffn_g1']:SO['ffn_g1'] + KC] = cols(f(inp['ffn_norm'])[l, 1])
        sm[l, :, SO['mix_g']:SO['mix_g'] + KC] = cols(f(inp['mix_norm'])[l])
        sm[l, :, SO['mem_g']:SO['mem_g'] + KC] = cols(f(inp['mem_norm'])[l])
        sm[l, :, SO['memq_g']:SO['memq_g'] + 2] = cols(f(inp['mem_qk_gain'])[l, 0])
        sm[l, :, SO['memk_g']:SO['memk_g'] + 2] = cols(f(inp['mem_qk_gain'])[l, 1])
        if l % 2 == 0:
            cw = f(inp['rec_conv_w'])[j]
            sm[l, :, SO['conv_w']:SO['conv_w'] + 4 * KC] = np.stack([cols(cw[t]) for t in range(4)], -1).reshape(128, 4 * KC)
            sm[l, :, SO['conv_b']:SO['conv_b'] + KC] = cols(f(inp['rec_conv_b'])[j])
            gb = f(inp['rec_gate_b'])[j].reshape(2, 2, D)
            for d in range(2):
                for gt in range(2):
                    o = SO['gb'] + (d * 2 + gt) * KC
                    sm[l, :, o:o + KC] = cols(gb[d, gt])
                o = SO['lam'] + d * KC
                sm[l, :, o:o + KC] = cols(f(inp['rec_lambda'])[j, d])
        else:
            qk = f(inp['att_qk_gain'])[j]
            sm[l, :, SO['qg']:SO['qg'] + 3] = qk[0].T
            sm[l, :, SO['kg']:SO['kg'] + 3] = qk[1].T
    rel, negm = const_tables()
    shared = {
        'smalls': sm, 'ident': np.eye(128, dtype=np.float32), 'c_rel': rel, 'c_negm': negm,
        'ffn_w_in': pieces(f(inp['ffn_w_in'])).reshape(L * 2, -1, 128, KC, 128),
        'ffn_w_out': pieces(f(inp['ffn_w_out'])).reshape(L * 2, KC, 128, -1, 128),
        'mem_w_kv': pieces(f(inp['mem_w_kv'])),
        'rec_w_in': pieces(f(inp['rec_w_in'])),
        'rec_w_out': pieces(f(inp['rec_w_out'])),
        'gate_w': pieces(f(inp['rec_gate_w'])),
    }
    if L // 2:
        shared['att_w_in'] = pieces(f(inp['att_w_in']))
        shared['att_w_out'] = pieces(f(inp['att_w_out']))
    x, mem = f(inp['x']), f(inp['mem'])
    B = x.shape[0]
    maps = []
    for b in range(B):
        m = dict(shared)
        m['xT'] = np.ascontiguousarray(x[b].T).reshape(KC, 128, S)
        m['memT'] = np.ascontiguousarray(mem[b].T).reshape(KC, 128, -1)
        maps.append(m)
    return maps


def run(cfg, inp):
    nc = build(cfg)
    maps = prep_inputs(cfg, inp)
    B = len(maps)
    res = run_bass_kernel_spmd(nc, maps, core_ids=list(range(B)))
    D, S = cfg['D'], cfg['S']
    if cfg.get('dbg'):
        cfg['dbg_out'] = res.results
    return np.stack([np.ascontiguousarray(res.results[b]['out'].reshape(D, S).T) for b in range(B)]).astype(np.float32)


def kernel(**inputs):
    return run(FULL, inputs)
```

```python
import numpy as np
from contextlib import ExitStack
import concourse.bass as bass
import concourse.mybir as mybir
from concourse.bass_utils import run_bass_kernel_spmd

F32 = mybir.dt.float32
BF16 = mybir.dt.bfloat16
BIG = ['ffn_w_in', 'ffn_w_out', 'rec_w_in', 'rec_w_out', 'att_w_in', 'att_w_out']
AF = mybir.ActivationFunctionType
ALU = mybir.AluOpType
ENG = ['pe', 'act', 'dve', 'pool', 'sp']

FULL = dict(D=2048, S=8192, FF=5632, DEPTH=4, MH=4, AH=8, M=256, PAT=((128, 1), (512, 4), (2048, 16)))


class Buf:
    def __init__(s, name, di=None):
        s.name, s.lw, s.rd, s.di = name, None, {}, di


class Prog:
    def __init__(s, nc, ndsem=40):
        s.nc = nc
        s.es = ExitStack()
        s.eng = {'pe': nc.tensor, 'act': nc.scalar, 'dve': nc.vector, 'pool': nc.gpsimd, 'sp': nc.sync}
        s.esem = {e: s.es.enter_context(nc.semaphore("es_" + e)) for e in ENG}
        s.ecnt = {e: 0 for e in ENG}
        s.dsem = [s.es.enter_context(nc.semaphore("ds%d" % i)) for i in range(ndsem)]
        s.dcnt = [0] * ndsem
        s.q = {e: [] for e in ENG}
        s.seen = {e: {} for e in ENG}
        s.nextd = 0

    def buf(s, name, dma=False):
        di = None
        if dma:
            di = s.nextd % len(s.dsem)
            s.nextd += 1
        return Buf(name, di)

    def _sem(s, key):
        return s.esem[key[1]] if key[0] == 'e' else s.dsem[key[1]]

    def _waits(s, eng, evs):
        out = []
        for ev in evs:
            if ev is None:
                continue
            key, val = ev
            if key == ('e', eng) and eng == 'pe':
                continue
            if s.seen[eng].get(key, 0) >= val:
                continue
            s.seen[eng][key] = val
            out.append(ev)
        return out

    def _deps(s, reads, writes):
        evs = [b.lw for b in reads]
        for b in writes:
            evs.append(b.lw)
            evs += list(b.rd.items())
        return evs

    def _upd(s, ev, reads, writes):
        for b in reads:
            b.rd[ev[0]] = max(b.rd.get(ev[0], 0), ev[1])
        for b in writes:
            b.lw = ev
            b.rd = {}

    def op(s, eng, fn, reads=(), writes=()):
        w = s._waits(eng, s._deps(reads, writes))
        s.ecnt[eng] += 1
        ev = (('e', eng), s.ecnt[eng])
        s.q[eng].append((fn, w, ('e', eng), 1))
        s._upd(ev, reads, writes)
        return ev

    def dma(s, eng, fn, sb, reads=(), writes=()):
        di = sb.di
        evs = s._deps(reads, writes)
        if s.dcnt[di] > 0:
            evs.append((('d', di), s.dcnt[di]))
        w = s._waits(eng, evs)
        s.dcnt[di] += 16
        ev = (('d', di), s.dcnt[di])
        s.q[eng].append((fn, w, ('d', di), 16))
        s._upd(ev, reads, writes)
        return ev

    def barrier(s):
        for e in ENG:
            evs = [(('e', e2), s.ecnt[e2]) for e2 in ENG if e2 != e and s.ecnt[e2] > 0]
            evs += [(('d', i), c) for i, c in enumerate(s.dcnt) if c > 0]
            s.q[e].append((None, s._waits(e, evs), None, 0))

    def flush(s):
        s.barrier()
        with s.nc.Block() as block:
            def mk(e):
                def run(eo):
                    for (fn, w, key, inc) in s.q[e]:
                        for (k, v) in w:
                            eo.wait_ge(s._sem(k), v)
                        if fn is not None:
                            fn(eo).then_inc(s._sem(key), inc)
                return run
            block.tensor(mk('pe'))
            block.scalar(mk('act'))
            block.vector(mk('dve'))
            block.gpsimd(mk('pool'))
            block.sync(mk('sp'))
        s.q = {e: [] for e in ENG}


class Ctx:
    def __init__(s, P, cfg, dr):
        s.P, s.cfg, s.dr, s.nc = P, cfg, dr, P.nc
        s.es = ExitStack()
        s.wi = 0

    _uid = [0]

    def sb(s, name, shape, dt=F32):
        Ctx._uid[0] += 1
        return s.es.enter_context(s.nc.sbuf_tensor("%s_u%d" % (name, Ctx._uid[0]), list(shape), dt))

    def ps(s, name, shape):
        Ctx._uid[0] += 1
        return s.es.enter_context(s.nc.psum_tensor("%s_u%d" % (name, Ctx._uid[0]), list(shape), F32))

    def common(s, nring=4, smalls_layer=None, wdt=F32):
        P = s.P
        s.ring = [s.sb("wr%d" % i, [128, 2048], wdt) for i in range(nring)]
        s.ringb = [P.buf("wr%d" % i, dma=True) for i in range(nring)]
        s.ones = s.sb("ones", [128, 128])
        s.onesb = P.buf("ones")
        s.cst = s.sb("cst", [128, 4])
        s.cstb = P.buf("cst")
        o, c = s.ones, s.cst
        P.op('pool', lambda e: e.memset(o[:, :], 1.0), writes=[s.onesb])
        P.op('pool', lambda e: e.memset(c[:, 0:1], 1e-6), writes=[s.cstb])
        P.op('pool', lambda e: e.memset(c[:, 1:2], 1.0), writes=[s.cstb])
        if smalls_layer is not None:
            ns = s.dr['smalls'].shape[2]
            s.sm = s.sb("smalls", [128, ns])
            s.smb = P.buf("smalls", dma=True)
            sm, src = s.sm, s.dr['smalls'][smalls_layer]
            P.dma('pool', lambda e: e.dma_start(out=sm[:, :], in_=src), s.smb, writes=[s.smb])

    def wload(s, src, kcn):
        i = s.wi % len(s.ring)
        s.wi += 1
        t, b = s.ring[i], s.ringb[i]
        s.P.dma('sp', lambda e: e.dma_start(out=t[:, 0:kcn * 128], in_=src.rearrange("p k c -> p (k c)")), b, writes=[b])
        return t, b

    def close(s):
        s.P.flush()
        s.es.close()


def mm_acc(P, ps, psb, n, wt, wb, acts, actb, first=True, last=True):
    K = len(acts)
    for n0 in range(0, n, 512):
        n1 = min(n, n0 + 512)
        for k in range(K):
            a = acts[k]
            pb = psb[n0 // 512] if isinstance(psb, list) else psb
            P.op('pe', lambda e, k=k, a=a, n0=n0, n1=n1: e.matmul(ps[:, n0:n1], lhsT=wt[:, k * 128:(k + 1) * 128], rhs=a(n0, n1),
                                                            start=(first and k == 0), stop=(last and k == K - 1)),
                 reads=[wb] + actb, writes=[pb])


def rms_rstd(P, C, srcs, srcb, n, dim, sqs, sqb, psn, psnb, rs, rsb):
    K = len(srcs)
    for k in range(K):
        sq, sb_ = sqs[k % 2], sqb[k % 2]
        a = srcs[k]
        P.op('act', lambda e, sq=sq, a=a: e.activation(out=sq[:, 0:n], in_=a, func=AF.Square), reads=srcb, writes=[sb_])
        for n0 in range(0, n, 512):
            n1 = min(n, n0 + 512)
            P.op('pe', lambda e, sq=sq, k=k, n0=n0, n1=n1: e.matmul(psn[:, n0:n1], lhsT=C.ones[:, :], rhs=sq[:, n0:n1],
                                                                start=(k == 0), stop=(k == K - 1)),
                 reads=[sb_, C.onesb], writes=[psnb])
    P.op('act', lambda e: e.activation(out=rs[:, 0:n], in_=psn[:, 0:n], func=AF.Sqrt, bias=C.cst[:, 0:1], scale=1.0 / dim),
         reads=[psnb, C.cstb], writes=[rsb])
    P.op('dve', lambda e: e.reciprocal(out=rs[:, 0:n], in_=rs[:, 0:n]), reads=[rsb], writes=[rsb])


class NormRes:
    def __init__(s, C, n, tag=""):
        P = C.P
        s.sq = [C.sb("nsq0" + tag, [128, n]), C.sb("nsq1" + tag, [128, n])]
        s.sqb = [P.buf("nsq0"), P.buf("nsq1")]
        s.psn = C.ps("npsn" + tag, [128, max(n, 512)])
        s.psnb = P.buf("npsn")
        s.rs = C.sb("nrs" + tag, [128, n])
        s.rsb = P.buf("nrs")


def rmsnorm_tile(P, C, NR, xt, xtb, xn, xnb, KC, n, gcol0):
    rms_rstd(P, C, [xt[:, k, 0:n] for k in range(KC)], [xtb], n, KC * 128, NR.sq, NR.sqb, NR.psn, NR.psnb, NR.rs, NR.rsb)
    for k in range(KC):
        P.op('dve', lambda e, k=k: e.scalar_tensor_tensor(out=xn[:, k, 0:n], in0=xt[:, k, 0:n], scalar=C.sm[:, gcol0 + k:gcol0 + k + 1],
                                                       in1=NR.rs[:, 0:n], op0=ALU.mult, op1=ALU.mult),
             reads=[xtb, NR.rsb, C.smb], writes=[xnb])


def smalls_layout(cfg):
    KC = cfg['D'] // 128
    off, o = {}, 0
    for name, n in [('ffn_g0', KC), ('ffn_g1', KC), ('mix_g', KC), ('mem_g', KC), ('memq_g', 2), ('memk_g', 2),
                    ('conv_w', KC * 4), ('conv_b', KC), ('gb', 4 * KC), ('lam', 2 * KC), ('qg', 3), ('kg', 3)]:
        off[name] = o
        o += n
    return off, o


def load_x(P, C, X, xt, xtb, t, N):
    src = X[:, :, t * N:(t + 1) * N].rearrange("k p n -> p k n")
    P.dma('pool', lambda e: e.dma_start(out=xt[:, :, :], in_=src), xtb, writes=[xtb])


def store_x(P, C, X, xt, xtb, t, N):
    dst = X[:, :, t * N:(t + 1) * N].rearrange("k p n -> p k n")
    P.dma('pool', lambda e: e.dma_start(out=dst, in_=xt[:, :, :]), xtb, reads=[xtb])


def phase_ffn(P, cfg, dr, Xs, Xd, l, i):
    D, S, FF = cfg['D'], cfg['S'], cfg['FF']
    KC, FC, N = D // 128, FF // 128, min(1024, cfg['S'])
    NBK = N // 512
    SO, _ = smalls_layout(cfg)
    C = Ctx(P, cfg, dr)
    C.common(nring=4, smalls_layer=l, wdt=BF16)
    w_in, w_out = dr['ffn_w_in_bf'][l * 2 + i], dr['ffn_w_out_bf'][l * 2 + i]
    kcn = max(k for k in range(1, 17) if FC % k == 0)
    npc = FC // kcn
    nhalf = 2 if npc % 2 == 0 else 1
    HC = FC // nhalf
    xt, xtb = C.sb("xt", [128, KC, N]), P.buf("xt", dma=True)
    xn, xnb = C.sb("xn", [128, KC, N], BF16), P.buf("xn")
    h, hb = C.sb("h", [128, HC, N], BF16), P.buf("h")
    NR = NormRes(C, N)
    psg, psu, pso = C.ps("psg", [128, N]), C.ps("psu", [128, N]), C.ps("pso", [128, N])
    psgb, psub, psob = [P.buf("g") for _ in range(NBK)], [P.buf("u") for _ in range(NBK)], [P.buf("o") for _ in range(NBK)]
    for t in range(S // N):
        load_x(P, C, Xs, xt, xtb, t, N)
        rmsnorm_tile(P, C, NR, xt, xtb, xn, xnb, KC, N, SO['ffn_g%d' % i])
        acts = [(lambda n0, n1, k=k: xn[:, k, n0:n1]) for k in range(KC)]
        for hf_ in range(nhalf):
            for jj in range(HC):
                j = hf_ * HC + jj
                wt, wb = C.wload(w_in[j], KC)
                mm_acc(P, psg, psgb, N, wt, wb, acts, [xnb])
                wt, wb = C.wload(w_in[FC + j], KC)
                mm_acc(P, psu, psub, N, wt, wb, acts, [xnb])
                for bk in range(NBK):
                    c0, c1 = bk * 512, (bk + 1) * 512
                    P.op('act', lambda e, jj=jj, c0=c0, c1=c1: e.activation(out=h[:, jj, c0:c1], in_=psg[:, c0:c1], func=AF.Silu), reads=[psgb[bk]], writes=[hb])
                    P.op('dve', lambda e, jj=jj, c0=c0, c1=c1: e.tensor_tensor(out=h[:, jj, c0:c1], in0=psu[:, c0:c1], in1=h[:, jj, c0:c1], op=ALU.mult),
                         reads=[psub[bk], hb], writes=[hb])
            for m in range(KC):
                pcs = list(range(hf_ * (npc // nhalf), (hf_ + 1) * (npc // nhalf)))
                for pi, pc in enumerate(pcs):
                    wt, wb = C.wload(w_out[m][:, pc * kcn:(pc + 1) * kcn, :], kcn)
                    acts2 = [(lambda n0, n1, k=k: h[:, k, n0:n1]) for k in range(pc * kcn - hf_ * HC, (pc + 1) * kcn - hf_ * HC)]
                    mm_acc(P, pso, psob, N, wt, wb, acts2, [hb], first=(pi == 0), last=(pi == len(pcs) - 1))
                for bk in range(NBK):
                    c0, c1 = bk * 512, (bk + 1) * 512
                    P.op('dve', lambda e, m=m, c0=c0, c1=c1: e.scalar_tensor_tensor(out=xt[:, m, c0:c1], in0=pso[:, c0:c1], scalar=0.5, in1=xt[:, m, c0:c1],
                                                                                op0=ALU.mult, op1=ALU.add), reads=[psob[bk], xtb], writes=[xtb])
        store_x(P, C, Xd, xt, xtb, t, N)
    C.close()


def phase_precast(P, cfg, dr):
    C = Ctx(P, cfg, dr)
    NSL = 4
    st = [C.sb("pcs%d" % i, [128, 2048]) for i in range(NSL)]
    stb = [P.buf("pcs", dma=True) for _ in range(NSL)]
    sb16 = [C.sb("pcb%d" % i, [128, 2048], BF16) for i in range(NSL)]
    sb16b = [P.buf("pcb", dma=True) for _ in range(NSL)]
    ci = 0
    for name in BIG:
        if name not in dr:
            continue
        for li in range(dr[name].shape[0]):
            src, dst = dr[name][li], dr[name + '_bf'][li]
            tot = int(np.prod(src.shape))
            per = tot // 128
            E = max(e for e in range(1, 2049) if per % e == 0)
            dims = " ".join("d%d" % i for i in range(len(src.shape)))
            sf = src.rearrange("%s -> (%s)" % (dims, dims)).rearrange("(n p e) -> n p e", p=128, e=E)
            df = dst.rearrange("%s -> (%s)" % (dims, dims)).rearrange("(n p e) -> n p e", p=128, e=E)
            for n in range(per // E):
                k = ci % NSL
                eng = ['dve', 'act', 'pool'][ci % 3]
                ci += 1
                a_, ab_, b_, bb_ = st[k], stb[k], sb16[k], sb16b[k]
                P.dma('sp', lambda e, a_=a_, n=n, sf=sf, E=E: e.dma_start(out=a_[:, 0:E], in_=sf[n]), ab_, writes=[ab_])
                if eng == 'act':
                    P.op('act', lambda e, a_=a_, b_=b_, E=E: e.activation(out=b_[:, 0:E], in_=a_[:, 0:E], func=AF.Copy), reads=[ab_], writes=[bb_])
                else:
                    P.op(eng, lambda e, a_=a_, b_=b_, E=E: e.tensor_copy(out=b_[:, 0:E], in_=a_[:, 0:E]), reads=[ab_], writes=[bb_])
                P.dma('sp', lambda e, b_=b_, n=n, df=df, E=E: e.dma_start(out=df[n], in_=b_[:, 0:E]), bb_, reads=[bb_])
    C.close()


def phase_memkv(P, cfg, dr, l):
    D, M, MH = cfg['D'], cfg['M'], cfg['MH']
    KC, MW = D // 128, cfg['MH'] * 256
    MC = 2 * MW // 128
    SO, _ = smalls_layout(cfg)
    C = Ctx(P, cfg, dr)
    C.common(nring=4, smalls_layer=l)
    ident, identb = C.sb("ident", [128, 128]), P.buf("ident", dma=True)
    P.dma('pool', lambda e: e.dma_start(out=ident[:, :], in_=dr['ident']), identb, writes=[identb])
    mt, mtb = C.sb("mt", [128, KC, M]), P.buf("mt", dma=True)
    mn, mnb = C.sb("mn", [128, KC, M]), P.buf("mn")
    kv, kvb = C.sb("kv", [128, MC, M]), P.buf("kv")
    kn, knb = C.sb("kn", [128, MC // 2, M]), P.buf("kn", dma=True)
    vm, vmb = C.sb("vm", [128, M // 128, MW]), P.buf("vm", dma=True)
    NR = NormRes(C, M)
    ps = [C.ps("ps%d" % j, [128, 512]) for j in range(2)]
    psb = [P.buf("ps") for _ in range(2)]
    src = dr['memT'].rearrange("k p n -> p k n")
    P.dma('pool', lambda e: e.dma_start(out=mt[:, :, :], in_=src), mtb, writes=[mtb])
    rmsnorm_tile(P, C, NR, mt, mtb, mn, mnb, KC, M, SO['mem_g'])
    acts = [(lambda n0, n1, k=k: mn[:, k, n0:n1]) for k in range(KC)]
    for m in range(MC):
        b = m % 2
        wt, wb = C.wload(dr['mem_w_kv'][l][m], KC)
        mm_acc(P, ps[b], psb[b], M, wt, wb, acts, [mnb])
        P.op('act', lambda e, b=b, m=m: e.activation(out=kv[:, m, 0:M], in_=ps[b][:, 0:M], func=AF.Copy), reads=[psb[b]], writes=[kvb])
    for hh in range(MH):
        rms_rstd(P, C, [kv[:, 2 * hh + f, 0:M] for f in range(2)], [kvb], M, 256, NR.sq, NR.sqb, NR.psn, NR.psnb, NR.rs, NR.rsb)
        for f in range(2):
            c = SO['memk_g'] + f
            P.op('dve', lambda e, hh=hh, f=f, c=c: e.scalar_tensor_tensor(out=kn[:, 2 * hh + f, 0:M], in0=kv[:, 2 * hh + f, 0:M],
                                                                      scalar=C.sm[:, c:c + 1], in1=NR.rs[:, 0:M], op0=ALU.mult, op1=ALU.mult),
                 reads=[kvb, NR.rsb, C.smb], writes=[knb])
    for m in range(MC // 2):
        for tc in range(M // 128):
            b = (m * 2 + tc) % 2
            P.op('pe', lambda e, b=b, m=m, tc=tc: e.transpose(out=ps[b][:, 0:128], in_=kv[:, MC // 2 + m, tc * 128:(tc + 1) * 128], identity=ident[:, :]),
                 reads=[kvb, identb], writes=[psb[b]])
            P.op('act', lambda e, b=b, m=m, tc=tc: e.activation(out=vm[:, tc, m * 128:(m + 1) * 128], in_=ps[b][:, 0:128], func=AF.Copy),
                 reads=[psb[b]], writes=[vmb])
    P.dma('pool', lambda e: e.dma_start(out=dr['KN'], in_=kn[:, :, :]), knb, reads=[knb])
    P.dma('pool', lambda e: e.dma_start(out=dr['VM'], in_=vm[:, :, :]), vmb, reads=[vmb])
    C.close()


class MemAtt:
    def __init__(s, C, N):
        P, cfg, dr = C.P, C.cfg, C.dr
        s.C, s.N = C, N
        M, MH = cfg['M'], cfg['MH']
        MW = MH * 256
        s.kn, s.knb = C.sb("kn", [128, 2 * MH, M]), P.buf("kn", dma=True)
        s.vm, s.vmb = C.sb("vm", [128, M // 128, MW]), P.buf("vm", dma=True)
        P.dma('pool', lambda e: e.dma_start(out=s.kn[:, :, :], in_=dr['KN']), s.knb, writes=[s.knb])
        P.dma('pool', lambda e: e.dma_start(out=s.vm[:, :, :], in_=dr['VM']), s.vmb, writes=[s.vmb])
        s.qm, s.qmb = C.sb("qm", [128, 2 * MH, N]), P.buf("qm")
        s.qn, s.qnb = C.sb("qn", [128, 2, N]), P.buf("qn")
        s.pm = [C.sb("pm%d" % j, [128, N]) for j in range(M // 128)]
        s.pmb = [P.buf("pm") for _ in range(M // 128)]
        s.rden, s.rdenb = C.sb("rden", [128, N]), P.buf("rden")
        s.ym, s.ymb = C.sb("ym", [128, 2 * MH, N], BF16), P.buf("ym", dma=True)

    def run(s, NR, psA, psAb, psB, psBb, t):
        C, N = s.C, s.N
        P, cfg, dr = C.P, C.cfg, C.dr
        M, MH = cfg['M'], cfg['MH']
        SO, _ = smalls_layout(cfg)
        MCk = M // 128
        for hh in range(MH):
            rms_rstd(P, C, [s.qm[:, 2 * hh + f, 0:N] for f in range(2)], [s.qmb], N, 256, NR.sq, NR.sqb, NR.psn, NR.psnb, NR.rs, NR.rsb)
            for f in range(2):
                c = SO['memq_g'] + f
                P.op('dve', lambda e, hh=hh, f=f, c=c: e.scalar_tensor_tensor(out=s.qn[:, f, 0:N], in0=s.qm[:, 2 * hh + f, 0:N], scalar=C.sm[:, c:c + 1],
                                                                          in1=NR.rs[:, 0:N], op0=ALU.mult, op1=ALU.mult),
                     reads=[s.qmb, NR.rsb, C.smb], writes=[s.qnb])
            for mc in range(MCk):
                for f in range(2):
                    P.op('pe', lambda e, hh=hh, f=f, mc=mc: e.matmul(psA[:, 0:N], lhsT=s.kn[:, 2 * hh + f, mc * 128:(mc + 1) * 128], rhs=s.qn[:, f, 0:N],
                                                                 start=(f == 0), stop=(f == 1)), reads=[s.knb, s.qnb], writes=[psAb])
                P.op('act', lambda e, mc=mc: e.activation(out=s.pm[mc][:, 0:N], in_=psA[:, 0:N], func=AF.Exp, scale=256.0 ** -0.5),
                     reads=[psAb], writes=[s.pmb[mc]])
            for mc in range(MCk):
                P.op('pe', lambda e, mc=mc: e.matmul(psB[:, 0:N], lhsT=C.ones[:, :], rhs=s.pm[mc][:, 0:N], start=(mc == 0), stop=(mc == MCk - 1)),
                     reads=[C.onesb, s.pmb[mc]], writes=[psBb])
            P.op('dve', lambda e: e.reciprocal(out=s.rden[:, 0:N], in_=psB[:, 0:N]), reads=[psBb], writes=[s.rdenb])
            for f in range(2):
                for mc in range(MCk):
                    c0 = (2 * hh + f) * 128
                    P.op('pe', lambda e, mc=mc, c0=c0: e.matmul(psA[:, 0:N], lhsT=s.vm[:, mc, c0:c0 + 128], rhs=s.pm[mc][:, 0:N],
                                                             start=(mc == 0), stop=(mc == MCk - 1)), reads=[s.vmb, s.pmb[mc]], writes=[psAb])
                P.op('dve', lambda e, hh=hh, f=f: e.tensor_tensor(out=s.ym[:, 2 * hh + f, 0:N], in0=psA[:, 0:N], in1=s.rden[:, 0:N], op=ALU.mult),
                     reads=[psAb, s.rdenb], writes=[s.ymb])
        dst = dr['YM'][:, :, t * N:(t + 1) * N].rearrange("k p n -> p k n")
        P.dma('pool', lambda e: e.dma_start(out=dst, in_=s.ym[:, :, :]), s.ymb, reads=[s.ymb])


def phase_rec_p1(P, cfg, dr, X, l, j):
    D, S, MH = cfg['D'], cfg['S'], cfg['MH']
    KC, N = D // 128, min(512, cfg['S'])
    SO, _ = smalls_layout(cfg)
    C = Ctx(P, cfg, dr)
    C.common(nring=4, smalls_layer=l, wdt=BF16)
    w_in = dr['rec_w_in_bf'][j]
    xt, xtb = C.sb("xt", [128, KC, N]), P.buf("xt", dma=True)
    xn, xnb = C.sb("xn", [128, KC, N], BF16), P.buf("xn")
    NR = NormRes(C, N)
    MA = MemAtt(C, N)
    ps = [C.ps("ps%d" % i, [128, 512]) for i in range(2)]
    psb = [P.buf("ps") for _ in range(2)]
    psA, psAb, psB, psBb = C.ps("psA", [128, 512]), P.buf("psA"), C.ps("psB", [128, 512]), P.buf("psB")
    stg = [C.sb("stg%d" % i, [128, N]) for i in range(3)]
    stgb = [P.buf("stg", dma=True) for _ in range(3)]
    t1, t1b = C.sb("t1", [128, N]), P.buf("t1")
    t2, t2b = C.sb("t2", [128, N]), P.buf("t2")
    zt, ztb = C.sb("zt", [128, KC, 2]), P.buf("zt", dma=True)
    P.op('pool', lambda e: e.memset(zt[:, :, :], 0.0), writes=[ztb])
    RB = dr['RB']
    nc_ = P.nc
    def _zp(e, a, b):
        with nc_.allow_non_contiguous_dma(reason="tiny zero pad"):
            return e.dma_start(out=RB[:, :, a:b].rearrange("k p n -> p k n"), in_=zt[:, :, 0:b - a])
    P.dma('pool', lambda e: _zp(e, 0, 2), ztb, reads=[ztb])
    P.dma('pool', lambda e: _zp(e, S + 2, S + 3), ztb, reads=[ztb])
    si = 0
    for t in range(S // N):
        load_x(P, C, X, xt, xtb, t, N)
        rmsnorm_tile(P, C, NR, xt, xtb, xn, xnb, KC, N, SO['mix_g'])
        acts = [(lambda n0, n1, k=k: xn[:, k, n0:n1]) for k in range(KC)]
        for m in range(2 * KC + 2 * MH):
            b = m % 2
            wt, wb = C.wload(w_in[m], KC)
            mm_acc(P, ps[b], psb[b], N, wt, wb, acts, [xnb])
            if m < KC:
                s_, sb_ = stg[si % 3], stgb[si % 3]
                si += 1
                P.op('act', lambda e, b=b: e.activation(out=t1[:, 0:N], in_=ps[b][:, 0:N], func=AF.Square), reads=[psb[b]], writes=[t1b])
                P.op('dve', lambda e: e.tensor_scalar(out=t1[:, 0:N], in0=t1[:, 0:N], scalar1=0.044715, scalar2=1.0, op0=ALU.mult, op1=ALU.add),
                     reads=[t1b], writes=[t1b])
                P.op('dve', lambda e, b=b: e.tensor_tensor(out=t1[:, 0:N], in0=ps[b][:, 0:N], in1=t1[:, 0:N], op=ALU.mult), reads=[psb[b], t1b], writes=[t1b])
                P.op('act', lambda e: e.activation(out=t2[:, 0:N], in_=t1[:, 0:N], func=AF.Sigmoid, scale=1.5957691216), reads=[t1b], writes=[t2b])
                P.op('dve', lambda e, b=b, s_=s_: e.tensor_tensor(out=s_[:, 0:N], in0=ps[b][:, 0:N], in1=t2[:, 0:N], op=ALU.mult),
                     reads=[psb[b], t2b], writes=[sb_])
                dst = dr['GG'][m][:, t * N:(t + 1) * N]
                P.dma('pool', lambda e, s_=s_, dst=dst: e.dma_start(out=dst, in_=s_[:, 0:N]), sb_, reads=[sb_])
            elif m < 2 * KC:
                s_, sb_ = stg[si % 3], stgb[si % 3]
                si += 1
                P.op('act', lambda e, b=b, s_=s_: e.activation(out=s_[:, 0:N], in_=ps[b][:, 0:N], func=AF.Copy), reads=[psb[b]], writes=[sb_])
                dst = RB[m - KC][:, 2 + t * N:2 + (t + 1) * N]
                P.dma('pool', lambda e, s_=s_, dst=dst: e.dma_start(out=dst, in_=s_[:, 0:N]), sb_, reads=[sb_])
            else:
                P.op('act', lambda e, b=b, m=m: e.activation(out=MA.qm[:, m - 2 * KC, 0:N], in_=ps[b][:, 0:N], func=AF.Copy), reads=[psb[b]], writes=[MA.qmb])
        MA.run(NR, psA, psAb, psB, psBb, t)
    C.close()


def phase_rec_p2(P, cfg, dr, l, j, d):
    D, S = cfg['D'], cfg['S']
    KC, NB = D // 128, D // 256
    T = min(2048, S)
    nseg = S // T
    SO, _ = smalls_layout(cfg)
    C = Ctx(P, cfg, dr)
    C.common(nring=4, smalls_layer=l)
    sm = C.sm
    cs, csb = C.sb("cs", [128, 4 * KC]), P.buf("cs")
    lo = SO['lam']
    P.op('act', lambda e: e.activation(out=cs[:, 0:2 * KC], in_=sm[:, lo:lo + 2 * KC], func=AF.Exp, scale=-1.0), reads=[C.smb], writes=[csb])
    P.op('act', lambda e: e.activation(out=cs[:, 0:2 * KC], in_=cs[:, 0:2 * KC], func=AF.Ln, bias=C.cst[:, 1:2], scale=1.0), reads=[csb, C.cstb], writes=[csb])
    P.op('dve', lambda e: e.tensor_scalar(out=cs[:, 2 * KC:4 * KC], in0=cs[:, 0:2 * KC], scalar1=-16.0, scalar2=None, op0=ALU.mult), reads=[csb], writes=[csb])
    P.op('dve', lambda e: e.tensor_scalar(out=cs[:, 0:2 * KC], in0=cs[:, 0:2 * KC], scalar1=-8.0, scalar2=None, op0=ALU.mult), reads=[csb], writes=[csb])
    rb, rbb = C.sb("rb", [128, 2, T + 3]), P.buf("rb", dma=True)
    xc, xcb = C.sb("xc", [128, 2, T]), P.buf("xc")
    names = ['rr', 'ii', 'aa', 'bb', 'hh']
    tl = {n: C.sb(n, [128, T]) for n in names}
    tb = {n: P.buf(n, dma=(n == 'hh')) for n in names}
    hf, hfb = C.sb("hf", [128, T]), P.buf("hf", dma=True)
    gg, ggb = C.sb("gg", [128, T]), P.buf("gg", dma=True)
    st, stb = C.sb("st", [128, 2]), P.buf("st")
    yb, ybb = C.sb("yb", [128, T], BF16), P.buf("yb", dma=True)
    nb512 = max(1, T // 512)
    psr, psrb = C.ps("psr", [128, nb512 * 512]), P.buf("psr")
    psi, psib = C.ps("psi", [128, nb512 * 512]), P.buf("psi")
    gw = dr['gate_w'][j]
    rr, ii, aa, bb, hh = (tl[n] for n in names)
    for n in range(NB):
        for d in [d]:
            segs = list(range(nseg)) if d == 0 else list(range(nseg - 1, -1, -1))
            for si_, seg in enumerate(segs):
                src = dr['RB'][2 * n:2 * n + 2, :, seg * T:seg * T + T + 3].rearrange("k p n -> p k n")
                P.dma('pool', lambda e, src=src: e.dma_start(out=rb[:, :, :], in_=src), rbb, writes=[rbb])
                for c in range(2):
                    ch = 2 * n + c
                    w0, bcol = SO['conv_w'] + ch * 4, SO['conv_b'] + ch
                    P.op('dve', lambda e, c=c, w0=w0, bcol=bcol: e.tensor_scalar(out=xc[:, c, :], in0=rb[:, c, 0:T], scalar1=sm[:, w0:w0 + 1],
                                                                             scalar2=sm[:, bcol:bcol + 1], op0=ALU.mult, op1=ALU.add),
                         reads=[rbb, C.smb], writes=[xcb])
                    for jj in range(1, 4):
                        P.op('dve', lambda e, c=c, w0=w0, jj=jj: e.scalar_tensor_tensor(out=xc[:, c, :], in0=rb[:, c, jj:jj + T], scalar=sm[:, w0 + jj:w0 + jj + 1],
                                                                                    in1=xc[:, c, :], op0=ALU.mult, op1=ALU.add),
                             reads=[rbb, C.smb, xcb], writes=[xcb])
                for oc in range(2):
                    ch = 2 * n + oc
                    for gt, (pst, pstb, dst, dstb) in enumerate([(psr, psrb, rr, tb['rr']), (psi, psib, ii, tb['ii'])]):
                        wt, wb = C.wload(gw[d][gt][n][oc], 2)
                        mm_acc(P, pst, pstb, T, wt, wb, [(lambda n0, n1, k=k: xc[:, k, n0:n1]) for k in range(2)], [xcb])
                        bc = SO['gb'] + (d * 2 + gt) * KC + ch
                        P.op('act', lambda e, pst=pst, dst=dst, bc=bc: e.activation(out=dst[:, 0:T], in_=pst[:, 0:T], func=AF.Sigmoid, bias=sm[:, bc:bc + 1], scale=1.0),
                             reads=[pstb, C.smb], writes=[dstb])
                    c1, c2 = d * KC + ch, 2 * KC + d * KC + ch
                    P.op('act', lambda e, c1=c1: e.activation(out=aa[:, 0:T], in_=rr[:, 0:T], func=AF.Exp, scale=cs[:, c1:c1 + 1]), reads=[tb['rr'], csb], writes=[tb['aa']])
                    P.op('act', lambda e, c2=c2: e.activation(out=bb[:, 0:T], in_=rr[:, 0:T], func=AF.Exp, scale=cs[:, c2:c2 + 1]), reads=[tb['rr'], csb], writes=[tb['bb']])
                    P.op('act', lambda e: e.activation(out=bb[:, 0:T], in_=bb[:, 0:T], func=AF.Sqrt, bias=C.cst[:, 1:2], scale=-1.0), reads=[tb['bb'], C.cstb], writes=[tb['bb']])
                    P.op('dve', lambda e: e.tensor_tensor(out=bb[:, 0:T], in0=bb[:, 0:T], in1=ii[:, 0:T], op=ALU.mult), reads=[tb['bb'], tb['ii']], writes=[tb['bb']])
                    P.op('dve', lambda e, oc=oc: e.tensor_tensor(out=bb[:, 0:T], in0=bb[:, 0:T], in1=xc[:, oc, :], op=ALU.mult), reads=[tb['bb'], xcb], writes=[tb['bb']])
                    init = 0.0 if si_ == 0 else st[:, oc:oc + 1]
                    if d == 0:
                        P.op('dve', lambda e, init=init: e.tensor_tensor_scan(out=hh[:, 0:T], data0=aa[:, 0:T], data1=bb[:, 0:T], initial=init, op0=ALU.mult, op1=ALU.add),
                             reads=[tb['aa'], tb['bb'], stb], writes=[tb['hh']])
                        P.op('dve', lambda e, oc=oc: e.tensor_copy(out=st[:, oc:oc + 1], in_=hh[:, T - 1:T]), reads=[tb['hh']], writes=[stb])
                        dst = dr['HF'][ch][:, seg * T:(seg + 1) * T]
                        P.dma('pool', lambda e, dst=dst: e.dma_start(out=dst, in_=hh[:, 0:T]), tb['hh'], reads=[tb['hh']])
                    else:
                        s1 = dr['HF'][ch][:, seg * T:(seg + 1) * T]
                        s2 = dr['GG'][ch][:, seg * T:(seg + 1) * T]
                        P.dma('pool', lambda e, s1=s1: e.dma_start(out=hf[:, 0:T], in_=s1), hfb, writes=[hfb])
                        P.dma('pool', lambda e, s2=s2: e.dma_start(out=gg[:, 0:T], in_=s2), ggb, writes=[ggb])
                        P.op('dve', lambda e, init=init: e.tensor_tensor_scan(out=hh[:, ::-1], data0=aa[:, ::-1], data1=bb[:, ::-1], initial=init, op0=ALU.mult, op1=ALU.add),
                             reads=[tb['aa'], tb['bb'], stb], writes=[tb['hh']])
                        P.op('dve', lambda e, oc=oc: e.tensor_copy(out=st[:, oc:oc + 1], in_=hh[:, 0:1]), reads=[tb['hh']], writes=[stb])
                        P.op('dve', lambda e: e.tensor_tensor(out=hh[:, 0:T], in0=hh[:, 0:T], in1=hf[:, 0:T], op=ALU.add), reads=[tb['hh'], hfb], writes=[tb['hh']])
                        P.op('dve', lambda e: e.tensor_tensor(out=yb[:, 0:T], in0=hh[:, 0:T], in1=gg[:, 0:T], op=ALU.mult), reads=[tb['hh'], ggb], writes=[ybb])
                        dst = dr['YR'][ch][:, seg * T:(seg + 1) * T]
                        P.dma('pool', lambda e, dst=dst: e.dma_start(out=dst, in_=yb[:, 0:T]), ybb, reads=[ybb])
    C.close()


def phase_wout(P, cfg, dr, X, Xd, w_out, YMAIN, nmain):
    D, S, MH = cfg['D'], cfg['S'], cfg['MH']
    KC, N = D // 128, min(512, cfg['S'])
    YC = nmain + 2 * MH
    C = Ctx(P, cfg, dr)
    C.common(nring=4, wdt=BF16)
    xt, xtb = C.sb("xt", [128, KC, N]), P.buf("xt", dma=True)
    yt, ytb = C.sb("yt", [128, YC, N], BF16), P.buf("yt", dma=True)
    pso = [C.ps("pso%d" % j, [128, 512]) for j in range(2)]
    psob = [P.buf("o") for _ in range(2)]
    kcn = max(k for k in range(1, 17) if YC % k == 0)
    for t in range(S // N):
        load_x(P, C, X, xt, xtb, t, N)
        s1 = YMAIN[:, :, t * N:(t + 1) * N].rearrange("k p n -> p k n")
        s2 = dr['YM'][:, :, t * N:(t + 1) * N].rearrange("k p n -> p k n")
        P.dma('pool', lambda e, s1=s1: e.dma_start(out=yt[:, 0:nmain, :], in_=s1), ytb, writes=[ytb])
        P.dma('pool', lambda e, s2=s2: e.dma_start(out=yt[:, nmain:YC, :], in_=s2), ytb, writes=[ytb])
        for m in range(KC):
            b = m % 2
            npc = YC // kcn
            for pc in range(npc):
                wt, wb = C.wload(w_out[m][:, pc * kcn:(pc + 1) * kcn, :], kcn)
                acts2 = [(lambda n0, n1, k=k: yt[:, k, n0:n1]) for k in range(pc * kcn, (pc + 1) * kcn)]
                mm_acc(P, pso[b], psob[b], N, wt, wb, acts2, [ytb], first=(pc == 0), last=(pc == npc - 1))
            P.op('dve', lambda e, b=b, m=m: e.tensor_tensor(out=xt[:, m, 0:N], in0=pso[b][:, 0:N], in1=xt[:, m, 0:N], op=ALU.add),
                 reads=[psob[b], xtb], writes=[xtb])
        store_x(P, C, Xd, xt, xtb, t, N)
    C.close()


def phase_att_p1(P, cfg, dr, X, l, j):
    D, S, MH, AH = cfg['D'], cfg['S'], cfg['MH'], cfg['AH']
    KC, N = D // 128, min(512, cfg['S'])
    SO, _ = smalls_layout(cfg)
    C = Ctx(P, cfg, dr)
    C.common(nring=4, smalls_layer=l, wdt=BF16)
    ident, identb = C.sb("ident", [128, 128]), P.buf("ident", dma=True)
    P.dma('pool', lambda e: e.dma_start(out=ident[:, :], in_=dr['ident']), identb, writes=[identb])
    w_in = dr['att_w_in_bf'][j]
    xt, xtb = C.sb("xt", [128, KC, N]), P.buf("xt", dma=True)
    xn, xnb = C.sb("xn", [128, KC, N], BF16), P.buf("xn")
    NR = NormRes(C, N)
    NR2 = NormRes(C, N, "b")
    MA = MemAtt(C, N)
    ps = [C.ps("ps%d" % i, [128, 512]) for i in range(2)]
    psb = [P.buf("ps") for _ in range(2)]
    psA, psAb, psB, psBb = C.ps("psA", [128, 512]), P.buf("psA"), C.ps("psB", [128, 512]), P.buf("psB")
    stg = [C.sb("stg%d" % i, [128, N]) for i in range(3)]
    stgb = [P.buf("stg", dma=True) for _ in range(3)]
    vs, vsb = C.sb("vs", [128, N]), P.buf("vs")
    vt, vtb = C.sb("vt", [128, N // 128, 128]), P.buf("vt", dma=True)
    si = 0
    nqkv = 9 * AH
    for t in range(S // N):
        load_x(P, C, X, xt, xtb, t, N)
        rmsnorm_tile(P, C, NR, xt, xtb, xn, xnb, KC, N, SO['mix_g'])
        acts = [(lambda n0, n1, k=k: xn[:, k, n0:n1]) for k in range(KC)]
        for m in range(nqkv + 2 * MH):
            b = m % 2
            wt, wb = C.wload(w_in[m], KC)
            mm_acc(P, ps[b], psb[b], N, wt, wb, acts, [xnb])
            if m >= nqkv:
                P.op('act', lambda e, b=b, m=m: e.activation(out=MA.qm[:, m - nqkv, 0:N], in_=ps[b][:, 0:N], func=AF.Copy), reads=[psb[b]], writes=[MA.qmb])
                continue
            g, kind, hd_ = m // (3 * AH), (m % (3 * AH)) // AH, m % AH
            if kind < 2:
                s_, sb_ = stg[si % 3], stgb[si % 3]
                si += 1
                rms_rstd(P, C, [ps[b][:, 0:N]], [psb[b]], N, 128, NR2.sq, NR2.sqb, NR2.psn, NR2.psnb, NR2.rs, NR2.rsb)
                gc = SO['qg' if kind == 0 else 'kg'] + g
                P.op('dve', lambda e, b=b, s_=s_, gc=gc: e.scalar_tensor_tensor(out=s_[:, 0:N], in0=ps[b][:, 0:N], scalar=C.sm[:, gc:gc + 1], in1=NR2.rs[:, 0:N],
                                                                            op0=ALU.mult, op1=ALU.mult), reads=[psb[b], NR2.rsb, C.smb], writes=[sb_])
                dst = dr['QK'][(g * 2 + kind) * AH + hd_][:, t * N:(t + 1) * N]
                P.dma('pool', lambda e, s_=s_, dst=dst: e.dma_start(out=dst, in_=s_[:, 0:N]), sb_, reads=[sb_])
            else:
                P.op('act', lambda e, b=b: e.activation(out=vs[:, 0:N], in_=ps[b][:, 0:N], func=AF.Copy), reads=[psb[b]], writes=[vsb])
                for bi in range(N // 128):
                    P.op('pe', lambda e, bi=bi: e.transpose(out=psA[:, 0:128], in_=vs[:, bi * 128:(bi + 1) * 128], identity=ident[:, :]),
                         reads=[vsb, identb], writes=[psAb])
                    P.op('act', lambda e, bi=bi: e.activation(out=vt[:, bi, :], in_=psA[:, 0:128], func=AF.Copy), reads=[psAb], writes=[vtb])
                dst = dr['V'][g * AH + hd_][t * N:(t + 1) * N, :].rearrange("(c p) d -> p c d", p=128)
                P.dma('pool', lambda e, dst=dst: e.dma_start(out=dst, in_=vt[:, :, :]), vtb, reads=[vtb])
        MA.run(NR, psA, psAb, psB, psBb, t)
    C.close()


def alibi_slope(cfg, g, h):
    n = 3 * cfg['AH']
    return float(np.exp2(np.float32(-8.0) * (np.float32(g * cfg['AH'] + h) + np.float32(1.0)) / np.float32(n)))


def phase_att_p2(P, cfg, dr):
    S, AH = cfg['S'], cfg['AH']
    C = Ctx(P, cfg, dr)
    C.common(nring=0)
    rel, relb = C.sb("rel", [128, 6, 512]), P.buf("rel", dma=True)
    P.dma('pool', lambda e: e.dma_start(out=rel[:, :, :], in_=dr['c_rel'].rearrange("c p n -> p c n")), relb, writes=[relb])
    bt, btb = C.sb("bt", [128, 6, 512]), P.buf("bt")
    qT, qTb = C.sb("qT", [128, S]), P.buf("qT", dma=True)
    kT, kTb = C.sb("kT", [128, S]), P.buf("kT", dma=True)
    vc, vcb = C.sb("vc", [128, S // 128, 128]), P.buf("vc", dma=True)
    aO, aOb = C.sb("aO", [128, S]), P.buf("aO", dma=True)
    aD, aDb = C.sb("aD", [128, S]), P.buf("aD")
    tmp = [C.sb("tmp%d" % i, [128, 512]) for i in range(2)]
    tmpb = [P.buf("tmp") for _ in range(2)]
    ya = [C.sb("ya%d" % i, [128, min(2048, S)], BF16) for i in range(2)]
    yab = [P.buf("ya", dma=True) for _ in range(2)]
    pt = [C.sb("pt%d" % i, [128, 512]) for i in range(2)]
    ptb = [P.buf("pt") for _ in range(2)]
    psS = [C.ps("psS%d" % i, [128, 512]) for i in range(2)]
    psSb = [P.buf("psS") for _ in range(2)]
    psO = [C.ps("psO%d" % i, [128, 512]) for i in range(2)]
    psOb = [P.buf("psO") for _ in range(2)]
    psD = [C.ps("psD%d" % i, [128, 512]) for i in range(2)]
    psDb = [P.buf("psD") for _ in range(2)]
    it = 0
    qi = 0
    for h in range(AH):
        for g, (win, d) in enumerate(cfg['PAT']):
            L = S // d
            QT = min(512, L)
            nck = L // 128
            P.dma('pool', lambda e, g=g, h=h: e.dma_start(out=qT[:, :], in_=dr['QK'][(g * 2 + 0) * AH + h]), qTb, writes=[qTb])
            P.dma('pool', lambda e, g=g, h=h: e.dma_start(out=kT[:, :], in_=dr['QK'][(g * 2 + 1) * AH + h]), kTb, writes=[kTb])
            Vd = dr['V'][g * AH + h].rearrange("(c i r) e -> r i c e", i=128, r=d)
            for r in range(d):
                P.dma('pool', lambda e, r=r, Vd=Vd, nck=nck: e.dma_start(out=vc[:, r * nck:(r + 1) * nck, :], in_=Vd[r]), vcb, writes=[vcb])
            sl = -alibi_slope(cfg, g, h) * d
            for ci in range(6):
                P.op('dve', lambda e, ci=ci, sl=sl: e.tensor_scalar(out=bt[:, ci, :], in0=rel[:, ci, :], scalar1=sl, scalar2=None, op0=ALU.mult),
                     reads=[relb], writes=[btb])
            for r in range(d):
                for j0 in range(0, L, QT):
                    qb = qi % 2
                    qi += 1
                    qap = qT[:, r + d * j0: r + d * (j0 + QT - 1) + 1: d]
                    cis = [ci for ci in range(QT // 128 + 2) if 0 <= j0 - 128 + 128 * ci and j0 - 128 + 128 * ci + 128 <= L]
                    for idx, ci in enumerate(cis):
                        kj = j0 - 128 + 128 * ci
                        b = it % 2
                        it += 1
                        kap = kT[:, r + d * kj: r + d * (kj + 127) + 1: d]
                        P.op('pe', lambda e, b=b, kap=kap, qap=qap, QT=QT: e.matmul(psS[b][:, 0:QT], lhsT=kap, rhs=qap, start=True, stop=True),
                             reads=[kTb, qTb], writes=[psSb[b]])
                        P.op('dve', lambda e, b=b, ci=ci, QT=QT: e.scalar_tensor_tensor(out=tmp[b][:, 0:QT], in0=psS[b][:, 0:QT], scalar=128.0 ** -0.5, in1=bt[:, ci, 0:QT],
                                                                             op0=ALU.mult, op1=ALU.add), reads=[psSb[b], btb], writes=[tmpb[b]])
                        P.op('act', lambda e, b=b, QT=QT: e.activation(out=pt[b][:, 0:QT], in_=tmp[b][:, 0:QT], func=AF.Exp), reads=[tmpb[b]], writes=[ptb[b]])
                        vck = r * nck + kj // 128
                        P.op('pe', lambda e, b=b, qb=qb, vck=vck, idx=idx, QT=QT, nci=len(cis): e.matmul(psO[qb][:, 0:QT], lhsT=vc[:, vck, :], rhs=pt[b][:, 0:QT],
                                                                                 start=(idx == 0), stop=(idx == nci - 1)), reads=[vcb, ptb[b]], writes=[psOb[qb]])
                        P.op('pe', lambda e, b=b, qb=qb, idx=idx, QT=QT, nci=len(cis): e.matmul(psD[qb][:, 0:QT], lhsT=C.ones[:, :], rhs=pt[b][:, 0:QT],
                                                                        start=(idx == 0), stop=(idx == nci - 1)), reads=[C.onesb, ptb[b]], writes=[psDb[qb]])
                    oap = aO[:, r + d * j0: r + d * (j0 + QT - 1) + 1: d]
                    dap = aD[:, r + d * j0: r + d * (j0 + QT - 1) + 1: d]
                    if g == 0:
                        P.op('act', lambda e, qb=qb, oap=oap, QT=QT: e.activation(out=oap, in_=psO[qb][:, 0:QT], func=AF.Copy), reads=[psOb[qb]], writes=[aOb])
                        P.op('dve', lambda e, qb=qb, dap=dap, QT=QT: e.tensor_copy(out=dap, in_=psD[qb][:, 0:QT]), reads=[psDb[qb]], writes=[aDb])
                    else:
                        P.op('dve', lambda e, qb=qb, oap=oap, QT=QT: e.tensor_tensor(out=oap, in0=psO[qb][:, 0:QT], in1=oap, op=ALU.add), reads=[psOb[qb], aOb], writes=[aOb])
                        P.op('dve', lambda e, qb=qb, dap=dap, QT=QT: e.tensor_tensor(out=dap, in0=psD[qb][:, 0:QT], in1=dap, op=ALU.add), reads=[psDb[qb], aDb], writes=[aDb])
        P.op('dve', lambda e: e.reciprocal(out=aD[:, :], in_=aD[:, :]), reads=[aDb], writes=[aDb])
        CW = min(2048, S)
        for cc in range(S // CW):
            yb_, ybb_ = ya[cc % 2], yab[cc % 2]
            P.op('dve', lambda e, cc=cc, yb_=yb_: e.tensor_tensor(out=yb_[:, 0:CW], in0=aO[:, cc * CW:(cc + 1) * CW], in1=aD[:, cc * CW:(cc + 1) * CW], op=ALU.mult),
                 reads=[aOb, aDb], writes=[ybb_])
            P.dma('pool', lambda e, h=h, cc=cc, yb_=yb_: e.dma_start(out=dr['YA'][h][:, cc * CW:(cc + 1) * CW], in_=yb_[:, 0:CW]), ybb_, reads=[ybb_])
    C.close()


def build(cfg):
    D, S, FF, L, MH, AH, M = cfg['D'], cfg['S'], cfg['FF'], cfg['DEPTH'], cfg['MH'], cfg['AH'], cfg['M']
    KC, FC = D // 128, FF // 128
    NREC, NATT = (L + 1) // 2, L // 2
    MW = MH * 256
    _, NS = smalls_layout(cfg)
    nc = bass.Bass("TRN2", target_bir_lowering=False)
    dr = {}

    def din(name, shape):
        dr[name] = nc.dram_tensor(name, list(shape), F32, kind="ExternalInput").ap()

    def dsc(name, shape, dt=F32):
        dr[name] = nc.dram_tensor(name, list(shape), dt, kind="ExternalOutput" if cfg.get('dbg') else "Internal").ap()
    din('xT', [KC, 128, S])
    din('memT', [KC, 128, M])
    din('smalls', [L, 128, NS])
    din('ident', [128, 128])
    din('c_rel', [6, 128, 512])
    din('c_negm', [6, 128, 512])
    din('ffn_w_in', [L * 2, 2 * FC, 128, KC, 128])
    din('ffn_w_out', [L * 2, KC, 128, FC, 128])
    din('mem_w_kv', [L, 2 * MW // 128, 128, KC, 128])
    din('rec_w_in', [NREC, 2 * KC + 2 * MH, 128, KC, 128])
    din('rec_w_out', [NREC, KC, 128, KC + 2 * MH, 128])
    din('gate_w', [NREC, 2, 2, D // 256, 2, 128, 2, 128])
    if NATT:
        din('att_w_in', [NATT, 9 * AH + 2 * MH, 128, KC, 128])
        din('att_w_out', [NATT, KC, 128, AH + 2 * MH, 128])
    dr['out'] = nc.dram_tensor('out', [KC, 128, S], F32, kind="ExternalOutput").ap()
    dsc('KN', [128, 2 * MH, M])
    dsc('VM', [128, M // 128, MW])
    dsc('YM', [2 * MH, 128, S], BF16)
    dsc('GG', [KC, 128, S])
    dsc('RB', [KC, 128, S + 3])
    dsc('HF', [KC, 128, S])
    dsc('YR', [KC, 128, S], BF16)
    dsc('QK', [6 * AH, 128, S])
    dsc('V', [3 * AH, S, 128])
    dsc('YA', [AH, 128, S], BF16)
    for name in BIG:
        if name in dr:
            dr[name + '_bf'] = [nc.dram_tensor("%s_bf%d" % (name, i), list(dr[name].shape[1:]), BF16, kind="Internal").ap()
                                for i in range(dr[name].shape[0])]
    P = Prog(nc)
    phase_precast(P, cfg, dr)
    X = dr['out']
    stop = cfg.get('stop')
    for l in range(L):
        phase_ffn(P, cfg, dr, dr['xT'] if l == 0 else X, X, l, 0)
        if stop == 'ffn0':
            break
        phase_memkv(P, cfg, dr, l)
        j = l // 2
        if l % 2 == 0:
            phase_rec_p1(P, cfg, dr, X, l, j)
            phase_rec_p2(P, cfg, dr, l, j, 0)
            phase_rec_p2(P, cfg, dr, l, j, 1)
            phase_wout(P, cfg, dr, X, X, dr['rec_w_out_bf'][j], dr['YR'], KC)
            if stop == 'mix0':
                break
        else:
            phase_att_p1(P, cfg, dr, X, l, j)
            phase_att_p2(P, cfg, dr)
            phase_wout(P, cfg, dr, X, X, dr['att_w_out_bf'][j], dr['YA'], AH)
        phase_ffn(P, cfg, dr, X, X, l, 1)
    return nc


def pieces(W):
    K, Mo = W.shape[-2], W.shape[-1]
    lead = W.shape[:-2]
    Wr = W.reshape(*lead, K // 128, 128, Mo // 128, 128)
    nl = len(lead)
    perm = list(range(nl)) + [nl + 2, nl + 1, nl + 0, nl + 3]
    return np.ascontiguousarray(Wr.transpose(perm))


def cols(v):
    n = v.shape[-1] // 128
    return np.swapaxes(v.reshape(*v.shape[:-1], n, 128), -1, -2)


def const_tables():
    k = np.arange(128)[:, None]
    q = np.arange(512)[None, :]
    rel = np.stack([np.abs(128 * (ci - 1) + k - q) for ci in range(6)]).astype(np.float32)
    rel = np.where(rel <= 64, rel, 1.0e6).astype(np.float32)
    return rel, rel


def prep_inputs(cfg, inp):
    D, S, L = cfg['D'], cfg['S'], cfg['DEPTH']
    KC = D // 128
    SO, NS = smalls_layout(cfg)
    f = lambda a: np.asarray(a, dtype=np.float32)
    sm = np.zeros((L, 128, NS), np.float32)
    for l in range(L):
        j = l // 2
        sm[l, :, SO['ffn_g0']:SO['ffn_g0'] + KC] = cols(f(inp['ffn_norm'])[l, 0])
        sm[l, :, SO['ffn_g1']:SO['ffn_g1'] + KC] = cols(f(inp['ffn_norm'])[l, 1])
        sm[l, :, SO['mix_g']:SO['mix_g'] + KC] = cols(f(inp['mix_norm'])[l])
        sm[l, :, SO['mem_g']:SO['mem_g'] + KC] = cols(f(inp['mem_norm'])[l])
        sm[l, :, SO['memq_g']:SO['memq_g'] + 2] = cols(f(inp['mem_qk_gain'])[l, 0])
        sm[l, :, SO['memk_g']:SO['memk_g'] + 2] = cols(f(inp['mem_qk_gain'])[l, 1])
        if l % 2 == 0:
            cw = f(inp['rec_conv_w'])[j]
            sm[l, :, SO['conv_w']:SO['conv_w'] + 4 * KC] = np.stack([cols(cw[t]) for t in range(4)], -1).reshape(128, 4 * KC)
            sm[l, :, SO['conv_b']:SO['conv_b'] + KC] = cols(f(inp['rec_conv_b'])[j])
            gb = f(inp['rec_gate_b'])[j].reshape(2, 2, D)
            for d in range(2):
                for gt in range(2):
                    o = SO['gb'] + (d * 2 + gt) * KC
                    sm[l, :, o:o + KC] = cols(gb[d, gt])
                o = SO['lam'] + d * KC
                sm[l, :, o:o + KC] = cols(f(inp['rec_lambda'])[j, d])
        else:
            qk = f(inp['att_qk_gain'])[j]
            sm[l, :, SO['qg']:SO['qg'] + 3] = qk[0].T
            sm[l, :, SO['kg']:SO['kg'] + 3] = qk[1].T
    rel, negm = const_tables()
    shared = {
        'smalls': sm, 'ident': np.eye(128, dtype=np.float32), 'c_rel': rel, 'c_negm': negm,
        'ffn_w_in': pieces(f(inp['ffn_w_in'])).reshape(L * 2, -1, 128, KC, 128),
        'ffn_w_out': pieces(f(inp['ffn_w_out'])).reshape(L * 2, KC, 128, -1, 128),
        'mem_w_kv': pieces(f(inp['mem_w_kv'])),
        'rec_w_in': pieces(f(inp['rec_w_in'])),
        'rec_w_out': pieces(f(inp['rec_w_out'])),
        'gate_w': pieces(f(inp['rec_gate_w'])),
    }
    if L // 2:
        shared['att_w_in'] = pieces(f(inp['att_w_in']))
        shared['att_w_out'] = pieces(f(inp['att_w_out']))
    x, mem = f(inp['x']), f(inp['mem'])
    B = x.shape[0]
    maps = []
    for b in range(B):
        m = dict(shared)
        m['xT'] = np.ascontiguousarray(x[b].T).reshape(KC, 128, S)
        m['memT'] = np.ascontiguousarray(mem[b].T).reshape(KC, 128, -1)
        maps.append(m)
    return maps


def run(cfg, inp):
    nc = build(cfg)
    maps = prep_inputs(cfg, inp)
    B = len(maps)
    res = run_bass_kernel_spmd(nc, maps, core_ids=list(range(B)))
    D, S = cfg['D'], cfg['S']
    if cfg.get('dbg'):
        cfg['dbg_out'] = res.results
    return np.stack([np.ascontiguousarray(res.results[b]['out'].reshape(D, S).T) for b in range(B)]).astype(np.float32)


def kernel(**inputs):
    return run(FULL, inputs)
```

```python
import numpy as np
from contextlib import ExitStack
import concourse.bass as bass
import concourse.mybir as mybir
from concourse.bass_utils import run_bass_kernel_spmd

F32 = mybir.dt.float32
BF16 = mybir.dt.bfloat16
BIG = ['ffn_w_in', 'ffn_w_out', 'rec_w_in', 'rec_w_out', 'att_w_in', 'att_w_out']
AF = mybir.ActivationFunctionType
ALU = mybir.AluOpType
ENG = ['pe', 'act', 'dve', 'pool', 'sp']

FULL = dict(D=2048, S=8192, FF=5632, DEPTH=4, MH=4, AH=8, M=256, PAT=((128, 1), (512, 4), (2048, 16)))


class Buf:
    def __init__(s, name, di=None):
        s.name, s.lw, s.rd, s.di = name, None, {}, di


class Prog:
    def __init__(s, nc, ndsem=40):
        s.nc = nc
        s.es = ExitStack()
        s.eng = {'pe': nc.tensor, 'act': nc.scalar, 'dve': nc.vector, 'pool': nc.gpsimd, 'sp': nc.sync}
        s.esem = {e: s.es.enter_context(nc.semaphore("es_" + e)) for e in ENG}
        s.ecnt = {e: 0 for e in ENG}
        s.dsem = [s.es.enter_context(nc.semaphore("ds%d" % i)) for i in range(ndsem)]
        s.dcnt = [0] * ndsem
        s.q = {e: [] for e in ENG}
        s.seen = {e: {} for e in ENG}
        s.nextd = 0

    def buf(s, name, dma=False):
        di = None
        if dma:
            di = s.nextd % len(s.dsem)
            s.nextd += 1
        return Buf(name, di)

    def _sem(s, key):
        return s.esem[key[1]] if key[0] == 'e' else s.dsem[key[1]]

    def _waits(s, eng, evs):
        out = []
        for ev in evs:
            if ev is None:
                continue
            key, val = ev
            if key == ('e', eng) and eng == 'pe':
                continue
            if s.seen[eng].get(key, 0) >= val:
                continue
            s.seen[eng][key] = val
            out.append(ev)
        return out

    def _deps(s, reads, writes):
        evs = [b.lw for b in reads]
        for b in writes:
            evs.append(b.lw)
            evs += list(b.rd.items())
        return evs

    def _upd(s, ev, reads, writes):
        for b in reads:
            b.rd[ev[0]] = max(b.rd.get(ev[0], 0), ev[1])
        for b in writes:
            b.lw = ev
            b.rd = {}

    def op(s, eng, fn, reads=(), writes=()):
        w = s._waits(eng, s._deps(reads, writes))
        s.ecnt[eng] += 1
        ev = (('e', eng), s.ecnt[eng])
        s.q[eng].append((fn, w, ('e', eng), 1))
        s._upd(ev, reads, writes)
        return ev

    def dma(s, eng, fn, sb, reads=(), writes=()):
        di = sb.di
        evs = s._deps(reads, writes)
        if s.dcnt[di] > 0:
            evs.append((('d', di), s.dcnt[di]))
        w = s._waits(eng, evs)
        s.dcnt[di] += 16
        ev = (('d', di), s.dcnt[di])
        s.q[eng].append((fn, w, ('d', di), 16))
        s._upd(ev, reads, writes)
        return ev

    def barrier(s):
        for e in ENG:
            evs = [(('e', e2), s.ecnt[e2]) for e2 in ENG if e2 != e and s.ecnt[e2] > 0]
            evs += [(('d', i), c) for i, c in enumerate(s.dcnt) if c > 0]
            s.q[e].append((None, s._waits(e, evs), None, 0))

    def flush(s):
        s.barrier()
        with s.nc.Block() as block:
            def mk(e):
                def run(eo):
                    for (fn, w, key, inc) in s.q[e]:
                        for (k, v) in w:
                            eo.wait_ge(s._sem(k), v)
                        if fn is not None:
                            fn(eo).then_inc(s._sem(key), inc)
                return run
            block.tensor(mk('pe'))
            block.scalar(mk('act'))
            block.vector(mk('dve'))
            block.gpsimd(mk('pool'))
            block.sync(mk('sp'))
        s.q = {e: [] for e in ENG}


class Ctx:
    def __init__(s, P, cfg, dr):
        s.P, s.cfg, s.dr, s.nc = P, cfg, dr, P.nc
        s.es = ExitStack()
        s.wi = 0

    _uid = [0]

    def sb(s, name, shape, dt=F32):
        Ctx._uid[0] += 1
        return s.es.enter_context(s.nc.sbuf_tensor("%s_u%d" % (name, Ctx._uid[0]), list(shape), dt))

    def ps(s, name, shape):
        Ctx._uid[0] += 1
        return s.es.enter_context(s.nc.psum_tensor("%s_u%d" % (name, Ctx._uid[0]), list(shape), F32))

    def common(s, nring=4, smalls_layer=None, wdt=F32):
        P = s.P
        s.ring = [s.sb("wr%d" % i, [128, 2048], wdt) for i in range(nring)]
        s.ringb = [P.buf("wr%d" % i, dma=True) for i in range(nring)]
        s.ones = s.sb("ones", [128, 128])
        s.onesb = P.buf("ones")
        s.cst = s.sb("cst", [128, 4])
        s.cstb = P.buf("cst")
        o, c = s.ones, s.cst
        P.op('pool', lambda e: e.memset(o[:, :], 1.0), writes=[s.onesb])
        P.op('pool', lambda e: e.memset(c[:, 0:1], 1e-6), writes=[s.cstb])
        P.op('pool', lambda e: e.memset(c[:, 1:2], 1.0), writes=[s.cstb])
        if smalls_layer is not None:
            ns = s.dr['smalls'].shape[2]
            s.sm = s.sb("smalls", [128, ns])
            s.smb = P.buf("smalls", dma=True)
            sm, src = s.sm, s.dr['smalls'][smalls_layer]
            P.dma('pool', lambda e: e.dma_start(out=sm[:, :], in_=src), s.smb, writes=[s.smb])

    def wload(s, src, kcn):
        i = s.wi % len(s.ring)
        s.wi += 1
        t, b = s.ring[i], s.ringb[i]
        s.P.dma('sp', lambda e: e.dma_start(out=t[:, 0:kcn * 128], in_=src.rearrange("p k c -> p (k c)")), b, writes=[b])
        return t, b

    def close(s):
        s.P.flush()
        s.es.close()


def mm_acc(P, ps, psb, n, wt, wb, acts, actb, first=True, last=True):
    K = len(acts)
    for n0 in range(0, n, 512):
        n1 = min(n, n0 + 512)
        for k in range(K):
            a = acts[k]
            pb = psb[n0 // 512] if isinstance(psb, list) else psb
            P.op('pe', lambda e, k=k, a=a, n0=n0, n1=n1: e.matmul(ps[:, n0:n1], lhsT=wt[:, k * 128:(k + 1) * 128], rhs=a(n0, n1),
                                                            start=(first and k == 0), stop=(last and k == K - 1)),
                 reads=[wb] + actb, writes=[pb])


def rms_rstd(P, C, srcs, srcb, n, dim, sqs, sqb, psn, psnb, rs, rsb):
    K = len(srcs)
    for k in range(K):
        sq, sb_ = sqs[k % 2], sqb[k % 2]
        a = srcs[k]
        P.op('act', lambda e, sq=sq, a=a: e.activation(out=sq[:, 0:n], in_=a, func=AF.Square), reads=srcb, writes=[sb_])
        for n0 in range(0, n, 512):
            n1 = min(n, n0 + 512)
            P.op('pe', lambda e, sq=sq, k=k, n0=n0, n1=n1: e.matmul(psn[:, n0:n1], lhsT=C.ones[:, :], rhs=sq[:, n0:n1],
                                                                start=(k == 0), stop=(k == K - 1)),
                 reads=[sb_, C.onesb], writes=[psnb])
    P.op('act', lambda e: e.activation(out=rs[:, 0:n], in_=psn[:, 0:n], func=AF.Sqrt, bias=C.cst[:, 0:1], scale=1.0 / dim),
         reads=[psnb, C.cstb], writes=[rsb])
    P.op('dve', lambda e: e.reciprocal(out=rs[:, 0:n], in_=rs[:, 0:n]), reads=[rsb], writes=[rsb])


class NormRes:
    def __init__(s, C, n, tag=""):
        P = C.P
        s.sq = [C.sb("nsq0" + tag, [128, n]), C.sb("nsq1" + tag, [128, n])]
        s.sqb = [P.buf("nsq0"), P.buf("nsq1")]
        s.psn = C.ps("npsn" + tag, [128, max(n, 512)])
        s.psnb = P.buf("npsn")
        s.rs = C.sb("nrs" + tag, [128, n])
        s.rsb = P.buf("nrs")


def rmsnorm_tile(P, C, NR, xt, xtb, xn, xnb, KC, n, gcol0):
    rms_rstd(P, C, [xt[:, k, 0:n] for k in range(KC)], [xtb], n, KC * 128, NR.sq, NR.sqb, NR.psn, NR.psnb, NR.rs, NR.rsb)
    for k in range(KC):
        P.op('dve', lambda e, k=k: e.scalar_tensor_tensor(out=xn[:, k, 0:n], in0=xt[:, k, 0:n], scalar=C.sm[:, gcol0 + k:gcol0 + k + 1],
                                                       in1=NR.rs[:, 0:n], op0=ALU.mult, op1=ALU.mult),
             reads=[xtb, NR.rsb, C.smb], writes=[xnb])


def smalls_layout(cfg):
    KC = cfg['D'] // 128
    off, o = {}, 0
    for name, n in [('ffn_g0', KC), ('ffn_g1', KC), ('mix_g', KC), ('mem_g', KC), ('memq_g', 2), ('memk_g', 2),
                    ('conv_w', KC * 4), ('conv_b', KC), ('gb', 4 * KC), ('lam', 2 * KC), ('qg', 3), ('kg', 3)]:
        off[name] = o
        o += n
    return off, o


def load_x(P, C, X, xt, xtb, t, N):
    src = X[:, :, t * N:(t + 1) * N].rearrange("k p n -> p k n")
    P.dma('pool', lambda e: e.dma_start(out=xt[:, :, :], in_=src), xtb, writes=[xtb])


def store_x(P, C, X, xt, xtb, t, N):
    dst = X[:, :, t * N:(t + 1) * N].rearrange("k p n -> p k n")
    P.dma('pool', lambda e: e.dma_start(out=dst, in_=xt[:, :, :]), xtb, reads=[xtb])


def phase_ffn(P, cfg, dr, Xs, Xd, l, i, PC=None):
    D, S, FF = cfg['D'], cfg['S'], cfg['FF']
    KC, FC, N = D // 128, FF // 128, min(1024, cfg['S'])
    NBK = N // 512
    SO, _ = smalls_layout(cfg)
    C = Ctx(P, cfg, dr)
    C.common(nring=4, smalls_layer=l, wdt=BF16)
    w_in, w_out = dr['ffn_w_in_bf'][l * 2 + i], dr['ffn_w_out_bf'][l * 2 + i]
    kcn = max(k for k in range(1, 17) if FC % k == 0)
    npc = FC // kcn
    nhalf = 2 if npc % 2 == 0 else 1
    HC = FC // nhalf
    xt, xtb = C.sb("xt", [128, KC, N]), P.buf("xt", dma=True)
    xn, xnb = C.sb("xn", [128, KC, N], BF16), P.buf("xn")
    h, hb = C.sb("h", [128, HC, N], BF16), P.buf("h")
    NR = NormRes(C, N)
    psg, psu, pso = C.ps("psg", [128, N]), C.ps("psu", [128, N]), C.ps("pso", [128, N])
    psgb, psub, psob = [P.buf("g") for _ in range(NBK)], [P.buf("u") for _ in range(NBK)], [P.buf("o") for _ in range(NBK)]
    if PC is not None:
        PC.alloc(C, 2)
    npieces = (S // N) * (2 * FC + KC * npc)
    todo = len(PC.chunks.get(l + 1, [])) if PC is not None else 0
    stride = max(1, (npieces * (2 - i)) // max(1, todo) - 1) if todo else 0
    cnt = [0]

    def tick():
        cnt[0] += 1
        if stride and cnt[0] % stride == 0:
            PC.pump(l + 1, 1)
    for t in range(S // N):
        load_x(P, C, Xs, xt, xtb, t, N)
        rmsnorm_tile(P, C, NR, xt, xtb, xn, xnb, KC, N, SO['ffn_g%d' % i])
        acts = [(lambda n0, n1, k=k: xn[:, k, n0:n1]) for k in range(KC)]
        for hf_ in range(nhalf):
            for jj in range(HC):
                j = hf_ * HC + jj
                wt, wb = C.wload(w_in[j], KC)
                mm_acc(P, psg, psgb, N, wt, wb, acts, [xnb])
                wt, wb = C.wload(w_in[FC + j], KC)
                mm_acc(P, psu, psub, N, wt, wb, acts, [xnb])
                tick()
                tick()
                for bk in range(NBK):
                    c0, c1 = bk * 512, (bk + 1) * 512
                    P.op('act', lambda e, jj=jj, c0=c0, c1=c1: e.activation(out=h[:, jj, c0:c1], in_=psg[:, c0:c1], func=AF.Silu), reads=[psgb[bk]], writes=[hb])
                    P.op('dve', lambda e, jj=jj, c0=c0, c1=c1: e.tensor_tensor(out=h[:, jj, c0:c1], in0=psu[:, c0:c1], in1=h[:, jj, c0:c1], op=ALU.mult),
                         reads=[psub[bk], hb], writes=[hb])
            for m in range(KC):
                pcs = list(range(hf_ * (npc // nhalf), (hf_ + 1) * (npc // nhalf)))
                for pi, pc in enumerate(pcs):
                    wt, wb = C.wload(w_out[m][:, pc * kcn:(pc + 1) * kcn, :], kcn)
                    acts2 = [(lambda n0, n1, k=k: h[:, k, n0:n1]) for k in range(pc * kcn - hf_ * HC, (pc + 1) * kcn - hf_ * HC)]
                    mm_acc(P, pso, psob, N, wt, wb, acts2, [hb], first=(pi == 0), last=(pi == len(pcs) - 1))
                    tick()
                for bk in range(NBK):
                    c0, c1 = bk * 512, (bk + 1) * 512
                    P.op('dve', lambda e, m=m, c0=c0, c1=c1: e.scalar_tensor_tensor(out=xt[:, m, c0:c1], in0=pso[:, c0:c1], scalar=0.5, in1=xt[:, m, c0:c1],
                                                                                op0=ALU.mult, op1=ALU.add), reads=[psob[bk], xtb], writes=[xtb])
        store_x(P, C, Xd, xt, xtb, t, N)
    if PC is not None and i == 1:
        PC.pump(l + 1, 10 ** 9)
    C.close()


class Precast:
    def __init__(s, P, cfg, dr):
        s.P, s.dr = P, dr
        L = cfg['DEPTH']
        s.chunks = {l: [] for l in range(L + 1)}
        for name in BIG:
            if name not in dr:
                continue
            for li in range(dr[name].shape[0]):
                layer = li // 2 if name.startswith('ffn') else (2 * li if name.startswith('rec') else 2 * li + 1)
                src, dst = dr[name][li], dr[name + '_bf'][li]
                per = int(np.prod(src.shape)) // 128
                E = max(e for e in range(1, 2049) if per % e == 0)
                dims = " ".join("d%d" % i for i in range(len(src.shape)))
                sf = src.rearrange("%s -> (%s)" % (dims, dims)).rearrange("(n p e) -> n p e", p=128, e=E)
                df = dst.rearrange("%s -> (%s)" % (dims, dims)).rearrange("(n p e) -> n p e", p=128, e=E)
                for n in range(per // E):
                    s.chunks[layer].append((sf[n], df[n], E))
        s.ci = 0

    def alloc(s, C, nsl):
        P = s.P
        s.nsl = nsl
        s.st = [C.sb("pcs%d" % i, [128, 2048]) for i in range(nsl)]
        s.stb = [P.buf("pcs", dma=True) for _ in range(nsl)]
        s.b16 = [C.sb("pcb%d" % i, [128, 2048], BF16) for i in range(nsl)]
        s.b16b = [P.buf("pcb", dma=True) for _ in range(nsl)]

    def pump(s, layer, k, bg=True):
        P = s.P
        lst = s.chunks.get(layer, [])
        for _ in range(min(k, len(lst))):
            src, dst, E = lst.pop(0)
            i = s.ci % s.nsl
            eng = 'pool' if bg else ['dve', 'act', 'pool'][s.ci % 3]
            q = 'pool' if bg else 'sp'
            s.ci += 1
            a_, ab_, b_, bb_ = s.st[i], s.stb[i], s.b16[i], s.b16b[i]
            P.dma(q, lambda e, a_=a_, src=src, E=E: e.dma_start(out=a_[:, 0:E], in_=src), ab_, writes=[ab_])
            if eng == 'act':
                P.op('act', lambda e, a_=a_, b_=b_, E=E: e.activation(out=b_[:, 0:E], in_=a_[:, 0:E], func=AF.Copy), reads=[ab_], writes=[bb_])
            else:
                P.op(eng, lambda e, a_=a_, b_=b_, E=E: e.tensor_copy(out=b_[:, 0:E], in_=a_[:, 0:E]), reads=[ab_], writes=[bb_])
            P.dma(q, lambda e, b_=b_, dst=dst, E=E: e.dma_start(out=dst, in_=b_[:, 0:E]), bb_, reads=[bb_])


def phase_precast(P, cfg, dr, PC):
    C = Ctx(P, cfg, dr)
    PC.alloc(C, 4)
    PC.pump(0, 10 ** 9, bg=False)
    C.close()


def phase_memkv(P, cfg, dr, l):
    D, M, MH = cfg['D'], cfg['M'], cfg['MH']
    KC, MW = D // 128, cfg['MH'] * 256
    MC = 2 * MW // 128
    SO, _ = smalls_layout(cfg)
    C = Ctx(P, cfg, dr)
    C.common(nring=4, smalls_layer=l)
    ident, identb = C.sb("ident", [128, 128]), P.buf("ident", dma=True)
    P.dma('pool', lambda e: e.dma_start(out=ident[:, :], in_=dr['ident']), identb, writes=[identb])
    mt, mtb = C.sb("mt", [128, KC, M]), P.buf("mt", dma=True)
    mn, mnb = C.sb("mn", [128, KC, M]), P.buf("mn")
    kv, kvb = C.sb("kv", [128, MC, M]), P.buf("kv")
    kn, knb = C.sb("kn", [128, MC // 2, M]), P.buf("kn", dma=True)
    vm, vmb = C.sb("vm", [128, M // 128, MW]), P.buf("vm", dma=True)
    NR = NormRes(C, M)
    ps = [C.ps("ps%d" % j, [128, 512]) for j in range(2)]
    psb = [P.buf("ps") for _ in range(2)]
    src = dr['memT'].rearrange("k p n -> p k n")
    P.dma('pool', lambda e: e.dma_start(out=mt[:, :, :], in_=src), mtb, writes=[mtb])
    rmsnorm_tile(P, C, NR, mt, mtb, mn, mnb, KC, M, SO['mem_g'])
    acts = [(lambda n0, n1, k=k: mn[:, k, n0:n1]) for k in range(KC)]
    for m in range(MC):
        b = m % 2
        wt, wb = C.wload(dr['mem_w_kv'][l][m], KC)
        mm_acc(P, ps[b], psb[b], M, wt, wb, acts, [mnb])
        P.op('act', lambda e, b=b, m=m: e.activation(out=kv[:, m, 0:M], in_=ps[b][:, 0:M], func=AF.Copy), reads=[psb[b]], writes=[kvb])
    for hh in range(MH):
        rms_rstd(P, C, [kv[:, 2 * hh + f, 0:M] for f in range(2)], [kvb], M, 256, NR.sq, NR.sqb, NR.psn, NR.psnb, NR.rs, NR.rsb)
        for f in range(2):
            c = SO['memk_g'] + f
            P.op('dve', lambda e, hh=hh, f=f, c=c: e.scalar_tensor_tensor(out=kn[:, 2 * hh + f, 0:M], in0=kv[:, 2 * hh + f, 0:M],
                                                                      scalar=C.sm[:, c:c + 1], in1=NR.rs[:, 0:M], op0=ALU.mult, op1=ALU.mult),
                 reads=[kvb, NR.rsb, C.smb], writes=[knb])
    for m in range(MC // 2):
        for tc in range(M // 128):
            b = (m * 2 + tc) % 2
            P.op('pe', lambda e, b=b, m=m, tc=tc: e.transpose(out=ps[b][:, 0:128], in_=kv[:, MC // 2 + m, tc * 128:(tc + 1) * 128], identity=ident[:, :]),
                 reads=[kvb, identb], writes=[psb[b]])
            P.op('act', lambda e, b=b, m=m, tc=tc: e.activation(out=vm[:, tc, m * 128:(m + 1) * 128], in_=ps[b][:, 0:128], func=AF.Copy),
                 reads=[psb[b]], writes=[vmb])
    P.dma('pool', lambda e: e.dma_start(out=dr['KN'], in_=kn[:, :, :]), knb, reads=[knb])
    P.dma('pool', lambda e: e.dma_start(out=dr['VM'], in_=vm[:, :, :]), vmb, reads=[vmb])
    C.close()


class MemAtt:
    def __init__(s, C, N):
        P, cfg, dr = C.P, C.cfg, C.dr
        s.C, s.N = C, N
        M, MH = cfg['M'], cfg['MH']
        MW = MH * 256
        s.kn, s.knb = C.sb("kn", [128, 2 * MH, M]), P.buf("kn", dma=True)
        s.vm, s.vmb = C.sb("vm", [128, M // 128, MW]), P.buf("vm", dma=True)
        P.dma('pool', lambda e: e.dma_start(out=s.kn[:, :, :], in_=dr['KN']), s.knb, writes=[s.knb])
        P.dma('pool', lambda e: e.dma_start(out=s.vm[:, :, :], in_=dr['VM']), s.vmb, writes=[s.vmb])
        s.qm, s.qmb = C.sb("qm", [128, 2 * MH, N]), P.buf("qm")
        s.qn, s.qnb = C.sb("qn", [128, 2, N]), P.buf("qn")
        s.pm = [C.sb("pm%d" % j, [128, N]) for j in range(M // 128)]
        s.pmb = [P.buf("pm") for _ in range(M // 128)]
        s.rden, s.rdenb = C.sb("rden", [128, N]), P.buf("rden")
        s.ym, s.ymb = C.sb("ym", [128, 2 * MH, N], BF16), P.buf("ym", dma=True)

    def run(s, NR, psA, psAb, psB, psBb, t):
        C, N = s.C, s.N
        P, cfg, dr = C.P, C.cfg, C.dr
        M, MH = cfg['M'], cfg['MH']
        SO, _ = smalls_layout(cfg)
        MCk = M // 128
        for hh in range(MH):
            rms_rstd(P, C, [s.qm[:, 2 * hh + f, 0:N] for f in range(2)], [s.qmb], N, 256, NR.sq, NR.sqb, NR.psn, NR.psnb, NR.rs, NR.rsb)
            for f in range(2):
                c = SO['memq_g'] + f
                P.op('dve', lambda e, hh=hh, f=f, c=c: e.scalar_tensor_tensor(out=s.qn[:, f, 0:N], in0=s.qm[:, 2 * hh + f, 0:N], scalar=C.sm[:, c:c + 1],
                                                                          in1=NR.rs[:, 0:N], op0=ALU.mult, op1=ALU.mult),
                     reads=[s.qmb, NR.rsb, C.smb], writes=[s.qnb])
            for mc in range(MCk):
                for f in range(2):
                    P.op('pe', lambda e, hh=hh, f=f, mc=mc: e.matmul(psA[:, 0:N], lhsT=s.kn[:, 2 * hh + f, mc * 128:(mc + 1) * 128], rhs=s.qn[:, f, 0:N],
                                                                 start=(f == 0), stop=(f == 1)), reads=[s.knb, s.qnb], writes=[psAb])
                P.op('act', lambda e, mc=mc: e.activation(out=s.pm[mc][:, 0:N], in_=psA[:, 0:N], func=AF.Exp, scale=256.0 ** -0.5),
                     reads=[psAb], writes=[s.pmb[mc]])
            for mc in range(MCk):
                P.op('pe', lambda e, mc=mc: e.matmul(psB[:, 0:N], lhsT=C.ones[:, :], rhs=s.pm[mc][:, 0:N], start=(mc == 0), stop=(mc == MCk - 1)),
                     reads=[C.onesb, s.pmb[mc]], writes=[psBb])
            P.op('dve', lambda e: e.reciprocal(out=s.rden[:, 0:N], in_=psB[:, 0:N]), reads=[psBb], writes=[s.rdenb])
            for f in range(2):
                for mc in range(MCk):
                    c0 = (2 * hh + f) * 128
                    P.op('pe', lambda e, mc=mc, c0=c0: e.matmul(psA[:, 0:N], lhsT=s.vm[:, mc, c0:c0 + 128], rhs=s.pm[mc][:, 0:N],
                                                             start=(mc == 0), stop=(mc == MCk - 1)), reads=[s.vmb, s.pmb[mc]], writes=[psAb])
                P.op('dve', lambda e, hh=hh, f=f: e.tensor_tensor(out=s.ym[:, 2 * hh + f, 0:N], in0=psA[:, 0:N], in1=s.rden[:, 0:N], op=ALU.mult),
                     reads=[psAb, s.rdenb], writes=[s.ymb])
        dst = dr['YM'][:, :, t * N:(t + 1) * N].rearrange("k p n -> p k n")
        P.dma('pool', lambda e: e.dma_start(out=dst, in_=s.ym[:, :, :]), s.ymb, reads=[s.ymb])


def phase_rec_p1(P, cfg, dr, X, l, j):
    D, S, MH = cfg['D'], cfg['S'], cfg['MH']
    KC, N = D // 128, min(512, cfg['S'])
    SO, _ = smalls_layout(cfg)
    C = Ctx(P, cfg, dr)
    C.common(nring=4, smalls_layer=l, wdt=BF16)
    w_in = dr['rec_w_in_bf'][j]
    xt, xtb = C.sb("xt", [128, KC, N]), P.buf("xt", dma=True)
    xn, xnb = C.sb("xn", [128, KC, N], BF16), P.buf("xn")
    NR = NormRes(C, N)
    MA = MemAtt(C, N)
    ps = [C.ps("ps%d" % i, [128, 512]) for i in range(2)]
    psb = [P.buf("ps") for _ in range(2)]
    psA, psAb, psB, psBb = C.ps("psA", [128, 512]), P.buf("psA"), C.ps("psB", [128, 512]), P.buf("psB")
    stg = [C.sb("stg%d" % i, [128, N]) for i in range(3)]
    stgb = [P.buf("stg", dma=True) for _ in range(3)]
    t1, t1b = C.sb("t1", [128, N]), P.buf("t1")
    t2, t2b = C.sb("t2", [128, N]), P.buf("t2")
    zt, ztb = C.sb("zt", [128, KC, 2]), P.buf("zt", dma=True)
    P.op('pool', lambda e: e.memset(zt[:, :, :], 0.0), writes=[ztb])
    RB = dr['RB']
    nc_ = P.nc
    def _zp(e, a, b):
        with nc_.allow_non_contiguous_dma(reason="tiny zero pad"):
            return e.dma_start(out=RB[:, :, a:b].rearrange("k p n -> p k n"), in_=zt[:, :, 0:b - a])
    P.dma('pool', lambda e: _zp(e, 0, 2), ztb, reads=[ztb])
    P.dma('pool', lambda e: _zp(e, S + 2, S + 3), ztb, reads=[ztb])
    si = 0
    for t in range(S // N):
        load_x(P, C, X, xt, xtb, t, N)
        rmsnorm_tile(P, C, NR, xt, xtb, xn, xnb, KC, N, SO['mix_g'])
        acts = [(lambda n0, n1, k=k: xn[:, k, n0:n1]) for k in range(KC)]
        for m in range(2 * KC + 2 * MH):
            b = m % 2
            wt, wb = C.wload(w_in[m], KC)
            mm_acc(P, ps[b], psb[b], N, wt, wb, acts, [xnb])
            if m < KC:
                s_, sb_ = stg[si % 3], stgb[si % 3]
                si += 1
                P.op('act', lambda e, b=b: e.activation(out=t1[:, 0:N], in_=ps[b][:, 0:N], func=AF.Square), reads=[psb[b]], writes=[t1b])
                P.op('dve', lambda e: e.tensor_scalar(out=t1[:, 0:N], in0=t1[:, 0:N], scalar1=0.044715, scalar2=1.0, op0=ALU.mult, op1=ALU.add),
                     reads=[t1b], writes=[t1b])
                P.op('dve', lambda e, b=b: e.tensor_tensor(out=t1[:, 0:N], in0=ps[b][:, 0:N], in1=t1[:, 0:N], op=ALU.mult), reads=[psb[b], t1b], writes=[t1b])
                P.op('act', lambda e: e.activation(out=t2[:, 0:N], in_=t1[:, 0:N], func=AF.Sigmoid, scale=1.5957691216), reads=[t1b], writes=[t2b])
                P.op('dve', lambda e, b=b, s_=s_: e.tensor_tensor(out=s_[:, 0:N], in0=ps[b][:, 0:N], in1=t2[:, 0:N], op=ALU.mult),
                     reads=[psb[b], t2b], writes=[sb_])
                dst = dr['GG'][m][:, t * N:(t + 1) * N]
                P.dma('pool', lambda e, s_=s_, dst=dst: e.dma_start(out=dst, in_=s_[:, 0:N]), sb_, reads=[sb_])
            elif m < 2 * KC:
                s_, sb_ = stg[si % 3], stgb[si % 3]
                si += 1
                P.op('act', lambda e, b=b, s_=s_: e.activation(out=s_[:, 0:N], in_=ps[b][:, 0:N], func=AF.Copy), reads=[psb[b]], writes=[sb_])
                dst = RB[m - KC][:, 2 + t * N:2 + (t + 1) * N]
                P.dma('pool', lambda e, s_=s_, dst=dst: e.dma_start(out=dst, in_=s_[:, 0:N]), sb_, reads=[sb_])
            else:
                P.op('act', lambda e, b=b, m=m: e.activation(out=MA.qm[:, m - 2 * KC, 0:N], in_=ps[b][:, 0:N], func=AF.Copy), reads=[psb[b]], writes=[MA.qmb])
        MA.run(NR, psA, psAb, psB, psBb, t)
    C.close()


def phase_rec_p2(P, cfg, dr, l, j, d):
    D, S = cfg['D'], cfg['S']
    KC, NB = D // 128, D // 256
    T = min(2048, S)
    nseg = S // T
    SO, _ = smalls_layout(cfg)
    C = Ctx(P, cfg, dr)
    C.common(nring=4, smalls_layer=l)
    sm = C.sm
    cs, csb = C.sb("cs", [128, 4 * KC]), P.buf("cs")
    lo = SO['lam']
    P.op('act', lambda e: e.activation(out=cs[:, 0:2 * KC], in_=sm[:, lo:lo + 2 * KC], func=AF.Exp, scale=-1.0), reads=[C.smb], writes=[csb])
    P.op('act', lambda e: e.activation(out=cs[:, 0:2 * KC], in_=cs[:, 0:2 * KC], func=AF.Ln, bias=C.cst[:, 1:2], scale=1.0), reads=[csb, C.cstb], writes=[csb])
    P.op('dve', lambda e: e.tensor_scalar(out=cs[:, 2 * KC:4 * KC], in0=cs[:, 0:2 * KC], scalar1=-16.0, scalar2=None, op0=ALU.mult), reads=[csb], writes=[csb])
    P.op('dve', lambda e: e.tensor_scalar(out=cs[:, 0:2 * KC], in0=cs[:, 0:2 * KC], scalar1=-8.0, scalar2=None, op0=ALU.mult), reads=[csb], writes=[csb])
    rb, rbb = C.sb("rb", [128, 2, T + 3]), P.buf("rb", dma=True)
    xc, xcb = C.sb("xc", [128, 2, T]), P.buf("xc")
    names = ['rr', 'ii', 'aa', 'bb', 'hh']
    tl = {n: C.sb(n, [128, T]) for n in names}
    tb = {n: P.buf(n, dma=(n == 'hh')) for n in names}
    hf, hfb = C.sb("hf", [128, T]), P.buf("hf", dma=True)
    gg, ggb = C.sb("gg", [128, T]), P.buf("gg", dma=True)
    st, stb = C.sb("st", [128, 2]), P.buf("st")
    yb, ybb = C.sb("yb", [128, T], BF16), P.buf("yb", dma=True)
    nb512 = max(1, T // 512)
    psr, psrb = C.ps("psr", [128, nb512 * 512]), P.buf("psr")
    psi, psib = C.ps("psi", [128, nb512 * 512]), P.buf("psi")
    gw = dr['gate_w'][j]
    rr, ii, aa, bb, hh = (tl[n] for n in names)
    for n in range(NB):
        for d in [d]:
            segs = list(range(nseg)) if d == 0 else list(range(nseg - 1, -1, -1))
            for si_, seg in enumerate(segs):
                src = dr['RB'][2 * n:2 * n + 2, :, seg * T:seg * T + T + 3].rearrange("k p n -> p k n")
                P.dma('pool', lambda e, src=src: e.dma_start(out=rb[:, :, :], in_=src), rbb, writes=[rbb])
                for c in range(2):
                    ch = 2 * n + c
                    w0, bcol = SO['conv_w'] + ch * 4, SO['conv_b'] + ch
                    P.op('dve', lambda e, c=c, w0=w0, bcol=bcol: e.tensor_scalar(out=xc[:, c, :], in0=rb[:, c, 0:T], scalar1=sm[:, w0:w0 + 1],
                                                                             scalar2=sm[:, bcol:bcol + 1], op0=ALU.mult, op1=ALU.add),
                         reads=[rbb, C.smb], writes=[xcb])
                    for jj in range(1, 4):
                        P.op('dve', lambda e, c=c, w0=w0, jj=jj: e.scalar_tensor_tensor(out=xc[:, c, :], in0=rb[:, c, jj:jj + T], scalar=sm[:, w0 + jj:w0 + jj + 1],
                                                                                    in1=xc[:, c, :], op0=ALU.mult, op1=ALU.add),
                             reads=[rbb, C.smb, xcb], writes=[xcb])
                for oc in range(2):
                    ch = 2 * n + oc
                    for gt, (pst, pstb, dst, dstb) in enumerate([(psr, psrb, rr, tb['rr']), (psi, psib, ii, tb['ii'])]):
                        wt, wb = C.wload(gw[d][gt][n][oc], 2)
                        mm_acc(P, pst, pstb, T, wt, wb, [(lambda n0, n1, k=k: xc[:, k, n0:n1]) for k in range(2)], [xcb])
                        bc = SO['gb'] + (d * 2 + gt) * KC + ch
                        P.op('act', lambda e, pst=pst, dst=dst, bc=bc: e.activation(out=dst[:, 0:T], in_=pst[:, 0:T], func=AF.Sigmoid, bias=sm[:, bc:bc + 1], scale=1.0),
                             reads=[pstb, C.smb], writes=[dstb])
                    c1, c2 = d * KC + ch, 2 * KC + d * KC + ch
                    P.op('act', lambda e, c1=c1: e.activation(out=aa[:, 0:T], in_=rr[:, 0:T], func=AF.Exp, scale=cs[:, c1:c1 + 1]), reads=[tb['rr'], csb], writes=[tb['aa']])
                    P.op('act', lambda e, c2=c2: e.activation(out=bb[:, 0:T], in_=rr[:, 0:T], func=AF.Exp, scale=cs[:, c2:c2 + 1]), reads=[tb['rr'], csb], writes=[tb['bb']])
                    P.op('act', lambda e: e.activation(out=bb[:, 0:T], in_=bb[:, 0:T], func=AF.Sqrt, bias=C.cst[:, 1:2], scale=-1.0), reads=[tb['bb'], C.cstb], writes=[tb['bb']])
                    P.op('dve', lambda e: e.tensor_tensor(out=bb[:, 0:T], in0=bb[:, 0:T], in1=ii[:, 0:T], op=ALU.mult), reads=[tb['bb'], tb['ii']], writes=[tb['bb']])
                    P.op('dve', lambda e, oc=oc: e.tensor_tensor(out=bb[:, 0:T], in0=bb[:, 0:T], in1=xc[:, oc, :], op=ALU.mult), reads=[tb['bb'], xcb], writes=[tb['bb']])
                    init = 0.0 if si_ == 0 else st[:, oc:oc + 1]
                    if d == 0:
                        P.op('dve', lambda e, init=init: e.tensor_tensor_scan(out=hh[:, 0:T], data0=aa[:, 0:T], data1=bb[:, 0:T], initial=init, op0=ALU.mult, op1=ALU.add),
                             reads=[tb['aa'], tb['bb'], stb], writes=[tb['hh']])
                        P.op('dve', lambda e, oc=oc: e.tensor_copy(out=st[:, oc:oc + 1], in_=hh[:, T - 1:T]), reads=[tb['hh']], writes=[stb])
                        dst = dr['HF'][ch][:, seg * T:(seg + 1) * T]
                        P.dma('pool', lambda e, dst=dst: e.dma_start(out=dst, in_=hh[:, 0:T]), tb['hh'], reads=[tb['hh']])
                    else:
                        s1 = dr['HF'][ch][:, seg * T:(seg + 1) * T]
                        s2 = dr['GG'][ch][:, seg * T:(seg + 1) * T]
                        P.dma('pool', lambda e, s1=s1: e.dma_start(out=hf[:, 0:T], in_=s1), hfb, writes=[hfb])
                        P.dma('pool', lambda e, s2=s2: e.dma_start(out=gg[:, 0:T], in_=s2), ggb, writes=[ggb])
                        P.op('dve', lambda e, init=init: e.tensor_tensor_scan(out=hh[:, ::-1], data0=aa[:, ::-1], data1=bb[:, ::-1], initial=init, op0=ALU.mult, op1=ALU.add),
                             reads=[tb['aa'], tb['bb'], stb], writes=[tb['hh']])
                        P.op('dve', lambda e, oc=oc: e.tensor_copy(out=st[:, oc:oc + 1], in_=hh[:, 0:1]), reads=[tb['hh']], writes=[stb])
                        P.op('dve', lambda e: e.tensor_tensor(out=hh[:, 0:T], in0=hh[:, 0:T], in1=hf[:, 0:T], op=ALU.add), reads=[tb['hh'], hfb], writes=[tb['hh']])
                        P.op('dve', lambda e: e.tensor_tensor(out=yb[:, 0:T], in0=hh[:, 0:T], in1=gg[:, 0:T], op=ALU.mult), reads=[tb['hh'], ggb], writes=[ybb])
                        dst = dr['YR'][ch][:, seg * T:(seg + 1) * T]
                        P.dma('pool', lambda e, dst=dst: e.dma_start(out=dst, in_=yb[:, 0:T]), ybb, reads=[ybb])
    C.close()


def phase_wout(P, cfg, dr, X, Xd, w_out, YMAIN, nmain):
    D, S, MH = cfg['D'], cfg['S'], cfg['MH']
    KC, N = D // 128, min(512, cfg['S'])
    YC = nmain + 2 * MH
    C = Ctx(P, cfg, dr)
    C.common(nring=4, wdt=BF16)
    xt, xtb = C.sb("xt", [128, KC, N]), P.buf("xt", dma=True)
    yt, ytb = C.sb("yt", [128, YC, N], BF16), P.buf("yt", dma=True)
    pso = [C.ps("pso%d" % j, [128, 512]) for j in range(2)]
    psob = [P.buf("o") for _ in range(2)]
    kcn = max(k for k in range(1, 17) if YC % k == 0)
    for t in range(S // N):
        load_x(P, C, X, xt, xtb, t, N)
        s1 = YMAIN[:, :, t * N:(t + 1) * N].rearrange("k p n -> p k n")
        s2 = dr['YM'][:, :, t * N:(t + 1) * N].rearrange("k p n -> p k n")
        P.dma('pool', lambda e, s1=s1: e.dma_start(out=yt[:, 0:nmain, :], in_=s1), ytb, writes=[ytb])
        P.dma('pool', lambda e, s2=s2: e.dma_start(out=yt[:, nmain:YC, :], in_=s2), ytb, writes=[ytb])
        for m in range(KC):
            b = m % 2
            npc = YC // kcn
            for pc in range(npc):
                wt, wb = C.wload(w_out[m][:, pc * kcn:(pc + 1) * kcn, :], kcn)
                acts2 = [(lambda n0, n1, k=k: yt[:, k, n0:n1]) for k in range(pc * kcn, (pc + 1) * kcn)]
                mm_acc(P, pso[b], psob[b], N, wt, wb, acts2, [ytb], first=(pc == 0), last=(pc == npc - 1))
            P.op('dve', lambda e, b=b, m=m: e.tensor_tensor(out=xt[:, m, 0:N], in0=pso[b][:, 0:N], in1=xt[:, m, 0:N], op=ALU.add),
                 reads=[psob[b], xtb], writes=[xtb])
        store_x(P, C, Xd, xt, xtb, t, N)
    C.close()


def phase_att_p1(P, cfg, dr, X, l, j):
    D, S, MH, AH = cfg['D'], cfg['S'], cfg['MH'], cfg['AH']
    KC, N = D // 128, min(512, cfg['S'])
    SO, _ = smalls_layout(cfg)
    C = Ctx(P, cfg, dr)
    C.common(nring=4, smalls_layer=l, wdt=BF16)
    identf, identfb = C.sb("identf", [128, 128]), P.buf("identf", dma=True)
    P.dma('pool', lambda e: e.dma_start(out=identf[:, :], in_=dr['ident']), identfb, writes=[identfb])
    ident, identb = C.sb("ident", [128, 128], BF16), P.buf("ident")
    P.op('dve', lambda e: e.tensor_copy(out=ident[:, :], in_=identf[:, :]), reads=[identfb], writes=[identb])
    Ctx._uid[0] += 1
    psT = C.es.enter_context(C.nc.psum_tensor("psT_att_u%d" % Ctx._uid[0], [128, 128], BF16))
    psTb = P.buf("psT")
    w_in = dr['att_w_in_bf'][j]
    xt, xtb = C.sb("xt", [128, KC, N]), P.buf("xt", dma=True)
    xn, xnb = C.sb("xn", [128, KC, N], BF16), P.buf("xn")
    NR = NormRes(C, N)
    NR2 = NormRes(C, N, "b")
    MA = MemAtt(C, N)
    ps = [C.ps("ps%d" % i, [128, 512]) for i in range(2)]
    psb = [P.buf("ps") for _ in range(2)]
    psA, psAb, psB, psBb = C.ps("psA", [128, 512]), P.buf("psA"), C.ps("psB", [128, 512]), P.buf("psB")
    stg = [C.sb("stg%d" % i, [128, N], BF16) for i in range(3)]
    stgb = [P.buf("stg", dma=True) for _ in range(3)]
    vs, vsb = C.sb("vs", [128, N], BF16), P.buf("vs")
    vt, vtb = C.sb("vt", [128, N // 128, 128], BF16), P.buf("vt", dma=True)
    si = 0
    nqkv = 9 * AH
    for t in range(S // N):
        load_x(P, C, X, xt, xtb, t, N)
        rmsnorm_tile(P, C, NR, xt, xtb, xn, xnb, KC, N, SO['mix_g'])
        acts = [(lambda n0, n1, k=k: xn[:, k, n0:n1]) for k in range(KC)]
        for m in range(nqkv + 2 * MH):
            b = m % 2
            wt, wb = C.wload(w_in[m], KC)
            mm_acc(P, ps[b], psb[b], N, wt, wb, acts, [xnb])
            if m >= nqkv:
                P.op('act', lambda e, b=b, m=m: e.activation(out=MA.qm[:, m - nqkv, 0:N], in_=ps[b][:, 0:N], func=AF.Copy), reads=[psb[b]], writes=[MA.qmb])
                continue
            g, kind, hd_ = m // (3 * AH), (m % (3 * AH)) // AH, m % AH
            if kind < 2:
                s_, sb_ = stg[si % 3], stgb[si % 3]
                si += 1
                rms_rstd(P, C, [ps[b][:, 0:N]], [psb[b]], N, 128, NR2.sq, NR2.sqb, NR2.psn, NR2.psnb, NR2.rs, NR2.rsb)
                gc = SO['qg' if kind == 0 else 'kg'] + g
                P.op('dve', lambda e, b=b, s_=s_, gc=gc: e.scalar_tensor_tensor(out=s_[:, 0:N], in0=ps[b][:, 0:N], scalar=C.sm[:, gc:gc + 1], in1=NR2.rs[:, 0:N],
                                                                            op0=ALU.mult, op1=ALU.mult), reads=[psb[b], NR2.rsb, C.smb], writes=[sb_])
                dst = dr['QK'][(g * 2 + kind) * AH + hd_][:, t * N:(t + 1) * N]
                P.dma('pool', lambda e, s_=s_, dst=dst: e.dma_start(out=dst, in_=s_[:, 0:N]), sb_, reads=[sb_])
            else:
                P.op('act', lambda e, b=b: e.activation(out=vs[:, 0:N], in_=ps[b][:, 0:N], func=AF.Copy), reads=[psb[b]], writes=[vsb])
                for bi in range(N // 128):
                    P.op('pe', lambda e, bi=bi: e.transpose(out=psT[:, 0:128], in_=vs[:, bi * 128:(bi + 1) * 128], identity=ident[:, :]),
                         reads=[vsb, identb], writes=[psTb])
                    P.op('act', lambda e, bi=bi: e.activation(out=vt[:, bi, :], in_=psT[:, 0:128], func=AF.Copy), reads=[psTb], writes=[vtb])
                dst = dr['V'][g * AH + hd_][t * N:(t + 1) * N, :].rearrange("(c p) d -> p c d", p=128)
                P.dma('pool', lambda e, dst=dst: e.dma_start(out=dst, in_=vt[:, :, :]), vtb, reads=[vtb])
        MA.run(NR, psA, psAb, psB, psBb, t)
    C.close()


def alibi_slope(cfg, g, h):
    n = 3 * cfg['AH']
    return float(np.exp2(np.float32(-8.0) * (np.float32(g * cfg['AH'] + h) + np.float32(1.0)) / np.float32(n)))


def phase_att_p2(P, cfg, dr):
    S, AH = cfg['S'], cfg['AH']
    C = Ctx(P, cfg, dr)
    C.common(nring=0)
    rel, relb = C.sb("rel", [128, 6, 512]), P.buf("rel", dma=True)
    P.dma('pool', lambda e: e.dma_start(out=rel[:, :, :], in_=dr['c_rel'].rearrange("c p n -> p c n")), relb, writes=[relb])
    bt, btb = C.sb("bt", [128, 6, 512]), P.buf("bt")
    qT, qTb = C.sb("qT", [128, S], BF16), P.buf("qT", dma=True)
    kT, kTb = C.sb("kT", [128, S], BF16), P.buf("kT", dma=True)
    vc, vcb = C.sb("vc", [128, S // 128, 128], BF16), P.buf("vc", dma=True)
    ones16, ones16b = C.sb("ones16", [128, 128], BF16), P.buf("ones16")
    P.op('pool', lambda e: e.memset(ones16[:, :], 1.0), writes=[ones16b])
    aO, aOb = C.sb("aO", [128, S]), P.buf("aO", dma=True)
    aD, aDb = C.sb("aD", [128, S]), P.buf("aD")
    tmp = [C.sb("tmp%d" % i, [128, 512]) for i in range(2)]
    tmpb = [P.buf("tmp") for _ in range(2)]
    ya = [C.sb("ya%d" % i, [128, min(2048, S)], BF16) for i in range(2)]
    yab = [P.buf("ya", dma=True) for _ in range(2)]
    pt = [C.sb("pt%d" % i, [128, 512], BF16) for i in range(2)]
    ptb = [P.buf("pt") for _ in range(2)]
    psS = [C.ps("psS%d" % i, [128, 512]) for i in range(2)]
    psSb = [P.buf("psS") for _ in range(2)]
    psO = [C.ps("psO%d" % i, [128, 512]) for i in range(2)]
    psOb = [P.buf("psO") for _ in range(2)]
    psD = [C.ps("psD%d" % i, [128, 512]) for i in range(2)]
    psDb = [P.buf("psD") for _ in range(2)]
    it = 0
    qi = 0
    for h in range(AH):
        for g, (win, d) in enumerate(cfg['PAT']):
            L = S // d
            QT = min(512, L)
            nck = L // 128
            P.dma('pool', lambda e, g=g, h=h: e.dma_start(out=qT[:, :], in_=dr['QK'][(g * 2 + 0) * AH + h]), qTb, writes=[qTb])
            P.dma('pool', lambda e, g=g, h=h: e.dma_start(out=kT[:, :], in_=dr['QK'][(g * 2 + 1) * AH + h]), kTb, writes=[kTb])
            Vd = dr['V'][g * AH + h].rearrange("(c i r) e -> r i c e", i=128, r=d)
            for r in range(d):
                P.dma('pool', lambda e, r=r, Vd=Vd, nck=nck: e.dma_start(out=vc[:, r * nck:(r + 1) * nck, :], in_=Vd[r]), vcb, writes=[vcb])
            sl = -alibi_slope(cfg, g, h) * d
            for ci in range(6):
                P.op('dve', lambda e, ci=ci, sl=sl: e.tensor_scalar(out=bt[:, ci, :], in0=rel[:, ci, :], scalar1=sl, scalar2=None, op0=ALU.mult),
                     reads=[relb], writes=[btb])
            for r in range(d):
                for j0 in range(0, L, QT):
                    qb = qi % 2
                    qi += 1
                    qap = qT[:, r + d * j0: r + d * (j0 + QT - 1) + 1: d]
                    cis = [ci for ci in range(QT // 128 + 2) if 0 <= j0 - 128 + 128 * ci and j0 - 128 + 128 * ci + 128 <= L]
                    for idx, ci in enumerate(cis):
                        kj = j0 - 128 + 128 * ci
                        b = it % 2
                        it += 1
                        kap = kT[:, r + d * kj: r + d * (kj + 127) + 1: d]
                        P.op('pe', lambda e, b=b, kap=kap, qap=qap, QT=QT: e.matmul(psS[b][:, 0:QT], lhsT=kap, rhs=qap, start=True, stop=True),
                             reads=[kTb, qTb], writes=[psSb[b]])
                        P.op('dve', lambda e, b=b, ci=ci, QT=QT: e.scalar_tensor_tensor(out=tmp[b][:, 0:QT], in0=psS[b][:, 0:QT], scalar=128.0 ** -0.5, in1=bt[:, ci, 0:QT],
                                                                             op0=ALU.mult, op1=ALU.add), reads=[psSb[b], btb], writes=[tmpb[b]])
                        P.op('act', lambda e, b=b, QT=QT: e.activation(out=pt[b][:, 0:QT], in_=tmp[b][:, 0:QT], func=AF.Exp), reads=[tmpb[b]], writes=[ptb[b]])
                        vck = r * nck + kj // 128
                        P.op('pe', lambda e, b=b, qb=qb, vck=vck, idx=idx, QT=QT, nci=len(cis): e.matmul(psO[qb][:, 0:QT], lhsT=vc[:, vck, :], rhs=pt[b][:, 0:QT],
                                                                                 start=(idx == 0), stop=(idx == nci - 1)), reads=[vcb, ptb[b]], writes=[psOb[qb]])
                        P.op('pe', lambda e, b=b, qb=qb, idx=idx, QT=QT, nci=len(cis): e.matmul(psD[qb][:, 0:QT], lhsT=ones16[:, :], rhs=pt[b][:, 0:QT],
                                                                        start=(idx == 0), stop=(idx == nci - 1)), reads=[ones16b, ptb[b]], writes=[psDb[qb]])
                    oap = aO[:, r + d * j0: r + d * (j0 + QT - 1) + 1: d]
                    dap = aD[:, r + d * j0: r + d * (j0 + QT - 1) + 1: d]
                    if g == 0:
                        P.op('act', lambda e, qb=qb, oap=oap, QT=QT: e.activation(out=oap, in_=psO[qb][:, 0:QT], func=AF.Copy), reads=[psOb[qb]], writes=[aOb])
                        P.op('dve', lambda e, qb=qb, dap=dap, QT=QT: e.tensor_copy(out=dap, in_=psD[qb][:, 0:QT]), reads=[psDb[qb]], writes=[aDb])
                    else:
                        P.op('dve', lambda e, qb=qb, oap=oap, QT=QT: e.tensor_tensor(out=oap, in0=psO[qb][:, 0:QT], in1=oap, op=ALU.add), reads=[psOb[qb], aOb], writes=[aOb])
                        P.op('dve', lambda e, qb=qb, dap=dap, QT=QT: e.tensor_tensor(out=dap, in0=psD[qb][:, 0:QT], in1=dap, op=ALU.add), reads=[psDb[qb], aDb], writes=[aDb])
        P.op('dve', lambda e: e.reciprocal(out=aD[:, :], in_=aD[:, :]), reads=[aDb], writes=[aDb])
        CW = min(2048, S)
        for cc in range(S // CW):
            yb_, ybb_ = ya[cc % 2], yab[cc % 2]
            P.op('dve', lambda e, cc=cc, yb_=yb_: e.tensor_tensor(out=yb_[:, 0:CW], in0=aO[:, cc * CW:(cc + 1) * CW], in1=aD[:, cc * CW:(cc + 1) * CW], op=ALU.mult),
                 reads=[aOb, aDb], writes=[ybb_])
            P.dma('pool', lambda e, h=h, cc=cc, yb_=yb_: e.dma_start(out=dr['YA'][h][:, cc * CW:(cc + 1) * CW], in_=yb_[:, 0:CW]), ybb_, reads=[ybb_])
    C.close()


def build(cfg):
    D, S, FF, L, MH, AH, M = cfg['D'], cfg['S'], cfg['FF'], cfg['DEPTH'], cfg['MH'], cfg['AH'], cfg['M']
    KC, FC = D // 128, FF // 128
    NREC, NATT = (L + 1) // 2, L // 2
    MW = MH * 256
    _, NS = smalls_layout(cfg)
    nc = bass.Bass("TRN2", target_bir_lowering=False)
    dr = {}

    def din(name, shape):
        dr[name] = nc.dram_tensor(name, list(shape), F32, kind="ExternalInput").ap()

    def dsc(name, shape, dt=F32):
        dr[name] = nc.dram_tensor(name, list(shape), dt, kind="ExternalOutput" if cfg.get('dbg') else "Internal").ap()
    din('xT', [KC, 128, S])
    din('memT', [KC, 128, M])
    din('smalls', [L, 128, NS])
    din('ident', [128, 128])
    din('c_rel', [6, 128, 512])
    din('c_negm', [6, 128, 512])
    din('ffn_w_in', [L * 2, 2 * FC, 128, KC, 128])
    din('ffn_w_out', [L * 2, KC, 128, FC, 128])
    din('mem_w_kv', [L, 2 * MW // 128, 128, KC, 128])
    din('rec_w_in', [NREC, 2 * KC + 2 * MH, 128, KC, 128])
    din('rec_w_out', [NREC, KC, 128, KC + 2 * MH, 128])
    din('gate_w', [NREC, 2, 2, D // 256, 2, 128, 2, 128])
    if NATT:
        din('att_w_in', [NATT, 9 * AH + 2 * MH, 128, KC, 128])
        din('att_w_out', [NATT, KC, 128, AH + 2 * MH, 128])
    dr['out'] = nc.dram_tensor('out', [KC, 128, S], F32, kind="ExternalOutput").ap()
    dsc('KN', [128, 2 * MH, M])
    dsc('VM', [128, M // 128, MW])
    dsc('YM', [2 * MH, 128, S], BF16)
    dsc('GG', [KC, 128, S])
    dsc('RB', [KC, 128, S + 3])
    dsc('HF', [KC, 128, S])
    dsc('YR', [KC, 128, S], BF16)
    dsc('QK', [6 * AH, 128, S], BF16)
    dsc('V', [3 * AH, S, 128], BF16)
    dsc('YA', [AH, 128, S], BF16)
    for name in BIG:
        if name in dr:
            dr[name + '_bf'] = [nc.dram_tensor("%s_bf%d" % (name, i), list(dr[name].shape[1:]), BF16, kind="Internal").ap()
                                for i in range(dr[name].shape[0])]
    P = Prog(nc)
    PC = Precast(P, cfg, dr)
    phase_precast(P, cfg, dr, PC)
    X = dr['out']
    stop = cfg.get('stop')
    for l in range(L):
        phase_ffn(P, cfg, dr, dr['xT'] if l == 0 else X, X, l, 0, PC)
        if stop == 'ffn0':
            break
        phase_memkv(P, cfg, dr, l)
        j = l // 2
        if l % 2 == 0:
            phase_rec_p1(P, cfg, dr, X, l, j)
            phase_rec_p2(P, cfg, dr, l, j, 0)
            phase_rec_p2(P, cfg, dr, l, j, 1)
            phase_wout(P, cfg, dr, X, X, dr['rec_w_out_bf'][j], dr['YR'], KC)
            if stop == 'mix0':
                break
        else:
            phase_att_p1(P, cfg, dr, X, l, j)
            phase_att_p2(P, cfg, dr)
            phase_wout(P, cfg, dr, X, X, dr['att_w_out_bf'][j], dr['YA'], AH)
        phase_ffn(P, cfg, dr, X, X, l, 1, PC)
    return nc


def pieces(W):
    K, Mo = W.shape[-2], W.shape[-1]
    lead = W.shape[:-2]
    Wr = W.reshape(*lead, K // 128, 128, Mo // 128, 128)
    nl = len(lead)
    perm = list(range(nl)) + [nl + 2, nl + 1, nl + 0, nl + 3]
    return np.ascontiguousarray(Wr.transpose(perm))


def cols(v):
    n = v.shape[-1] // 128
    return np.swapaxes(v.reshape(*v.shape[:-1], n, 128), -1, -2)


def const_tables():
    k = np.arange(128)[:, None]
    q = np.arange(512)[None, :]
    rel = np.stack([np.abs(128 * (ci - 1) + k - q) for ci in range(6)]).astype(np.float32)
    rel = np.where(rel <= 64, rel, 1.0e6).astype(np.float32)
    return rel, rel


def prep_inputs(cfg, inp):
    D, S, L = cfg['D'], cfg['S'], cfg['DEPTH']
    KC = D // 128
    SO, NS = smalls_layout(cfg)
    f = lambda a: np.asarray(a, dtype=np.float32)
    sm = np.zeros((L, 128, NS), np.float32)
    for l in range(L):
        j = l // 2
        sm[l, :, SO['ffn_g0']:SO['ffn_g0'] + KC] = cols(f(inp['ffn_norm'])[l, 0])
        sm[l, :, SO['ffn_g1']:SO['ffn_g1'] + KC] = cols(f(inp['ffn_norm'])[l, 1])
        sm[l, :, SO['mix_g']:SO['mix_g'] + KC] = cols(f(inp['mix_norm'])[l])
        sm[l, :, SO['mem_g']:SO['mem_g'] + KC] = cols(f(inp['mem_norm'])[l])
        sm[l, :, SO['memq_g']:SO['memq_g'] + 2] = cols(f(inp['mem_qk_gain'])[l, 0])
        sm[l, :, SO['memk_g']:SO['memk_g'] + 2] = cols(f(inp['mem_qk_gain'])[l, 1])
        if l % 2 == 0:
            cw = f(inp['rec_conv_w'])[j]
            sm[l, :, SO['conv_w']:SO['conv_w'] + 4 * KC] = np.stack([cols(cw[t]) for t in range(4)], -1).reshape(128, 4 * KC)
            sm[l, :, SO['conv_b']:SO['conv_b'] + KC] = cols(f(inp['rec_conv_b'])[j])
            gb = f(inp['rec_gate_b'])[j].reshape(2, 2, D)
            for d in range(2):
                for gt in range(2):
                    o = SO['gb'] + (d * 2 + gt) * KC
                    sm[l, :, o:o + KC] = cols(gb[d, gt])
                o = SO['lam'] + d * KC
                sm[l, :, o:o + KC] = cols(f(inp['rec_lambda'])[j, d])
        else:
            qk = f(inp['att_qk_gain'])[j]
            sm[l, :, SO['qg']:SO['qg'] + 3] = qk[0].T
            sm[l, :, SO['kg']:SO['kg'] + 3] = qk[1].T
    rel, negm = const_tables()
    shared = {
        'smalls': sm, 'ident': np.eye(128, dtype=np.float32), 'c_rel': rel, 'c_negm': negm,
        'ffn_w_in': pieces(f(inp['ffn_w_in'])).reshape(L * 2, -1, 128, KC, 128),
        'ffn_w_out': pieces(f(inp['ffn_w_out'])).reshape(L * 2, KC, 128, -1, 128),
        'mem_w_kv': pieces(f(inp['mem_w_kv'])),
        'rec_w_in': pieces(f(inp['rec_w_in'])),
        'rec_w_out': pieces(f(inp['rec_w_out'])),
        'gate_w': pieces(f(inp['rec_gate_w'])),
    }
    if L // 2:
        shared['att_w_in'] = pieces(f(inp['att_w_in']))
        shared['att_w_out'] = pieces(f(inp['att_w_out']))
    x, mem = f(inp['x']), f(inp['mem'])
    B = x.shape[0]
    maps = []
    for b in range(B):
        m = dict(shared)
        m['xT'] = np.ascontiguousarray(x[b].T).reshape(KC, 128, S)
        m['memT'] = np.ascontiguousarray(mem[b].T).reshape(KC, 128, -1)
        maps.append(m)
    return maps


def run(cfg, inp):
    nc = build(cfg)
    maps = prep_inputs(cfg, inp)
    B = len(maps)
    res = run_bass_kernel_spmd(nc, maps, core_ids=list(range(B)))
    D, S = cfg['D'], cfg['S']
    if cfg.get('dbg'):
        cfg['dbg_out'] = res.results
    return np.stack([np.ascontiguousarray(res.results[b]['out'].reshape(D, S).T) for b in range(B)]).astype(np.float32)


def kernel(**inputs):
    return run(FULL, inputs)
```

```python
import numpy as np
from contextlib import ExitStack
import concourse.bass as bass
import concourse.mybir as mybir
from concourse.bass_utils import run_bass_kernel_spmd

F32 = mybir.dt.float32
BF16 = mybir.dt.bfloat16
BIG = ['ffn_w_in', 'ffn_w_out', 'rec_w_in', 'rec_w_out', 'att_w_in', 'att_w_out']
AF = mybir.ActivationFunctionType
ALU = mybir.AluOpType
ENG = ['pe', 'act', 'dve', 'pool', 'sp']

FULL = dict(D=2048, S=8192, FF=5632, DEPTH=4, MH=4, AH=8, M=256, PAT=((128, 1), (512, 4), (2048, 16)))


class Buf:
    def __init__(s, name, di=None):
        s.name, s.lw, s.rd, s.di = name, None, {}, di


class Prog:
    def __init__(s, nc, ndsem=40):
        s.nc = nc
        s.es = ExitStack()
        s.eng = {'pe': nc.tensor, 'act': nc.scalar, 'dve': nc.vector, 'pool': nc.gpsimd, 'sp': nc.sync}
        s.esem = {e: s.es.enter_context(nc.semaphore("es_" + e)) for e in ENG}
        s.ecnt = {e: 0 for e in ENG}
        s.dsem = [s.es.enter_context(nc.semaphore("ds%d" % i)) for i in range(ndsem)]
        s.dcnt = [0] * ndsem
        s.q = {e: [] for e in ENG}
        s.seen = {e: {} for e in ENG}
        s.nextd = 0

    def buf(s, name, dma=False):
        di = None
        if dma:
            di = s.nextd % len(s.dsem)
            s.nextd += 1
        return Buf(name, di)

    def _sem(s, key):
        return s.esem[key[1]] if key[0] == 'e' else s.dsem[key[1]]

    def _waits(s, eng, evs):
        out = []
        for ev in evs:
            if ev is None:
                continue
            key, val = ev
            if key == ('e', eng) and eng == 'pe':
                continue
            if s.seen[eng].get(key, 0) >= val:
                continue
            s.seen[eng][key] = val
            out.append(ev)
        return out

    def _deps(s, reads, writes):
        evs = [b.lw for b in reads]
        for b in writes:
            evs.append(b.lw)
            evs += list(b.rd.items())
        return evs

    def _upd(s, ev, reads, writes):
        for b in reads:
            b.rd[ev[0]] = max(b.rd.get(ev[0], 0), ev[1])
        for b in writes:
            b.lw = ev
            b.rd = {}

    def op(s, eng, fn, reads=(), writes=()):
        w = s._waits(eng, s._deps(reads, writes))
        s.ecnt[eng] += 1
        ev = (('e', eng), s.ecnt[eng])
        s.q[eng].append((fn, w, ('e', eng), 1))
        s._upd(ev, reads, writes)
        return ev

    def dma(s, eng, fn, sb, reads=(), writes=()):
        di = sb.di
        evs = s._deps(reads, writes)
        if s.dcnt[di] > 0:
            evs.append((('d', di), s.dcnt[di]))
        w = s._waits(eng, evs)
        s.dcnt[di] += 16
        ev = (('d', di), s.dcnt[di])
        s.q[eng].append((fn, w, ('d', di), 16))
        s._upd(ev, reads, writes)
        return ev

    def barrier(s):
        for e in ENG:
            evs = [(('e', e2), s.ecnt[e2]) for e2 in ENG if e2 != e and s.ecnt[e2] > 0]
            evs += [(('d', i), c) for i, c in enumerate(s.dcnt) if c > 0]
            s.q[e].append((None, s._waits(e, evs), None, 0))

    def flush(s):
        s.barrier()
        with s.nc.Block() as block:
            def mk(e):
                def run(eo):
                    for (fn, w, key, inc) in s.q[e]:
                        for (k, v) in w:
                            eo.wait_ge(s._sem(k), v)
                        if fn is not None:
                            fn(eo).then_inc(s._sem(key), inc)
                return run
            block.tensor(mk('pe'))
            block.scalar(mk('act'))
            block.vector(mk('dve'))
            block.gpsimd(mk('pool'))
            block.sync(mk('sp'))
        s.q = {e: [] for e in ENG}


class Ctx:
    def __init__(s, P, cfg, dr):
        s.P, s.cfg, s.dr, s.nc = P, cfg, dr, P.nc
        s.es = ExitStack()
        s.wi = 0

    _uid = [0]

    def sb(s, name, shape, dt=F32):
        Ctx._uid[0] += 1
        return s.es.enter_context(s.nc.sbuf_tensor("%s_u%d" % (name, Ctx._uid[0]), list(shape), dt))

    def ps(s, name, shape):
        Ctx._uid[0] += 1
        return s.es.enter_context(s.nc.psum_tensor("%s_u%d" % (name, Ctx._uid[0]), list(shape), F32))

    def common(s, nring=4, smalls_layer=None, wdt=F32):
        P = s.P
        s.ring = [s.sb("wr%d" % i, [128, 2048], wdt) for i in range(nring)]
        s.ringb = [P.buf("wr%d" % i, dma=True) for i in range(nring)]
        s.ones = s.sb("ones", [128, 128])
        s.onesb = P.buf("ones")
        s.cst = s.sb("cst", [128, 4])
        s.cstb = P.buf("cst")
        o, c = s.ones, s.cst
        P.op('pool', lambda e: e.memset(o[:, :], 1.0), writes=[s.onesb])
        P.op('pool', lambda e: e.memset(c[:, 0:1], 1e-6), writes=[s.cstb])
        P.op('pool', lambda e: e.memset(c[:, 1:2], 1.0), writes=[s.cstb])
        if smalls_layer is not None:
            ns = s.dr['smalls'].shape[2]
            s.sm = s.sb("smalls", [128, ns])
            s.smb = P.buf("smalls", dma=True)
            sm, src = s.sm, s.dr['smalls'][smalls_layer]
            P.dma('pool', lambda e: e.dma_start(out=sm[:, :], in_=src), s.smb, writes=[s.smb])

    def wload(s, src, kcn):
        i = s.wi % len(s.ring)
        s.wi += 1
        t, b = s.ring[i], s.ringb[i]
        s.P.dma('sp', lambda e: e.dma_start(out=t[:, 0:kcn * 128], in_=src.rearrange("p k c -> p (k c)")), b, writes=[b])
        return t, b

    def close(s):
        s.P.flush()
        s.es.close()


def mm_acc(P, ps, psb, n, wt, wb, acts, actb, first=True, last=True):
    K = len(acts)
    for n0 in range(0, n, 512):
        n1 = min(n, n0 + 512)
        for k in range(K):
            a = acts[k]
            pb = psb[n0 // 512] if isinstance(psb, list) else psb
            P.op('pe', lambda e, k=k, a=a, n0=n0, n1=n1: e.matmul(ps[:, n0:n1], lhsT=wt[:, k * 128:(k + 1) * 128], rhs=a(n0, n1),
                                                            start=(first and k == 0), stop=(last and k == K - 1)),
                 reads=[wb] + actb, writes=[pb])


def rms_rstd(P, C, srcs, srcb, n, dim, sqs, sqb, psn, psnb, rs, rsb):
    K = len(srcs)
    for k in range(K):
        sq, sb_ = sqs[k % 2], sqb[k % 2]
        a = srcs[k]
        P.op('act', lambda e, sq=sq, a=a: e.activation(out=sq[:, 0:n], in_=a, func=AF.Square), reads=srcb, writes=[sb_])
        for n0 in range(0, n, 512):
            n1 = min(n, n0 + 512)
            P.op('pe', lambda e, sq=sq, k=k, n0=n0, n1=n1: e.matmul(psn[:, n0:n1], lhsT=C.ones[:, :], rhs=sq[:, n0:n1],
                                                                start=(k == 0), stop=(k == K - 1)),
                 reads=[sb_, C.onesb], writes=[psnb])
    P.op('act', lambda e: e.activation(out=rs[:, 0:n], in_=psn[:, 0:n], func=AF.Sqrt, bias=C.cst[:, 0:1], scale=1.0 / dim),
         reads=[psnb, C.cstb], writes=[rsb])
    P.op('dve', lambda e: e.reciprocal(out=rs[:, 0:n], in_=rs[:, 0:n]), reads=[rsb], writes=[rsb])


class NormRes:
    def __init__(s, C, n, tag=""):
        P = C.P
        s.sq = [C.sb("nsq0" + tag, [128, n]), C.sb("nsq1" + tag, [128, n])]
        s.sqb = [P.buf("nsq0"), P.buf("nsq1")]
        s.psn = C.ps("npsn" + tag, [128, max(n, 512)])
        s.psnb = P.buf("npsn")
        s.rs = C.sb("nrs" + tag, [128, n])
        s.rsb = P.buf("nrs")


def rmsnorm_tile(P, C, NR, xt, xtb, xn, xnb, KC, n, gcol0):
    rms_rstd(P, C, [xt[:, k, 0:n] for k in range(KC)], [xtb], n, KC * 128, NR.sq, NR.sqb, NR.psn, NR.psnb, NR.rs, NR.rsb)
    for k in range(KC):
        P.op('dve', lambda e, k=k: e.scalar_tensor_tensor(out=xn[:, k, 0:n], in0=xt[:, k, 0:n], scalar=C.sm[:, gcol0 + k:gcol0 + k + 1],
                                                       in1=NR.rs[:, 0:n], op0=ALU.mult, op1=ALU.mult),
             reads=[xtb, NR.rsb, C.smb], writes=[xnb])


def smalls_layout(cfg):
    KC = cfg['D'] // 128
    off, o = {}, 0
    for name, n in [('ffn_g0', KC), ('ffn_g1', KC), ('mix_g', KC), ('mem_g', KC), ('memq_g', 2), ('memk_g', 2),
                    ('conv_w', KC * 4), ('conv_b', KC), ('gb', 4 * KC), ('lam', 2 * KC), ('qg', 3), ('kg', 3)]:
        off[name] = o
        o += n
    return off, o


def load_x(P, C, X, xt, xtb, t, N):
    src = X[:, :, t * N:(t + 1) * N].rearrange("k p n -> p k n")
    P.dma('pool', lambda e: e.dma_start(out=xt[:, :, :], in_=src), xtb, writes=[xtb])


def store_x(P, C, X, xt, xtb, t, N):
    dst = X[:, :, t * N:(t + 1) * N].rearrange("k p n -> p k n")
    P.dma('pool', lambda e: e.dma_start(out=dst, in_=xt[:, :, :]), xtb, reads=[xtb])


def phase_ffn(P, cfg, dr, Xs, Xd, l, i, PC=None):
    D, S, FF = cfg['D'], cfg['S'], cfg['FF']
    KC, FC, N = D // 128, FF // 128, min(1024, cfg['S'])
    NBK = N // 512
    SO, _ = smalls_layout(cfg)
    C = Ctx(P, cfg, dr)
    C.common(nring=4, smalls_layer=l, wdt=BF16)
    w_in, w_out = dr['ffn_w_in_bf'][l * 2 + i], dr['ffn_w_out_bf'][l * 2 + i]
    kcn = max(k for k in range(1, 17) if FC % k == 0)
    npc = FC // kcn
    nhalf = 2 if npc % 2 == 0 else 1
    HC = FC // nhalf
    xt, xtb = C.sb("xt", [128, KC, N]), P.buf("xt", dma=True)
    xn, xnb = C.sb("xn", [128, KC, N], BF16), P.buf("xn")
    h, hb = C.sb("h", [128, HC, N], BF16), P.buf("h")
    NR = NormRes(C, N)
    psg, psu, pso = C.ps("psg", [128, N]), C.ps("psu", [128, N]), C.ps("pso", [128, N])
    psgb, psub, psob = [P.buf("g") for _ in range(NBK)], [P.buf("u") for _ in range(NBK)], [P.buf("o") for _ in range(NBK)]
    if PC is not None:
        PC.alloc(C, 2)
    npieces = (S // N) * (2 * FC + KC * npc)
    todo = len(PC.chunks.get(l + 1, [])) if PC is not None else 0
    stride = max(1, (npieces * (2 - i)) // max(1, todo) - 1) if todo else 0
    cnt = [0]

    def tick():
        cnt[0] += 1
        if stride and cnt[0] % stride == 0:
            PC.pump(l + 1, 1)
    for t in range(S // N):
        load_x(P, C, Xs, xt, xtb, t, N)
        rmsnorm_tile(P, C, NR, xt, xtb, xn, xnb, KC, N, SO['ffn_g%d' % i])
        acts = [(lambda n0, n1, k=k: xn[:, k, n0:n1]) for k in range(KC)]
        for hf_ in range(nhalf):
            for jj in range(HC):
                j = hf_ * HC + jj
                wt, wb = C.wload(w_in[j], KC)
                mm_acc(P, psg, psgb, N, wt, wb, acts, [xnb])
                wt, wb = C.wload(w_in[FC + j], KC)
                mm_acc(P, psu, psub, N, wt, wb, acts, [xnb])
                tick()
                tick()
                for bk in range(NBK):
                    c0, c1 = bk * 512, (bk + 1) * 512
                    P.op('act', lambda e, jj=jj, c0=c0, c1=c1: e.activation(out=h[:, jj, c0:c1], in_=psg[:, c0:c1], func=AF.Silu), reads=[psgb[bk]], writes=[hb])
                    P.op('dve', lambda e, jj=jj, c0=c0, c1=c1: e.tensor_tensor(out=h[:, jj, c0:c1], in0=psu[:, c0:c1], in1=h[:, jj, c0:c1], op=ALU.mult),
                         reads=[psub[bk], hb], writes=[hb])
            for m in range(KC):
                pcs = list(range(hf_ * (npc // nhalf), (hf_ + 1) * (npc // nhalf)))
                for pi, pc in enumerate(pcs):
                    wt, wb = C.wload(w_out[m][:, pc * kcn:(pc + 1) * kcn, :], kcn)
                    acts2 = [(lambda n0, n1, k=k: h[:, k, n0:n1]) for k in range(pc * kcn - hf_ * HC, (pc + 1) * kcn - hf_ * HC)]
                    mm_acc(P, pso, psob, N, wt, wb, acts2, [hb], first=(pi == 0), last=(pi == len(pcs) - 1))
                    tick()
                for bk in range(NBK):
                    c0, c1 = bk * 512, (bk + 1) * 512
                    P.op('dve', lambda e, m=m, c0=c0, c1=c1: e.scalar_tensor_tensor(out=xt[:, m, c0:c1], in0=pso[:, c0:c1], scalar=0.5, in1=xt[:, m, c0:c1],
                                                                                op0=ALU.mult, op1=ALU.add), reads=[psob[bk], xtb], writes=[xtb])
        store_x(P, C, Xd, xt, xtb, t, N)
    if PC is not None and i == 1:
        PC.pump(l + 1, 10 ** 9)
    C.close()


class Precast:
    def __init__(s, P, cfg, dr):
        s.P, s.dr = P, dr
        L = cfg['DEPTH']
        s.chunks = {l: [] for l in range(L + 1)}
        for name in BIG:
            if name not in dr:
                continue
            for li in range(dr[name].shape[0]):
                layer = li // 2 if name.startswith('ffn') else (2 * li if name.startswith('rec') else 2 * li + 1)
                src, dst = dr[name][li], dr[name + '_bf'][li]
                per = int(np.prod(src.shape)) // 128
                E = max(e for e in range(1, 2049) if per % e == 0)
                dims = " ".join("d%d" % i for i in range(len(src.shape)))
                sf = src.rearrange("%s -> (%s)" % (dims, dims)).rearrange("(n p e) -> n p e", p=128, e=E)
                df = dst.rearrange("%s -> (%s)" % (dims, dims)).rearrange("(n p e) -> n p e", p=128, e=E)
                for n in range(per // E):
                    s.chunks[layer].append((sf[n], df[n], E))
        s.ci = 0

    def alloc(s, C, nsl):
        P = s.P
        s.nsl = nsl
        s.st = [C.sb("pcs%d" % i, [128, 2048]) for i in range(nsl)]
        s.stb = [P.buf("pcs", dma=True) for _ in range(nsl)]
        s.b16 = [C.sb("pcb%d" % i, [128, 2048], BF16) for i in range(nsl)]
        s.b16b = [P.buf("pcb", dma=True) for _ in range(nsl)]

    def pump(s, layer, k, bg=True):
        P = s.P
        lst = s.chunks.get(layer, [])
        for _ in range(min(k, len(lst))):
            src, dst, E = lst.pop(0)
            i = s.ci % s.nsl
            eng = 'pool' if bg else ['dve', 'act', 'pool'][s.ci % 3]
            q = 'pool' if bg else 'sp'
            s.ci += 1
            a_, ab_, b_, bb_ = s.st[i], s.stb[i], s.b16[i], s.b16b[i]
            P.dma(q, lambda e, a_=a_, src=src, E=E: e.dma_start(out=a_[:, 0:E], in_=src), ab_, writes=[ab_])
            if eng == 'act':
                P.op('act', lambda e, a_=a_, b_=b_, E=E: e.activation(out=b_[:, 0:E], in_=a_[:, 0:E], func=AF.Copy), reads=[ab_], writes=[bb_])
            else:
                P.op(eng, lambda e, a_=a_, b_=b_, E=E: e.tensor_copy(out=b_[:, 0:E], in_=a_[:, 0:E]), reads=[ab_], writes=[bb_])
            P.dma(q, lambda e, b_=b_, dst=dst, E=E: e.dma_start(out=dst, in_=b_[:, 0:E]), bb_, reads=[bb_])


def phase_precast(P, cfg, dr, PC):
    C = Ctx(P, cfg, dr)
    PC.alloc(C, 4)
    PC.pump(0, 10 ** 9, bg=False)
    C.close()


def phase_memkv(P, cfg, dr, l):
    D, M, MH = cfg['D'], cfg['M'], cfg['MH']
    KC, MW = D // 128, cfg['MH'] * 256
    MC = 2 * MW // 128
    SO, _ = smalls_layout(cfg)
    C = Ctx(P, cfg, dr)
    C.common(nring=4, smalls_layer=l)
    ident, identb = C.sb("ident", [128, 128]), P.buf("ident", dma=True)
    P.dma('pool', lambda e: e.dma_start(out=ident[:, :], in_=dr['ident']), identb, writes=[identb])
    mt, mtb = C.sb("mt", [128, KC, M]), P.buf("mt", dma=True)
    mn, mnb = C.sb("mn", [128, KC, M]), P.buf("mn")
    kv, kvb = C.sb("kv", [128, MC, M]), P.buf("kv")
    kn, knb = C.sb("kn", [128, MC // 2, M]), P.buf("kn", dma=True)
    vm, vmb = C.sb("vm", [128, M // 128, MW]), P.buf("vm", dma=True)
    NR = NormRes(C, M)
    ps = [C.ps("ps%d" % j, [128, 512]) for j in range(2)]
    psb = [P.buf("ps") for _ in range(2)]
    src = dr['memT'].rearrange("k p n -> p k n")
    P.dma('pool', lambda e: e.dma_start(out=mt[:, :, :], in_=src), mtb, writes=[mtb])
    rmsnorm_tile(P, C, NR, mt, mtb, mn, mnb, KC, M, SO['mem_g'])
    acts = [(lambda n0, n1, k=k: mn[:, k, n0:n1]) for k in range(KC)]
    for m in range(MC):
        b = m % 2
        wt, wb = C.wload(dr['mem_w_kv'][l][m], KC)
        mm_acc(P, ps[b], psb[b], M, wt, wb, acts, [mnb])
        P.op('act', lambda e, b=b, m=m: e.activation(out=kv[:, m, 0:M], in_=ps[b][:, 0:M], func=AF.Copy), reads=[psb[b]], writes=[kvb])
    for hh in range(MH):
        rms_rstd(P, C, [kv[:, 2 * hh + f, 0:M] for f in range(2)], [kvb], M, 256, NR.sq, NR.sqb, NR.psn, NR.psnb, NR.rs, NR.rsb)
        for f in range(2):
            c = SO['memk_g'] + f
            P.op('dve', lambda e, hh=hh, f=f, c=c: e.scalar_tensor_tensor(out=kn[:, 2 * hh + f, 0:M], in0=kv[:, 2 * hh + f, 0:M],
                                                                      scalar=C.sm[:, c:c + 1], in1=NR.rs[:, 0:M], op0=ALU.mult, op1=ALU.mult),
                 reads=[kvb, NR.rsb, C.smb], writes=[knb])
    for m in range(MC // 2):
        for tc in range(M // 128):
            b = (m * 2 + tc) % 2
            P.op('pe', lambda e, b=b, m=m, tc=tc: e.transpose(out=ps[b][:, 0:128], in_=kv[:, MC // 2 + m, tc * 128:(tc + 1) * 128], identity=ident[:, :]),
                 reads=[kvb, identb], writes=[psb[b]])
            P.op('act', lambda e, b=b, m=m, tc=tc: e.activation(out=vm[:, tc, m * 128:(m + 1) * 128], in_=ps[b][:, 0:128], func=AF.Copy),
                 reads=[psb[b]], writes=[vmb])
    P.dma('pool', lambda e: e.dma_start(out=dr['KN'], in_=kn[:, :, :]), knb, reads=[knb])
    P.dma('pool', lambda e: e.dma_start(out=dr['VM'], in_=vm[:, :, :]), vmb, reads=[vmb])
    C.close()


class MemAtt:
    def __init__(s, C, N):
        P, cfg, dr = C.P, C.cfg, C.dr
        s.C, s.N = C, N
        M, MH = cfg['M'], cfg['MH']
        MW = MH * 256
        s.kn, s.knb = C.sb("kn", [128, 2 * MH, M]), P.buf("kn", dma=True)
        s.vm, s.vmb = C.sb("vm", [128, M // 128, MW]), P.buf("vm", dma=True)
        P.dma('pool', lambda e: e.dma_start(out=s.kn[:, :, :], in_=dr['KN']), s.knb, writes=[s.knb])
        P.dma('pool', lambda e: e.dma_start(out=s.vm[:, :, :], in_=dr['VM']), s.vmb, writes=[s.vmb])
        s.qm, s.qmb = C.sb("qm", [128, 2 * MH, N]), P.buf("qm")
        s.qn, s.qnb = C.sb("qn", [128, 2, N]), P.buf("qn")
        s.pm = [C.sb("pm%d" % j, [128, N]) for j in range(M // 128)]
        s.pmb = [P.buf("pm") for _ in range(M // 128)]
        s.rden, s.rdenb = C.sb("rden", [128, N]), P.buf("rden")
        s.ym, s.ymb = C.sb("ym", [128, 2 * MH, N], BF16), P.buf("ym", dma=True)

    def run(s, NR, psA, psAb, psB, psBb, t):
        C, N = s.C, s.N
        P, cfg, dr = C.P, C.cfg, C.dr
        M, MH = cfg['M'], cfg['MH']
        SO, _ = smalls_layout(cfg)
        MCk = M // 128
        for hh in range(MH):
            rms_rstd(P, C, [s.qm[:, 2 * hh + f, 0:N] for f in range(2)], [s.qmb], N, 256, NR.sq, NR.sqb, NR.psn, NR.psnb, NR.rs, NR.rsb)
            for f in range(2):
                c = SO['memq_g'] + f
                P.op('dve', lambda e, hh=hh, f=f, c=c: e.scalar_tensor_tensor(out=s.qn[:, f, 0:N], in0=s.qm[:, 2 * hh + f, 0:N], scalar=C.sm[:, c:c + 1],
                                                                          in1=NR.rs[:, 0:N], op0=ALU.mult, op1=ALU.mult),
                     reads=[s.qmb, NR.rsb, C.smb], writes=[s.qnb])
            for mc in range(MCk):
                for f in range(2):
                    P.op('pe', lambda e, hh=hh, f=f, mc=mc: e.matmul(psA[:, 0:N], lhsT=s.kn[:, 2 * hh + f, mc * 128:(mc + 1) * 128], rhs=s.qn[:, f, 0:N],
                                                                 start=(f == 0), stop=(f == 1)), reads=[s.knb, s.qnb], writes=[psAb])
                P.op('act', lambda e, mc=mc: e.activation(out=s.pm[mc][:, 0:N], in_=psA[:, 0:N], func=AF.Exp, scale=256.0 ** -0.5),
                     reads=[psAb], writes=[s.pmb[mc]])
            for mc in range(MCk):
                P.op('pe', lambda e, mc=mc: e.matmul(psB[:, 0:N], lhsT=C.ones[:, :], rhs=s.pm[mc][:, 0:N], start=(mc == 0), stop=(mc == MCk - 1)),
                     reads=[C.onesb, s.pmb[mc]], writes=[psBb])
            P.op('dve', lambda e: e.reciprocal(out=s.rden[:, 0:N], in_=psB[:, 0:N]), reads=[psBb], writes=[s.rdenb])
            for f in range(2):
                for mc in range(MCk):
                    c0 = (2 * hh + f) * 128
                    P.op('pe', lambda e, mc=mc, c0=c0: e.matmul(psA[:, 0:N], lhsT=s.vm[:, mc, c0:c0 + 128], rhs=s.pm[mc][:, 0:N],
                                                             start=(mc == 0), stop=(mc == MCk - 1)), reads=[s.vmb, s.pmb[mc]], writes=[psAb])
                P.op('dve', lambda e, hh=hh, f=f: e.tensor_tensor(out=s.ym[:, 2 * hh + f, 0:N], in0=psA[:, 0:N], in1=s.rden[:, 0:N], op=ALU.mult),
                     reads=[psAb, s.rdenb], writes=[s.ymb])
        dst = dr['YM'][:, :, t * N:(t + 1) * N].rearrange("k p n -> p k n")
        P.dma('pool', lambda e: e.dma_start(out=dst, in_=s.ym[:, :, :]), s.ymb, reads=[s.ymb])


def phase_rec_p1(P, cfg, dr, X, l, j):
    D, S, MH = cfg['D'], cfg['S'], cfg['MH']
    KC, N = D // 128, min(512, cfg['S'])
    SO, _ = smalls_layout(cfg)
    C = Ctx(P, cfg, dr)
    C.common(nring=4, smalls_layer=l, wdt=BF16)
    w_in = dr['rec_w_in_bf'][j]
    xt, xtb = C.sb("xt", [128, KC, N]), P.buf("xt", dma=True)
    xn, xnb = C.sb("xn", [128, KC, N], BF16), P.buf("xn")
    NR = NormRes(C, N)
    MA = MemAtt(C, N)
    ps = [C.ps("ps%d" % i, [128, 512]) for i in range(3)]
    psb = [P.buf("ps") for _ in range(3)]
    psA, psAb, psB, psBb = C.ps("psA", [128, 512]), P.buf("psA"), C.ps("psB", [128, 512]), P.buf("psB")
    stg = [C.sb("stg%d" % i, [128, N]) for i in range(3)]
    stgb = [P.buf("stg", dma=True) for _ in range(3)]
    t1, t1b = C.sb("t1", [128, N]), P.buf("t1")
    t2, t2b = C.sb("t2", [128, N]), P.buf("t2")
    zt, ztb = C.sb("zt", [128, KC, 2]), P.buf("zt", dma=True)
    P.op('pool', lambda e: e.memset(zt[:, :, :], 0.0), writes=[ztb])
    RB = dr['RB']
    nc_ = P.nc
    def _zp(e, a, b):
        with nc_.allow_non_contiguous_dma(reason="tiny zero pad"):
            return e.dma_start(out=RB[:, :, a:b].rearrange("k p n -> p k n"), in_=zt[:, :, 0:b - a])
    P.dma('pool', lambda e: _zp(e, 0, 2), ztb, reads=[ztb])
    P.dma('pool', lambda e: _zp(e, S + 2, S + 3), ztb, reads=[ztb])
    si = 0
    for t in range(S // N):
        load_x(P, C, X, xt, xtb, t, N)
        rmsnorm_tile(P, C, NR, xt, xtb, xn, xnb, KC, N, SO['mix_g'])
        acts = [(lambda n0, n1, k=k: xn[:, k, n0:n1]) for k in range(KC)]
        for m in range(2 * KC + 2 * MH):
            b = m % 3
            wt, wb = C.wload(w_in[m], KC)
            mm_acc(P, ps[b], psb[b], N, wt, wb, acts, [xnb])
            if m < KC:
                s_, sb_ = stg[si % 3], stgb[si % 3]
                si += 1
                P.op('act', lambda e, b=b: e.activation(out=t1[:, 0:N], in_=ps[b][:, 0:N], func=AF.Square), reads=[psb[b]], writes=[t1b])
                P.op('dve', lambda e: e.tensor_scalar(out=t1[:, 0:N], in0=t1[:, 0:N], scalar1=0.044715, scalar2=1.0, op0=ALU.mult, op1=ALU.add),
                     reads=[t1b], writes=[t1b])
                P.op('dve', lambda e, b=b: e.tensor_tensor(out=t1[:, 0:N], in0=ps[b][:, 0:N], in1=t1[:, 0:N], op=ALU.mult), reads=[psb[b], t1b], writes=[t1b])
                P.op('act', lambda e: e.activation(out=t2[:, 0:N], in_=t1[:, 0:N], func=AF.Sigmoid, scale=1.5957691216), reads=[t1b], writes=[t2b])
                P.op('dve', lambda e, b=b, s_=s_: e.tensor_tensor(out=s_[:, 0:N], in0=ps[b][:, 0:N], in1=t2[:, 0:N], op=ALU.mult),
                     reads=[psb[b], t2b], writes=[sb_])
                dst = dr['GG'][m][:, t * N:(t + 1) * N]
                P.dma('pool', lambda e, s_=s_, dst=dst: e.dma_start(out=dst, in_=s_[:, 0:N]), sb_, reads=[sb_])
            elif m < 2 * KC:
                s_, sb_ = stg[si % 3], stgb[si % 3]
                si += 1
                P.op('act', lambda e, b=b, s_=s_: e.activation(out=s_[:, 0:N], in_=ps[b][:, 0:N], func=AF.Copy), reads=[psb[b]], writes=[sb_])
                dst = RB[m - KC][:, 2 + t * N:2 + (t + 1) * N]
                P.dma('pool', lambda e, s_=s_, dst=dst: e.dma_start(out=dst, in_=s_[:, 0:N]), sb_, reads=[sb_])
            else:
                P.op('act', lambda e, b=b, m=m: e.activation(out=MA.qm[:, m - 2 * KC, 0:N], in_=ps[b][:, 0:N], func=AF.Copy), reads=[psb[b]], writes=[MA.qmb])
        MA.run(NR, psA, psAb, psB, psBb, t)
    C.close()


def phase_rec_p2(P, cfg, dr, l, j, d):
    D, S = cfg['D'], cfg['S']
    KC, NB = D // 128, D // 256
    T = min(2048, S)
    nseg = S // T
    SO, _ = smalls_layout(cfg)
    C = Ctx(P, cfg, dr)
    C.common(nring=4, smalls_layer=l)
    sm = C.sm
    cs, csb = C.sb("cs", [128, 4 * KC]), P.buf("cs")
    lo = SO['lam']
    P.op('act', lambda e: e.activation(out=cs[:, 0:2 * KC], in_=sm[:, lo:lo + 2 * KC], func=AF.Exp, scale=-1.0), reads=[C.smb], writes=[csb])
    P.op('act', lambda e: e.activation(out=cs[:, 0:2 * KC], in_=cs[:, 0:2 * KC], func=AF.Ln, bias=C.cst[:, 1:2], scale=1.0), reads=[csb, C.cstb], writes=[csb])
    P.op('dve', lambda e: e.tensor_scalar(out=cs[:, 2 * KC:4 * KC], in0=cs[:, 0:2 * KC], scalar1=-16.0, scalar2=None, op0=ALU.mult), reads=[csb], writes=[csb])
    P.op('dve', lambda e: e.tensor_scalar(out=cs[:, 0:2 * KC], in0=cs[:, 0:2 * KC], scalar1=-8.0, scalar2=None, op0=ALU.mult), reads=[csb], writes=[csb])
    rb, rbb = C.sb("rb", [128, 2, T + 3]), P.buf("rb", dma=True)
    xc, xcb = C.sb("xc", [128, 2, T]), P.buf("xc")
    names = ['rr', 'ii', 'aa', 'bb', 'hh']
    tl = {n: C.sb(n, [128, T]) for n in names}
    tb = {n: P.buf(n, dma=(n == 'hh')) for n in names}
    hf, hfb = C.sb("hf", [128, T]), P.buf("hf", dma=True)
    gg, ggb = C.sb("gg", [128, T]), P.buf("gg", dma=True)
    st, stb = C.sb("st", [128, 2]), P.buf("st")
    yb, ybb = C.sb("yb", [128, T], BF16), P.buf("yb", dma=True)
    nb512 = max(1, T // 512)
    psr, psrb = C.ps("psr", [128, nb512 * 512]), P.buf("psr")
    psi, psib = C.ps("psi", [128, nb512 * 512]), P.buf("psi")
    gw = dr['gate_w'][j]
    rr, ii, aa, bb, hh = (tl[n] for n in names)
    for n in range(NB):
        for d in [d]:
            segs = list(range(nseg)) if d == 0 else list(range(nseg - 1, -1, -1))
            for si_, seg in enumerate(segs):
                src = dr['RB'][2 * n:2 * n + 2, :, seg * T:seg * T + T + 3].rearrange("k p n -> p k n")
                P.dma('pool', lambda e, src=src: e.dma_start(out=rb[:, :, :], in_=src), rbb, writes=[rbb])
                for c in range(2):
                    ch = 2 * n + c
                    w0, bcol = SO['conv_w'] + ch * 4, SO['conv_b'] + ch
                    P.op('dve', lambda e, c=c, w0=w0, bcol=bcol: e.tensor_scalar(out=xc[:, c, :], in0=rb[:, c, 0:T], scalar1=sm[:, w0:w0 + 1],
                                                                             scalar2=sm[:, bcol:bcol + 1], op0=ALU.mult, op1=ALU.add),
                         reads=[rbb, C.smb], writes=[xcb])
                    for jj in range(1, 4):
                        P.op('dve', lambda e, c=c, w0=w0, jj=jj: e.scalar_tensor_tensor(out=xc[:, c, :], in0=rb[:, c, jj:jj + T], scalar=sm[:, w0 + jj:w0 + jj + 1],
                                                                                    in1=xc[:, c, :], op0=ALU.mult, op1=ALU.add),
                             reads=[rbb, C.smb, xcb], writes=[xcb])
                for oc in range(2):
                    ch = 2 * n + oc
                    for gt, (pst, pstb, dst, dstb) in enumerate([(psr, psrb, rr, tb['rr']), (psi, psib, ii, tb['ii'])]):
                        wt, wb = C.wload(gw[d][gt][n][oc], 2)
                        mm_acc(P, pst, pstb, T, wt, wb, [(lambda n0, n1, k=k: xc[:, k, n0:n1]) for k in range(2)], [xcb])
                        bc = SO['gb'] + (d * 2 + gt) * KC + ch
                        P.op('act', lambda e, pst=pst, dst=dst, bc=bc: e.activation(out=dst[:, 0:T], in_=pst[:, 0:T], func=AF.Sigmoid, bias=sm[:, bc:bc + 1], scale=1.0),
                             reads=[pstb, C.smb], writes=[dstb])
                    c1, c2 = d * KC + ch, 2 * KC + d * KC + ch
                    P.op('act', lambda e, c1=c1: e.activation(out=aa[:, 0:T], in_=rr[:, 0:T], func=AF.Exp, scale=cs[:, c1:c1 + 1]), reads=[tb['rr'], csb], writes=[tb['aa']])
                    P.op('act', lambda e, c2=c2: e.activation(out=bb[:, 0:T], in_=rr[:, 0:T], func=AF.Exp, scale=cs[:, c2:c2 + 1]), reads=[tb['rr'], csb], writes=[tb['bb']])
                    P.op('act', lambda e: e.activation(out=bb[:, 0:T], in_=bb[:, 0:T], func=AF.Sqrt, bias=C.cst[:, 1:2], scale=-1.0), reads=[tb['bb'], C.cstb], writes=[tb['bb']])
                    P.op('dve', lambda e: e.tensor_tensor(out=bb[:, 0:T], in0=bb[:, 0:T], in1=ii[:, 0:T], op=ALU.mult), reads=[tb['bb'], tb['ii']], writes=[tb['bb']])
                    P.op('dve', lambda e, oc=oc: e.tensor_tensor(out=bb[:, 0:T], in0=bb[:, 0:T], in1=xc[:, oc, :], op=ALU.mult), reads=[tb['bb'], xcb], writes=[tb['bb']])
                    init = 0.0 if si_ == 0 else st[:, oc:oc + 1]
                    if d == 0:
                        P.op('dve', lambda e, init=init: e.tensor_tensor_scan(out=hh[:, 0:T], data0=aa[:, 0:T], data1=bb[:, 0:T], initial=init, op0=ALU.mult, op1=ALU.add),
                             reads=[tb['aa'], tb['bb'], stb], writes=[tb['hh']])
                        P.op('dve', lambda e, oc=oc: e.tensor_copy(out=st[:, oc:oc + 1], in_=hh[:, T - 1:T]), reads=[tb['hh']], writes=[stb])
                        dst = dr['HF'][ch][:, seg * T:(seg + 1) * T]
                        P.dma('pool', lambda e, dst=dst: e.dma_start(out=dst, in_=hh[:, 0:T]), tb['hh'], reads=[tb['hh']])
                    else:
                        s1 = dr['HF'][ch][:, seg * T:(seg + 1) * T]
                        s2 = dr['GG'][ch][:, seg * T:(seg + 1) * T]
                        P.dma('pool', lambda e, s1=s1: e.dma_start(out=hf[:, 0:T], in_=s1), hfb, writes=[hfb])
                        P.dma('pool', lambda e, s2=s2: e.dma_start(out=gg[:, 0:T], in_=s2), ggb, writes=[ggb])
                        P.op('dve', lambda e, init=init: e.tensor_tensor_scan(out=hh[:, ::-1], data0=aa[:, ::-1], data1=bb[:, ::-1], initial=init, op0=ALU.mult, op1=ALU.add),
                             reads=[tb['aa'], tb['bb'], stb], writes=[tb['hh']])
                        P.op('dve', lambda e, oc=oc: e.tensor_copy(out=st[:, oc:oc + 1], in_=hh[:, 0:1]), reads=[tb['hh']], writes=[stb])
                        P.op('dve', lambda e: e.tensor_tensor(out=hh[:, 0:T], in0=hh[:, 0:T], in1=hf[:, 0:T], op=ALU.add), reads=[tb['hh'], hfb], writes=[tb['hh']])
                        P.op('dve', lambda e: e.tensor_tensor(out=yb[:, 0:T], in0=hh[:, 0:T], in1=gg[:, 0:T], op=ALU.mult), reads=[tb['hh'], ggb], writes=[ybb])
                        dst = dr['YR'][ch][:, seg * T:(seg + 1) * T]
                        P.dma('pool', lambda e, dst=dst: e.dma_start(out=dst, in_=yb[:, 0:T]), ybb, reads=[ybb])
    C.close()


def phase_wout(P, cfg, dr, X, Xd, w_out, YMAIN, nmain):
    D, S, MH = cfg['D'], cfg['S'], cfg['MH']
    KC, N = D // 128, min(512, cfg['S'])
    YC = nmain + 2 * MH
    C = Ctx(P, cfg, dr)
    C.common(nring=4, wdt=BF16)
    xt, xtb = C.sb("xt", [128, KC, N]), P.buf("xt", dma=True)
    yt, ytb = C.sb("yt", [128, YC, N], BF16), P.buf("yt", dma=True)
    pso = [C.ps("pso%d" % j, [128, 512]) for j in range(2)]
    psob = [P.buf("o") for _ in range(2)]
    kcn = max(k for k in range(1, 17) if YC % k == 0)
    for t in range(S // N):
        load_x(P, C, X, xt, xtb, t, N)
        s1 = YMAIN[:, :, t * N:(t + 1) * N].rearrange("k p n -> p k n")
        s2 = dr['YM'][:, :, t * N:(t + 1) * N].rearrange("k p n -> p k n")
        P.dma('pool', lambda e, s1=s1: e.dma_start(out=yt[:, 0:nmain, :], in_=s1), ytb, writes=[ytb])
        P.dma('pool', lambda e, s2=s2: e.dma_start(out=yt[:, nmain:YC, :], in_=s2), ytb, writes=[ytb])
        for m in range(KC):
            b = m % 2
            npc = YC // kcn
            for pc in range(npc):
                wt, wb = C.wload(w_out[m][:, pc * kcn:(pc + 1) * kcn, :], kcn)
                acts2 = [(lambda n0, n1, k=k: yt[:, k, n0:n1]) for k in range(pc * kcn, (pc + 1) * kcn)]
                mm_acc(P, pso[b], psob[b], N, wt, wb, acts2, [ytb], first=(pc == 0), last=(pc == npc - 1))
            P.op('dve', lambda e, b=b, m=m: e.tensor_tensor(out=xt[:, m, 0:N], in0=pso[b][:, 0:N], in1=xt[:, m, 0:N], op=ALU.add),
                 reads=[psob[b], xtb], writes=[xtb])
        store_x(P, C, Xd, xt, xtb, t, N)
    C.close()


def phase_att_p1(P, cfg, dr, X, l, j):
    D, S, MH, AH = cfg['D'], cfg['S'], cfg['MH'], cfg['AH']
    KC, N = D // 128, min(512, cfg['S'])
    SO, _ = smalls_layout(cfg)
    C = Ctx(P, cfg, dr)
    C.common(nring=4, smalls_layer=l, wdt=BF16)
    identf, identfb = C.sb("identf", [128, 128]), P.buf("identf", dma=True)
    P.dma('pool', lambda e: e.dma_start(out=identf[:, :], in_=dr['ident']), identfb, writes=[identfb])
    ident, identb = C.sb("ident", [128, 128], BF16), P.buf("ident")
    P.op('dve', lambda e: e.tensor_copy(out=ident[:, :], in_=identf[:, :]), reads=[identfb], writes=[identb])
    Ctx._uid[0] += 1
    psT = C.es.enter_context(C.nc.psum_tensor("psT_att_u%d" % Ctx._uid[0], [128, 128], BF16))
    psTb = P.buf("psT")
    w_in = dr['att_w_in_bf'][j]
    xt, xtb = C.sb("xt", [128, KC, N]), P.buf("xt", dma=True)
    xn, xnb = C.sb("xn", [128, KC, N], BF16), P.buf("xn")
    NR = NormRes(C, N)
    NR2 = NormRes(C, N, "b")
    MA = MemAtt(C, N)
    ps = [C.ps("ps%d" % i, [128, 512]) for i in range(3)]
    psb = [P.buf("ps") for _ in range(3)]
    psA, psAb, psB, psBb = C.ps("psA", [128, 512]), P.buf("psA"), C.ps("psB", [128, 512]), P.buf("psB")
    stg = [C.sb("stg%d" % i, [128, N], BF16) for i in range(3)]
    stgb = [P.buf("stg", dma=True) for _ in range(3)]
    vs2 = [C.sb("vs%d" % i, [128, N], BF16) for i in range(2)]
    vs2b = [P.buf("vs") for _ in range(2)]
    vt, vtb = C.sb("vt", [128, N // 128, 128], BF16), P.buf("vt", dma=True)
    si = 0
    nqkv = 9 * AH
    for t in range(S // N):
        load_x(P, C, X, xt, xtb, t, N)
        rmsnorm_tile(P, C, NR, xt, xtb, xn, xnb, KC, N, SO['mix_g'])
        acts = [(lambda n0, n1, k=k: xn[:, k, n0:n1]) for k in range(KC)]
        pend = None
        for m in range(nqkv + 2 * MH):
            b = m % 3
            wt, wb = C.wload(w_in[m], KC)
            mm_acc(P, ps[b], psb[b], N, wt, wb, acts, [xnb])
            post = None
            if m >= nqkv:
                P.op('act', lambda e, b=b, m=m: e.activation(out=MA.qm[:, m - nqkv, 0:N], in_=ps[b][:, 0:N], func=AF.Copy), reads=[psb[b]], writes=[MA.qmb])
            else:
                g, kind, hd_ = m // (3 * AH), (m % (3 * AH)) // AH, m % AH
                if kind < 2:
                    sq, sqb_ = NR2.sq[m % 2], NR2.sqb[m % 2]
                    P.op('act', lambda e, b=b, sq=sq: e.activation(out=sq[:, 0:N], in_=ps[b][:, 0:N], func=AF.Square), reads=[psb[b]], writes=[sqb_])

                    def post(b=b, sq=sq, sqb_=sqb_, g=g, kind=kind, hd_=hd_, t=t):
                        nonlocal si
                        s_, sb_ = stg[si % 3], stgb[si % 3]
                        si += 1
                        P.op('pe', lambda e: e.matmul(NR2.psn[:, 0:N], lhsT=C.ones[:, :], rhs=sq[:, 0:N], start=True, stop=True),
                             reads=[sqb_, C.onesb], writes=[NR2.psnb])
                        P.op('act', lambda e: e.activation(out=NR2.rs[:, 0:N], in_=NR2.psn[:, 0:N], func=AF.Sqrt, bias=C.cst[:, 0:1], scale=1.0 / 128),
                             reads=[NR2.psnb, C.cstb], writes=[NR2.rsb])
                        P.op('dve', lambda e: e.reciprocal(out=NR2.rs[:, 0:N], in_=NR2.rs[:, 0:N]), reads=[NR2.rsb], writes=[NR2.rsb])
                        gc = SO['qg' if kind == 0 else 'kg'] + g
                        P.op('dve', lambda e: e.scalar_tensor_tensor(out=s_[:, 0:N], in0=ps[b][:, 0:N], scalar=C.sm[:, gc:gc + 1], in1=NR2.rs[:, 0:N],
                                                                   op0=ALU.mult, op1=ALU.mult), reads=[psb[b], NR2.rsb, C.smb], writes=[sb_])
                        dst = dr['QK'][(g * 2 + kind) * AH + hd_][:, t * N:(t + 1) * N]
                        P.dma('pool', lambda e: e.dma_start(out=dst, in_=s_[:, 0:N]), sb_, reads=[sb_])
                else:
                    vs, vsb = vs2[m % 2], vs2b[m % 2]
                    P.op('act', lambda e, b=b, vs=vs: e.activation(out=vs[:, 0:N], in_=ps[b][:, 0:N], func=AF.Copy), reads=[psb[b]], writes=[vsb])

                    def post(vs=vs, vsb=vsb, g=g, hd_=hd_, t=t):
                        for bi in range(N // 128):
                            P.op('pe', lambda e, bi=bi: e.transpose(out=psT[:, 0:128], in_=vs[:, bi * 128:(bi + 1) * 128], identity=ident[:, :]),
                                 reads=[vsb, identb], writes=[psTb])
                            P.op('act', lambda e, bi=bi: e.activation(out=vt[:, bi, :], in_=psT[:, 0:128], func=AF.Copy), reads=[psTb], writes=[vtb])
                        dst = dr['V'][g * AH + hd_][t * N:(t + 1) * N, :].rearrange("(c p) d -> p c d", p=128)
                        P.dma('pool', lambda e: e.dma_start(out=dst, in_=vt[:, :, :]), vtb, reads=[vtb])
            if pend is not None:
                pend()
            pend = post
        if pend is not None:
            pend()
        MA.run(NR, psA, psAb, psB, psBb, t)
    C.close()


def alibi_slope(cfg, g, h):
    n = 3 * cfg['AH']
    return float(np.exp2(np.float32(-8.0) * (np.float32(g * cfg['AH'] + h) + np.float32(1.0)) / np.float32(n)))


def phase_att_p2(P, cfg, dr):
    S, AH = cfg['S'], cfg['AH']
    C = Ctx(P, cfg, dr)
    C.common(nring=0)
    rel, relb = C.sb("rel", [128, 6, 512]), P.buf("rel", dma=True)
    P.dma('pool', lambda e: e.dma_start(out=rel[:, :, :], in_=dr['c_rel'].rearrange("c p n -> p c n")), relb, writes=[relb])
    bt, btb = C.sb("bt", [128, 6, 512]), P.buf("bt")
    qT, qTb = C.sb("qT", [128, S], BF16), P.buf("qT", dma=True)
    kT, kTb = C.sb("kT", [128, S], BF16), P.buf("kT", dma=True)
    vc, vcb = C.sb("vc", [128, S // 128, 128], BF16), P.buf("vc", dma=True)
    ones16, ones16b = C.sb("ones16", [128, 128], BF16), P.buf("ones16")
    P.op('pool', lambda e: e.memset(ones16[:, :], 1.0), writes=[ones16b])
    aO, aOb = C.sb("aO", [128, S]), P.buf("aO", dma=True)
    aD, aDb = C.sb("aD", [128, S]), P.buf("aD")
    tmp = [C.sb("tmp%d" % i, [128, 512]) for i in range(4)]
    tmpb = [P.buf("tmp") for _ in range(4)]
    ya = [C.sb("ya%d" % i, [128, min(2048, S)], BF16) for i in range(2)]
    yab = [P.buf("ya", dma=True) for _ in range(2)]
    pt = [C.sb("pt%d" % i, [128, 512], BF16) for i in range(4)]
    ptb = [P.buf("pt") for _ in range(4)]
    psS = [C.ps("psS%d" % i, [128, 512]) for i in range(4)]
    psSb = [P.buf("psS") for _ in range(4)]
    psO = [C.ps("psO%d" % i, [128, 512]) for i in range(2)]
    psOb = [P.buf("psO") for _ in range(2)]
    psD = [C.ps("psD%d" % i, [128, 512]) for i in range(2)]
    psDb = [P.buf("psD") for _ in range(2)]
    it = 0
    qi = 0
    for h in range(AH):
        for g, (win, d) in enumerate(cfg['PAT']):
            L = S // d
            QT = min(512, L)
            nck = L // 128
            P.dma('pool', lambda e, g=g, h=h: e.dma_start(out=qT[:, :], in_=dr['QK'][(g * 2 + 0) * AH + h]), qTb, writes=[qTb])
            P.dma('pool', lambda e, g=g, h=h: e.dma_start(out=kT[:, :], in_=dr['QK'][(g * 2 + 1) * AH + h]), kTb, writes=[kTb])
            Vd = dr['V'][g * AH + h].rearrange("(c i r) e -> r i c e", i=128, r=d)
            for r in range(d):
                P.dma('pool', lambda e, r=r, Vd=Vd, nck=nck: e.dma_start(out=vc[:, r * nck:(r + 1) * nck, :], in_=Vd[r]), vcb, writes=[vcb])
            sl = -alibi_slope(cfg, g, h) * d
            for ci in range(6):
                P.op('dve', lambda e, ci=ci, sl=sl: e.tensor_scalar(out=bt[:, ci, :], in0=rel[:, ci, :], scalar1=sl, scalar2=None, op0=ALU.mult),
                     reads=[relb], writes=[btb])
            pairs = []
            for r in range(d):
                for j0 in range(0, L, QT):
                    qb = qi % 2
                    qi += 1
                    cis = [ci for ci in range(QT // 128 + 2) if 0 <= j0 - 128 + 128 * ci and j0 - 128 + 128 * ci + 128 <= L]
                    for idx, ci in enumerate(cis):
                        pairs.append((r, j0, qb, ci, j0 - 128 + 128 * ci, idx, len(cis)))

            def stageA(p, b, d=d, QT=QT):
                r, j0, qb, ci, kj, idx, nci = p
                qap = qT[:, r + d * j0: r + d * (j0 + QT - 1) + 1: d]
                kap = kT[:, r + d * kj: r + d * (kj + 127) + 1: d]
                P.op('pe', lambda e: e.matmul(psS[b][:, 0:QT], lhsT=kap, rhs=qap, start=True, stop=True), reads=[kTb, qTb], writes=[psSb[b]])
                P.op('dve', lambda e: e.scalar_tensor_tensor(out=tmp[b][:, 0:QT], in0=psS[b][:, 0:QT], scalar=128.0 ** -0.5, in1=bt[:, ci, 0:QT],
                                                           op0=ALU.mult, op1=ALU.add), reads=[psSb[b], btb], writes=[tmpb[b]])
                P.op('act', lambda e: e.activation(out=pt[b][:, 0:QT], in_=tmp[b][:, 0:QT], func=AF.Exp), reads=[tmpb[b]], writes=[ptb[b]])

            def stageB(p, b, d=d, QT=QT, nck=nck, g=g):
                r, j0, qb, ci, kj, idx, nci = p
                vck = r * nck + kj // 128
                P.op('pe', lambda e: e.matmul(psO[qb][:, 0:QT], lhsT=vc[:, vck, :], rhs=pt[b][:, 0:QT], start=(idx == 0), stop=(idx == nci - 1)),
                     reads=[vcb, ptb[b]], writes=[psOb[qb]])
                P.op('pe', lambda e: e.matmul(psD[qb][:, 0:QT], lhsT=ones16[:, :], rhs=pt[b][:, 0:QT], start=(idx == 0), stop=(idx == nci - 1)),
                     reads=[ones16b, ptb[b]], writes=[psDb[qb]])
                if idx == nci - 1:
                    oap = aO[:, r + d * j0: r + d * (j0 + QT - 1) + 1: d]
                    dap = aD[:, r + d * j0: r + d * (j0 + QT - 1) + 1: d]
                    if g == 0:
                        P.op('act', lambda e: e.activation(out=oap, in_=psO[qb][:, 0:QT], func=AF.Copy), reads=[psOb[qb]], writes=[aOb])
                        P.op('dve', lambda e: e.tensor_copy(out=dap, in_=psD[qb][:, 0:QT]), reads=[psDb[qb]], writes=[aDb])
                    else:
                        P.op('dve', lambda e: e.tensor_tensor(out=oap, in0=psO[qb][:, 0:QT], in1=oap, op=ALU.add), reads=[psOb[qb], aOb], writes=[aOb])
                        P.op('dve', lambda e: e.tensor_tensor(out=dap, in0=psD[qb][:, 0:QT], in1=dap, op=ALU.add), reads=[psDb[qb], aDb], writes=[aDb])
            LA = 3
            base = it
            for i in range(len(pairs) + LA):
                if i < len(pairs):
                    stageA(pairs[i], (base + i) % 4)
                if i >= LA:
                    stageB(pairs[i - LA], (base + i - LA) % 4)
            it += len(pairs)
        P.op('dve', lambda e: e.reciprocal(out=aD[:, :], in_=aD[:, :]), reads=[aDb], writes=[aDb])
        CW = min(2048, S)
        for cc in range(S // CW):
            yb_, ybb_ = ya[cc % 2], yab[cc % 2]
            P.op('dve', lambda e, cc=cc, yb_=yb_: e.tensor_tensor(out=yb_[:, 0:CW], in0=aO[:, cc * CW:(cc + 1) * CW], in1=aD[:, cc * CW:(cc + 1) * CW], op=ALU.mult),
                 reads=[aOb, aDb], writes=[ybb_])
            P.dma('pool', lambda e, h=h, cc=cc, yb_=yb_: e.dma_start(out=dr['YA'][h][:, cc * CW:(cc + 1) * CW], in_=yb_[:, 0:CW]), ybb_, reads=[ybb_])
    C.close()


def build(cfg):
    D, S, FF, L, MH, AH, M = cfg['D'], cfg['S'], cfg['FF'], cfg['DEPTH'], cfg['MH'], cfg['AH'], cfg['M']
    KC, FC = D // 128, FF // 128
    NREC, NATT = (L + 1) // 2, L // 2
    MW = MH * 256
    _, NS = smalls_layout(cfg)
    nc = bass.Bass("TRN2", target_bir_lowering=False)
    dr = {}

    def din(name, shape):
        dr[name] = nc.dram_tensor(name, list(shape), F32, kind="ExternalInput").ap()

    def dsc(name, shape, dt=F32):
        dr[name] = nc.dram_tensor(name, list(shape), dt, kind="ExternalOutput" if cfg.get('dbg') else "Internal").ap()
    din('xT', [KC, 128, S])
    din('memT', [KC, 128, M])
    din('smalls', [L, 128, NS])
    din('ident', [128, 128])
    din('c_rel', [6, 128, 512])
    din('c_negm', [6, 128, 512])
    din('ffn_w_in', [L * 2, 2 * FC, 128, KC, 128])
    din('ffn_w_out', [L * 2, KC, 128, FC, 128])
    din('mem_w_kv', [L, 2 * MW // 128, 128, KC, 128])
    din('rec_w_in', [NREC, 2 * KC + 2 * MH, 128, KC, 128])
    din('rec_w_out', [NREC, KC, 128, KC + 2 * MH, 128])
    din('gate_w', [NREC, 2, 2, D // 256, 2, 128, 2, 128])
    if NATT:
        din('att_w_in', [NATT, 9 * AH + 2 * MH, 128, KC, 128])
        din('att_w_out', [NATT, KC, 128, AH + 2 * MH, 128])
    dr['out'] = nc.dram_tensor('out', [KC, 128, S], F32, kind="ExternalOutput").ap()
    dsc('KN', [128, 2 * MH, M])
    dsc('VM', [128, M // 128, MW])
    dsc('YM', [2 * MH, 128, S], BF16)
    dsc('GG', [KC, 128, S])
    dsc('RB', [KC, 128, S + 3])
    dsc('HF', [KC, 128, S])
    dsc('YR', [KC, 128, S], BF16)
    dsc('QK', [6 * AH, 128, S], BF16)
    dsc('V', [3 * AH, S, 128], BF16)
    dsc('YA', [AH, 128, S], BF16)
    for name in BIG:
        if name in dr:
            dr[name + '_bf'] = [nc.dram_tensor("%s_bf%d" % (name, i), list(dr[name].shape[1:]), BF16, kind="Internal").ap()
                                for i in range(dr[name].shape[0])]
    P = Prog(nc)
    PC = Precast(P, cfg, dr)
    phase_precast(P, cfg, dr, PC)
    X = dr['out']
    stop = cfg.get('stop')
    for l in range(L):
        phase_ffn(P, cfg, dr, dr['xT'] if l == 0 else X, X, l, 0, PC)
        if stop == 'ffn0':
            break
        phase_memkv(P, cfg, dr, l)
        j = l // 2
        if l % 2 == 0:
            phase_rec_p1(P, cfg, dr, X, l, j)
            phase_rec_p2(P, cfg, dr, l, j, 0)
            phase_rec_p2(P, cfg, dr, l, j, 1)
            phase_wout(P, cfg, dr, X, X, dr['rec_w_out_bf'][j], dr['YR'], KC)
            if stop == 'mix0':
                break
        else:
            phase_att_p1(P, cfg, dr, X, l, j)
            phase_att_p2(P, cfg, dr)
            phase_wout(P, cfg, dr, X, X, dr['att_w_out_bf'][j], dr['YA'], AH)
        phase_ffn(P, cfg, dr, X, X, l, 1, PC)
    return nc


def pieces(W):
    K, Mo = W.shape[-2], W.shape[-1]
    lead = W.shape[:-2]
    Wr = W.reshape(*lead, K // 128, 128, Mo // 128, 128)
    nl = len(lead)
    perm = list(range(nl)) + [nl + 2, nl + 1, nl + 0, nl + 3]
    return np.ascontiguousarray(Wr.transpose(perm))


def cols(v):
    n = v.shape[-1] // 128
    return np.swapaxes(v.reshape(*v.shape[:-1], n, 128), -1, -2)


def const_tables():
    k = np.arange(128)[:, None]
    q = np.arange(512)[None, :]
    rel = np.stack([np.abs(128 * (ci - 1) + k - q) for ci in range(6)]).astype(np.float32)
    rel = np.where(rel <= 64, rel, 1.0e6).astype(np.float32)
    return rel, rel


def prep_inputs(cfg, inp):
    D, S, L = cfg['D'], cfg['S'], cfg['DEPTH']
    KC = D // 128
    SO, NS = smalls_layout(cfg)
    f = lambda a: np.asarray(a, dtype=np.float32)
    sm = np.zeros((L, 128, NS), np.float32)
    for l in range(L):
        j = l // 2
        sm[l, :, SO['ffn_g0']:SO['ffn_g0'] + KC] = cols(f(inp['ffn_norm'])[l, 0])
        sm[l, :, SO['ffn_g1']:SO['ffn_g1'] + KC] = cols(f(inp['ffn_norm'])[l, 1])
        sm[l, :, SO['mix_g']:SO['mix_g'] + KC] = cols(f(inp['mix_norm'])[l])
        sm[l, :, SO['mem_g']:SO['mem_g'] + KC] = cols(f(inp['mem_norm'])[l])
        sm[l, :, SO['memq_g']:SO['memq_g'] + 2] = cols(f(inp['mem_qk_gain'])[l, 0])
        sm[l, :, SO['memk_g']:SO['memk_g'] + 2] = cols(f(inp['mem_qk_gain'])[l, 1])
        if l % 2 == 0:
            cw = f(inp['rec_conv_w'])[j]
            sm[l, :, SO['conv_w']:SO['conv_w'] + 4 * KC] = np.stack([cols(cw[t]) for t in range(4)], -1).reshape(128, 4 * KC)
            sm[l, :, SO['conv_b']:SO['conv_b'] + KC] = cols(f(inp['rec_conv_b'])[j])
            gb = f(inp['rec_gate_b'])[j].reshape(2, 2, D)
            for d in range(2):
                for gt in range(2):
                    o = SO['gb'] + (d * 2 + gt) * KC
                    sm[l, :, o:o + KC] = cols(gb[d, gt])
                o = SO['lam'] + d * KC
                sm[l, :, o:o + KC] = cols(f(inp['rec_lambda'])[j, d])
        else:
            qk = f(inp['att_qk_gain'])[j]
            sm[l, :, SO['qg']:SO['qg'] + 3] = qk[0].T
            sm[l, :, SO['kg']:SO['kg'] + 3] = qk[1].T
    rel, negm = const_tables()
    shared = {
        'smalls': sm, 'ident': np.eye(128, dtype=np.float32), 'c_rel': rel, 'c_negm': negm,
        'ffn_w_in': pieces(f(inp['ffn_w_in'])).reshape(L * 2, -1, 128, KC, 128),
        'ffn_w_out': pieces(f(inp['ffn_w_out'])).reshape(L * 2, KC, 128, -1, 128),
        'mem_w_kv': pieces(f(inp['mem_w_kv'])),
        'rec_w_in': pieces(f(inp['rec_w_in'])),
        'rec_w_out': pieces(f(inp['rec_w_out'])),
        'gate_w': pieces(f(inp['rec_gate_w'])),
    }
    if L // 2:
        shared['att_w_in'] = pieces(f(inp['att_w_in']))
        shared['att_w_out'] = pieces(f(inp['att_w_out']))
    x, mem = f(inp['x']), f(inp['mem'])
    B = x.shape[0]
    maps = []
    for b in range(B):
        m = dict(shared)
        m['xT'] = np.ascontiguousarray(x[b].T).reshape(KC, 128, S)
        m['memT'] = np.ascontiguousarray(mem[b].T).reshape(KC, 128, -1)
        maps.append(m)
    return maps


def run(cfg, inp):
    nc = build(cfg)
    maps = prep_inputs(cfg, inp)
    B = len(maps)
    res = run_bass_kernel_spmd(nc, maps, core_ids=list(range(B)))
    D, S = cfg['D'], cfg['S']
    if cfg.get('dbg'):
        cfg['dbg_out'] = res.results
    return np.stack([np.ascontiguousarray(res.results[b]['out'].reshape(D, S).T) for b in range(B)]).astype(np.float32)


def kernel(**inputs):
    return run(FULL, inputs)
```

```python
import numpy as np
from contextlib import ExitStack
import concourse.bass as bass
import concourse.mybir as mybir
from concourse.bass_utils import run_bass_kernel_spmd

F32 = mybir.dt.float32
BF16 = mybir.dt.bfloat16
BIG = ['ffn_w_in', 'ffn_w_out', 'rec_w_in', 'rec_w_out', 'att_w_in', 'att_w_out']
AF = mybir.ActivationFunctionType
ALU = mybir.AluOpType
ENG = ['pe', 'act', 'dve', 'pool', 'sp']

FULL = dict(D=2048, S=8192, FF=5632, DEPTH=4, MH=4, AH=8, M=256, PAT=((128, 1), (512, 4), (2048, 16)))


class Buf:
    def __init__(s, name, di=None):
        s.name, s.lw, s.rd, s.di = name, None, {}, di


class Prog:
    def __init__(s, nc, ndsem=40):
        s.nc = nc
        s.es = ExitStack()
        s.eng = {'pe': nc.tensor, 'act': nc.scalar, 'dve': nc.vector, 'pool': nc.gpsimd, 'sp': nc.sync}
        s.esem = {e: s.es.enter_context(nc.semaphore("es_" + e)) for e in ENG}
        s.ecnt = {e: 0 for e in ENG}
        s.dsem = [s.es.enter_context(nc.semaphore("ds%d" % i)) for i in range(ndsem)]
        s.dcnt = [0] * ndsem
        s.q = {e: [] for e in ENG}
        s.seen = {e: {} for e in ENG}
        s.nextd = 0

    def buf(s, name, dma=False):
        di = None
        if dma:
            di = s.nextd % len(s.dsem)
            s.nextd += 1
        return Buf(name, di)

    def _sem(s, key):
        return s.esem[key[1]] if key[0] == 'e' else s.dsem[key[1]]

    def _waits(s, eng, evs):
        out = []
        for ev in evs:
            if ev is None:
                continue
            key, val = ev
            if key == ('e', eng) and eng == 'pe':
                continue
            if s.seen[eng].get(key, 0) >= val:
                continue
            s.seen[eng][key] = val
            out.append(ev)
        return out

    def _deps(s, reads, writes):
        evs = [b.lw for b in reads]
        for b in writes:
            evs.append(b.lw)
            evs += list(b.rd.items())
        return evs

    def _upd(s, ev, reads, writes):
        for b in reads:
            b.rd[ev[0]] = max(b.rd.get(ev[0], 0), ev[1])
        for b in writes:
            b.lw = ev
            b.rd = {}

    def op(s, eng, fn, reads=(), writes=()):
        w = s._waits(eng, s._deps(reads, writes))
        s.ecnt[eng] += 1
        ev = (('e', eng), s.ecnt[eng])
        s.q[eng].append((fn, w, ('e', eng), 1))
        s._upd(ev, reads, writes)
        return ev

    def dma(s, eng, fn, sb, reads=(), writes=()):
        di = sb.di
        evs = s._deps(reads, writes)
        if s.dcnt[di] > 0:
            evs.append((('d', di), s.dcnt[di]))
        w = s._waits(eng, evs)
        s.dcnt[di] += 16
        ev = (('d', di), s.dcnt[di])
        s.q[eng].append((fn, w, ('d', di), 16))
        s._upd(ev, reads, writes)
        return ev

    def barrier(s):
        for e in ENG:
            evs = [(('e', e2), s.ecnt[e2]) for e2 in ENG if e2 != e and s.ecnt[e2] > 0]
            evs += [(('d', i), c) for i, c in enumerate(s.dcnt) if c > 0]
            s.q[e].append((None, s._waits(e, evs), None, 0))

    def flush(s):
        s.barrier()
        with s.nc.Block() as block:
            def mk(e):
                def run(eo):
                    for (fn, w, key, inc) in s.q[e]:
                        for (k, v) in w:
                            eo.wait_ge(s._sem(k), v)
                        if fn is not None:
                            fn(eo).then_inc(s._sem(key), inc)
                return run
            block.tensor(mk('pe'))
            block.scalar(mk('act'))
            block.vector(mk('dve'))
            block.gpsimd(mk('pool'))
            block.sync(mk('sp'))
        s.q = {e: [] for e in ENG}


class Ctx:
    def __init__(s, P, cfg, dr):
        s.P, s.cfg, s.dr, s.nc = P, cfg, dr, P.nc
        s.es = ExitStack()
        s.wi = 0

    _uid = [0]

    def sb(s, name, shape, dt=F32):
        Ctx._uid[0] += 1
        return s.es.enter_context(s.nc.sbuf_tensor("%s_u%d" % (name, Ctx._uid[0]), list(shape), dt))

    def ps(s, name, shape):
        Ctx._uid[0] += 1
        return s.es.enter_context(s.nc.psum_tensor("%s_u%d" % (name, Ctx._uid[0]), list(shape), F32))

    def common(s, nring=4, smalls_layer=None, wdt=F32):
        P = s.P
        s.ring = [s.sb("wr%d" % i, [128, 2048], wdt) for i in range(nring)]
        s.ringb = [P.buf("wr%d" % i, dma=True) for i in range(nring)]
        s.ones = s.sb("ones", [128, 128])
        s.onesb = P.buf("ones")
        s.cst = s.sb("cst", [128, 4])
        s.cstb = P.buf("cst")
        o, c = s.ones, s.cst
        P.op('pool', lambda e: e.memset(o[:, :], 1.0), writes=[s.onesb])
        P.op('pool', lambda e: e.memset(c[:, 0:1], 1e-6), writes=[s.cstb])
        P.op('pool', lambda e: e.memset(c[:, 1:2], 1.0), writes=[s.cstb])
        if smalls_layer is not None:
            ns = s.dr['smalls'].shape[2]
            s.sm = s.sb("smalls", [128, ns])
            s.smb = P.buf("smalls", dma=True)
            sm, src = s.sm, s.dr['smalls'][smalls_layer]
            P.dma('pool', lambda e: e.dma_start(out=sm[:, :], in_=src), s.smb, writes=[s.smb])

    def wload(s, src, kcn):
        i = s.wi % len(s.ring)
        s.wi += 1
        t, b = s.ring[i], s.ringb[i]
        s.P.dma('sp', lambda e: e.dma_start(out=t[:, 0:kcn * 128], in_=src.rearrange("p k c -> p (k c)")), b, writes=[b])
        return t, b

    def close(s):
        s.P.flush()
        s.es.close()


def mm_acc(P, ps, psb, n, wt, wb, acts, actb, first=True, last=True):
    K = len(acts)
    for n0 in range(0, n, 512):
        n1 = min(n, n0 + 512)
        for k in range(K):
            a = acts[k]
            pb = psb[n0 // 512] if isinstance(psb, list) else psb
            P.op('pe', lambda e, k=k, a=a, n0=n0, n1=n1: e.matmul(ps[:, n0:n1], lhsT=wt[:, k * 128:(k + 1) * 128], rhs=a(n0, n1),
                                                            start=(first and k == 0), stop=(last and k == K - 1)),
                 reads=[wb] + actb, writes=[pb])


def rms_rstd(P, C, srcs, srcb, n, dim, sqs, sqb, psn, psnb, rs, rsb):
    K = len(srcs)
    for k in range(K):
        sq, sb_ = sqs[k % 2], sqb[k % 2]
        a = srcs[k]
        P.op('act', lambda e, sq=sq, a=a: e.activation(out=sq[:, 0:n], in_=a, func=AF.Square), reads=srcb, writes=[sb_])
        for n0 in range(0, n, 512):
            n1 = min(n, n0 + 512)
            P.op('pe', lambda e, sq=sq, k=k, n0=n0, n1=n1: e.matmul(psn[:, n0:n1], lhsT=C.ones[:, :], rhs=sq[:, n0:n1],
                                                                start=(k == 0), stop=(k == K - 1)),
                 reads=[sb_, C.onesb], writes=[psnb])
    P.op('act', lambda e: e.activation(out=rs[:, 0:n], in_=psn[:, 0:n], func=AF.Sqrt, bias=C.cst[:, 0:1], scale=1.0 / dim),
         reads=[psnb, C.cstb], writes=[rsb])
    P.op('dve', lambda e: e.reciprocal(out=rs[:, 0:n], in_=rs[:, 0:n]), reads=[rsb], writes=[rsb])


class NormRes:
    def __init__(s, C, n, tag=""):
        P = C.P
        s.sq = [C.sb("nsq0" + tag, [128, n]), C.sb("nsq1" + tag, [128, n])]
        s.sqb = [P.buf("nsq0"), P.buf("nsq1")]
        s.psn = C.ps("npsn" + tag, [128, max(n, 512)])
        s.psnb = P.buf("npsn")
        s.rs = C.sb("nrs" + tag, [128, n])
        s.rsb = P.buf("nrs")


def rmsnorm_tile(P, C, NR, xt, xtb, xn, xnb, KC, n, gcol0):
    rms_rstd(P, C, [xt[:, k, 0:n] for k in range(KC)], [xtb], n, KC * 128, NR.sq, NR.sqb, NR.psn, NR.psnb, NR.rs, NR.rsb)
    for k in range(KC):
        P.op('dve', lambda e, k=k: e.scalar_tensor_tensor(out=xn[:, k, 0:n], in0=xt[:, k, 0:n], scalar=C.sm[:, gcol0 + k:gcol0 + k + 1],
                                                       in1=NR.rs[:, 0:n], op0=ALU.mult, op1=ALU.mult),
             reads=[xtb, NR.rsb, C.smb], writes=[xnb])


def smalls_layout(cfg):
    KC = cfg['D'] // 128
    off, o = {}, 0
    for name, n in [('ffn_g0', KC), ('ffn_g1', KC), ('mix_g', KC), ('mem_g', KC), ('memq_g', 2), ('memk_g', 2),
                    ('conv_w', KC * 4), ('conv_b', KC), ('gb', 4 * KC), ('lam', 2 * KC), ('qg', 3), ('kg', 3)]:
        off[name] = o
        o += n
    return off, o


def load_x(P, C, X, xt, xtb, t, N):
    src = X[:, :, t * N:(t + 1) * N].rearrange("k p n -> p k n")
    P.dma('pool', lambda e: e.dma_start(out=xt[:, :, :], in_=src), xtb, writes=[xtb])


def store_x(P, C, X, xt, xtb, t, N):
    dst = X[:, :, t * N:(t + 1) * N].rearrange("k p n -> p k n")
    P.dma('pool', lambda e: e.dma_start(out=dst, in_=xt[:, :, :]), xtb, reads=[xtb])


def phase_ffn(P, cfg, dr, Xs, Xd, l, i, PC=None):
    D, S, FF = cfg['D'], cfg['S'], cfg['FF']
    KC, FC, N = D // 128, FF // 128, min(1024, cfg['S'])
    NBK = N // 512
    SO, _ = smalls_layout(cfg)
    C = Ctx(P, cfg, dr)
    C.common(nring=4, smalls_layer=l, wdt=BF16)
    w_in, w_out = dr['ffn_w_in_bf'][l * 2 + i], dr['ffn_w_out_bf'][l * 2 + i]
    kcn = max(k for k in range(1, 17) if FC % k == 0)
    npc = FC // kcn
    nhalf = 2 if npc % 2 == 0 else 1
    HC = FC // nhalf
    xt, xtb = C.sb("xt", [128, KC, N]), P.buf("xt", dma=True)
    xn, xnb = C.sb("xn", [128, KC, N], BF16), P.buf("xn")
    h, hb = C.sb("h", [128, HC, N], BF16), P.buf("h")
    NR = NormRes(C, N)
    psg, psu, pso = C.ps("psg", [128, N]), C.ps("psu", [128, N]), C.ps("pso", [128, N])
    psgb, psub, psob = [P.buf("g") for _ in range(NBK)], [P.buf("u") for _ in range(NBK)], [P.buf("o") for _ in range(NBK)]
    if PC is not None:
        PC.alloc(C, 2)
    npieces = (S // N) * (2 * FC + KC * npc)
    todo = len(PC.chunks.get(l + 1, [])) if PC is not None else 0
    stride = max(1, (npieces * (2 - i)) // max(1, todo) - 1) if todo else 0
    cnt = [0]

    def tick():
        cnt[0] += 1
        if stride and cnt[0] % stride == 0:
            PC.pump(l + 1, 1)
    for t in range(S // N):
        load_x(P, C, Xs, xt, xtb, t, N)
        rmsnorm_tile(P, C, NR, xt, xtb, xn, xnb, KC, N, SO['ffn_g%d' % i])
        acts = [(lambda n0, n1, k=k: xn[:, k, n0:n1]) for k in range(KC)]
        for hf_ in range(nhalf):
            for jj in range(HC):
                j = hf_ * HC + jj
                wt, wb = C.wload(w_in[j], KC)
                mm_acc(P, psg, psgb, N, wt, wb, acts, [xnb])
                wt, wb = C.wload(w_in[FC + j], KC)
                mm_acc(P, psu, psub, N, wt, wb, acts, [xnb])
                tick()
                tick()
                for bk in range(NBK):
                    c0, c1 = bk * 512, (bk + 1) * 512
                    P.op('act', lambda e, jj=jj, c0=c0, c1=c1: e.activation(out=h[:, jj, c0:c1], in_=psg[:, c0:c1], func=AF.Silu), reads=[psgb[bk]], writes=[hb])
                    P.op('dve', lambda e, jj=jj, c0=c0, c1=c1: e.tensor_tensor(out=h[:, jj, c0:c1], in0=psu[:, c0:c1], in1=h[:, jj, c0:c1], op=ALU.mult),
                         reads=[psub[bk], hb], writes=[hb])
            for m in range(KC):
                pcs = list(range(hf_ * (npc // nhalf), (hf_ + 1) * (npc // nhalf)))
                for pi, pc in enumerate(pcs):
                    wt, wb = C.wload(w_out[m][:, pc * kcn:(pc + 1) * kcn, :], kcn)
                    acts2 = [(lambda n0, n1, k=k: h[:, k, n0:n1]) for k in range(pc * kcn - hf_ * HC, (pc + 1) * kcn - hf_ * HC)]
                    mm_acc(P, pso, psob, N, wt, wb, acts2, [hb], first=(pi == 0), last=(pi == len(pcs) - 1))
                    tick()
                for bk in range(NBK):
                    c0, c1 = bk * 512, (bk + 1) * 512
                    P.op('dve', lambda e, m=m, c0=c0, c1=c1: e.scalar_tensor_tensor(out=xt[:, m, c0:c1], in0=pso[:, c0:c1], scalar=0.5, in1=xt[:, m, c0:c1],
                                                                                op0=ALU.mult, op1=ALU.add), reads=[psob[bk], xtb], writes=[xtb])
        store_x(P, C, Xd, xt, xtb, t, N)
    if PC is not None and i == 1:
        PC.pump(l + 1, 10 ** 9)
    C.close()


class Precast:
    def __init__(s, P, cfg, dr):
        s.P, s.dr = P, dr
        L = cfg['DEPTH']
        s.chunks = {l: [] for l in range(L + 1)}
        for name in BIG:
            if name not in dr:
                continue
            for li in range(dr[name].shape[0]):
                layer = li // 2 if name.startswith('ffn') else (2 * li if name.startswith('rec') else 2 * li + 1)
                src, dst = dr[name][li], dr[name + '_bf'][li]
                per = int(np.prod(src.shape)) // 128
                E = max(e for e in range(1, 2049) if per % e == 0)
                dims = " ".join("d%d" % i for i in range(len(src.shape)))
                sf = src.rearrange("%s -> (%s)" % (dims, dims)).rearrange("(n p e) -> n p e", p=128, e=E)
                df = dst.rearrange("%s -> (%s)" % (dims, dims)).rearrange("(n p e) -> n p e", p=128, e=E)
                for n in range(per // E):
                    s.chunks[layer].append((sf[n], df[n], E))
        s.ci = 0

    def alloc(s, C, nsl):
        P = s.P
        s.nsl = nsl
        s.st = [C.sb("pcs%d" % i, [128, 2048]) for i in range(nsl)]
        s.stb = [P.buf("pcs", dma=True) for _ in range(nsl)]
        s.b16 = [C.sb("pcb%d" % i, [128, 2048], BF16) for i in range(nsl)]
        s.b16b = [P.buf("pcb", dma=True) for _ in range(nsl)]

    def pump(s, layer, k, bg=True):
        P = s.P
        lst = s.chunks.get(layer, [])
        for _ in range(min(k, len(lst))):
            src, dst, E = lst.pop(0)
            i = s.ci % s.nsl
            eng = 'pool' if bg else ['dve', 'act', 'pool'][s.ci % 3]
            q = 'pool' if bg else 'sp'
            s.ci += 1
            a_, ab_, b_, bb_ = s.st[i], s.stb[i], s.b16[i], s.b16b[i]
            P.dma(q, lambda e, a_=a_, src=src, E=E: e.dma_start(out=a_[:, 0:E], in_=src), ab_, writes=[ab_])
            if eng == 'act':
                P.op('act', lambda e, a_=a_, b_=b_, E=E: e.activation(out=b_[:, 0:E], in_=a_[:, 0:E], func=AF.Copy), reads=[ab_], writes=[bb_])
            else:
                P.op(eng, lambda e, a_=a_, b_=b_, E=E: e.tensor_copy(out=b_[:, 0:E], in_=a_[:, 0:E]), reads=[ab_], writes=[bb_])
            P.dma(q, lambda e, b_=b_, dst=dst, E=E: e.dma_start(out=dst, in_=b_[:, 0:E]), bb_, reads=[bb_])


def phase_precast(P, cfg, dr, PC):
    C = Ctx(P, cfg, dr)
    PC.alloc(C, 4)
    PC.pump(0, 10 ** 9, bg=False)
    C.close()


def phase_memkv(P, cfg, dr, l):
    D, M, MH = cfg['D'], cfg['M'], cfg['MH']
    KC, MW = D // 128, cfg['MH'] * 256
    MC = 2 * MW // 128
    SO, _ = smalls_layout(cfg)
    C = Ctx(P, cfg, dr)
    C.common(nring=4, smalls_layer=l)
    ident, identb = C.sb("ident", [128, 128]), P.buf("ident", dma=True)
    P.dma('pool', lambda e: e.dma_start(out=ident[:, :], in_=dr['ident']), identb, writes=[identb])
    mt, mtb = C.sb("mt", [128, KC, M]), P.buf("mt", dma=True)
    mn, mnb = C.sb("mn", [128, KC, M]), P.buf("mn")
    kv, kvb = C.sb("kv", [128, MC, M]), P.buf("kv")
    kn, knb = C.sb("kn", [128, MC // 2, M]), P.buf("kn", dma=True)
    vm, vmb = C.sb("vm", [128, M // 128, MW]), P.buf("vm", dma=True)
    NR = NormRes(C, M)
    ps = [C.ps("ps%d" % j, [128, 512]) for j in range(2)]
    psb = [P.buf("ps") for _ in range(2)]
    src = dr['memT'].rearrange("k p n -> p k n")
    P.dma('pool', lambda e: e.dma_start(out=mt[:, :, :], in_=src), mtb, writes=[mtb])
    rmsnorm_tile(P, C, NR, mt, mtb, mn, mnb, KC, M, SO['mem_g'])
    acts = [(lambda n0, n1, k=k: mn[:, k, n0:n1]) for k in range(KC)]
    for m in range(MC):
        b = m % 2
        wt, wb = C.wload(dr['mem_w_kv'][l][m], KC)
        mm_acc(P, ps[b], psb[b], M, wt, wb, acts, [mnb])
        P.op('act', lambda e, b=b, m=m: e.activation(out=kv[:, m, 0:M], in_=ps[b][:, 0:M], func=AF.Copy), reads=[psb[b]], writes=[kvb])
    for hh in range(MH):
        rms_rstd(P, C, [kv[:, 2 * hh + f, 0:M] for f in range(2)], [kvb], M, 256, NR.sq, NR.sqb, NR.psn, NR.psnb, NR.rs, NR.rsb)
        for f in range(2):
            c = SO['memk_g'] + f
            P.op('dve', lambda e, hh=hh, f=f, c=c: e.scalar_tensor_tensor(out=kn[:, 2 * hh + f, 0:M], in0=kv[:, 2 * hh + f, 0:M],
                                                                      scalar=C.sm[:, c:c + 1], in1=NR.rs[:, 0:M], op0=ALU.mult, op1=ALU.mult),
                 reads=[kvb, NR.rsb, C.smb], writes=[knb])
    for m in range(MC // 2):
        for tc in range(M // 128):
            b = (m * 2 + tc) % 2
            P.op('pe', lambda e, b=b, m=m, tc=tc: e.transpose(out=ps[b][:, 0:128], in_=kv[:, MC // 2 + m, tc * 128:(tc + 1) * 128], identity=ident[:, :]),
                 reads=[kvb, identb], writes=[psb[b]])
            P.op('act', lambda e, b=b, m=m, tc=tc: e.activation(out=vm[:, tc, m * 128:(m + 1) * 128], in_=ps[b][:, 0:128], func=AF.Copy),
                 reads=[psb[b]], writes=[vmb])
    P.dma('pool', lambda e: e.dma_start(out=dr['KN'], in_=kn[:, :, :]), knb, reads=[knb])
    P.dma('pool', lambda e: e.dma_start(out=dr['VM'], in_=vm[:, :, :]), vmb, reads=[vmb])
    C.close()


class MemAtt:
    def __init__(s, C, N):
        P, cfg, dr = C.P, C.cfg, C.dr
        s.C, s.N = C, N
        M, MH = cfg['M'], cfg['MH']
        MW = MH * 256
        s.kn, s.knb = C.sb("kn", [128, 2 * MH, M]), P.buf("kn", dma=True)
        s.vm, s.vmb = C.sb("vm", [128, M // 128, MW]), P.buf("vm", dma=True)
        P.dma('pool', lambda e: e.dma_start(out=s.kn[:, :, :], in_=dr['KN']), s.knb, writes=[s.knb])
        P.dma('pool', lambda e: e.dma_start(out=s.vm[:, :, :], in_=dr['VM']), s.vmb, writes=[s.vmb])
        s.qms = [C.sb("qm%d" % i, [128, 2 * MH, N]) for i in range(2)]
        s.qmbs = [P.buf("qm") for _ in range(2)]
        s.qm, s.qmb = s.qms[0], s.qmbs[0]
        s.qn, s.qnb = C.sb("qn", [128, 2, N]), P.buf("qn")
        s.pm = [C.sb("pm%d" % j, [128, N]) for j in range(M // 128)]
        s.pmb = [P.buf("pm") for _ in range(M // 128)]
        s.rden, s.rdenb = C.sb("rden", [128, N]), P.buf("rden")
        s.ym, s.ymb = C.sb("ym", [128, 2 * MH, N], BF16), P.buf("ym", dma=True)

    def run(s, NR, psA, psAb, psB, psBb, t, sub=0):
        C, N = s.C, s.N
        s.qm, s.qmb = s.qms[sub], s.qmbs[sub]
        s_qm, s_qmb = s.qms[sub], s.qmbs[sub]
        P, cfg, dr = C.P, C.cfg, C.dr
        M, MH = cfg['M'], cfg['MH']
        SO, _ = smalls_layout(cfg)
        MCk = M // 128
        for hh in range(MH):
            rms_rstd(P, C, [s_qm[:, 2 * hh + f, 0:N] for f in range(2)], [s_qmb], N, 256, NR.sq, NR.sqb, NR.psn, NR.psnb, NR.rs, NR.rsb)
            for f in range(2):
                c = SO['memq_g'] + f
                P.op('dve', lambda e, hh=hh, f=f, c=c: e.scalar_tensor_tensor(out=s.qn[:, f, 0:N], in0=s_qm[:, 2 * hh + f, 0:N], scalar=C.sm[:, c:c + 1],
                                                                          in1=NR.rs[:, 0:N], op0=ALU.mult, op1=ALU.mult),
                     reads=[s_qmb, NR.rsb, C.smb], writes=[s.qnb])
            for mc in range(MCk):
                for f in range(2):
                    P.op('pe', lambda e, hh=hh, f=f, mc=mc: e.matmul(psA[:, 0:N], lhsT=s.kn[:, 2 * hh + f, mc * 128:(mc + 1) * 128], rhs=s.qn[:, f, 0:N],
                                                                 start=(f == 0), stop=(f == 1)), reads=[s.knb, s.qnb], writes=[psAb])
                P.op('act', lambda e, mc=mc: e.activation(out=s.pm[mc][:, 0:N], in_=psA[:, 0:N], func=AF.Exp, scale=256.0 ** -0.5),
                     reads=[psAb], writes=[s.pmb[mc]])
            for mc in range(MCk):
                P.op('pe', lambda e, mc=mc: e.matmul(psB[:, 0:N], lhsT=C.ones[:, :], rhs=s.pm[mc][:, 0:N], start=(mc == 0), stop=(mc == MCk - 1)),
                     reads=[C.onesb, s.pmb[mc]], writes=[psBb])
            P.op('dve', lambda e: e.reciprocal(out=s.rden[:, 0:N], in_=psB[:, 0:N]), reads=[psBb], writes=[s.rdenb])
            for f in range(2):
                for mc in range(MCk):
                    c0 = (2 * hh + f) * 128
                    P.op('pe', lambda e, mc=mc, c0=c0: e.matmul(psA[:, 0:N], lhsT=s.vm[:, mc, c0:c0 + 128], rhs=s.pm[mc][:, 0:N],
                                                             start=(mc == 0), stop=(mc == MCk - 1)), reads=[s.vmb, s.pmb[mc]], writes=[psAb])
                P.op('dve', lambda e, hh=hh, f=f: e.tensor_tensor(out=s.ym[:, 2 * hh + f, 0:N], in0=psA[:, 0:N], in1=s.rden[:, 0:N], op=ALU.mult),
                     reads=[psAb, s.rdenb], writes=[s.ymb])
        dst = dr['YM'][:, :, t * N:(t + 1) * N].rearrange("k p n -> p k n")
        P.dma('pool', lambda e: e.dma_start(out=dst, in_=s.ym[:, :, :]), s.ymb, reads=[s.ymb])


def phase_rec_p1(P, cfg, dr, X, l, j):
    D, S, MH = cfg['D'], cfg['S'], cfg['MH']
    KC, N = D // 128, min(512, cfg['S'])
    SO, _ = smalls_layout(cfg)
    C = Ctx(P, cfg, dr)
    C.common(nring=4, smalls_layer=l, wdt=BF16)
    w_in = dr['rec_w_in_bf'][j]
    xt, xtb = C.sb("xt", [128, KC, N]), P.buf("xt", dma=True)
    xns = [C.sb("xn%d" % i, [128, KC, N], BF16) for i in range(2)]
    xnbs = [P.buf("xn") for _ in range(2)]
    NR = NormRes(C, N)
    MA = MemAtt(C, N)
    ps = [C.ps("ps%d" % i, [128, 512]) for i in range(3)]
    psb = [P.buf("ps") for _ in range(3)]
    psA, psAb, psB, psBb = C.ps("psA", [128, 512]), P.buf("psA"), C.ps("psB", [128, 512]), P.buf("psB")
    stg = [C.sb("stg%d" % i, [128, N]) for i in range(3)]
    stgb = [P.buf("stg", dma=True) for _ in range(3)]
    t1, t1b = C.sb("t1", [128, N]), P.buf("t1")
    t2, t2b = C.sb("t2g", [128, N]), P.buf("t2")
    zt, ztb = C.sb("zt", [128, KC, 2]), P.buf("zt", dma=True)
    P.op('pool', lambda e: e.memset(zt[:, :, :], 0.0), writes=[ztb])
    RB = dr['RB']
    nc_ = P.nc
    def _zp(e, a, b):
        with nc_.allow_non_contiguous_dma(reason="tiny zero pad"):
            return e.dma_start(out=RB[:, :, a:b].rearrange("k p n -> p k n"), in_=zt[:, :, 0:b - a])
    P.dma('pool', lambda e: _zp(e, 0, 2), ztb, reads=[ztb])
    P.dma('pool', lambda e: _zp(e, S + 2, S + 3), ztb, reads=[ztb])
    si = 0
    NSUB = 2 if (S // N) % 2 == 0 else 1
    cnt = 0
    for tp in range(S // (N * NSUB)):
        for sub in range(NSUB):
            load_x(P, C, X, xt, xtb, tp * NSUB + sub, N)
            rmsnorm_tile(P, C, NR, xt, xtb, xns[sub], xnbs[sub], KC, N, SO['mix_g'])
        for m in range(2 * KC + 2 * MH):
            wt, wb = C.wload(w_in[m], KC)
            for sub in range(NSUB):
                t = tp * NSUB + sub
                xn, xnb = xns[sub], xnbs[sub]
                acts = [(lambda n0, n1, k=k, xn=xn: xn[:, k, n0:n1]) for k in range(KC)]
                b = cnt % 3
                cnt += 1
                mm_acc(P, ps[b], psb[b], N, wt, wb, acts, [xnb])
                if m < KC:
                    s_, sb_ = stg[si % 3], stgb[si % 3]
                    si += 1
                    P.op('act', lambda e, b=b: e.activation(out=t1[:, 0:N], in_=ps[b][:, 0:N], func=AF.Square), reads=[psb[b]], writes=[t1b])
                    P.op('dve', lambda e: e.tensor_scalar(out=t1[:, 0:N], in0=t1[:, 0:N], scalar1=0.044715, scalar2=1.0, op0=ALU.mult, op1=ALU.add),
                         reads=[t1b], writes=[t1b])
                    P.op('dve', lambda e, b=b: e.tensor_tensor(out=t1[:, 0:N], in0=ps[b][:, 0:N], in1=t1[:, 0:N], op=ALU.mult), reads=[psb[b], t1b], writes=[t1b])
                    P.op('act', lambda e: e.activation(out=t2[:, 0:N], in_=t1[:, 0:N], func=AF.Sigmoid, scale=1.5957691216), reads=[t1b], writes=[t2b])
                    P.op('dve', lambda e, b=b, s_=s_: e.tensor_tensor(out=s_[:, 0:N], in0=ps[b][:, 0:N], in1=t2[:, 0:N], op=ALU.mult),
                         reads=[psb[b], t2b], writes=[sb_])
                    dst = dr['GG'][m][:, t * N:(t + 1) * N]
                    P.dma('pool', lambda e, s_=s_, dst=dst: e.dma_start(out=dst, in_=s_[:, 0:N]), sb_, reads=[sb_])
                elif m < 2 * KC:
                    s_, sb_ = stg[si % 3], stgb[si % 3]
                    si += 1
                    P.op('act', lambda e, b=b, s_=s_: e.activation(out=s_[:, 0:N], in_=ps[b][:, 0:N], func=AF.Copy), reads=[psb[b]], writes=[sb_])
                    dst = RB[m - KC][:, 2 + t * N:2 + (t + 1) * N]
                    P.dma('pool', lambda e, s_=s_, dst=dst: e.dma_start(out=dst, in_=s_[:, 0:N]), sb_, reads=[sb_])
                else:
                    P.op('act', lambda e, b=b, m=m, sub=sub: e.activation(out=MA.qms[sub][:, m - 2 * KC, 0:N], in_=ps[b][:, 0:N], func=AF.Copy),
                         reads=[psb[b]], writes=[MA.qmbs[sub]])
        for sub in range(NSUB):
            MA.run(NR, psA, psAb, psB, psBb, tp * NSUB + sub, sub)
    C.close()


def phase_rec_p2(P, cfg, dr, l, j, d):
    D, S = cfg['D'], cfg['S']
    KC, NB = D // 128, D // 256
    T = min(2048, S)
    nseg = S // T
    SO, _ = smalls_layout(cfg)
    C = Ctx(P, cfg, dr)
    C.common(nring=4, smalls_layer=l)
    sm = C.sm
    cs, csb = C.sb("cs", [128, 4 * KC]), P.buf("cs")
    lo = SO['lam']
    P.op('act', lambda e: e.activation(out=cs[:, 0:2 * KC], in_=sm[:, lo:lo + 2 * KC], func=AF.Exp, scale=-1.0), reads=[C.smb], writes=[csb])
    P.op('act', lambda e: e.activation(out=cs[:, 0:2 * KC], in_=cs[:, 0:2 * KC], func=AF.Ln, bias=C.cst[:, 1:2], scale=1.0), reads=[csb, C.cstb], writes=[csb])
    P.op('dve', lambda e: e.tensor_scalar(out=cs[:, 2 * KC:4 * KC], in0=cs[:, 0:2 * KC], scalar1=-16.0, scalar2=None, op0=ALU.mult), reads=[csb], writes=[csb])
    P.op('dve', lambda e: e.tensor_scalar(out=cs[:, 0:2 * KC], in0=cs[:, 0:2 * KC], scalar1=-8.0, scalar2=None, op0=ALU.mult), reads=[csb], writes=[csb])
    rb, rbb = C.sb("rb", [128, 2, T + 3]), P.buf("rb", dma=True)
    xc, xcb = C.sb("xc", [128, 2, T]), P.buf("xc")
    names = ['rr', 'ii', 'aa', 'bb', 'hh']
    tl = {n: C.sb(n, [128, T]) for n in names}
    tb = {n: P.buf(n, dma=(n == 'hh')) for n in names}
    hf, hfb = C.sb("hf", [128, T]), P.buf("hf", dma=True)
    gg, ggb = C.sb("gg", [128, T]), P.buf("gg", dma=True)
    st, stb = C.sb("st", [128, 2]), P.buf("st")
    yb, ybb = C.sb("yb", [128, T], BF16), P.buf("yb", dma=True)
    nb512 = max(1, T // 512)
    psr, psrb = C.ps("psr", [128, nb512 * 512]), P.buf("psr")
    psi, psib = C.ps("psi", [128, nb512 * 512]), P.buf("psi")
    gw = dr['gate_w'][j]
    rr, ii, aa, bb, hh = (tl[n] for n in names)
    for n in range(NB):
        for d in [d]:
            segs = list(range(nseg)) if d == 0 else list(range(nseg - 1, -1, -1))
            for si_, seg in enumerate(segs):
                src = dr['RB'][2 * n:2 * n + 2, :, seg * T:seg * T + T + 3].rearrange("k p n -> p k n")
                P.dma('pool', lambda e, src=src: e.dma_start(out=rb[:, :, :], in_=src), rbb, writes=[rbb])
                for c in range(2):
                    ch = 2 * n + c
                    w0, bcol = SO['conv_w'] + ch * 4, SO['conv_b'] + ch
                    P.op('dve', lambda e, c=c, w0=w0, bcol=bcol: e.tensor_scalar(out=xc[:, c, :], in0=rb[:, c, 0:T], scalar1=sm[:, w0:w0 + 1],
                                                                             scalar2=sm[:, bcol:bcol + 1], op0=ALU.mult, op1=ALU.add),
                         reads=[rbb, C.smb], writes=[xcb])
                    for jj in range(1, 4):
                        P.op('dve', lambda e, c=c, w0=w0, jj=jj: e.scalar_tensor_tensor(out=xc[:, c, :], in0=rb[:, c, jj:jj + T], scalar=sm[:, w0 + jj:w0 + jj + 1],
                                                                                    in1=xc[:, c, :], op0=ALU.mult, op1=ALU.add),
                             reads=[rbb, C.smb, xcb], writes=[xcb])
                for oc in range(2):
                    ch = 2 * n + oc
                    for gt, (pst, pstb, dst, dstb) in enumerate([(psr, psrb, rr, tb['rr']), (psi, psib, ii, tb['ii'])]):
                        wt, wb = C.wload(gw[d][gt][n][oc], 2)
                        mm_acc(P, pst, pstb, T, wt, wb, [(lambda n0, n1, k=k: xc[:, k, n0:n1]) for k in range(2)], [xcb])
                        bc = SO['gb'] + (d * 2 + gt) * KC + ch
                        P.op('act', lambda e, pst=pst, dst=dst, bc=bc: e.activation(out=dst[:, 0:T], in_=pst[:, 0:T], func=AF.Sigmoid, bias=sm[:, bc:bc + 1], scale=1.0),
                             reads=[pstb, C.smb], writes=[dstb])
                    c1, c2 = d * KC + ch, 2 * KC + d * KC + ch
                    P.op('act', lambda e, c1=c1: e.activation(out=aa[:, 0:T], in_=rr[:, 0:T], func=AF.Exp, scale=cs[:, c1:c1 + 1]), reads=[tb['rr'], csb], writes=[tb['aa']])
                    P.op('act', lambda e, c2=c2: e.activation(out=bb[:, 0:T], in_=rr[:, 0:T], func=AF.Exp, scale=cs[:, c2:c2 + 1]), reads=[tb['rr'], csb], writes=[tb['bb']])
                    P.op('act', lambda e: e.activation(out=bb[:, 0:T], in_=bb[:, 0:T], func=AF.Sqrt, bias=C.cst[:, 1:2], scale=-1.0), reads=[tb['bb'], C.cstb], writes=[tb['bb']])
                    P.op('dve', lambda e: e.tensor_tensor(out=bb[:, 0:T], in0=bb[:, 0:T], in1=ii[:, 0:T], op=ALU.mult), reads=[tb['bb'], tb['ii']], writes=[tb['bb']])
                    P.op('dve', lambda e, oc=oc: e.tensor_tensor(out=bb[:, 0:T], in0=bb[:, 0:T], in1=xc[:, oc, :], op=ALU.mult), reads=[tb['bb'], xcb], writes=[tb['bb']])
                    init = 0.0 if si_ == 0 else st[:, oc:oc + 1]
                    if d == 0:
                        P.op('dve', lambda e, init=init: e.tensor_tensor_scan(out=hh[:, 0:T], data0=aa[:, 0:T], data1=bb[:, 0:T], initial=init, op0=ALU.mult, op1=ALU.add),
                             reads=[tb['aa'], tb['bb'], stb], writes=[tb['hh']])
                        P.op('dve', lambda e, oc=oc: e.tensor_copy(out=st[:, oc:oc + 1], in_=hh[:, T - 1:T]), reads=[tb['hh']], writes=[stb])
                        dst = dr['HF'][ch][:, seg * T:(seg + 1) * T]
                        P.dma('pool', lambda e, dst=dst: e.dma_start(out=dst, in_=hh[:, 0:T]), tb['hh'], reads=[tb['hh']])
                    else:
                        s1 = dr['HF'][ch][:, seg * T:(seg + 1) * T]
                        s2 = dr['GG'][ch][:, seg * T:(seg + 1) * T]
                        P.dma('pool', lambda e, s1=s1: e.dma_start(out=hf[:, 0:T], in_=s1), hfb, writes=[hfb])
                        P.dma('pool', lambda e, s2=s2: e.dma_start(out=gg[:, 0:T], in_=s2), ggb, writes=[ggb])
                        P.op('dve', lambda e, init=init: e.tensor_tensor_scan(out=hh[:, ::-1], data0=aa[:, ::-1], data1=bb[:, ::-1], initial=init, op0=ALU.mult, op1=ALU.add),
                             reads=[tb['aa'], tb['bb'], stb], writes=[tb['hh']])
                        P.op('dve', lambda e, oc=oc: e.tensor_copy(out=st[:, oc:oc + 1], in_=hh[:, 0:1]), reads=[tb['hh']], writes=[stb])
                        P.op('dve', lambda e: e.tensor_tensor(out=hh[:, 0:T], in0=hh[:, 0:T], in1=hf[:, 0:T], op=ALU.add), reads=[tb['hh'], hfb], writes=[tb['hh']])
                        P.op('dve', lambda e: e.tensor_tensor(out=yb[:, 0:T], in0=hh[:, 0:T], in1=gg[:, 0:T], op=ALU.mult), reads=[tb['hh'], ggb], writes=[ybb])
                        dst = dr['YR'][ch][:, seg * T:(seg + 1) * T]
                        P.dma('pool', lambda e, dst=dst: e.dma_start(out=dst, in_=yb[:, 0:T]), ybb, reads=[ybb])
    C.close()


def phase_wout(P, cfg, dr, X, Xd, w_out, YMAIN, nmain):
    D, S, MH = cfg['D'], cfg['S'], cfg['MH']
    KC, N = D // 128, min(512, cfg['S'])
    YC = nmain + 2 * MH
    C = Ctx(P, cfg, dr)
    C.common(nring=4, wdt=BF16)
    xt, xtb = C.sb("xt", [128, KC, N]), P.buf("xt", dma=True)
    yt, ytb = C.sb("yt", [128, YC, N], BF16), P.buf("yt", dma=True)
    pso = [C.ps("pso%d" % j, [128, 512]) for j in range(2)]
    psob = [P.buf("o") for _ in range(2)]
    kcn = max(k for k in range(1, 17) if YC % k == 0)
    for t in range(S // N):
        load_x(P, C, X, xt, xtb, t, N)
        s1 = YMAIN[:, :, t * N:(t + 1) * N].rearrange("k p n -> p k n")
        s2 = dr['YM'][:, :, t * N:(t + 1) * N].rearrange("k p n -> p k n")
        P.dma('pool', lambda e, s1=s1: e.dma_start(out=yt[:, 0:nmain, :], in_=s1), ytb, writes=[ytb])
        P.dma('pool', lambda e, s2=s2: e.dma_start(out=yt[:, nmain:YC, :], in_=s2), ytb, writes=[ytb])
        for m in range(KC):
            b = m % 2
            npc = YC // kcn
            for pc in range(npc):
                wt, wb = C.wload(w_out[m][:, pc * kcn:(pc + 1) * kcn, :], kcn)
                acts2 = [(lambda n0, n1, k=k: yt[:, k, n0:n1]) for k in range(pc * kcn, (pc + 1) * kcn)]
                mm_acc(P, pso[b], psob[b], N, wt, wb, acts2, [ytb], first=(pc == 0), last=(pc == npc - 1))
            P.op('dve', lambda e, b=b, m=m: e.tensor_tensor(out=xt[:, m, 0:N], in0=pso[b][:, 0:N], in1=xt[:, m, 0:N], op=ALU.add),
                 reads=[psob[b], xtb], writes=[xtb])
        store_x(P, C, Xd, xt, xtb, t, N)
    C.close()


def phase_att_p1(P, cfg, dr, X, l, j):
    D, S, MH, AH = cfg['D'], cfg['S'], cfg['MH'], cfg['AH']
    KC, N = D // 128, min(512, cfg['S'])
    SO, _ = smalls_layout(cfg)
    C = Ctx(P, cfg, dr)
    C.common(nring=4, smalls_layer=l, wdt=BF16)
    identf, identfb = C.sb("identf", [128, 128]), P.buf("identf", dma=True)
    P.dma('pool', lambda e: e.dma_start(out=identf[:, :], in_=dr['ident']), identfb, writes=[identfb])
    ident, identb = C.sb("ident", [128, 128], BF16), P.buf("ident")
    P.op('dve', lambda e: e.tensor_copy(out=ident[:, :], in_=identf[:, :]), reads=[identfb], writes=[identb])
    Ctx._uid[0] += 1
    psT = C.es.enter_context(C.nc.psum_tensor("psT_att_u%d" % Ctx._uid[0], [128, 128], BF16))
    psTb = P.buf("psT")
    w_in = dr['att_w_in_bf'][j]
    xt, xtb = C.sb("xt", [128, KC, N]), P.buf("xt", dma=True)
    xns = [C.sb("xn%d" % i, [128, KC, N], BF16) for i in range(2)]
    xnbs = [P.buf("xn") for _ in range(2)]
    NR = NormRes(C, N)
    NR2 = NormRes(C, N, "b")
    MA = MemAtt(C, N)
    ps = [C.ps("ps%d" % i, [128, 512]) for i in range(3)]
    psb = [P.buf("ps") for _ in range(3)]
    psA, psAb, psB, psBb = C.ps("psA", [128, 512]), P.buf("psA"), C.ps("psB", [128, 512]), P.buf("psB")
    stg = [C.sb("stg%d" % i, [128, N], BF16) for i in range(3)]
    stgb = [P.buf("stg", dma=True) for _ in range(3)]
    vs2 = [C.sb("vs%d" % i, [128, N], BF16) for i in range(2)]
    vs2b = [P.buf("vs") for _ in range(2)]
    vt, vtb = C.sb("vt", [128, N // 128, 128], BF16), P.buf("vt", dma=True)
    si = 0
    nqkv = 9 * AH
    NSUB = 2 if (S // N) % 2 == 0 else 1
    cnt = 0
    for tp in range(S // (N * NSUB)):
        for sub in range(NSUB):
            load_x(P, C, X, xt, xtb, tp * NSUB + sub, N)
            rmsnorm_tile(P, C, NR, xt, xtb, xns[sub], xnbs[sub], KC, N, SO['mix_g'])
        pend = None
        for m in range(nqkv + 2 * MH):
            wt, wb = C.wload(w_in[m], KC)
            for sub in range(NSUB):
                t = tp * NSUB + sub
                xn, xnb = xns[sub], xnbs[sub]
                acts = [(lambda n0, n1, k=k, xn=xn: xn[:, k, n0:n1]) for k in range(KC)]
                b = cnt % 3
                c2 = cnt % 2
                cnt += 1
                mm_acc(P, ps[b], psb[b], N, wt, wb, acts, [xnb])
                post = None
                if m >= nqkv:
                    P.op('act', lambda e, b=b, m=m, sub=sub: e.activation(out=MA.qms[sub][:, m - nqkv, 0:N], in_=ps[b][:, 0:N], func=AF.Copy),
                         reads=[psb[b]], writes=[MA.qmbs[sub]])
                else:
                    g, kind, hd_ = m // (3 * AH), (m % (3 * AH)) // AH, m % AH
                    if kind < 2:
                        sq, sqb_ = NR2.sq[c2], NR2.sqb[c2]
                        P.op('act', lambda e, b=b, sq=sq: e.activation(out=sq[:, 0:N], in_=ps[b][:, 0:N], func=AF.Square), reads=[psb[b]], writes=[sqb_])

                        def post(b=b, sq=sq, sqb_=sqb_, g=g, kind=kind, hd_=hd_, t=t):
                            nonlocal si
                            s_, sb_ = stg[si % 3], stgb[si % 3]
                            si += 1
                            P.op('pe', lambda e: e.matmul(NR2.psn[:, 0:N], lhsT=C.ones[:, :], rhs=sq[:, 0:N], start=True, stop=True),
                                 reads=[sqb_, C.onesb], writes=[NR2.psnb])
                            P.op('act', lambda e: e.activation(out=NR2.rs[:, 0:N], in_=NR2.psn[:, 0:N], func=AF.Sqrt, bias=C.cst[:, 0:1], scale=1.0 / 128),
                                 reads=[NR2.psnb, C.cstb], writes=[NR2.rsb])
                            P.op('dve', lambda e: e.reciprocal(out=NR2.rs[:, 0:N], in_=NR2.rs[:, 0:N]), reads=[NR2.rsb], writes=[NR2.rsb])
                            gc = SO['qg' if kind == 0 else 'kg'] + g
                            P.op('dve', lambda e: e.scalar_tensor_tensor(out=s_[:, 0:N], in0=ps[b][:, 0:N], scalar=C.sm[:, gc:gc + 1], in1=NR2.rs[:, 0:N],
                                                                       op0=ALU.mult, op1=ALU.mult), reads=[psb[b], NR2.rsb, C.smb], writes=[sb_])
                            dst = dr['QK'][(g * 2 + kind) * AH + hd_][:, t * N:(t + 1) * N]
                            P.dma('pool', lambda e: e.dma_start(out=dst, in_=s_[:, 0:N]), sb_, reads=[sb_])
                    else:
                        vs, vsb = vs2[c2], vs2b[c2]
                        P.op('act', lambda e, b=b, vs=vs: e.activation(out=vs[:, 0:N], in_=ps[b][:, 0:N], func=AF.Copy), reads=[psb[b]], writes=[vsb])

                        def post(vs=vs, vsb=vsb, g=g, hd_=hd_, t=t):
                            for bi in range(N // 128):
                                P.op('pe', lambda e, bi=bi: e.transpose(out=psT[:, 0:128], in_=vs[:, bi * 128:(bi + 1) * 128], identity=ident[:, :]),
                                     reads=[vsb, identb], writes=[psTb])
                                P.op('act', lambda e, bi=bi: e.activation(out=vt[:, bi, :], in_=psT[:, 0:128], func=AF.Copy), reads=[psTb], writes=[vtb])
                            dst = dr['V'][g * AH + hd_][t * N:(t + 1) * N, :].rearrange("(c p) d -> p c d", p=128)
                            P.dma('pool', lambda e: e.dma_start(out=dst, in_=vt[:, :, :]), vtb, reads=[vtb])
                if pend is not None:
                    pend()
                pend = post
        if pend is not None:
            pend()
        for sub in range(NSUB):
            MA.run(NR, psA, psAb, psB, psBb, tp * NSUB + sub, sub)
    C.close()


def alibi_slope(cfg, g, h):
    n = 3 * cfg['AH']
    return float(np.exp2(np.float32(-8.0) * (np.float32(g * cfg['AH'] + h) + np.float32(1.0)) / np.float32(n)))


def phase_att_p2(P, cfg, dr):
    S, AH = cfg['S'], cfg['AH']
    C = Ctx(P, cfg, dr)
    C.common(nring=0)
    rel, relb = C.sb("rel", [128, 6, 512]), P.buf("rel", dma=True)
    P.dma('pool', lambda e: e.dma_start(out=rel[:, :, :], in_=dr['c_rel'].rearrange("c p n -> p c n")), relb, writes=[relb])
    bt, btb = C.sb("bt", [128, 6, 512]), P.buf("bt")
    qT, qTb = C.sb("qT", [128, S], BF16), P.buf("qT", dma=True)
    kT, kTb = C.sb("kT", [128, S], BF16), P.buf("kT", dma=True)
    vc, vcb = C.sb("vc", [128, S // 128, 128], BF16), P.buf("vc", dma=True)
    ones16, ones16b = C.sb("ones16", [128, 128], BF16), P.buf("ones16")
    P.op('pool', lambda e: e.memset(ones16[:, :], 1.0), writes=[ones16b])
    aO, aOb = C.sb("aO", [128, S]), P.buf("aO", dma=True)
    aD, aDb = C.sb("aD", [128, S]), P.buf("aD")
    tmp = [C.sb("tmp%d" % i, [128, 512]) for i in range(4)]
    tmpb = [P.buf("tmp") for _ in range(4)]
    ya = [C.sb("ya%d" % i, [128, min(2048, S)], BF16) for i in range(2)]
    yab = [P.buf("ya", dma=True) for _ in range(2)]
    pt = [C.sb("pt%d" % i, [128, 512], BF16) for i in range(4)]
    ptb = [P.buf("pt") for _ in range(4)]
    psS = [C.ps("psS%d" % i, [128, 512]) for i in range(4)]
    psSb = [P.buf("psS") for _ in range(4)]
    psO = [C.ps("psO%d" % i, [128, 512]) for i in range(2)]
    psOb = [P.buf("psO") for _ in range(2)]
    psD = [C.ps("psD%d" % i, [128, 512]) for i in range(2)]
    psDb = [P.buf("psD") for _ in range(2)]
    it = 0
    qi = 0
    for h in range(AH):
        for g, (win, d) in enumerate(cfg['PAT']):
            L = S // d
            QT = min(512, L)
            nck = L // 128
            P.dma('pool', lambda e, g=g, h=h: e.dma_start(out=qT[:, :], in_=dr['QK'][(g * 2 + 0) * AH + h]), qTb, writes=[qTb])
            P.dma('pool', lambda e, g=g, h=h: e.dma_start(out=kT[:, :], in_=dr['QK'][(g * 2 + 1) * AH + h]), kTb, writes=[kTb])
            Vd = dr['V'][g * AH + h].rearrange("(c i r) e -> r i c e", i=128, r=d)
            for r in range(d):
                P.dma('pool', lambda e, r=r, Vd=Vd, nck=nck: e.dma_start(out=vc[:, r * nck:(r + 1) * nck, :], in_=Vd[r]), vcb, writes=[vcb])
            sl = -alibi_slope(cfg, g, h) * d
            for ci in range(6):
                P.op('dve', lambda e, ci=ci, sl=sl: e.tensor_scalar(out=bt[:, ci, :], in0=rel[:, ci, :], scalar1=sl, scalar2=None, op0=ALU.mult),
                     reads=[relb], writes=[btb])
            pairs = []
            for r in range(d):
                for j0 in range(0, L, QT):
                    qb = qi % 2
                    qi += 1
                    cis = [ci for ci in range(QT // 128 + 2) if 0 <= j0 - 128 + 128 * ci and j0 - 128 + 128 * ci + 128 <= L]
                    for idx, ci in enumerate(cis):
                        pairs.append((r, j0, qb, ci, j0 - 128 + 128 * ci, idx, len(cis)))

            def stageA(p, b, d=d, QT=QT):
                r, j0, qb, ci, kj, idx, nci = p
                qap = qT[:, r + d * j0: r + d * (j0 + QT - 1) + 1: d]
                kap = kT[:, r + d * kj: r + d * (kj + 127) + 1: d]
                P.op('pe', lambda e: e.matmul(psS[b][:, 0:QT], lhsT=kap, rhs=qap, start=True, stop=True), reads=[kTb, qTb], writes=[psSb[b]])
                P.op('dve', lambda e: e.scalar_tensor_tensor(out=tmp[b][:, 0:QT], in0=psS[b][:, 0:QT], scalar=128.0 ** -0.5, in1=bt[:, ci, 0:QT],
                                                           op0=ALU.mult, op1=ALU.add), reads=[psSb[b], btb], writes=[tmpb[b]])
                P.op('act', lambda e: e.activation(out=pt[b][:, 0:QT], in_=tmp[b][:, 0:QT], func=AF.Exp), reads=[tmpb[b]], writes=[ptb[b]])

            def stageB(p, b, d=d, QT=QT, nck=nck, g=g):
                r, j0, qb, ci, kj, idx, nci = p
                vck = r * nck + kj // 128
                P.op('pe', lambda e: e.matmul(psO[qb][:, 0:QT], lhsT=vc[:, vck, :], rhs=pt[b][:, 0:QT], start=(idx == 0), stop=(idx == nci - 1)),
                     reads=[vcb, ptb[b]], writes=[psOb[qb]])
                P.op('pe', lambda e: e.matmul(psD[qb][:, 0:QT], lhsT=ones16[:, :], rhs=pt[b][:, 0:QT], start=(idx == 0), stop=(idx == nci - 1)),
                     reads=[ones16b, ptb[b]], writes=[psDb[qb]])
                if idx == nci - 1:
                    oap = aO[:, r + d * j0: r + d * (j0 + QT - 1) + 1: d]
                    dap = aD[:, r + d * j0: r + d * (j0 + QT - 1) + 1: d]
                    if g == 0:
                        P.op('act', lambda e: e.activation(out=oap, in_=psO[qb][:, 0:QT], func=AF.Copy), reads=[psOb[qb]], writes=[aOb])
                        P.op('dve', lambda e: e.tensor_copy(out=dap, in_=psD[qb][:, 0:QT]), reads=[psDb[qb]], writes=[aDb])
                    else:
                        P.op('dve', lambda e: e.tensor_tensor(out=oap, in0=psO[qb][:, 0:QT], in1=oap, op=ALU.add), reads=[psOb[qb], aOb], writes=[aOb])
                        P.op('dve', lambda e: e.tensor_tensor(out=dap, in0=psD[qb][:, 0:QT], in1=dap, op=ALU.add), reads=[psDb[qb], aDb], writes=[aDb])
            LA = 3
            base = it
            for i in range(len(pairs) + LA):
                if i < len(pairs):
                    stageA(pairs[i], (base + i) % 4)
                if i >= LA:
                    stageB(pairs[i - LA], (base + i - LA) % 4)
            it += len(pairs)
        P.op('dve', lambda e: e.reciprocal(out=aD[:, :], in_=aD[:, :]), reads=[aDb], writes=[aDb])
        CW = min(2048, S)
        for cc in range(S // CW):
            yb_, ybb_ = ya[cc % 2], yab[cc % 2]
            P.op('dve', lambda e, cc=cc, yb_=yb_: e.tensor_tensor(out=yb_[:, 0:CW], in0=aO[:, cc * CW:(cc + 1) * CW], in1=aD[:, cc * CW:(cc + 1) * CW], op=ALU.mult),
                 reads=[aOb, aDb], writes=[ybb_])
            P.dma('pool', lambda e, h=h, cc=cc, yb_=yb_: e.dma_start(out=dr['YA'][h][:, cc * CW:(cc + 1) * CW], in_=yb_[:, 0:CW]), ybb_, reads=[ybb_])
    C.close()


def build(cfg):
    D, S, FF, L, MH, AH, M = cfg['D'], cfg['S'], cfg['FF'], cfg['DEPTH'], cfg['MH'], cfg['AH'], cfg['M']
    KC, FC = D // 128, FF // 128
    NREC, NATT = (L + 1) // 2, L // 2
    MW = MH * 256
    _, NS = smalls_layout(cfg)
    nc = bass.Bass("TRN2", target_bir_lowering=False)
    dr = {}

    def din(name, shape):
        dr[name] = nc.dram_tensor(name, list(shape), F32, kind="ExternalInput").ap()

    def dsc(name, shape, dt=F32):
        dr[name] = nc.dram_tensor(name, list(shape), dt, kind="ExternalOutput" if cfg.get('dbg') else "Internal").ap()
    din('xT', [KC, 128, S])
    din('memT', [KC, 128, M])
    din('smalls', [L, 128, NS])
    din('ident', [128, 128])
    din('c_rel', [6, 128, 512])
    din('c_negm', [6, 128, 512])
    din('ffn_w_in', [L * 2, 2 * FC, 128, KC, 128])
    din('ffn_w_out', [L * 2, KC, 128, FC, 128])
    din('mem_w_kv', [L, 2 * MW // 128, 128, KC, 128])
    din('rec_w_in', [NREC, 2 * KC + 2 * MH, 128, KC, 128])
    din('rec_w_out', [NREC, KC, 128, KC + 2 * MH, 128])
    din('gate_w', [NREC, 2, 2, D // 256, 2, 128, 2, 128])
    if NATT:
        din('att_w_in', [NATT, 9 * AH + 2 * MH, 128, KC, 128])
        din('att_w_out', [NATT, KC, 128, AH + 2 * MH, 128])
    dr['out'] = nc.dram_tensor('out', [KC, 128, S], F32, kind="ExternalOutput").ap()
    dsc('KN', [128, 2 * MH, M])
    dsc('VM', [128, M // 128, MW])
    dsc('YM', [2 * MH, 128, S], BF16)
    dsc('GG', [KC, 128, S])
    dsc('RB', [KC, 128, S + 3])
    dsc('HF', [KC, 128, S])
    dsc('YR', [KC, 128, S], BF16)
    dsc('QK', [6 * AH, 128, S], BF16)
    dsc('V', [3 * AH, S, 128], BF16)
    dsc('YA', [AH, 128, S], BF16)
    for name in BIG:
        if name in dr:
            dr[name + '_bf'] = [nc.dram_tensor("%s_bf%d" % (name, i), list(dr[name].shape[1:]), BF16, kind="Internal").ap()
                                for i in range(dr[name].shape[0])]
    P = Prog(nc)
    PC = Precast(P, cfg, dr)
    phase_precast(P, cfg, dr, PC)
    X = dr['out']
    stop = cfg.get('stop')
    for l in range(L):
        phase_ffn(P, cfg, dr, dr['xT'] if l == 0 else X, X, l, 0, PC)
        if stop == 'ffn0':
            break
        phase_memkv(P, cfg, dr, l)
        j = l // 2
        if l % 2 == 0:
            phase_rec_p1(P, cfg, dr, X, l, j)
            phase_rec_p2(P, cfg, dr, l, j, 0)
            phase_rec_p2(P, cfg, dr, l, j, 1)
            phase_wout(P, cfg, dr, X, X, dr['rec_w_out_bf'][j], dr['YR'], KC)
            if stop == 'mix0':
                break
        else:
            phase_att_p1(P, cfg, dr, X, l, j)
            phase_att_p2(P, cfg, dr)
            phase_wout(P, cfg, dr, X, X, dr['att_w_out_bf'][j], dr['YA'], AH)
        phase_ffn(P, cfg, dr, X, X, l, 1, PC)
    return nc


def pieces(W):
    K, Mo = W.shape[-2], W.shape[-1]
    lead = W.shape[:-2]
    Wr = W.reshape(*lead, K // 128, 128, Mo // 128, 128)
    nl = len(lead)
    perm = list(range(nl)) + [nl + 2, nl + 1, nl + 0, nl + 3]
    return np.ascontiguousarray(Wr.transpose(perm))


def cols(v):
    n = v.shape[-1] // 128
    return np.swapaxes(v.reshape(*v.shape[:-1], n, 128), -1, -2)


def const_tables():
    k = np.arange(128)[:, None]
    q = np.arange(512)[None, :]
    rel = np.stack([np.abs(128 * (ci - 1) + k - q) for ci in range(6)]).astype(np.float32)
    rel = np.where(rel <= 64, rel, 1.0e6).astype(np.float32)
    return rel, rel


def prep_inputs(cfg, inp):
    D, S, L = cfg['D'], cfg['S'], cfg['DEPTH']
    KC = D // 128
    SO, NS = smalls_layout(cfg)
    f = lambda a: np.asarray(a, dtype=np.float32)
    sm = np.zeros((L, 128, NS), np.float32)
    for l in range(L):
        j = l // 2
        sm[l, :, SO['ffn_g0']:SO['ffn_g0'] + KC] = cols(f(inp['ffn_norm'])[l, 0])
        sm[l, :, SO['ffn_g1']:SO['ffn_g1'] + KC] = cols(f(inp['ffn_norm'])[l, 1])
        sm[l, :, SO['mix_g']:SO['mix_g'] + KC] = cols(f(inp['mix_norm'])[l])
        sm[l, :, SO['mem_g']:SO['mem_g'] + KC] = cols(f(inp['mem_norm'])[l])
        sm[l, :, SO['memq_g']:SO['memq_g'] + 2] = cols(f(inp['mem_qk_gain'])[l, 0])
        sm[l, :, SO['memk_g']:SO['memk_g'] + 2] = cols(f(inp['mem_qk_gain'])[l, 1])
        if l % 2 == 0:
            cw = f(inp['rec_conv_w'])[j]
            sm[l, :, SO['conv_w']:SO['conv_w'] + 4 * KC] = np.stack([cols(cw[t]) for t in range(4)], -1).reshape(128, 4 * KC)
            sm[l, :, SO['conv_b']:SO['conv_b'] + KC] = cols(f(inp['rec_conv_b'])[j])
            gb = f(inp['rec_gate_b'])[j].reshape(2, 2, D)
            for d in range(2):
                for gt in range(2):
                    o = SO['gb'] + (d * 2 + gt) * KC
                    sm[l, :, o:o + KC] = cols(gb[d, gt])
                o = SO['lam'] + d * KC
                sm[l, :, o:o + KC] = cols(f(inp['rec_lambda'])[j, d])
        else:
            qk = f(inp['att_qk_gain'])[j]
            sm[l, :, SO['qg']:SO['qg'] + 3] = qk[0].T
            sm[l, :, SO['kg']:SO['kg'] + 3] = qk[1].T
    rel, negm = const_tables()
    shared = {
        'smalls': sm, 'ident': np.eye(128, dtype=np.float32), 'c_rel': rel, 'c_negm': negm,
        'ffn_w_in': pieces(f(inp['ffn_w_in'])).reshape(L * 2, -1, 128, KC, 128),
        'ffn_w_out': pieces(f(inp['ffn_w_out'])).reshape(L * 2, KC, 128, -1, 128),
        'mem_w_kv': pieces(f(inp['mem_w_kv'])),
        'rec_w_in': pieces(f(inp['rec_w_in'])),
        'rec_w_out': pieces(f(inp['rec_w_out'])),
        'gate_w': pieces(f(inp['rec_gate_w'])),
    }
    if L // 2:
        shared['att_w_in'] = pieces(f(inp['att_w_in']))
        shared['att_w_out'] = pieces(f(inp['att_w_out']))
    x, mem = f(inp['x']), f(inp['mem'])
    B = x.shape[0]
    maps = []
    for b in range(B):
        m = dict(shared)
        m['xT'] = np.ascontiguousarray(x[b].T).reshape(KC, 128, S)
        m['memT'] = np.ascontiguousarray(mem[b].T).reshape(KC, 128, -1)
        maps.append(m)
    return maps


def run(cfg, inp):
    nc = build(cfg)
    maps = prep_inputs(cfg, inp)
    B = len(maps)
    res = run_bass_kernel_spmd(nc, maps, core_ids=list(range(B)))
    D, S = cfg['D'], cfg['S']
    if cfg.get('dbg'):
        cfg['dbg_out'] = res.results
    return np.stack([np.ascontiguousarray(res.results[b]['out'].reshape(D, S).T) for b in range(B)]).astype(np.float32)


def kernel(**inputs):
    return run(FULL, inputs)
```

```python
import numpy as np
from contextlib import ExitStack
import concourse.bass as bass
import concourse.mybir as mybir
from concourse.bass_utils import run_bass_kernel_spmd

F32 = mybir.dt.float32
BF16 = mybir.dt.bfloat16
BIG = ['ffn_w_in', 'ffn_w_out', 'rec_w_in', 'rec_w_out', 'att_w_in', 'att_w_out']
AF = mybir.ActivationFunctionType
ALU = mybir.AluOpType
ENG = ['pe', 'act', 'dve', 'pool', 'sp']

FULL = dict(D=2048, S=8192, FF=5632, DEPTH=4, MH=4, AH=8, M=256, PAT=((128, 1), (512, 4), (2048, 16)))


class Buf:
    def __init__(s, name, di=None):
        s.name, s.lw, s.rd, s.di = name, None, {}, di


class Prog:
    def __init__(s, nc, ndsem=40):
        s.nc = nc
        s.es = ExitStack()
        s.eng = {'pe': nc.tensor, 'act': nc.scalar, 'dve': nc.vector, 'pool': nc.gpsimd, 'sp': nc.sync}
        s.esem = {e: s.es.enter_context(nc.semaphore("es_" + e)) for e in ENG}
        s.ecnt = {e: 0 for e in ENG}
        s.dsem = [s.es.enter_context(nc.semaphore("ds%d" % i)) for i in range(ndsem)]
        s.dcnt = [0] * ndsem
        s.q = {e: [] for e in ENG}
        s.seen = {e: {} for e in ENG}
        s.nextd = 0

    def buf(s, name, dma=False):
        di = None
        if dma:
            di = s.nextd % len(s.dsem)
            s.nextd += 1
        return Buf(name, di)

    def _sem(s, key):
        return s.esem[key[1]] if key[0] == 'e' else s.dsem[key[1]]

    def _waits(s, eng, evs):
        out = []
        for ev in evs:
            if ev is None:
                continue
            key, val = ev
            if key == ('e', eng) and eng == 'pe':
                continue
            if s.seen[eng].get(key, 0) >= val:
                continue
            s.seen[eng][key] = val
            out.append(ev)
        return out

    def _deps(s, reads, writes):
        evs = [b.lw for b in reads]
        for b in writes:
            evs.append(b.lw)
            evs += list(b.rd.items())
        return evs

    def _upd(s, ev, reads, writes):
        for b in reads:
            b.rd[ev[0]] = max(b.rd.get(ev[0], 0), ev[1])
        for b in writes:
            b.lw = ev
            b.rd = {}

    def op(s, eng, fn, reads=(), writes=()):
        w = s._waits(eng, s._deps(reads, writes))
        s.ecnt[eng] += 1
        ev = (('e', eng), s.ecnt[eng])
        s.q[eng].append((fn, w, ('e', eng), 1))
        s._upd(ev, reads, writes)
        return ev

    def dma(s, eng, fn, sb, reads=(), writes=()):
        di = sb.di
        evs = s._deps(reads, writes)
        if s.dcnt[di] > 0:
            evs.append((('d', di), s.dcnt[di]))
        w = s._waits(eng, evs)
        s.dcnt[di] += 16
        ev = (('d', di), s.dcnt[di])
        s.q[eng].append((fn, w, ('d', di), 16))
        s._upd(ev, reads, writes)
        return ev

    def barrier(s):
        for e in ENG:
            evs = [(('e', e2), s.ecnt[e2]) for e2 in ENG if e2 != e and s.ecnt[e2] > 0]
            evs += [(('d', i), c) for i, c in enumerate(s.dcnt) if c > 0]
            s.q[e].append((None, s._waits(e, evs), None, 0))

    def flush(s):
        s.barrier()
        with s.nc.Block() as block:
            def mk(e):
                def run(eo):
                    for (fn, w, key, inc) in s.q[e]:
                        for (k, v) in w:
                            eo.wait_ge(s._sem(k), v)
                        if fn is not None:
                            fn(eo).then_inc(s._sem(key), inc)
                return run
            block.tensor(mk('pe'))
            block.scalar(mk('act'))
            block.vector(mk('dve'))
            block.gpsimd(mk('pool'))
            block.sync(mk('sp'))
        s.q = {e: [] for e in ENG}


class Ctx:
    def __init__(s, P, cfg, dr):
        s.P, s.cfg, s.dr, s.nc = P, cfg, dr, P.nc
        s.es = ExitStack()
        s.wi = 0

    _uid = [0]

    def sb(s, name, shape, dt=F32):
        Ctx._uid[0] += 1
        return s.es.enter_context(s.nc.sbuf_tensor("%s_u%d" % (name, Ctx._uid[0]), list(shape), dt))

    def ps(s, name, shape):
        Ctx._uid[0] += 1
        return s.es.enter_context(s.nc.psum_tensor("%s_u%d" % (name, Ctx._uid[0]), list(shape), F32))

    def common(s, nring=4, smalls_layer=None, wdt=F32):
        P = s.P
        s.ring = [s.sb("wr%d" % i, [128, 2048], wdt) for i in range(nring)]
        s.ringb = [P.buf("wr%d" % i, dma=True) for i in range(nring)]
        s.ones = s.sb("ones", [128, 128])
        s.onesb = P.buf("ones")
        s.cst = s.sb("cst", [128, 4])
        s.cstb = P.buf("cst")
        o, c = s.ones, s.cst
        P.op('pool', lambda e: e.memset(o[:, :], 1.0), writes=[s.onesb])
        P.op('pool', lambda e: e.memset(c[:, 0:1], 1e-6), writes=[s.cstb])
        P.op('pool', lambda e: e.memset(c[:, 1:2], 1.0), writes=[s.cstb])
        if smalls_layer is not None:
            ns = s.dr['smalls'].shape[2]
            s.sm = s.sb("smalls", [128, ns])
            s.smb = P.buf("smalls", dma=True)
            sm, src = s.sm, s.dr['smalls'][smalls_layer]
            P.dma('pool', lambda e: e.dma_start(out=sm[:, :], in_=src), s.smb, writes=[s.smb])

    def wload(s, src, kcn):
        i = s.wi % len(s.ring)
        s.wi += 1
        t, b = s.ring[i], s.ringb[i]
        s.P.dma('sp', lambda e: e.dma_start(out=t[:, 0:kcn * 128], in_=src.rearrange("p k c -> p (k c)")), b, writes=[b])
        return t, b

    def close(s):
        s.P.flush()
        s.es.close()


def mm_acc(P, ps, psb, n, wt, wb, acts, actb, first=True, last=True):
    K = len(acts)
    for n0 in range(0, n, 512):
        n1 = min(n, n0 + 512)
        for k in range(K):
            a = acts[k]
            pb = psb[n0 // 512] if isinstance(psb, list) else psb
            P.op('pe', lambda e, k=k, a=a, n0=n0, n1=n1: e.matmul(ps[:, n0:n1], lhsT=wt[:, k * 128:(k + 1) * 128], rhs=a(n0, n1),
                                                            start=(first and k == 0), stop=(last and k == K - 1)),
                 reads=[wb] + actb, writes=[pb])


def rms_rstd(P, C, srcs, srcb, n, dim, sqs, sqb, psn, psnb, rs, rsb):
    K = len(srcs)
    for k in range(K):
        sq, sb_ = sqs[k % 2], sqb[k % 2]
        a = srcs[k]
        P.op('act', lambda e, sq=sq, a=a: e.activation(out=sq[:, 0:n], in_=a, func=AF.Square), reads=srcb, writes=[sb_])
        for n0 in range(0, n, 512):
            n1 = min(n, n0 + 512)
            P.op('pe', lambda e, sq=sq, k=k, n0=n0, n1=n1: e.matmul(psn[:, n0:n1], lhsT=C.ones[:, :], rhs=sq[:, n0:n1],
                                                                start=(k == 0), stop=(k == K - 1)),
                 reads=[sb_, C.onesb], writes=[psnb])
    P.op('act', lambda e: e.activation(out=rs[:, 0:n], in_=psn[:, 0:n], func=AF.Sqrt, bias=C.cst[:, 0:1], scale=1.0 / dim),
         reads=[psnb, C.cstb], writes=[rsb])
    P.op('dve', lambda e: e.reciprocal(out=rs[:, 0:n], in_=rs[:, 0:n]), reads=[rsb], writes=[rsb])


class NormRes:
    def __init__(s, C, n, tag=""):
        P = C.P
        s.sq = [C.sb("nsq0" + tag, [128, n]), C.sb("nsq1" + tag, [128, n])]
        s.sqb = [P.buf("nsq0"), P.buf("nsq1")]
        s.psn = C.ps("npsn" + tag, [128, max(n, 512)])
        s.psnb = P.buf("npsn")
        s.rs = C.sb("nrs" + tag, [128, n])
        s.rsb = P.buf("nrs")


def rmsnorm_tile(P, C, NR, xt, xtb, xn, xnb, KC, n, gcol0):
    rms_rstd(P, C, [xt[:, k, 0:n] for k in range(KC)], [xtb], n, KC * 128, NR.sq, NR.sqb, NR.psn, NR.psnb, NR.rs, NR.rsb)
    for k in range(KC):
        P.op('dve', lambda e, k=k: e.scalar_tensor_tensor(out=xn[:, k, 0:n], in0=xt[:, k, 0:n], scalar=C.sm[:, gcol0 + k:gcol0 + k + 1],
                                                       in1=NR.rs[:, 0:n], op0=ALU.mult, op1=ALU.mult),
             reads=[xtb, NR.rsb, C.smb], writes=[xnb])


def smalls_layout(cfg):
    KC = cfg['D'] // 128
    off, o = {}, 0
    for name, n in [('ffn_g0', KC), ('ffn_g1', KC), ('mix_g', KC), ('mem_g', KC), ('memq_g', 2), ('memk_g', 2),
                    ('conv_w', KC * 4), ('conv_b', KC), ('gb', 4 * KC), ('lam', 2 * KC), ('qg', 3), ('kg', 3)]:
        off[name] = o
        o += n
    return off, o


def load_x(P, C, X, xt, xtb, t, N):
    src = X[:, :, t * N:(t + 1) * N].rearrange("k p n -> p k n")
    P.dma('pool', lambda e: e.dma_start(out=xt[:, :, :], in_=src), xtb, writes=[xtb])


def store_x(P, C, X, xt, xtb, t, N):
    dst = X[:, :, t * N:(t + 1) * N].rearrange("k p n -> p k n")
    P.dma('pool', lambda e: e.dma_start(out=dst, in_=xt[:, :, :]), xtb, reads=[xtb])


def phase_ffn(P, cfg, dr, Xs, Xd, l, i, PC=None):
    D, S, FF = cfg['D'], cfg['S'], cfg['FF']
    KC, FC, N = D // 128, FF // 128, min(1024, cfg['S'])
    NBK = N // 512
    SO, _ = smalls_layout(cfg)
    C = Ctx(P, cfg, dr)
    C.common(nring=4, smalls_layer=l, wdt=BF16)
    w_in, w_out = dr['ffn_w_in_bf'][l * 2 + i], dr['ffn_w_out_bf'][l * 2 + i]
    kcn = max(k for k in range(1, 17) if FC % k == 0)
    npc = FC // kcn
    nhalf = 2 if npc % 2 == 0 else 1
    HC = FC // nhalf
    xt, xtb = C.sb("xt", [128, KC, N]), P.buf("xt", dma=True)
    xn, xnb = C.sb("xn", [128, KC, N], BF16), P.buf("xn")
    h, hb = C.sb("h", [128, HC, N], BF16), P.buf("h")
    NR = NormRes(C, N)
    psg, psu, pso = C.ps("psg", [128, N]), C.ps("psu", [128, N]), C.ps("pso", [128, N])
    psgb, psub, psob = [P.buf("g") for _ in range(NBK)], [P.buf("u") for _ in range(NBK)], [P.buf("o") for _ in range(NBK)]
    if PC is not None:
        PC.alloc(C, 2)
    npieces = (S // N) * (2 * FC + KC * npc)
    todo = len(PC.chunks.get(l + 1, [])) if PC is not None else 0
    stride = max(1, (npieces * (2 - i)) // max(1, todo) - 1) if todo else 0
    cnt = [0]

    def tick():
        cnt[0] += 1
        if stride and cnt[0] % stride == 0:
            PC.pump(l + 1, 1)
    for t in range(S // N):
        load_x(P, C, Xs, xt, xtb, t, N)
        rmsnorm_tile(P, C, NR, xt, xtb, xn, xnb, KC, N, SO['ffn_g%d' % i])
        acts = [(lambda n0, n1, k=k: xn[:, k, n0:n1]) for k in range(KC)]
        for hf_ in range(nhalf):
            for jj in range(HC):
                j = hf_ * HC + jj
                wt, wb = C.wload(w_in[j], KC)
                mm_acc(P, psg, psgb, N, wt, wb, acts, [xnb])
                wt, wb = C.wload(w_in[FC + j], KC)
                mm_acc(P, psu, psub, N, wt, wb, acts, [xnb])
                tick()
                tick()
                for bk in range(NBK):
                    c0, c1 = bk * 512, (bk + 1) * 512
                    P.op('act', lambda e, jj=jj, c0=c0, c1=c1: e.activation(out=h[:, jj, c0:c1], in_=psg[:, c0:c1], func=AF.Silu), reads=[psgb[bk]], writes=[hb])
                    P.op('dve', lambda e, jj=jj, c0=c0, c1=c1: e.tensor_tensor(out=h[:, jj, c0:c1], in0=psu[:, c0:c1], in1=h[:, jj, c0:c1], op=ALU.mult),
                         reads=[psub[bk], hb], writes=[hb])
            for m in range(KC):
                pcs = list(range(hf_ * (npc // nhalf), (hf_ + 1) * (npc // nhalf)))
                for pi, pc in enumerate(pcs):
                    wt, wb = C.wload(w_out[m][:, pc * kcn:(pc + 1) * kcn, :], kcn)
                    acts2 = [(lambda n0, n1, k=k: h[:, k, n0:n1]) for k in range(pc * kcn - hf_ * HC, (pc + 1) * kcn - hf_ * HC)]
                    mm_acc(P, pso, psob, N, wt, wb, acts2, [hb], first=(pi == 0), last=(pi == len(pcs) - 1))
                    tick()
                for bk in range(NBK):
                    c0, c1 = bk * 512, (bk + 1) * 512
                    P.op('dve', lambda e, m=m, c0=c0, c1=c1: e.scalar_tensor_tensor(out=xt[:, m, c0:c1], in0=pso[:, c0:c1], scalar=0.5, in1=xt[:, m, c0:c1],
                                                                                op0=ALU.mult, op1=ALU.add), reads=[psob[bk], xtb], writes=[xtb])
        store_x(P, C, Xd, xt, xtb, t, N)
    if PC is not None and i == 1:
        PC.pump(l + 1, 10 ** 9)
    C.close()


class Precast:
    def __init__(s, P, cfg, dr):
        s.P, s.dr = P, dr
        L = cfg['DEPTH']
        s.chunks = {l: [] for l in range(L + 1)}
        for name in BIG:
            if name not in dr:
                continue
            for li in range(dr[name].shape[0]):
                layer = li // 2 if name.startswith('ffn') else (2 * li if name.startswith('rec') else 2 * li + 1)
                src, dst = dr[name][li], dr[name + '_bf'][li]
                per = int(np.prod(src.shape)) // 128
                E = max(e for e in range(1, 2049) if per % e == 0)
                dims = " ".join("d%d" % i for i in range(len(src.shape)))
                sf = src.rearrange("%s -> (%s)" % (dims, dims)).rearrange("(n p e) -> n p e", p=128, e=E)
                df = dst.rearrange("%s -> (%s)" % (dims, dims)).rearrange("(n p e) -> n p e", p=128, e=E)
                for n in range(per // E):
                    s.chunks[layer].append((sf[n], df[n], E))
        s.ci = 0

    def alloc(s, C, nsl):
        P = s.P
        s.nsl = nsl
        s.st = [C.sb("pcs%d" % i, [128, 2048]) for i in range(nsl)]
        s.stb = [P.buf("pcs", dma=True) for _ in range(nsl)]
        s.b16 = [C.sb("pcb%d" % i, [128, 2048], BF16) for i in range(nsl)]
        s.b16b = [P.buf("pcb", dma=True) for _ in range(nsl)]

    def pump(s, layer, k, bg=True):
        P = s.P
        lst = s.chunks.get(layer, [])
        for _ in range(min(k, len(lst))):
            src, dst, E = lst.pop(0)
            i = s.ci % s.nsl
            eng = 'pool' if bg else ['dve', 'act', 'pool'][s.ci % 3]
            q = 'pool' if bg else 'sp'
            s.ci += 1
            a_, ab_, b_, bb_ = s.st[i], s.stb[i], s.b16[i], s.b16b[i]
            P.dma(q, lambda e, a_=a_, src=src, E=E: e.dma_start(out=a_[:, 0:E], in_=src), ab_, writes=[ab_])
            if eng == 'act':
                P.op('act', lambda e, a_=a_, b_=b_, E=E: e.activation(out=b_[:, 0:E], in_=a_[:, 0:E], func=AF.Copy), reads=[ab_], writes=[bb_])
            else:
                P.op(eng, lambda e, a_=a_, b_=b_, E=E: e.tensor_copy(out=b_[:, 0:E], in_=a_[:, 0:E]), reads=[ab_], writes=[bb_])
            P.dma(q, lambda e, b_=b_, dst=dst, E=E: e.dma_start(out=dst, in_=b_[:, 0:E]), bb_, reads=[bb_])


def phase_precast(P, cfg, dr, PC):
    C = Ctx(P, cfg, dr)
    PC.alloc(C, 4)
    PC.pump(0, 10 ** 9, bg=False)
    C.close()


def phase_memkv(P, cfg, dr, l):
    D, M, MH = cfg['D'], cfg['M'], cfg['MH']
    KC, MW = D // 128, cfg['MH'] * 256
    MC = 2 * MW // 128
    SO, _ = smalls_layout(cfg)
    C = Ctx(P, cfg, dr)
    C.common(nring=4, smalls_layer=l)
    ident, identb = C.sb("ident", [128, 128]), P.buf("ident", dma=True)
    P.dma('pool', lambda e: e.dma_start(out=ident[:, :], in_=dr['ident']), identb, writes=[identb])
    mt, mtb = C.sb("mt", [128, KC, M]), P.buf("mt", dma=True)
    mn, mnb = C.sb("mn", [128, KC, M]), P.buf("mn")
    kv, kvb = C.sb("kv", [128, MC, M]), P.buf("kv")
    kn, knb = C.sb("kn", [128, MC // 2, M]), P.buf("kn", dma=True)
    vm, vmb = C.sb("vm", [128, M // 128, MW]), P.buf("vm", dma=True)
    NR = NormRes(C, M)
    ps = [C.ps("ps%d" % j, [128, 512]) for j in range(2)]
    psb = [P.buf("ps") for _ in range(2)]
    src = dr['memT'].rearrange("k p n -> p k n")
    P.dma('pool', lambda e: e.dma_start(out=mt[:, :, :], in_=src), mtb, writes=[mtb])
    rmsnorm_tile(P, C, NR, mt, mtb, mn, mnb, KC, M, SO['mem_g'])
    acts = [(lambda n0, n1, k=k: mn[:, k, n0:n1]) for k in range(KC)]
    for m in range(MC):
        b = m % 2
        wt, wb = C.wload(dr['mem_w_kv'][l][m], KC)
        mm_acc(P, ps[b], psb[b], M, wt, wb, acts, [mnb])
        P.op('act', lambda e, b=b, m=m: e.activation(out=kv[:, m, 0:M], in_=ps[b][:, 0:M], func=AF.Copy), reads=[psb[b]], writes=[kvb])
    for hh in range(MH):
        rms_rstd(P, C, [kv[:, 2 * hh + f, 0:M] for f in range(2)], [kvb], M, 256, NR.sq, NR.sqb, NR.psn, NR.psnb, NR.rs, NR.rsb)
        for f in range(2):
            c = SO['memk_g'] + f
            P.op('dve', lambda e, hh=hh, f=f, c=c: e.scalar_tensor_tensor(out=kn[:, 2 * hh + f, 0:M], in0=kv[:, 2 * hh + f, 0:M],
                                                                      scalar=C.sm[:, c:c + 1], in1=NR.rs[:, 0:M], op0=ALU.mult, op1=ALU.mult),
                 reads=[kvb, NR.rsb, C.smb], writes=[knb])
    for m in range(MC // 2):
        for tc in range(M // 128):
            b = (m * 2 + tc) % 2
            P.op('pe', lambda e, b=b, m=m, tc=tc: e.transpose(out=ps[b][:, 0:128], in_=kv[:, MC // 2 + m, tc * 128:(tc + 1) * 128], identity=ident[:, :]),
                 reads=[kvb, identb], writes=[psb[b]])
            P.op('act', lambda e, b=b, m=m, tc=tc: e.activation(out=vm[:, tc, m * 128:(m + 1) * 128], in_=ps[b][:, 0:128], func=AF.Copy),
                 reads=[psb[b]], writes=[vmb])
    P.dma('pool', lambda e: e.dma_start(out=dr['KN'], in_=kn[:, :, :]), knb, reads=[knb])
    P.dma('pool', lambda e: e.dma_start(out=dr['VM'], in_=vm[:, :, :]), vmb, reads=[vmb])
    C.close()


class MemAtt:
    def __init__(s, C, N):
        P, cfg, dr = C.P, C.cfg, C.dr
        s.C, s.N = C, N
        M, MH = cfg['M'], cfg['MH']
        MW = MH * 256
        s.kn, s.knb = C.sb("kn", [128, 2 * MH, M]), P.buf("kn", dma=True)
        s.vm, s.vmb = C.sb("vm", [128, M // 128, MW]), P.buf("vm", dma=True)
        P.dma('pool', lambda e: e.dma_start(out=s.kn[:, :, :], in_=dr['KN']), s.knb, writes=[s.knb])
        P.dma('pool', lambda e: e.dma_start(out=s.vm[:, :, :], in_=dr['VM']), s.vmb, writes=[s.vmb])
        s.qms = [C.sb("qm%d" % i, [128, 2 * MH, N]) for i in range(2)]
        s.qmbs = [P.buf("qm") for _ in range(2)]
        s.qm, s.qmb = s.qms[0], s.qmbs[0]
        s.qn, s.qnb = C.sb("qn", [128, 2, N]), P.buf("qn")
        s.pm = [C.sb("pm%d" % j, [128, N]) for j in range(M // 128)]
        s.pmb = [P.buf("pm") for _ in range(M // 128)]
        s.rden, s.rdenb = C.sb("rden", [128, N]), P.buf("rden")
        s.ym, s.ymb = C.sb("ym", [128, 2 * MH, N], BF16), P.buf("ym", dma=True)

    def run(s, NR, psA, psAb, psB, psBb, t, sub=0):
        C, N = s.C, s.N
        s.qm, s.qmb = s.qms[sub], s.qmbs[sub]
        s_qm, s_qmb = s.qms[sub], s.qmbs[sub]
        P, cfg, dr = C.P, C.cfg, C.dr
        M, MH = cfg['M'], cfg['MH']
        SO, _ = smalls_layout(cfg)
        MCk = M // 128
        for hh in range(MH):
            rms_rstd(P, C, [s_qm[:, 2 * hh + f, 0:N] for f in range(2)], [s_qmb], N, 256, NR.sq, NR.sqb, NR.psn, NR.psnb, NR.rs, NR.rsb)
            for f in range(2):
                c = SO['memq_g'] + f
                P.op('dve', lambda e, hh=hh, f=f, c=c: e.scalar_tensor_tensor(out=s.qn[:, f, 0:N], in0=s_qm[:, 2 * hh + f, 0:N], scalar=C.sm[:, c:c + 1],
                                                                          in1=NR.rs[:, 0:N], op0=ALU.mult, op1=ALU.mult),
                     reads=[s_qmb, NR.rsb, C.smb], writes=[s.qnb])
            for mc in range(MCk):
                for f in range(2):
                    P.op('pe', lambda e, hh=hh, f=f, mc=mc: e.matmul(psA[:, 0:N], lhsT=s.kn[:, 2 * hh + f, mc * 128:(mc + 1) * 128], rhs=s.qn[:, f, 0:N],
                                                                 start=(f == 0), stop=(f == 1)), reads=[s.knb, s.qnb], writes=[psAb])
                P.op('act', lambda e, mc=mc: e.activation(out=s.pm[mc][:, 0:N], in_=psA[:, 0:N], func=AF.Exp, scale=256.0 ** -0.5),
                     reads=[psAb], writes=[s.pmb[mc]])
            for mc in range(MCk):
                P.op('pe', lambda e, mc=mc: e.matmul(psB[:, 0:N], lhsT=C.ones[:, :], rhs=s.pm[mc][:, 0:N], start=(mc == 0), stop=(mc == MCk - 1)),
                     reads=[C.onesb, s.pmb[mc]], writes=[psBb])
            P.op('dve', lambda e: e.reciprocal(out=s.rden[:, 0:N], in_=psB[:, 0:N]), reads=[psBb], writes=[s.rdenb])
            for f in range(2):
                for mc in range(MCk):
                    c0 = (2 * hh + f) * 128
                    P.op('pe', lambda e, mc=mc, c0=c0: e.matmul(psA[:, 0:N], lhsT=s.vm[:, mc, c0:c0 + 128], rhs=s.pm[mc][:, 0:N],
                                                             start=(mc == 0), stop=(mc == MCk - 1)), reads=[s.vmb, s.pmb[mc]], writes=[psAb])
                P.op('dve', lambda e, hh=hh, f=f: e.tensor_tensor(out=s.ym[:, 2 * hh + f, 0:N], in0=psA[:, 0:N], in1=s.rden[:, 0:N], op=ALU.mult),
                     reads=[psAb, s.rdenb], writes=[s.ymb])
        dst = dr['YM'][:, :, t * N:(t + 1) * N].rearrange("k p n -> p k n")
        P.dma('pool', lambda e: e.dma_start(out=dst, in_=s.ym[:, :, :]), s.ymb, reads=[s.ymb])


def phase_rec_p1(P, cfg, dr, X, l, j):
    D, S, MH = cfg['D'], cfg['S'], cfg['MH']
    KC, N = D // 128, min(512, cfg['S'])
    SO, _ = smalls_layout(cfg)
    C = Ctx(P, cfg, dr)
    C.common(nring=4, smalls_layer=l, wdt=BF16)
    w_in = dr['rec_w_in_bf'][j]
    xt, xtb = C.sb("xt", [128, KC, N]), P.buf("xt", dma=True)
    xns = [C.sb("xn%d" % i, [128, KC, N], BF16) for i in range(2)]
    xnbs = [P.buf("xn") for _ in range(2)]
    NR = NormRes(C, N)
    MA = MemAtt(C, N)
    ps = [C.ps("ps%d" % i, [128, 512]) for i in range(3)]
    psb = [P.buf("ps") for _ in range(3)]
    psA, psAb, psB, psBb = C.ps("psA", [128, 512]), P.buf("psA"), C.ps("psB", [128, 512]), P.buf("psB")
    stg = [C.sb("stg%d" % i, [128, N]) for i in range(3)]
    stgb = [P.buf("stg", dma=True) for _ in range(3)]
    t1s = [C.sb("t1_%d" % i, [128, N]) for i in range(3)]
    t1bs = [P.buf("t1") for _ in range(3)]
    t2s = [C.sb("t2_%d" % i, [128, N]) for i in range(3)]
    t2bs = [P.buf("t2") for _ in range(3)]
    zt, ztb = C.sb("zt", [128, KC, 2]), P.buf("zt", dma=True)
    P.op('pool', lambda e: e.memset(zt[:, :, :], 0.0), writes=[ztb])
    RB = dr['RB']
    nc_ = P.nc
    def _zp(e, a, b):
        with nc_.allow_non_contiguous_dma(reason="tiny zero pad"):
            return e.dma_start(out=RB[:, :, a:b].rearrange("k p n -> p k n"), in_=zt[:, :, 0:b - a])
    P.dma('pool', lambda e: _zp(e, 0, 2), ztb, reads=[ztb])
    P.dma('pool', lambda e: _zp(e, S + 2, S + 3), ztb, reads=[ztb])
    si = 0
    NSUB = 2 if (S // N) % 2 == 0 else 1
    cnt = 0
    for tp in range(S // (N * NSUB)):
        for sub in range(NSUB):
            load_x(P, C, X, xt, xtb, tp * NSUB + sub, N)
            rmsnorm_tile(P, C, NR, xt, xtb, xns[sub], xnbs[sub], KC, N, SO['mix_g'])
        for m in range(2 * KC + 2 * MH):
            wt, wb = C.wload(w_in[m], KC)
            for sub in range(NSUB):
                t = tp * NSUB + sub
                xn, xnb = xns[sub], xnbs[sub]
                acts = [(lambda n0, n1, k=k, xn=xn: xn[:, k, n0:n1]) for k in range(KC)]
                b = cnt % 3
                cnt += 1
                mm_acc(P, ps[b], psb[b], N, wt, wb, acts, [xnb])
                if m < KC:
                    s_, sb_ = stg[si % 3], stgb[si % 3]
                    t1, t1b, t2, t2b = t1s[si % 3], t1bs[si % 3], t2s[si % 3], t2bs[si % 3]
                    si += 1
                    P.op('act', lambda e, b=b, t1=t1: e.activation(out=t1[:, 0:N], in_=ps[b][:, 0:N], func=AF.Square), reads=[psb[b]], writes=[t1b])
                    P.op('dve', lambda e, t1=t1: e.tensor_scalar(out=t1[:, 0:N], in0=t1[:, 0:N], scalar1=0.044715, scalar2=1.0, op0=ALU.mult, op1=ALU.add),
                         reads=[t1b], writes=[t1b])
                    P.op('dve', lambda e, b=b, t1=t1: e.tensor_tensor(out=t1[:, 0:N], in0=ps[b][:, 0:N], in1=t1[:, 0:N], op=ALU.mult), reads=[psb[b], t1b], writes=[t1b])
                    P.op('act', lambda e, t1=t1, t2=t2: e.activation(out=t2[:, 0:N], in_=t1[:, 0:N], func=AF.Sigmoid, scale=1.5957691216), reads=[t1b], writes=[t2b])
                    P.op('dve', lambda e, b=b, s_=s_, t2=t2: e.tensor_tensor(out=s_[:, 0:N], in0=ps[b][:, 0:N], in1=t2[:, 0:N], op=ALU.mult),
                         reads=[psb[b], t2b], writes=[sb_])
                    dst = dr['GG'][m][:, t * N:(t + 1) * N]
                    P.dma('pool', lambda e, s_=s_, dst=dst: e.dma_start(out=dst, in_=s_[:, 0:N]), sb_, reads=[sb_])
                elif m < 2 * KC:
                    s_, sb_ = stg[si % 3], stgb[si % 3]
                    si += 1
                    P.op('act', lambda e, b=b, s_=s_: e.activation(out=s_[:, 0:N], in_=ps[b][:, 0:N], func=AF.Copy), reads=[psb[b]], writes=[sb_])
                    dst = RB[m - KC][:, 2 + t * N:2 + (t + 1) * N]
                    P.dma('pool', lambda e, s_=s_, dst=dst: e.dma_start(out=dst, in_=s_[:, 0:N]), sb_, reads=[sb_])
                else:
                    P.op('act', lambda e, b=b, m=m, sub=sub: e.activation(out=MA.qms[sub][:, m - 2 * KC, 0:N], in_=ps[b][:, 0:N], func=AF.Copy),
                         reads=[psb[b]], writes=[MA.qmbs[sub]])
        for sub in range(NSUB):
            MA.run(NR, psA, psAb, psB, psBb, tp * NSUB + sub, sub)
    C.close()


def phase_rec_p2(P, cfg, dr, l, j, d):
    D, S = cfg['D'], cfg['S']
    KC, NB = D // 128, D // 256
    T = min(2048, S)
    nseg = S // T
    SO, _ = smalls_layout(cfg)
    C = Ctx(P, cfg, dr)
    C.common(nring=4, smalls_layer=l)
    sm = C.sm
    cs, csb = C.sb("cs", [128, 4 * KC]), P.buf("cs")
    lo = SO['lam']
    P.op('act', lambda e: e.activation(out=cs[:, 0:2 * KC], in_=sm[:, lo:lo + 2 * KC], func=AF.Exp, scale=-1.0), reads=[C.smb], writes=[csb])
    P.op('act', lambda e: e.activation(out=cs[:, 0:2 * KC], in_=cs[:, 0:2 * KC], func=AF.Ln, bias=C.cst[:, 1:2], scale=1.0), reads=[csb, C.cstb], writes=[csb])
    P.op('dve', lambda e: e.tensor_scalar(out=cs[:, 2 * KC:4 * KC], in0=cs[:, 0:2 * KC], scalar1=-16.0, scalar2=None, op0=ALU.mult), reads=[csb], writes=[csb])
    P.op('dve', lambda e: e.tensor_scalar(out=cs[:, 0:2 * KC], in0=cs[:, 0:2 * KC], scalar1=-8.0, scalar2=None, op0=ALU.mult), reads=[csb], writes=[csb])
    rb, rbb = C.sb("rb", [128, 2, T + 3]), P.buf("rb", dma=True)
    xc, xcb = C.sb("xc", [128, 2, T]), P.buf("xc")
    names = ['rr', 'ii', 'aa', 'bb', 'hh']
    tl = {n: C.sb(n, [128, T]) for n in names}
    tb = {n: P.buf(n, dma=(n == 'hh')) for n in names}
    hf, hfb = C.sb("hf", [128, T]), P.buf("hf", dma=True)
    gg, ggb = C.sb("gg", [128, T]), P.buf("gg", dma=True)
    st, stb = C.sb("st", [128, 2]), P.buf("st")
    yb, ybb = C.sb("yb", [128, T], BF16), P.buf("yb", dma=True)
    nb512 = max(1, T // 512)
    psr, psrb = C.ps("psr", [128, nb512 * 512]), P.buf("psr")
    psi, psib = C.ps("psi", [128, nb512 * 512]), P.buf("psi")
    gw = dr['gate_w'][j]
    rr, ii, aa, bb, hh = (tl[n] for n in names)
    for n in range(NB):
        for d in [d]:
            segs = list(range(nseg)) if d == 0 else list(range(nseg - 1, -1, -1))
            for si_, seg in enumerate(segs):
                src = dr['RB'][2 * n:2 * n + 2, :, seg * T:seg * T + T + 3].rearrange("k p n -> p k n")
                P.dma('pool', lambda e, src=src: e.dma_start(out=rb[:, :, :], in_=src), rbb, writes=[rbb])
                for c in range(2):
                    ch = 2 * n + c
                    w0, bcol = SO['conv_w'] + ch * 4, SO['conv_b'] + ch
                    P.op('dve', lambda e, c=c, w0=w0, bcol=bcol: e.tensor_scalar(out=xc[:, c, :], in0=rb[:, c, 0:T], scalar1=sm[:, w0:w0 + 1],
                                                                             scalar2=sm[:, bcol:bcol + 1], op0=ALU.mult, op1=ALU.add),
                         reads=[rbb, C.smb], writes=[xcb])
                    for jj in range(1, 4):
                        P.op('dve', lambda e, c=c, w0=w0, jj=jj: e.scalar_tensor_tensor(out=xc[:, c, :], in0=rb[:, c, jj:jj + T], scalar=sm[:, w0 + jj:w0 + jj + 1],
                                                                                    in1=xc[:, c, :], op0=ALU.mult, op1=ALU.add),
                             reads=[rbb, C.smb, xcb], writes=[xcb])
                for oc in range(2):
                    ch = 2 * n + oc
                    for gt, (pst, pstb, dst, dstb) in enumerate([(psr, psrb, rr, tb['rr']), (psi, psib, ii, tb['ii'])]):
                        wt, wb = C.wload(gw[d][gt][n][oc], 2)
                        mm_acc(P, pst, pstb, T, wt, wb, [(lambda n0, n1, k=k: xc[:, k, n0:n1]) for k in range(2)], [xcb])
                        bc = SO['gb'] + (d * 2 + gt) * KC + ch
                        P.op('act', lambda e, pst=pst, dst=dst, bc=bc: e.activation(out=dst[:, 0:T], in_=pst[:, 0:T], func=AF.Sigmoid, bias=sm[:, bc:bc + 1], scale=1.0),
                             reads=[pstb, C.smb], writes=[dstb])
                    c1, c2 = d * KC + ch, 2 * KC + d * KC + ch
                    P.op('act', lambda e, c1=c1: e.activation(out=aa[:, 0:T], in_=rr[:, 0:T], func=AF.Exp, scale=cs[:, c1:c1 + 1]), reads=[tb['rr'], csb], writes=[tb['aa']])
                    P.op('act', lambda e, c2=c2: e.activation(out=bb[:, 0:T], in_=rr[:, 0:T], func=AF.Exp, scale=cs[:, c2:c2 + 1]), reads=[tb['rr'], csb], writes=[tb['bb']])
                    P.op('act', lambda e: e.activation(out=bb[:, 0:T], in_=bb[:, 0:T], func=AF.Sqrt, bias=C.cst[:, 1:2], scale=-1.0), reads=[tb['bb'], C.cstb], writes=[tb['bb']])
                    P.op('dve', lambda e: e.tensor_tensor(out=bb[:, 0:T], in0=bb[:, 0:T], in1=ii[:, 0:T], op=ALU.mult), reads=[tb['bb'], tb['ii']], writes=[tb['bb']])
                    P.op('dve', lambda e, oc=oc: e.tensor_tensor(out=bb[:, 0:T], in0=bb[:, 0:T], in1=xc[:, oc, :], op=ALU.mult), reads=[tb['bb'], xcb], writes=[tb['bb']])
                    init = 0.0 if si_ == 0 else st[:, oc:oc + 1]
                    if d == 0:
                        P.op('dve', lambda e, init=init: e.tensor_tensor_scan(out=hh[:, 0:T], data0=aa[:, 0:T], data1=bb[:, 0:T], initial=init, op0=ALU.mult, op1=ALU.add),
                             reads=[tb['aa'], tb['bb'], stb], writes=[tb['hh']])
                        P.op('dve', lambda e, oc=oc: e.tensor_copy(out=st[:, oc:oc + 1], in_=hh[:, T - 1:T]), reads=[tb['hh']], writes=[stb])
                        dst = dr['HF'][ch][:, seg * T:(seg + 1) * T]
                        P.dma('pool', lambda e, dst=dst: e.dma_start(out=dst, in_=hh[:, 0:T]), tb['hh'], reads=[tb['hh']])
                    else:
                        s1 = dr['HF'][ch][:, seg * T:(seg + 1) * T]
                        s2 = dr['GG'][ch][:, seg * T:(seg + 1) * T]
                        P.dma('pool', lambda e, s1=s1: e.dma_start(out=hf[:, 0:T], in_=s1), hfb, writes=[hfb])
                        P.dma('pool', lambda e, s2=s2: e.dma_start(out=gg[:, 0:T], in_=s2), ggb, writes=[ggb])
                        P.op('dve', lambda e, init=init: e.tensor_tensor_scan(out=hh[:, ::-1], data0=aa[:, ::-1], data1=bb[:, ::-1], initial=init, op0=ALU.mult, op1=ALU.add),
                             reads=[tb['aa'], tb['bb'], stb], writes=[tb['hh']])
                        P.op('dve', lambda e, oc=oc: e.tensor_copy(out=st[:, oc:oc + 1], in_=hh[:, 0:1]), reads=[tb['hh']], writes=[stb])
                        P.op('dve', lambda e: e.tensor_tensor(out=hh[:, 0:T], in0=hh[:, 0:T], in1=hf[:, 0:T], op=ALU.add), reads=[tb['hh'], hfb], writes=[tb['hh']])
                        P.op('dve', lambda e: e.tensor_tensor(out=yb[:, 0:T], in0=hh[:, 0:T], in1=gg[:, 0:T], op=ALU.mult), reads=[tb['hh'], ggb], writes=[ybb])
                        dst = dr['YR'][ch][:, seg * T:(seg + 1) * T]
                        P.dma('pool', lambda e, dst=dst: e.dma_start(out=dst, in_=yb[:, 0:T]), ybb, reads=[ybb])
    C.close()


def phase_wout(P, cfg, dr, X, Xd, w_out, YMAIN, nmain):
    D, S, MH = cfg['D'], cfg['S'], cfg['MH']
    KC, N = D // 128, min(512, cfg['S'])
    YC = nmain + 2 * MH
    C = Ctx(P, cfg, dr)
    C.common(nring=4, wdt=BF16)
    xt, xtb = C.sb("xt", [128, KC, N]), P.buf("xt", dma=True)
    yt, ytb = C.sb("yt", [128, YC, N], BF16), P.buf("yt", dma=True)
    pso = [C.ps("pso%d" % j, [128, 512]) for j in range(2)]
    psob = [P.buf("o") for _ in range(2)]
    kcn = max(k for k in range(1, 17) if YC % k == 0)
    for t in range(S // N):
        load_x(P, C, X, xt, xtb, t, N)
        s1 = YMAIN[:, :, t * N:(t + 1) * N].rearrange("k p n -> p k n")
        s2 = dr['YM'][:, :, t * N:(t + 1) * N].rearrange("k p n -> p k n")
        P.dma('pool', lambda e, s1=s1: e.dma_start(out=yt[:, 0:nmain, :], in_=s1), ytb, writes=[ytb])
        P.dma('pool', lambda e, s2=s2: e.dma_start(out=yt[:, nmain:YC, :], in_=s2), ytb, writes=[ytb])
        for m in range(KC):
            b = m % 2
            npc = YC // kcn
            for pc in range(npc):
                wt, wb = C.wload(w_out[m][:, pc * kcn:(pc + 1) * kcn, :], kcn)
                acts2 = [(lambda n0, n1, k=k: yt[:, k, n0:n1]) for k in range(pc * kcn, (pc + 1) * kcn)]
                mm_acc(P, pso[b], psob[b], N, wt, wb, acts2, [ytb], first=(pc == 0), last=(pc == npc - 1))
            P.op('dve', lambda e, b=b, m=m: e.tensor_tensor(out=xt[:, m, 0:N], in0=pso[b][:, 0:N], in1=xt[:, m, 0:N], op=ALU.add),
                 reads=[psob[b], xtb], writes=[xtb])
        store_x(P, C, Xd, xt, xtb, t, N)
    C.close()


def phase_att_p1(P, cfg, dr, X, l, j):
    D, S, MH, AH = cfg['D'], cfg['S'], cfg['MH'], cfg['AH']
    KC, N = D // 128, min(512, cfg['S'])
    SO, _ = smalls_layout(cfg)
    C = Ctx(P, cfg, dr)
    C.common(nring=4, smalls_layer=l, wdt=BF16)
    identf, identfb = C.sb("identf", [128, 128]), P.buf("identf", dma=True)
    P.dma('pool', lambda e: e.dma_start(out=identf[:, :], in_=dr['ident']), identfb, writes=[identfb])
    ident, identb = C.sb("ident", [128, 128], BF16), P.buf("ident")
    P.op('dve', lambda e: e.tensor_copy(out=ident[:, :], in_=identf[:, :]), reads=[identfb], writes=[identb])
    Ctx._uid[0] += 1
    psT = C.es.enter_context(C.nc.psum_tensor("psT_att_u%d" % Ctx._uid[0], [128, 128], BF16))
    psTb = P.buf("psT")
    w_in = dr['att_w_in_bf'][j]
    xt, xtb = C.sb("xt", [128, KC, N]), P.buf("xt", dma=True)
    xns = [C.sb("xn%d" % i, [128, KC, N], BF16) for i in range(2)]
    xnbs = [P.buf("xn") for _ in range(2)]
    NR = NormRes(C, N)
    NR2 = NormRes(C, N, "b")
    MA = MemAtt(C, N)
    ps = [C.ps("ps%d" % i, [128, 512]) for i in range(3)]
    psb = [P.buf("ps") for _ in range(3)]
    psA, psAb = C.ps("psA", [128, 512]), P.buf("psA")
    psB, psBb = NR.psn, NR.psnb
    psn2 = [NR2.psn, C.ps("psn2b", [128, 512])]
    psn2b = [NR2.psnb, P.buf("psn2b")]
    rs2 = [NR2.rs, C.sb("rs2b", [128, N])]
    rs2b = [NR2.rsb, P.buf("rs2b")]
    stg = [C.sb("stg%d" % i, [128, N], BF16) for i in range(3)]
    stgb = [P.buf("stg", dma=True) for _ in range(3)]
    vs2 = [C.sb("vs%d" % i, [128, N], BF16) for i in range(2)]
    vs2b = [P.buf("vs") for _ in range(2)]
    vt, vtb = C.sb("vt", [128, N // 128, 128], BF16), P.buf("vt", dma=True)
    si = 0
    nqkv = 9 * AH
    NSUB = 2 if (S // N) % 2 == 0 else 1
    cnt = 0
    for tp in range(S // (N * NSUB)):
        for sub in range(NSUB):
            load_x(P, C, X, xt, xtb, tp * NSUB + sub, N)
            rmsnorm_tile(P, C, NR, xt, xtb, xns[sub], xnbs[sub], KC, N, SO['mix_g'])
        pend = None
        for m in range(nqkv + 2 * MH):
            wt, wb = C.wload(w_in[m], KC)
            for sub in range(NSUB):
                t = tp * NSUB + sub
                xn, xnb = xns[sub], xnbs[sub]
                acts = [(lambda n0, n1, k=k, xn=xn: xn[:, k, n0:n1]) for k in range(KC)]
                b = cnt % 3
                c2 = cnt % 2
                cnt += 1
                mm_acc(P, ps[b], psb[b], N, wt, wb, acts, [xnb])
                post = None
                if m >= nqkv:
                    P.op('act', lambda e, b=b, m=m, sub=sub: e.activation(out=MA.qms[sub][:, m - nqkv, 0:N], in_=ps[b][:, 0:N], func=AF.Copy),
                         reads=[psb[b]], writes=[MA.qmbs[sub]])
                else:
                    g, kind, hd_ = m // (3 * AH), (m % (3 * AH)) // AH, m % AH
                    if kind < 2:
                        sq, sqb_ = NR2.sq[c2], NR2.sqb[c2]
                        P.op('act', lambda e, b=b, sq=sq: e.activation(out=sq[:, 0:N], in_=ps[b][:, 0:N], func=AF.Square), reads=[psb[b]], writes=[sqb_])

                        def post(b=b, sq=sq, sqb_=sqb_, g=g, kind=kind, hd_=hd_, t=t, c2=c2):
                            nonlocal si
                            s_, sb_ = stg[si % 3], stgb[si % 3]
                            si += 1
                            pn, pnb, rs_, rsb_ = psn2[c2], psn2b[c2], rs2[c2], rs2b[c2]
                            P.op('pe', lambda e: e.matmul(pn[:, 0:N], lhsT=C.ones[:, :], rhs=sq[:, 0:N], start=True, stop=True),
                                 reads=[sqb_, C.onesb], writes=[pnb])
                            P.op('act', lambda e: e.activation(out=rs_[:, 0:N], in_=pn[:, 0:N], func=AF.Sqrt, bias=C.cst[:, 0:1], scale=1.0 / 128),
                                 reads=[pnb, C.cstb], writes=[rsb_])
                            P.op('dve', lambda e: e.reciprocal(out=rs_[:, 0:N], in_=rs_[:, 0:N]), reads=[rsb_], writes=[rsb_])
                            gc = SO['qg' if kind == 0 else 'kg'] + g
                            P.op('dve', lambda e: e.scalar_tensor_tensor(out=s_[:, 0:N], in0=ps[b][:, 0:N], scalar=C.sm[:, gc:gc + 1], in1=rs_[:, 0:N],
                                                                       op0=ALU.mult, op1=ALU.mult), reads=[psb[b], rsb_, C.smb], writes=[sb_])
                            dst = dr['QK'][(g * 2 + kind) * AH + hd_][:, t * N:(t + 1) * N]
                            P.dma('pool', lambda e: e.dma_start(out=dst, in_=s_[:, 0:N]), sb_, reads=[sb_])
                    else:
                        vs, vsb = vs2[c2], vs2b[c2]
                        P.op('act', lambda e, b=b, vs=vs: e.activation(out=vs[:, 0:N], in_=ps[b][:, 0:N], func=AF.Copy), reads=[psb[b]], writes=[vsb])

                        def post(vs=vs, vsb=vsb, g=g, hd_=hd_, t=t):
                            for bi in range(N // 128):
                                P.op('pe', lambda e, bi=bi: e.transpose(out=psT[:, 0:128], in_=vs[:, bi * 128:(bi + 1) * 128], identity=ident[:, :]),
                                     reads=[vsb, identb], writes=[psTb])
                                P.op('act', lambda e, bi=bi: e.activation(out=vt[:, bi, :], in_=psT[:, 0:128], func=AF.Copy), reads=[psTb], writes=[vtb])
                            dst = dr['V'][g * AH + hd_][t * N:(t + 1) * N, :].rearrange("(c p) d -> p c d", p=128)
                            P.dma('pool', lambda e: e.dma_start(out=dst, in_=vt[:, :, :]), vtb, reads=[vtb])
                if pend is not None:
                    pend()
                pend = post
        if pend is not None:
            pend()
        for sub in range(NSUB):
            MA.run(NR, psA, psAb, psB, psBb, tp * NSUB + sub, sub)
    C.close()


def alibi_slope(cfg, g, h):
    n = 3 * cfg['AH']
    return float(np.exp2(np.float32(-8.0) * (np.float32(g * cfg['AH'] + h) + np.float32(1.0)) / np.float32(n)))


def phase_att_p2(P, cfg, dr):
    S, AH = cfg['S'], cfg['AH']
    C = Ctx(P, cfg, dr)
    C.common(nring=0)
    rel, relb = C.sb("rel", [128, 6, 512]), P.buf("rel", dma=True)
    P.dma('pool', lambda e: e.dma_start(out=rel[:, :, :], in_=dr['c_rel'].rearrange("c p n -> p c n")), relb, writes=[relb])
    bt, btb = C.sb("bt", [128, 6, 512]), P.buf("bt")
    qT, qTb = C.sb("qT", [128, S], BF16), P.buf("qT", dma=True)
    kT, kTb = C.sb("kT", [128, S], BF16), P.buf("kT", dma=True)
    vc, vcb = C.sb("vc", [128, S // 128, 128], BF16), P.buf("vc", dma=True)
    ones16, ones16b = C.sb("ones16", [128, 128], BF16), P.buf("ones16")
    P.op('pool', lambda e: e.memset(ones16[:, :], 1.0), writes=[ones16b])
    aO, aOb = C.sb("aO", [128, S]), P.buf("aO", dma=True)
    aD, aDb = C.sb("aD", [128, S]), P.buf("aD")
    tmp = [C.sb("tmp%d" % i, [128, 512]) for i in range(4)]
    tmpb = [P.buf("tmp") for _ in range(4)]
    ya = [C.sb("ya%d" % i, [128, min(2048, S)], BF16) for i in range(2)]
    yab = [P.buf("ya", dma=True) for _ in range(2)]
    pt = [C.sb("pt%d" % i, [128, 512], BF16) for i in range(4)]
    ptb = [P.buf("pt") for _ in range(4)]
    psS = [C.ps("psS%d" % i, [128, 512]) for i in range(4)]
    psSb = [P.buf("psS") for _ in range(4)]
    psO = [C.ps("psO%d" % i, [128, 512]) for i in range(2)]
    psOb = [P.buf("psO") for _ in range(2)]
    psD = [C.ps("psD%d" % i, [128, 512]) for i in range(2)]
    psDb = [P.buf("psD") for _ in range(2)]
    it = 0
    qi = 0
    for h in range(AH):
        for g, (win, d) in enumerate(cfg['PAT']):
            L = S // d
            QT = min(512, L)
            nck = L // 128
            P.dma('pool', lambda e, g=g, h=h: e.dma_start(out=qT[:, :], in_=dr['QK'][(g * 2 + 0) * AH + h]), qTb, writes=[qTb])
            P.dma('pool', lambda e, g=g, h=h: e.dma_start(out=kT[:, :], in_=dr['QK'][(g * 2 + 1) * AH + h]), kTb, writes=[kTb])
            Vd = dr['V'][g * AH + h].rearrange("(c i r) e -> r i c e", i=128, r=d)
            for r in range(d):
                P.dma('pool', lambda e, r=r, Vd=Vd, nck=nck: e.dma_start(out=vc[:, r * nck:(r + 1) * nck, :], in_=Vd[r]), vcb, writes=[vcb])
            sl = -alibi_slope(cfg, g, h) * d
            for ci in range(6):
                P.op('dve', lambda e, ci=ci, sl=sl: e.tensor_scalar(out=bt[:, ci, :], in0=rel[:, ci, :], scalar1=sl, scalar2=None, op0=ALU.mult),
                     reads=[relb], writes=[btb])
            pairs = []
            for r in range(d):
                for j0 in range(0, L, QT):
                    qb = qi % 2
                    qi += 1
                    cis = [ci for ci in range(QT // 128 + 2) if 0 <= j0 - 128 + 128 * ci and j0 - 128 + 128 * ci + 128 <= L]
                    for idx, ci in enumerate(cis):
                        pairs.append((r, j0, qb, ci, j0 - 128 + 128 * ci, idx, len(cis)))

            def stageA(p, b, d=d, QT=QT):
                r, j0, qb, ci, kj, idx, nci = p
                qap = qT[:, r + d * j0: r + d * (j0 + QT - 1) + 1: d]
                kap = kT[:, r + d * kj: r + d * (kj + 127) + 1: d]
                P.op('pe', lambda e: e.matmul(psS[b][:, 0:QT], lhsT=kap, rhs=qap, start=True, stop=True), reads=[kTb, qTb], writes=[psSb[b]])
                P.op('dve', lambda e: e.scalar_tensor_tensor(out=tmp[b][:, 0:QT], in0=psS[b][:, 0:QT], scalar=128.0 ** -0.5, in1=bt[:, ci, 0:QT],
                                                           op0=ALU.mult, op1=ALU.add), reads=[psSb[b], btb], writes=[tmpb[b]])
                P.op('act', lambda e: e.activation(out=pt[b][:, 0:QT], in_=tmp[b][:, 0:QT], func=AF.Exp), reads=[tmpb[b]], writes=[ptb[b]])

            def stageB(p, b, d=d, QT=QT, nck=nck, g=g):
                r, j0, qb, ci, kj, idx, nci = p
                vck = r * nck + kj // 128
                P.op('pe', lambda e: e.matmul(psO[qb][:, 0:QT], lhsT=vc[:, vck, :], rhs=pt[b][:, 0:QT], start=(idx == 0), stop=(idx == nci - 1)),
                     reads=[vcb, ptb[b]], writes=[psOb[qb]])
                P.op('pe', lambda e: e.matmul(psD[qb][:, 0:QT], lhsT=ones16[:, :], rhs=pt[b][:, 0:QT], start=(idx == 0), stop=(idx == nci - 1)),
                     reads=[ones16b, ptb[b]], writes=[psDb[qb]])
                if idx == nci - 1:
                    oap = aO[:, r + d * j0: r + d * (j0 + QT - 1) + 1: d]
                    dap = aD[:, r + d * j0: r + d * (j0 + QT - 1) + 1: d]
                    if g == 0:
                        P.op('act', lambda e: e.activation(out=oap, in_=psO[qb][:, 0:QT], func=AF.Copy), reads=[psOb[qb]], writes=[aOb])
                        P.op('dve', lambda e: e.tensor_copy(out=dap, in_=psD[qb][:, 0:QT]), reads=[psDb[qb]], writes=[aDb])
                    else:
                        P.op('dve', lambda e: e.tensor_tensor(out=oap, in0=psO[qb][:, 0:QT], in1=oap, op=ALU.add), reads=[psOb[qb], aOb], writes=[aOb])
                        P.op('dve', lambda e: e.tensor_tensor(out=dap, in0=psD[qb][:, 0:QT], in1=dap, op=ALU.add), reads=[psDb[qb], aDb], writes=[aDb])
            LA = 3
            base = it
            for i in range(len(pairs) + LA):
                if i < len(pairs):
                    stageA(pairs[i], (base + i) % 4)
                if i >= LA:
                    stageB(pairs[i - LA], (base + i - LA) % 4)
            it += len(pairs)
        P.op('dve', lambda e: e.reciprocal(out=aD[:, :], in_=aD[:, :]), reads=[aDb], writes=[aDb])
        CW = min(2048, S)
        for cc in range(S // CW):
            yb_, ybb_ = ya[cc % 2], yab[cc % 2]
            P.op('dve', lambda e, cc=cc, yb_=yb_: e.tensor_tensor(out=yb_[:, 0:CW], in0=aO[:, cc * CW:(cc + 1) * CW], in1=aD[:, cc * CW:(cc + 1) * CW], op=ALU.mult),
                 reads=[aOb, aDb], writes=[ybb_])
            P.dma('pool', lambda e, h=h, cc=cc, yb_=yb_: e.dma_start(out=dr['YA'][h][:, cc * CW:(cc + 1) * CW], in_=yb_[:, 0:CW]), ybb_, reads=[ybb_])
    C.close()


def build(cfg):
    D, S, FF, L, MH, AH, M = cfg['D'], cfg['S'], cfg['FF'], cfg['DEPTH'], cfg['MH'], cfg['AH'], cfg['M']
    KC, FC = D // 128, FF // 128
    NREC, NATT = (L + 1) // 2, L // 2
    MW = MH * 256
    _, NS = smalls_layout(cfg)
    nc = bass.Bass("TRN2", target_bir_lowering=False)
    dr = {}

    def din(name, shape):
        dr[name] = nc.dram_tensor(name, list(shape), F32, kind="ExternalInput").ap()

    def dsc(name, shape, dt=F32):
        dr[name] = nc.dram_tensor(name, list(shape), dt, kind="ExternalOutput" if cfg.get('dbg') else "Internal").ap()
    din('xT', [KC, 128, S])
    din('memT', [KC, 128, M])
    din('smalls', [L, 128, NS])
    din('ident', [128, 128])
    din('c_rel', [6, 128, 512])
    din('c_negm', [6, 128, 512])
    din('ffn_w_in', [L * 2, 2 * FC, 128, KC, 128])
    din('ffn_w_out', [L * 2, KC, 128, FC, 128])
    din('mem_w_kv', [L, 2 * MW // 128, 128, KC, 128])
    din('rec_w_in', [NREC, 2 * KC + 2 * MH, 128, KC, 128])
    din('rec_w_out', [NREC, KC, 128, KC + 2 * MH, 128])
    din('gate_w', [NREC, 2, 2, D // 256, 2, 128, 2, 128])
    if NATT:
        din('att_w_in', [NATT, 9 * AH + 2 * MH, 128, KC, 128])
        din('att_w_out', [NATT, KC, 128, AH + 2 * MH, 128])
    dr['out'] = nc.dram_tensor('out', [KC, 128, S], F32, kind="ExternalOutput").ap()
    dsc('KN', [128, 2 * MH, M])
    dsc('VM', [128, M // 128, MW])
    dsc('YM', [2 * MH, 128, S], BF16)
    dsc('GG', [KC, 128, S])
    dsc('RB', [KC, 128, S + 3])
    dsc('HF', [KC, 128, S])
    dsc('YR', [KC, 128, S], BF16)
    dsc('QK', [6 * AH, 128, S], BF16)
    dsc('V', [3 * AH, S, 128], BF16)
    dsc('YA', [AH, 128, S], BF16)
    for name in BIG:
        if name in dr:
            dr[name + '_bf'] = [nc.dram_tensor("%s_bf%d" % (name, i), list(dr[name].shape[1:]), BF16, kind="Internal").ap()
                                for i in range(dr[name].shape[0])]
    P = Prog(nc)
    PC = Precast(P, cfg, dr)
    phase_precast(P, cfg, dr, PC)
    X = dr['out']
    stop = cfg.get('stop')
    for l in range(L):
        phase_ffn(P, cfg, dr, dr['xT'] if l == 0 else X, X, l, 0, PC)
        if stop == 'ffn0':
            break
        phase_memkv(P, cfg, dr, l)
        j = l // 2
        if l % 2 == 0:
            phase_rec_p1(P, cfg, dr, X, l, j)
            phase_rec_p2(P, cfg, dr, l, j, 0)
            phase_rec_p2(P, cfg, dr, l, j, 1)
            phase_wout(P, cfg, dr, X, X, dr['rec_w_out_bf'][j], dr['YR'], KC)
            if stop == 'mix0':
                break
        else:
            phase_att_p1(P, cfg, dr, X, l, j)
            phase_att_p2(P, cfg, dr)
            phase_wout(P, cfg, dr, X, X, dr['att_w_out_bf'][j], dr['YA'], AH)
        phase_ffn(P, cfg, dr, X, X, l, 1, PC)
    return nc


def pieces(W):
    K, Mo = W.shape[-2], W.shape[-1]
    lead = W.shape[:-2]
    Wr = W.reshape(*lead, K // 128, 128, Mo // 128, 128)
    nl = len(lead)
    perm = list(range(nl)) + [nl + 2, nl + 1, nl + 0, nl + 3]
    return np.ascontiguousarray(Wr.transpose(perm))


def cols(v):
    n = v.shape[-1] // 128
    return np.swapaxes(v.reshape(*v.shape[:-1], n, 128), -1, -2)


def const_tables():
    k = np.arange(128)[:, None]
    q = np.arange(512)[None, :]
    rel = np.stack([np.abs(128 * (ci - 1) + k - q) for ci in range(6)]).astype(np.float32)
    rel = np.where(rel <= 64, rel, 1.0e6).astype(np.float32)
    return rel, rel


def prep_inputs(cfg, inp):
    D, S, L = cfg['D'], cfg['S'], cfg['DEPTH']
    KC = D // 128
    SO, NS = smalls_layout(cfg)
    f = lambda a: np.asarray(a, dtype=np.float32)
    sm = np.zeros((L, 128, NS), np.float32)
    for l in range(L):
        j = l // 2
        sm[l, :, SO['ffn_g0']:SO['ffn_g0'] + KC] = cols(f(inp['ffn_norm'])[l, 0])
        sm[l, :, SO['ffn_g1']:SO['ffn_g1'] + KC] = cols(f(inp['ffn_norm'])[l, 1])
        sm[l, :, SO['mix_g']:SO['mix_g'] + KC] = cols(f(inp['mix_norm'])[l])
        sm[l, :, SO['mem_g']:SO['mem_g'] + KC] = cols(f(inp['mem_norm'])[l])
        sm[l, :, SO['memq_g']:SO['memq_g'] + 2] = cols(f(inp['mem_qk_gain'])[l, 0])
        sm[l, :, SO['memk_g']:SO['memk_g'] + 2] = cols(f(inp['mem_qk_gain'])[l, 1])
        if l % 2 == 0:
            cw = f(inp['rec_conv_w'])[j]
            sm[l, :, SO['conv_w']:SO['conv_w'] + 4 * KC] = np.stack([cols(cw[t]) for t in range(4)], -1).reshape(128, 4 * KC)
            sm[l, :, SO['conv_b']:SO['conv_b'] + KC] = cols(f(inp['rec_conv_b'])[j])
            gb = f(inp['rec_gate_b'])[j].reshape(2, 2, D)
            for d in range(2):
                for gt in range(2):
                    o = SO['gb'] + (d * 2 + gt) * KC
                    sm[l, :, o:o + KC] = cols(gb[d, gt])
                o = SO['lam'] + d * KC
                sm[l, :, o:o + KC] = cols(f(inp['rec_lambda'])[j, d])
        else:
            qk = f(inp['att_qk_gain'])[j]
            sm[l, :, SO['qg']:SO['qg'] + 3] = qk[0].T
            sm[l, :, SO['kg']:SO['kg'] + 3] = qk[1].T
    rel, negm = const_tables()
    shared = {
        'smalls': sm, 'ident': np.eye(128, dtype=np.float32), 'c_rel': rel, 'c_negm': negm,
        'ffn_w_in': pieces(f(inp['ffn_w_in'])).reshape(L * 2, -1, 128, KC, 128),
        'ffn_w_out': pieces(f(inp['ffn_w_out'])).reshape(L * 2, KC, 128, -1, 128),
        'mem_w_kv': pieces(f(inp['mem_w_kv'])),
        'rec_w_in': pieces(f(inp['rec_w_in'])),
        'rec_w_out': pieces(f(inp['rec_w_out'])),
        'gate_w': pieces(f(inp['rec_gate_w'])),
    }
    if L // 2:
        shared['att_w_in'] = pieces(f(inp['att_w_in']))
        shared['att_w_out'] = pieces(f(inp['att_w_out']))
    x, mem = f(inp['x']), f(inp['mem'])
    B = x.shape[0]
    maps = []
    for b in range(B):
        m = dict(shared)
        m['xT'] = np.ascontiguousarray(x[b].T).reshape(KC, 128, S)
        m['memT'] = np.ascontiguousarray(mem[b].T).reshape(KC, 128, -1)
        maps.append(m)
    return maps


def run(cfg, inp):
    nc = build(cfg)
    maps = prep_inputs(cfg, inp)
    B = len(maps)
    res = run_bass_kernel_spmd(nc, maps, core_ids=list(range(B)))
    D, S = cfg['D'], cfg['S']
    if cfg.get('dbg'):
        cfg['dbg_out'] = res.results
    return np.stack([np.ascontiguousarray(res.results[b]['out'].reshape(D, S).T) for b in range(B)]).astype(np.float32)


def kernel(**inputs):
    return run(FULL, inputs)
```

```python
import numpy as np
from contextlib import ExitStack
import concourse.bass as bass
import concourse.mybir as mybir
from concourse.bass_utils import run_bass_kernel_spmd

F32 = mybir.dt.float32
BF16 = mybir.dt.bfloat16
BIG = ['ffn_w_in', 'ffn_w_out', 'rec_w_in', 'rec_w_out', 'att_w_in', 'att_w_out']
AF = mybir.ActivationFunctionType
ALU = mybir.AluOpType
ENG = ['pe', 'act', 'dve', 'pool', 'sp']

FULL = dict(D=2048, S=8192, FF=5632, DEPTH=4, MH=4, AH=8, M=256, PAT=((128, 1), (512, 4), (2048, 16)))


class Buf:
    def __init__(s, name, di=None):
        s.name, s.lw, s.rd, s.di = name, None, {}, di


class Prog:
    def __init__(s, nc, ndsem=40):
        s.nc = nc
        s.es = ExitStack()
        s.eng = {'pe': nc.tensor, 'act': nc.scalar, 'dve': nc.vector, 'pool': nc.gpsimd, 'sp': nc.sync}
        s.esem = {e: s.es.enter_context(nc.semaphore("es_" + e)) for e in ENG}
        s.ecnt = {e: 0 for e in ENG}
        s.dsem = [s.es.enter_context(nc.semaphore("ds%d" % i)) for i in range(ndsem)]
        s.dcnt = [0] * ndsem
        s.q = {e: [] for e in ENG}
        s.seen = {e: {} for e in ENG}
        s.nextd = 0

    def buf(s, name, dma=False):
        di = None
        if dma:
            di = s.nextd % len(s.dsem)
            s.nextd += 1
        return Buf(name, di)

    def _sem(s, key):
        return s.esem[key[1]] if key[0] == 'e' else s.dsem[key[1]]

    def _waits(s, eng, evs):
        out = []
        for ev in evs:
            if ev is None:
                continue
            key, val = ev
            if key == ('e', eng) and eng == 'pe':
                continue
            if s.seen[eng].get(key, 0) >= val:
                continue
            s.seen[eng][key] = val
            out.append(ev)
        return out

    def _deps(s, reads, writes):
        evs = [b.lw for b in reads]
        for b in writes:
            evs.append(b.lw)
            evs += list(b.rd.items())
        return evs

    def _upd(s, ev, reads, writes):
        for b in reads:
            b.rd[ev[0]] = max(b.rd.get(ev[0], 0), ev[1])
        for b in writes:
            b.lw = ev
            b.rd = {}

    def op(s, eng, fn, reads=(), writes=()):
        w = s._waits(eng, s._deps(reads, writes))
        s.ecnt[eng] += 1
        ev = (('e', eng), s.ecnt[eng])
        s.q[eng].append((fn, w, ('e', eng), 1))
        s._upd(ev, reads, writes)
        return ev

    def dma(s, eng, fn, sb, reads=(), writes=()):
        di = sb.di
        evs = s._deps(reads, writes)
        if s.dcnt[di] > 0:
            evs.append((('d', di), s.dcnt[di]))
        w = s._waits(eng, evs)
        s.dcnt[di] += 16
        ev = (('d', di), s.dcnt[di])
        s.q[eng].append((fn, w, ('d', di), 16))
        s._upd(ev, reads, writes)
        return ev

    def barrier(s):
        for e in ENG:
            evs = [(('e', e2), s.ecnt[e2]) for e2 in ENG if e2 != e and s.ecnt[e2] > 0]
            evs += [(('d', i), c) for i, c in enumerate(s.dcnt) if c > 0]
            s.q[e].append((None, s._waits(e, evs), None, 0))

    def flush(s):
        s.barrier()
        with s.nc.Block() as block:
            def mk(e):
                def run(eo):
                    for (fn, w, key, inc) in s.q[e]:
                        for (k, v) in w:
                            eo.wait_ge(s._sem(k), v)
                        if fn is not None:
                            fn(eo).then_inc(s._sem(key), inc)
                return run
            block.tensor(mk('pe'))
            block.scalar(mk('act'))
            block.vector(mk('dve'))
            block.gpsimd(mk('pool'))
            block.sync(mk('sp'))
        s.q = {e: [] for e in ENG}


class Ctx:
    def __init__(s, P, cfg, dr):
        s.P, s.cfg, s.dr, s.nc = P, cfg, dr, P.nc
        s.es = ExitStack()
        s.wi = 0

    _uid = [0]

    def sb(s, name, shape, dt=F32):
        Ctx._uid[0] += 1
        return s.es.enter_context(s.nc.sbuf_tensor("%s_u%d" % (name, Ctx._uid[0]), list(shape), dt))

    def ps(s, name, shape):
        Ctx._uid[0] += 1
        return s.es.enter_context(s.nc.psum_tensor("%s_u%d" % (name, Ctx._uid[0]), list(shape), F32))

    def common(s, nring=4, smalls_layer=None, wdt=F32):
        P = s.P
        s.ring = [s.sb("wr%d" % i, [128, 2048], wdt) for i in range(nring)]
        s.ringb = [P.buf("wr%d" % i, dma=True) for i in range(nring)]
        s.ones = s.sb("ones", [128, 128])
        s.onesb = P.buf("ones")
        s.cst = s.sb("cst", [128, 4])
        s.cstb = P.buf("cst")
        o, c = s.ones, s.cst
        P.op('pool', lambda e: e.memset(o[:, :], 1.0), writes=[s.onesb])
        P.op('pool', lambda e: e.memset(c[:, 0:1], 1e-6), writes=[s.cstb])
        P.op('pool', lambda e: e.memset(c[:, 1:2], 1.0), writes=[s.cstb])
        if smalls_layer is not None:
            ns = s.dr['smalls'].shape[2]
            s.sm = s.sb("smalls", [128, ns])
            s.smb = P.buf("smalls", dma=True)
            sm, src = s.sm, s.dr['smalls'][smalls_layer]
            P.dma('pool', lambda e: e.dma_start(out=sm[:, :], in_=src), s.smb, writes=[s.smb])

    def wload(s, src, kcn):
        i = s.wi % len(s.ring)
        s.wi += 1
        t, b = s.ring[i], s.ringb[i]
        s.P.dma('sp', lambda e: e.dma_start(out=t[:, 0:kcn * 128], in_=src.rearrange("p k c -> p (k c)")), b, writes=[b])
        return t, b

    def close(s):
        s.P.flush()
        s.es.close()


def mm_acc(P, ps, psb, n, wt, wb, acts, actb, first=True, last=True):
    K = len(acts)
    for n0 in range(0, n, 512):
        n1 = min(n, n0 + 512)
        for k in range(K):
            a = acts[k]
            pb = psb[n0 // 512] if isinstance(psb, list) else psb
            P.op('pe', lambda e, k=k, a=a, n0=n0, n1=n1: e.matmul(ps[:, n0:n1], lhsT=wt[:, k * 128:(k + 1) * 128], rhs=a(n0, n1),
                                                            start=(first and k == 0), stop=(last and k == K - 1)),
                 reads=[wb] + actb, writes=[pb])


def rms_rstd(P, C, srcs, srcb, n, dim, sqs, sqb, psn, psnb, rs, rsb):
    K = len(srcs)
    for k in range(K):
        sq, sb_ = sqs[k % 2], sqb[k % 2]
        a = srcs[k]
        P.op('act', lambda e, sq=sq, a=a: e.activation(out=sq[:, 0:n], in_=a, func=AF.Square), reads=srcb, writes=[sb_])
        for n0 in range(0, n, 512):
            n1 = min(n, n0 + 512)
            P.op('pe', lambda e, sq=sq, k=k, n0=n0, n1=n1: e.matmul(psn[:, n0:n1], lhsT=C.ones[:, :], rhs=sq[:, n0:n1],
                                                                start=(k == 0), stop=(k == K - 1)),
                 reads=[sb_, C.onesb], writes=[psnb])
    P.op('act', lambda e: e.activation(out=rs[:, 0:n], in_=psn[:, 0:n], func=AF.Sqrt, bias=C.cst[:, 0:1], scale=1.0 / dim),
         reads=[psnb, C.cstb], writes=[rsb])
    P.op('dve', lambda e: e.reciprocal(out=rs[:, 0:n], in_=rs[:, 0:n]), reads=[rsb], writes=[rsb])


class NormRes:
    def __init__(s, C, n, tag=""):
        P = C.P
        s.sq = [C.sb("nsq0" + tag, [128, n]), C.sb("nsq1" + tag, [128, n])]
        s.sqb = [P.buf("nsq0"), P.buf("nsq1")]
        s.psn = C.ps("npsn" + tag, [128, max(n, 512)])
        s.psnb = P.buf("npsn")
        s.rs = C.sb("nrs" + tag, [128, n])
        s.rsb = P.buf("nrs")


def rmsnorm_tile(P, C, NR, xt, xtb, xn, xnb, KC, n, gcol0):
    rms_rstd(P, C, [xt[:, k, 0:n] for k in range(KC)], [xtb], n, KC * 128, NR.sq, NR.sqb, NR.psn, NR.psnb, NR.rs, NR.rsb)
    for k in range(KC):
        P.op('dve', lambda e, k=k: e.scalar_tensor_tensor(out=xn[:, k, 0:n], in0=xt[:, k, 0:n], scalar=C.sm[:, gcol0 + k:gcol0 + k + 1],
                                                       in1=NR.rs[:, 0:n], op0=ALU.mult, op1=ALU.mult),
             reads=[xtb, NR.rsb, C.smb], writes=[xnb])


def smalls_layout(cfg):
    KC = cfg['D'] // 128
    off, o = {}, 0
    for name, n in [('ffn_g0', KC), ('ffn_g1', KC), ('mix_g', KC), ('mem_g', KC), ('memq_g', 2), ('memk_g', 2),
                    ('conv_w', KC * 4), ('conv_b', KC), ('gb', 4 * KC), ('lam', 2 * KC), ('qg', 3), ('kg', 3)]:
        off[name] = o
        o += n
    return off, o


def load_x(P, C, X, xt, xtb, t, N):
    src = X[:, :, t * N:(t + 1) * N].rearrange("k p n -> p k n")
    P.dma('pool', lambda e: e.dma_start(out=xt[:, :, :], in_=src), xtb, writes=[xtb])


def store_x(P, C, X, xt, xtb, t, N):
    dst = X[:, :, t * N:(t + 1) * N].rearrange("k p n -> p k n")
    P.dma('pool', lambda e: e.dma_start(out=dst, in_=xt[:, :, :]), xtb, reads=[xtb])


def phase_ffn(P, cfg, dr, Xs, Xd, l, i, PC=None):
    D, S, FF = cfg['D'], cfg['S'], cfg['FF']
    KC, FC, N = D // 128, FF // 128, min(1024, cfg['S'])
    NBK = N // 512
    SO, _ = smalls_layout(cfg)
    C = Ctx(P, cfg, dr)
    C.common(nring=4, smalls_layer=l, wdt=BF16)
    w_in, w_out = dr['ffn_w_in_bf'][l * 2 + i], dr['ffn_w_out_bf'][l * 2 + i]
    kcn = max(k for k in range(1, 17) if FC % k == 0)
    npc = FC // kcn
    nhalf = 2 if npc % 2 == 0 else 1
    HC = FC // nhalf
    xt, xtb = C.sb("xt", [128, KC, N]), P.buf("xt", dma=True)
    xn, xnb = C.sb("xn", [128, KC, N], BF16), P.buf("xn")
    h, hb = C.sb("h", [128, HC, N], BF16), P.buf("h")
    NR = NormRes(C, N)
    psg, psu, pso = C.ps("psg", [128, N]), C.ps("psu", [128, N]), C.ps("pso", [128, N])
    psgb, psub, psob = [P.buf("g") for _ in range(NBK)], [P.buf("u") for _ in range(NBK)], [P.buf("o") for _ in range(NBK)]
    if PC is not None:
        PC.alloc(C, 2)
    npieces = (S // N) * (2 * FC + KC * npc)
    todo = len(PC.chunks.get(l + 1, [])) if PC is not None else 0
    stride = max(1, (npieces * (2 - i)) // max(1, todo) - 1) if todo else 0
    cnt = [0]

    def tick():
        cnt[0] += 1
        if stride and cnt[0] % stride == 0:
            PC.pump(l + 1, 1)
    for t in range(S // N):
        load_x(P, C, Xs, xt, xtb, t, N)
        rmsnorm_tile(P, C, NR, xt, xtb, xn, xnb, KC, N, SO['ffn_g%d' % i])
        acts = [(lambda n0, n1, k=k: xn[:, k, n0:n1]) for k in range(KC)]
        for hf_ in range(nhalf):
            for jj in range(HC):
                j = hf_ * HC + jj
                wt, wb = C.wload(w_in[j], KC)
                mm_acc(P, psg, psgb, N, wt, wb, acts, [xnb])
                wt, wb = C.wload(w_in[FC + j], KC)
                mm_acc(P, psu, psub, N, wt, wb, acts, [xnb])
                tick()
                tick()
                for bk in range(NBK):
                    c0, c1 = bk * 512, (bk + 1) * 512
                    P.op('act', lambda e, jj=jj, c0=c0, c1=c1: e.activation(out=h[:, jj, c0:c1], in_=psg[:, c0:c1], func=AF.Silu), reads=[psgb[bk]], writes=[hb])
                    P.op('dve', lambda e, jj=jj, c0=c0, c1=c1: e.tensor_tensor(out=h[:, jj, c0:c1], in0=psu[:, c0:c1], in1=h[:, jj, c0:c1], op=ALU.mult),
                         reads=[psub[bk], hb], writes=[hb])
            for m in range(KC):
                pcs = list(range(hf_ * (npc // nhalf), (hf_ + 1) * (npc // nhalf)))
                for pi, pc in enumerate(pcs):
                    wt, wb = C.wload(w_out[m][:, pc * kcn:(pc + 1) * kcn, :], kcn)
                    acts2 = [(lambda n0, n1, k=k: h[:, k, n0:n1]) for k in range(pc * kcn - hf_ * HC, (pc + 1) * kcn - hf_ * HC)]
                    mm_acc(P, pso, psob, N, wt, wb, acts2, [hb], first=(pi == 0), last=(pi == len(pcs) - 1))
                    tick()
                for bk in range(NBK):
                    c0, c1 = bk * 512, (bk + 1) * 512
                    P.op('dve', lambda e, m=m, c0=c0, c1=c1: e.scalar_tensor_tensor(out=xt[:, m, c0:c1], in0=pso[:, c0:c1], scalar=0.5, in1=xt[:, m, c0:c1],
                                                                                op0=ALU.mult, op1=ALU.add), reads=[psob[bk], xtb], writes=[xtb])
        store_x(P, C, Xd, xt, xtb, t, N)
    if PC is not None and i == 1:
        PC.pump(l + 1, 10 ** 9)
    C.close()


class Precast:
    def __init__(s, P, cfg, dr):
        s.P, s.dr = P, dr
        L = cfg['DEPTH']
        s.chunks = {l: [] for l in range(L + 1)}
        for name in BIG:
            if name not in dr:
                continue
            for li in range(dr[name].shape[0]):
                layer = li // 2 if name.startswith('ffn') else (2 * li if name.startswith('rec') else 2 * li + 1)
                src, dst = dr[name][li], dr[name + '_bf'][li]
                per = int(np.prod(src.shape)) // 128
                E = max(e for e in range(1, 2049) if per % e == 0)
                dims = " ".join("d%d" % i for i in range(len(src.shape)))
                sf = src.rearrange("%s -> (%s)" % (dims, dims)).rearrange("(n p e) -> n p e", p=128, e=E)
                df = dst.rearrange("%s -> (%s)" % (dims, dims)).rearrange("(n p e) -> n p e", p=128, e=E)
                for n in range(per // E):
                    s.chunks[layer].append((sf[n], df[n], E))
        s.ci = 0

    def alloc(s, C, nsl):
        P = s.P
        s.nsl = nsl
        s.st = [C.sb("pcs%d" % i, [128, 2048]) for i in range(nsl)]
        s.stb = [P.buf("pcs", dma=True) for _ in range(nsl)]
        s.b16 = [C.sb("pcb%d" % i, [128, 2048], BF16) for i in range(nsl)]
        s.b16b = [P.buf("pcb", dma=True) for _ in range(nsl)]

    def pump(s, layer, k, bg=True):
        P = s.P
        lst = s.chunks.get(layer, [])
        for _ in range(min(k, len(lst))):
            src, dst, E = lst.pop(0)
            i = s.ci % s.nsl
            eng = 'pool' if bg else ['dve', 'act', 'pool'][s.ci % 3]
            q = 'pool' if bg else 'sp'
            s.ci += 1
            a_, ab_, b_, bb_ = s.st[i], s.stb[i], s.b16[i], s.b16b[i]
            P.dma(q, lambda e, a_=a_, src=src, E=E: e.dma_start(out=a_[:, 0:E], in_=src), ab_, writes=[ab_])
            if eng == 'act':
                P.op('act', lambda e, a_=a_, b_=b_, E=E: e.activation(out=b_[:, 0:E], in_=a_[:, 0:E], func=AF.Copy), reads=[ab_], writes=[bb_])
            else:
                P.op(eng, lambda e, a_=a_, b_=b_, E=E: e.tensor_copy(out=b_[:, 0:E], in_=a_[:, 0:E]), reads=[ab_], writes=[bb_])
            P.dma(q, lambda e, b_=b_, dst=dst, E=E: e.dma_start(out=dst, in_=b_[:, 0:E]), bb_, reads=[bb_])


def phase_precast(P, cfg, dr, PC):
    C = Ctx(P, cfg, dr)
    PC.alloc(C, 4)
    PC.pump(0, 10 ** 9, bg=False)
    C.close()


def phase_memkv(P, cfg, dr, l):
    D, M, MH = cfg['D'], cfg['M'], cfg['MH']
    KC, MW = D // 128, cfg['MH'] * 256
    MC = 2 * MW // 128
    SO, _ = smalls_layout(cfg)
    C = Ctx(P, cfg, dr)
    C.common(nring=4, smalls_layer=l)
    ident, identb = C.sb("ident", [128, 128]), P.buf("ident", dma=True)
    P.dma('pool', lambda e: e.dma_start(out=ident[:, :], in_=dr['ident']), identb, writes=[identb])
    mt, mtb = C.sb("mt", [128, KC, M]), P.buf("mt", dma=True)
    mn, mnb = C.sb("mn", [128, KC, M]), P.buf("mn")
    kv, kvb = C.sb("kv", [128, MC, M]), P.buf("kv")
    kn, knb = C.sb("kn", [128, MC // 2, M]), P.buf("kn", dma=True)
    vm, vmb = C.sb("vm", [128, M // 128, MW]), P.buf("vm", dma=True)
    NR = NormRes(C, M)
    ps = [C.ps("ps%d" % j, [128, 512]) for j in range(2)]
    psb = [P.buf("ps") for _ in range(2)]
    src = dr['memT'].rearrange("k p n -> p k n")
    P.dma('pool', lambda e: e.dma_start(out=mt[:, :, :], in_=src), mtb, writes=[mtb])
    rmsnorm_tile(P, C, NR, mt, mtb, mn, mnb, KC, M, SO['mem_g'])
    acts = [(lambda n0, n1, k=k: mn[:, k, n0:n1]) for k in range(KC)]
    for m in range(MC):
        b = m % 2
        wt, wb = C.wload(dr['mem_w_kv'][l][m], KC)
        mm_acc(P, ps[b], psb[b], M, wt, wb, acts, [mnb])
        P.op('act', lambda e, b=b, m=m: e.activation(out=kv[:, m, 0:M], in_=ps[b][:, 0:M], func=AF.Copy), reads=[psb[b]], writes=[kvb])
    for hh in range(MH):
        rms_rstd(P, C, [kv[:, 2 * hh + f, 0:M] for f in range(2)], [kvb], M, 256, NR.sq, NR.sqb, NR.psn, NR.psnb, NR.rs, NR.rsb)
        for f in range(2):
            c = SO['memk_g'] + f
            P.op('dve', lambda e, hh=hh, f=f, c=c: e.scalar_tensor_tensor(out=kn[:, 2 * hh + f, 0:M], in0=kv[:, 2 * hh + f, 0:M],
                                                                      scalar=C.sm[:, c:c + 1], in1=NR.rs[:, 0:M], op0=ALU.mult, op1=ALU.mult),
                 reads=[kvb, NR.rsb, C.smb], writes=[knb])
    for m in range(MC // 2):
        for tc in range(M // 128):
            b = (m * 2 + tc) % 2
            P.op('pe', lambda e, b=b, m=m, tc=tc: e.transpose(out=ps[b][:, 0:128], in_=kv[:, MC // 2 + m, tc * 128:(tc + 1) * 128], identity=ident[:, :]),
                 reads=[kvb, identb], writes=[psb[b]])
            P.op('act', lambda e, b=b, m=m, tc=tc: e.activation(out=vm[:, tc, m * 128:(m + 1) * 128], in_=ps[b][:, 0:128], func=AF.Copy),
                 reads=[psb[b]], writes=[vmb])
    P.dma('pool', lambda e: e.dma_start(out=dr['KN'], in_=kn[:, :, :]), knb, reads=[knb])
    P.dma('pool', lambda e: e.dma_start(out=dr['VM'], in_=vm[:, :, :]), vmb, reads=[vmb])
    C.close()


class MemAtt:
    def __init__(s, C, N):
        P, cfg, dr = C.P, C.cfg, C.dr
        s.C, s.N = C, N
        M, MH = cfg['M'], cfg['MH']
        MW = MH * 256
        s.kn, s.knb = C.sb("kn", [128, 2 * MH, M]), P.buf("kn", dma=True)
        s.vm, s.vmb = C.sb("vm", [128, M // 128, MW]), P.buf("vm", dma=True)
        P.dma('pool', lambda e: e.dma_start(out=s.kn[:, :, :], in_=dr['KN']), s.knb, writes=[s.knb])
        P.dma('pool', lambda e: e.dma_start(out=s.vm[:, :, :], in_=dr['VM']), s.vmb, writes=[s.vmb])
        s.qms = [C.sb("qm%d" % i, [128, 2 * MH, N]) for i in range(2)]
        s.qmbs = [P.buf("qm") for _ in range(2)]
        s.qm, s.qmb = s.qms[0], s.qmbs[0]
        s.qn, s.qnb = C.sb("qn", [128, 2, N]), P.buf("qn")
        s.pm = [C.sb("pm%d" % j, [128, N]) for j in range(M // 128)]
        s.pmb = [P.buf("pm") for _ in range(M // 128)]
        s.rden, s.rdenb = C.sb("rden", [128, N]), P.buf("rden")
        s.ym, s.ymb = C.sb("ym", [128, 2 * MH, N], BF16), P.buf("ym", dma=True)

    def run(s, NR, psA, psAb, psB, psBb, t, sub=0):
        C, N = s.C, s.N
        s.qm, s.qmb = s.qms[sub], s.qmbs[sub]
        s_qm, s_qmb = s.qms[sub], s.qmbs[sub]
        P, cfg, dr = C.P, C.cfg, C.dr
        M, MH = cfg['M'], cfg['MH']
        SO, _ = smalls_layout(cfg)
        MCk = M // 128
        for hh in range(MH):
            rms_rstd(P, C, [s_qm[:, 2 * hh + f, 0:N] for f in range(2)], [s_qmb], N, 256, NR.sq, NR.sqb, NR.psn, NR.psnb, NR.rs, NR.rsb)
            for f in range(2):
                c = SO['memq_g'] + f
                P.op('dve', lambda e, hh=hh, f=f, c=c: e.scalar_tensor_tensor(out=s.qn[:, f, 0:N], in0=s_qm[:, 2 * hh + f, 0:N], scalar=C.sm[:, c:c + 1],
                                                                          in1=NR.rs[:, 0:N], op0=ALU.mult, op1=ALU.mult),
                     reads=[s_qmb, NR.rsb, C.smb], writes=[s.qnb])
            for mc in range(MCk):
                for f in range(2):
                    P.op('pe', lambda e, hh=hh, f=f, mc=mc: e.matmul(psA[:, 0:N], lhsT=s.kn[:, 2 * hh + f, mc * 128:(mc + 1) * 128], rhs=s.qn[:, f, 0:N],
                                                                 start=(f == 0), stop=(f == 1)), reads=[s.knb, s.qnb], writes=[psAb])
                P.op('act', lambda e, mc=mc: e.activation(out=s.pm[mc][:, 0:N], in_=psA[:, 0:N], func=AF.Exp, scale=256.0 ** -0.5),
                     reads=[psAb], writes=[s.pmb[mc]])
            for mc in range(MCk):
                P.op('pe', lambda e, mc=mc: e.matmul(psB[:, 0:N], lhsT=C.ones[:, :], rhs=s.pm[mc][:, 0:N], start=(mc == 0), stop=(mc == MCk - 1)),
                     reads=[C.onesb, s.pmb[mc]], writes=[psBb])
            P.op('dve', lambda e: e.reciprocal(out=s.rden[:, 0:N], in_=psB[:, 0:N]), reads=[psBb], writes=[s.rdenb])
            for f in range(2):
                for mc in range(MCk):
                    c0 = (2 * hh + f) * 128
                    P.op('pe', lambda e, mc=mc, c0=c0: e.matmul(psA[:, 0:N], lhsT=s.vm[:, mc, c0:c0 + 128], rhs=s.pm[mc][:, 0:N],
                                                             start=(mc == 0), stop=(mc == MCk - 1)), reads=[s.vmb, s.pmb[mc]], writes=[psAb])
                P.op('dve', lambda e, hh=hh, f=f: e.tensor_tensor(out=s.ym[:, 2 * hh + f, 0:N], in0=psA[:, 0:N], in1=s.rden[:, 0:N], op=ALU.mult),
                     reads=[psAb, s.rdenb], writes=[s.ymb])
        dst = dr['YM'][:, :, t * N:(t + 1) * N].rearrange("k p n -> p k n")
        P.dma('pool', lambda e: e.dma_start(out=dst, in_=s.ym[:, :, :]), s.ymb, reads=[s.ymb])


def phase_rec_p1(P, cfg, dr, X, l, j):
    D, S, MH = cfg['D'], cfg['S'], cfg['MH']
    KC, N = D // 128, min(512, cfg['S'])
    SO, _ = smalls_layout(cfg)
    C = Ctx(P, cfg, dr)
    C.common(nring=4, smalls_layer=l, wdt=BF16)
    w_in = dr['rec_w_in_bf'][j]
    xt, xtb = C.sb("xt", [128, KC, N]), P.buf("xt", dma=True)
    xns = [C.sb("xn%d" % i, [128, KC, N], BF16) for i in range(2)]
    xnbs = [P.buf("xn") for _ in range(2)]
    NR = NormRes(C, N)
    MA = MemAtt(C, N)
    ps = [C.ps("ps%d" % i, [128, 512]) for i in range(3)]
    psb = [P.buf("ps") for _ in range(3)]
    psA, psAb, psB, psBb = C.ps("psA", [128, 512]), P.buf("psA"), C.ps("psB", [128, 512]), P.buf("psB")
    stg = [C.sb("stg%d" % i, [128, N]) for i in range(3)]
    stgb = [P.buf("stg", dma=True) for _ in range(3)]
    t1s = [C.sb("t1_%d" % i, [128, N]) for i in range(3)]
    t1bs = [P.buf("t1") for _ in range(3)]
    t2s = [C.sb("t2_%d" % i, [128, N]) for i in range(3)]
    t2bs = [P.buf("t2") for _ in range(3)]
    zt, ztb = C.sb("zt", [128, KC, 2]), P.buf("zt", dma=True)
    P.op('pool', lambda e: e.memset(zt[:, :, :], 0.0), writes=[ztb])
    RB = dr['RB']
    nc_ = P.nc
    def _zp(e, a, b):
        with nc_.allow_non_contiguous_dma(reason="tiny zero pad"):
            return e.dma_start(out=RB[:, :, a:b].rearrange("k p n -> p k n"), in_=zt[:, :, 0:b - a])
    P.dma('pool', lambda e: _zp(e, 0, 2), ztb, reads=[ztb])
    P.dma('pool', lambda e: _zp(e, S + 2, S + 3), ztb, reads=[ztb])
    si = 0
    NSUB = 2 if (S // N) % 2 == 0 else 1
    cnt = 0
    for tp in range(S // (N * NSUB)):
        for sub in range(NSUB):
            load_x(P, C, X, xt, xtb, tp * NSUB + sub, N)
            rmsnorm_tile(P, C, NR, xt, xtb, xns[sub], xnbs[sub], KC, N, SO['mix_g'])
        for m in range(2 * KC + 2 * MH):
            wt, wb = C.wload(w_in[m], KC)
            for sub in range(NSUB):
                t = tp * NSUB + sub
                xn, xnb = xns[sub], xnbs[sub]
                acts = [(lambda n0, n1, k=k, xn=xn: xn[:, k, n0:n1]) for k in range(KC)]
                b = cnt % 3
                cnt += 1
                mm_acc(P, ps[b], psb[b], N, wt, wb, acts, [xnb])
                if m < KC:
                    s_, sb_ = stg[si % 3], stgb[si % 3]
                    t1, t1b, t2, t2b = t1s[si % 3], t1bs[si % 3], t2s[si % 3], t2bs[si % 3]
                    si += 1
                    P.op('act', lambda e, b=b, t1=t1: e.activation(out=t1[:, 0:N], in_=ps[b][:, 0:N], func=AF.Square), reads=[psb[b]], writes=[t1b])
                    P.op('dve', lambda e, t1=t1: e.tensor_scalar(out=t1[:, 0:N], in0=t1[:, 0:N], scalar1=0.044715, scalar2=1.0, op0=ALU.mult, op1=ALU.add),
                         reads=[t1b], writes=[t1b])
                    P.op('dve', lambda e, b=b, t1=t1: e.tensor_tensor(out=t1[:, 0:N], in0=ps[b][:, 0:N], in1=t1[:, 0:N], op=ALU.mult), reads=[psb[b], t1b], writes=[t1b])
                    P.op('act', lambda e, t1=t1, t2=t2: e.activation(out=t2[:, 0:N], in_=t1[:, 0:N], func=AF.Sigmoid, scale=1.5957691216), reads=[t1b], writes=[t2b])
                    P.op('dve', lambda e, b=b, s_=s_, t2=t2: e.tensor_tensor(out=s_[:, 0:N], in0=ps[b][:, 0:N], in1=t2[:, 0:N], op=ALU.mult),
                         reads=[psb[b], t2b], writes=[sb_])
                    dst = dr['GG'][m][:, t * N:(t + 1) * N]
                    P.dma('pool', lambda e, s_=s_, dst=dst: e.dma_start(out=dst, in_=s_[:, 0:N]), sb_, reads=[sb_])
                elif m < 2 * KC:
                    s_, sb_ = stg[si % 3], stgb[si % 3]
                    si += 1
                    P.op('act', lambda e, b=b, s_=s_: e.activation(out=s_[:, 0:N], in_=ps[b][:, 0:N], func=AF.Copy), reads=[psb[b]], writes=[sb_])
                    dst = RB[m - KC][:, 2 + t * N:2 + (t + 1) * N]
                    P.dma('pool', lambda e, s_=s_, dst=dst: e.dma_start(out=dst, in_=s_[:, 0:N]), sb_, reads=[sb_])
                else:
                    P.op('act', lambda e, b=b, m=m, sub=sub: e.activation(out=MA.qms[sub][:, m - 2 * KC, 0:N], in_=ps[b][:, 0:N], func=AF.Copy),
                         reads=[psb[b]], writes=[MA.qmbs[sub]])
        for sub in range(NSUB):
            MA.run(NR, psA, psAb, psB, psBb, tp * NSUB + sub, sub)
    C.close()


def phase_rec_p2(P, cfg, dr, l, j, d):
    D, S = cfg['D'], cfg['S']
    KC, NB = D // 128, D // 256
    T = min(2048, S)
    nseg = S // T
    SO, _ = smalls_layout(cfg)
    C = Ctx(P, cfg, dr)
    C.common(nring=4, smalls_layer=l)
    sm = C.sm
    cs, csb = C.sb("cs", [128, 4 * KC]), P.buf("cs")
    lo = SO['lam']
    P.op('act', lambda e: e.activation(out=cs[:, 0:2 * KC], in_=sm[:, lo:lo + 2 * KC], func=AF.Exp, scale=-1.0), reads=[C.smb], writes=[csb])
    P.op('act', lambda e: e.activation(out=cs[:, 0:2 * KC], in_=cs[:, 0:2 * KC], func=AF.Ln, bias=C.cst[:, 1:2], scale=1.0), reads=[csb, C.cstb], writes=[csb])
    P.op('dve', lambda e: e.tensor_scalar(out=cs[:, 2 * KC:4 * KC], in0=cs[:, 0:2 * KC], scalar1=-16.0, scalar2=None, op0=ALU.mult), reads=[csb], writes=[csb])
    P.op('dve', lambda e: e.tensor_scalar(out=cs[:, 0:2 * KC], in0=cs[:, 0:2 * KC], scalar1=-8.0, scalar2=None, op0=ALU.mult), reads=[csb], writes=[csb])
    rb, rbb = C.sb("rb", [128, 2, T + 3]), P.buf("rb", dma=True)
    xc, xcb = C.sb("xc", [128, 2, T]), P.buf("xc")
    names = ['rr', 'ii', 'aa', 'bb', 'hh']
    tl = {n: C.sb(n, [128, T]) for n in names}
    tb = {n: P.buf(n, dma=(n == 'hh')) for n in names}
    hf, hfb = C.sb("hf", [128, T]), P.buf("hf", dma=True)
    gg, ggb = C.sb("gg", [128, T]), P.buf("gg", dma=True)
    st, stb = C.sb("st", [128, 2]), P.buf("st")
    yb, ybb = C.sb("yb", [128, T], BF16), P.buf("yb", dma=True)
    nb512 = max(1, T // 512)
    psr, psrb = C.ps("psr", [128, nb512 * 512]), P.buf("psr")
    psi, psib = C.ps("psi", [128, nb512 * 512]), P.buf("psi")
    gw = dr['gate_w'][j]
    rr, ii, aa, bb, hh = (tl[n] for n in names)
    for n in range(NB):
        for d in [d]:
            segs = list(range(nseg)) if d == 0 else list(range(nseg - 1, -1, -1))
            for si_, seg in enumerate(segs):
                src = dr['RB'][2 * n:2 * n + 2, :, seg * T:seg * T + T + 3].rearrange("k p n -> p k n")
                P.dma('pool', lambda e, src=src: e.dma_start(out=rb[:, :, :], in_=src), rbb, writes=[rbb])
                for c in range(2):
                    ch = 2 * n + c
                    w0, bcol = SO['conv_w'] + ch * 4, SO['conv_b'] + ch
                    P.op('dve', lambda e, c=c, w0=w0, bcol=bcol: e.tensor_scalar(out=xc[:, c, :], in0=rb[:, c, 0:T], scalar1=sm[:, w0:w0 + 1],
                                                                             scalar2=sm[:, bcol:bcol + 1], op0=ALU.mult, op1=ALU.add),
                         reads=[rbb, C.smb], writes=[xcb])
                    for jj in range(1, 4):
                        P.op('dve', lambda e, c=c, w0=w0, jj=jj: e.scalar_tensor_tensor(out=xc[:, c, :], in0=rb[:, c, jj:jj + T], scalar=sm[:, w0 + jj:w0 + jj + 1],
                                                                                    in1=xc[:, c, :], op0=ALU.mult, op1=ALU.add),
                             reads=[rbb, C.smb, xcb], writes=[xcb])
                for oc in range(2):
                    ch = 2 * n + oc
                    for gt, (pst, pstb, dst, dstb) in enumerate([(psr, psrb, rr, tb['rr']), (psi, psib, ii, tb['ii'])]):
                        wt, wb = C.wload(gw[d][gt][n][oc], 2)
                        mm_acc(P, pst, pstb, T, wt, wb, [(lambda n0, n1, k=k: xc[:, k, n0:n1]) for k in range(2)], [xcb])
                        bc = SO['gb'] + (d * 2 + gt) * KC + ch
                        P.op('act', lambda e, pst=pst, dst=dst, bc=bc: e.activation(out=dst[:, 0:T], in_=pst[:, 0:T], func=AF.Sigmoid, bias=sm[:, bc:bc + 1], scale=1.0),
                             reads=[pstb, C.smb], writes=[dstb])
                    c1, c2 = d * KC + ch, 2 * KC + d * KC + ch
                    P.op('act', lambda e, c1=c1: e.activation(out=aa[:, 0:T], in_=rr[:, 0:T], func=AF.Exp, scale=cs[:, c1:c1 + 1]), reads=[tb['rr'], csb], writes=[tb['aa']])
                    P.op('act', lambda e, c2=c2: e.activation(out=bb[:, 0:T], in_=rr[:, 0:T], func=AF.Exp, scale=cs[:, c2:c2 + 1]), reads=[tb['rr'], csb], writes=[tb['bb']])
                    P.op('act', lambda e: e.activation(out=bb[:, 0:T], in_=bb[:, 0:T], func=AF.Sqrt, bias=C.cst[:, 1:2], scale=-1.0), reads=[tb['bb'], C.cstb], writes=[tb['bb']])
                    P.op('dve', lambda e: e.tensor_tensor(out=bb[:, 0:T], in0=bb[:, 0:T], in1=ii[:, 0:T], op=ALU.mult), reads=[tb['bb'], tb['ii']], writes=[tb['bb']])
                    P.op('dve', lambda e, oc=oc: e.tensor_tensor(out=bb[:, 0:T], in0=bb[:, 0:T], in1=xc[:, oc, :], op=ALU.mult), reads=[tb['bb'], xcb], writes=[tb['bb']])
                    init = 0.0 if si_ == 0 else st[:, oc:oc + 1]
                    if d == 0:
                        P.op('dve', lambda e, init=init: e.tensor_tensor_scan(out=hh[:, 0:T], data0=aa[:, 0:T], data1=bb[:, 0:T], initial=init, op0=ALU.mult, op1=ALU.add),
                             reads=[tb['aa'], tb['bb'], stb], writes=[tb['hh']])
                        P.op('dve', lambda e, oc=oc: e.tensor_copy(out=st[:, oc:oc + 1], in_=hh[:, T - 1:T]), reads=[tb['hh']], writes=[stb])
                        dst = dr['HF'][ch][:, seg * T:(seg + 1) * T]
                        P.dma('pool', lambda e, dst=dst: e.dma_start(out=dst, in_=hh[:, 0:T]), tb['hh'], reads=[tb['hh']])
                    else:
                        s1 = dr['HF'][ch][:, seg * T:(seg + 1) * T]
                        s2 = dr['GG'][ch][:, seg * T:(seg + 1) * T]
                        P.dma('pool', lambda e, s1=s1: e.dma_start(out=hf[:, 0:T], in_=s1), hfb, writes=[hfb])
                        P.dma('pool', lambda e, s2=s2: e.dma_start(out=gg[:, 0:T], in_=s2), ggb, writes=[ggb])
                        P.op('dve', lambda e, init=init: e.tensor_tensor_scan(out=hh[:, ::-1], data0=aa[:, ::-1], data1=bb[:, ::-1], initial=init, op0=ALU.mult, op1=ALU.add),
                             reads=[tb['aa'], tb['bb'], stb], writes=[tb['hh']])
                        P.op('dve', lambda e, oc=oc: e.tensor_copy(out=st[:, oc:oc + 1], in_=hh[:, 0:1]), reads=[tb['hh']], writes=[stb])
                        P.op('dve', lambda e: e.tensor_tensor(out=hh[:, 0:T], in0=hh[:, 0:T], in1=hf[:, 0:T], op=ALU.add), reads=[tb['hh'], hfb], writes=[tb['hh']])
                        P.op('dve', lambda e: e.tensor_tensor(out=yb[:, 0:T], in0=hh[:, 0:T], in1=gg[:, 0:T], op=ALU.mult), reads=[tb['hh'], ggb], writes=[ybb])
                        dst = dr['YR'][ch][:, seg * T:(seg + 1) * T]
                        P.dma('pool', lambda e, dst=dst: e.dma_start(out=dst, in_=yb[:, 0:T]), ybb, reads=[ybb])
    C.close()


def phase_wout(P, cfg, dr, X, Xd, w_out, YMAIN, nmain):
    D, S, MH = cfg['D'], cfg['S'], cfg['MH']
    KC, N = D // 128, min(512, cfg['S'])
    YC = nmain + 2 * MH
    C = Ctx(P, cfg, dr)
    C.common(nring=4, wdt=BF16)
    xt, xtb = C.sb("xt", [128, KC, N]), P.buf("xt", dma=True)
    yt, ytb = C.sb("yt", [128, YC, N], BF16), P.buf("yt", dma=True)
    pso = [C.ps("pso%d" % j, [128, 512]) for j in range(2)]
    psob = [P.buf("o") for _ in range(2)]
    kcn = max(k for k in range(1, 17) if YC % k == 0)
    for t in range(S // N):
        load_x(P, C, X, xt, xtb, t, N)
        s1 = YMAIN[:, :, t * N:(t + 1) * N].rearrange("k p n -> p k n")
        s2 = dr['YM'][:, :, t * N:(t + 1) * N].rearrange("k p n -> p k n")
        P.dma('pool', lambda e, s1=s1: e.dma_start(out=yt[:, 0:nmain, :], in_=s1), ytb, writes=[ytb])
        P.dma('pool', lambda e, s2=s2: e.dma_start(out=yt[:, nmain:YC, :], in_=s2), ytb, writes=[ytb])
        for m in range(KC):
            b = m % 2
            npc = YC // kcn
            for pc in range(npc):
                wt, wb = C.wload(w_out[m][:, pc * kcn:(pc + 1) * kcn, :], kcn)
                acts2 = [(lambda n0, n1, k=k: yt[:, k, n0:n1]) for k in range(pc * kcn, (pc + 1) * kcn)]
                mm_acc(P, pso[b], psob[b], N, wt, wb, acts2, [ytb], first=(pc == 0), last=(pc == npc - 1))
            P.op('dve', lambda e, b=b, m=m: e.tensor_tensor(out=xt[:, m, 0:N], in0=pso[b][:, 0:N], in1=xt[:, m, 0:N], op=ALU.add),
                 reads=[psob[b], xtb], writes=[xtb])
        store_x(P, C, Xd, xt, xtb, t, N)
    C.close()


def phase_att_p1(P, cfg, dr, X, l, j):
    D, S, MH, AH = cfg['D'], cfg['S'], cfg['MH'], cfg['AH']
    KC, N = D // 128, min(512, cfg['S'])
    SO, _ = smalls_layout(cfg)
    C = Ctx(P, cfg, dr)
    C.common(nring=4, smalls_layer=l, wdt=BF16)
    identf, identfb = C.sb("identf", [128, 128]), P.buf("identf", dma=True)
    P.dma('pool', lambda e: e.dma_start(out=identf[:, :], in_=dr['ident']), identfb, writes=[identfb])
    ident, identb = C.sb("ident", [128, 128], BF16), P.buf("ident")
    P.op('dve', lambda e: e.tensor_copy(out=ident[:, :], in_=identf[:, :]), reads=[identfb], writes=[identb])
    Ctx._uid[0] += 1
    psT = C.es.enter_context(C.nc.psum_tensor("psT_att_u%d" % Ctx._uid[0], [128, 512], BF16))
    psTb = P.buf("psT")
    w_in = dr['att_w_in_bf'][j]
    xt, xtb = C.sb("xt", [128, KC, N]), P.buf("xt", dma=True)
    xns = [C.sb("xn%d" % i, [128, KC, N], BF16) for i in range(2)]
    xnbs = [P.buf("xn") for _ in range(2)]
    NR = NormRes(C, N)
    NR2 = NormRes(C, N, "b")
    MA = MemAtt(C, N)
    ps = [C.ps("ps%d" % i, [128, 512]) for i in range(3)]
    psb = [P.buf("ps") for _ in range(3)]
    psA, psAb = C.ps("psA", [128, 512]), P.buf("psA")
    psB, psBb = NR.psn, NR.psnb
    psn2 = [NR2.psn, C.ps("psn2b", [128, 512])]
    psn2b = [NR2.psnb, P.buf("psn2b")]
    rs2 = [NR2.rs, C.sb("rs2b", [128, N])]
    rs2b = [NR2.rsb, P.buf("rs2b")]
    stg = [C.sb("stg%d" % i, [128, N], BF16) for i in range(3)]
    stgb = [P.buf("stg", dma=True) for _ in range(3)]
    vs2 = [C.sb("vs%d" % i, [128, N], BF16) for i in range(2)]
    vs2b = [P.buf("vs") for _ in range(2)]
    vt, vtb = C.sb("vt", [128, N], BF16), P.buf("vt", dma=True)
    si = 0
    nqkv = 9 * AH
    NSUB = 2 if (S // N) % 2 == 0 else 1
    cnt = 0
    for tp in range(S // (N * NSUB)):
        for sub in range(NSUB):
            load_x(P, C, X, xt, xtb, tp * NSUB + sub, N)
            rmsnorm_tile(P, C, NR, xt, xtb, xns[sub], xnbs[sub], KC, N, SO['mix_g'])
        pend = None
        for m in range(nqkv + 2 * MH):
            wt, wb = C.wload(w_in[m], KC)
            for sub in range(NSUB):
                t = tp * NSUB + sub
                xn, xnb = xns[sub], xnbs[sub]
                acts = [(lambda n0, n1, k=k, xn=xn: xn[:, k, n0:n1]) for k in range(KC)]
                b = cnt % 3
                c2 = cnt % 2
                cnt += 1
                mm_acc(P, ps[b], psb[b], N, wt, wb, acts, [xnb])
                post = None
                if m >= nqkv:
                    P.op('act', lambda e, b=b, m=m, sub=sub: e.activation(out=MA.qms[sub][:, m - nqkv, 0:N], in_=ps[b][:, 0:N], func=AF.Copy),
                         reads=[psb[b]], writes=[MA.qmbs[sub]])
                else:
                    g, kind, hd_ = m // (3 * AH), (m % (3 * AH)) // AH, m % AH
                    if kind < 2:
                        sq, sqb_ = NR2.sq[c2], NR2.sqb[c2]
                        P.op('act', lambda e, b=b, sq=sq: e.activation(out=sq[:, 0:N], in_=ps[b][:, 0:N], func=AF.Square), reads=[psb[b]], writes=[sqb_])

                        def post(b=b, sq=sq, sqb_=sqb_, g=g, kind=kind, hd_=hd_, t=t, c2=c2):
                            nonlocal si
                            s_, sb_ = stg[si % 3], stgb[si % 3]
                            si += 1
                            pn, pnb, rs_, rsb_ = psn2[c2], psn2b[c2], rs2[c2], rs2b[c2]
                            P.op('pe', lambda e: e.matmul(pn[:, 0:N], lhsT=C.ones[:, :], rhs=sq[:, 0:N], start=True, stop=True),
                                 reads=[sqb_, C.onesb], writes=[pnb])
                            P.op('act', lambda e: e.activation(out=rs_[:, 0:N], in_=pn[:, 0:N], func=AF.Sqrt, bias=C.cst[:, 0:1], scale=1.0 / 128),
                                 reads=[pnb, C.cstb], writes=[rsb_])
                            P.op('dve', lambda e: e.reciprocal(out=rs_[:, 0:N], in_=rs_[:, 0:N]), reads=[rsb_], writes=[rsb_])
                            gc = SO['qg' if kind == 0 else 'kg'] + g
                            P.op('dve', lambda e: e.scalar_tensor_tensor(out=s_[:, 0:N], in0=ps[b][:, 0:N], scalar=C.sm[:, gc:gc + 1], in1=rs_[:, 0:N],
                                                                       op0=ALU.mult, op1=ALU.mult), reads=[psb[b], rsb_, C.smb], writes=[sb_])
                            dst = dr['QK'][(g * 2 + kind) * AH + hd_][:, t * N:(t + 1) * N]
                            P.dma('pool', lambda e: e.dma_start(out=dst, in_=s_[:, 0:N]), sb_, reads=[sb_])
                    else:
                        vs, vsb = vs2[c2], vs2b[c2]
                        P.op('act', lambda e, b=b, vs=vs: e.activation(out=vs[:, 0:N], in_=ps[b][:, 0:N], func=AF.Copy), reads=[psb[b]], writes=[vsb])

                        def post(vs=vs, vsb=vsb, g=g, hd_=hd_, t=t):
                            for bi in range(N // 128):
                                P.op('pe', lambda e, bi=bi: e.transpose(out=psT[:, bi * 128:(bi + 1) * 128], in_=vs[:, bi * 128:(bi + 1) * 128], identity=ident[:, :]),
                                     reads=[vsb, identb], writes=[psTb])
                            P.op('act', lambda e: e.activation(out=vt[:, 0:N], in_=psT[:, 0:N], func=AF.Copy), reads=[psTb], writes=[vtb])
                            dst = dr['V'][g * AH + hd_][t * N:(t + 1) * N, :].rearrange("(c p) d -> p c d", p=128)
                            P.dma('pool', lambda e: e.dma_start(out=dst, in_=vt[:, 0:N].rearrange("p (c d) -> p c d", d=128)), vtb, reads=[vtb])
                if pend is not None:
                    pend()
                pend = post
        if pend is not None:
            pend()
        for sub in range(NSUB):
            MA.run(NR, psA, psAb, psB, psBb, tp * NSUB + sub, sub)
    C.close()


def alibi_slope(cfg, g, h):
    n = 3 * cfg['AH']
    return float(np.exp2(np.float32(-8.0) * (np.float32(g * cfg['AH'] + h) + np.float32(1.0)) / np.float32(n)))


def phase_att_p2(P, cfg, dr):
    S, AH = cfg['S'], cfg['AH']
    C = Ctx(P, cfg, dr)
    C.common(nring=0)
    rel, relb = C.sb("rel", [128, 6, 512]), P.buf("rel", dma=True)
    P.dma('pool', lambda e: e.dma_start(out=rel[:, :, :], in_=dr['c_rel'].rearrange("c p n -> p c n")), relb, writes=[relb])
    bt, btb = C.sb("bt", [128, 6, 512]), P.buf("bt")
    qT, qTb = C.sb("qT", [128, S], BF16), P.buf("qT", dma=True)
    kT, kTb = C.sb("kT", [128, S], BF16), P.buf("kT", dma=True)
    vc, vcb = C.sb("vc", [128, S // 128, 128], BF16), P.buf("vc", dma=True)
    ones16, ones16b = C.sb("ones16", [128, 128], BF16), P.buf("ones16")
    P.op('pool', lambda e: e.memset(ones16[:, :], 1.0), writes=[ones16b])
    aO, aOb = C.sb("aO", [128, S]), P.buf("aO", dma=True)
    aD, aDb = C.sb("aD", [128, S]), P.buf("aD")
    tmp = [C.sb("tmp%d" % i, [128, 512]) for i in range(4)]
    tmpb = [P.buf("tmp") for _ in range(4)]
    ya = [C.sb("ya%d" % i, [128, min(2048, S)], BF16) for i in range(2)]
    yab = [P.buf("ya", dma=True) for _ in range(2)]
    pt = [C.sb("pt%d" % i, [128, 512], BF16) for i in range(4)]
    ptb = [P.buf("pt") for _ in range(4)]
    psS = [C.ps("psS%d" % i, [128, 512]) for i in range(4)]
    psSb = [P.buf("psS") for _ in range(4)]
    psO = [C.ps("psO%d" % i, [128, 512]) for i in range(2)]
    psOb = [P.buf("psO") for _ in range(2)]
    psD = [C.ps("psD%d" % i, [128, 512]) for i in range(2)]
    psDb = [P.buf("psD") for _ in range(2)]
    it = 0
    qi = 0
    for h in range(AH):
        for g, (win, d) in enumerate(cfg['PAT']):
            L = S // d
            QT = min(512, L)
            nck = L // 128
            P.dma('pool', lambda e, g=g, h=h: e.dma_start(out=qT[:, :], in_=dr['QK'][(g * 2 + 0) * AH + h]), qTb, writes=[qTb])
            P.dma('pool', lambda e, g=g, h=h: e.dma_start(out=kT[:, :], in_=dr['QK'][(g * 2 + 1) * AH + h]), kTb, writes=[kTb])
            Vd = dr['V'][g * AH + h].rearrange("(c i r) e -> r i c e", i=128, r=d)
            for r in range(d):
                P.dma('pool', lambda e, r=r, Vd=Vd, nck=nck: e.dma_start(out=vc[:, r * nck:(r + 1) * nck, :], in_=Vd[r]), vcb, writes=[vcb])
            sl = -alibi_slope(cfg, g, h) * d
            for ci in range(6):
                P.op('dve', lambda e, ci=ci, sl=sl: e.tensor_scalar(out=bt[:, ci, :], in0=rel[:, ci, :], scalar1=sl, scalar2=None, op0=ALU.mult),
                     reads=[relb], writes=[btb])
            pairs = []
            for r in range(d):
                for j0 in range(0, L, QT):
                    qb = qi % 2
                    qi += 1
                    cis = [ci for ci in range(QT // 128 + 2) if 0 <= j0 - 128 + 128 * ci and j0 - 128 + 128 * ci + 128 <= L]
                    for idx, ci in enumerate(cis):
                        pairs.append((r, j0, qb, ci, j0 - 128 + 128 * ci, idx, len(cis)))

            def stageA(p, b, d=d, QT=QT):
                r, j0, qb, ci, kj, idx, nci = p
                qap = qT[:, r + d * j0: r + d * (j0 + QT - 1) + 1: d]
                kap = kT[:, r + d * kj: r + d * (kj + 127) + 1: d]
                P.op('pe', lambda e: e.matmul(psS[b][:, 0:QT], lhsT=kap, rhs=qap, start=True, stop=True), reads=[kTb, qTb], writes=[psSb[b]])
                P.op('dve', lambda e: e.scalar_tensor_tensor(out=tmp[b][:, 0:QT], in0=psS[b][:, 0:QT], scalar=128.0 ** -0.5, in1=bt[:, ci, 0:QT],
                                                           op0=ALU.mult, op1=ALU.add), reads=[psSb[b], btb], writes=[tmpb[b]])
                P.op('act', lambda e: e.activation(out=pt[b][:, 0:QT], in_=tmp[b][:, 0:QT], func=AF.Exp), reads=[tmpb[b]], writes=[ptb[b]])

            def stageB(p, b, d=d, QT=QT, nck=nck, g=g):
                r, j0, qb, ci, kj, idx, nci = p
                vck = r * nck + kj // 128
                P.op('pe', lambda e: e.matmul(psO[qb][:, 0:QT], lhsT=vc[:, vck, :], rhs=pt[b][:, 0:QT], start=(idx == 0), stop=(idx == nci - 1)),
                     reads=[vcb, ptb[b]], writes=[psOb[qb]])
                P.op('pe', lambda e: e.matmul(psD[qb][:, 0:QT], lhsT=ones16[:, :], rhs=pt[b][:, 0:QT], start=(idx == 0), stop=(idx == nci - 1)),
                     reads=[ones16b, ptb[b]], writes=[psDb[qb]])
                if idx == nci - 1:
                    oap = aO[:, r + d * j0: r + d * (j0 + QT - 1) + 1: d]
                    dap = aD[:, r + d * j0: r + d * (j0 + QT - 1) + 1: d]
                    if g == 0:
                        P.op('act', lambda e: e.activation(out=oap, in_=psO[qb][:, 0:QT], func=AF.Copy), reads=[psOb[qb]], writes=[aOb])
                        P.op('dve', lambda e: e.tensor_copy(out=dap, in_=psD[qb][:, 0:QT]), reads=[psDb[qb]], writes=[aDb])
                    else:
                        P.op('dve', lambda e: e.tensor_tensor(out=oap, in0=psO[qb][:, 0:QT], in1=oap, op=ALU.add), reads=[psOb[qb], aOb], writes=[aOb])
                        P.op('dve', lambda e: e.tensor_tensor(out=dap, in0=psD[qb][:, 0:QT], in1=dap, op=ALU.add), reads=[psDb[qb], aDb], writes=[aDb])
            LA = 3
            base = it
            for i in range(len(pairs) + LA):
                if i < len(pairs):
                    stageA(pairs[i], (base + i) % 4)
                if i >= LA:
                    stageB(pairs[i - LA], (base + i - LA) % 4)
            it += len(pairs)
        P.op('dve', lambda e: e.reciprocal(out=aD[:, :], in_=aD[:, :]), reads=[aDb], writes=[aDb])
        CW = min(2048, S)
        for cc in range(S // CW):
            yb_, ybb_ = ya[cc % 2], yab[cc % 2]
            P.op('dve', lambda e, cc=cc, yb_=yb_: e.tensor_tensor(out=yb_[:, 0:CW], in0=aO[:, cc * CW:(cc + 1) * CW], in1=aD[:, cc * CW:(cc + 1) * CW], op=ALU.mult),
                 reads=[aOb, aDb], writes=[ybb_])
            P.dma('pool', lambda e, h=h, cc=cc, yb_=yb_: e.dma_start(out=dr['YA'][h][:, cc * CW:(cc + 1) * CW], in_=yb_[:, 0:CW]), ybb_, reads=[ybb_])
    C.close()


def build(cfg):
    D, S, FF, L, MH, AH, M = cfg['D'], cfg['S'], cfg['FF'], cfg['DEPTH'], cfg['MH'], cfg['AH'], cfg['M']
    KC, FC = D // 128, FF // 128
    NREC, NATT = (L + 1) // 2, L // 2
    MW = MH * 256
    _, NS = smalls_layout(cfg)
    nc = bass.Bass("TRN2", target_bir_lowering=False)
    dr = {}

    def din(name, shape):
        dr[name] = nc.dram_tensor(name, list(shape), F32, kind="ExternalInput").ap()

    def dsc(name, shape, dt=F32):
        dr[name] = nc.dram_tensor(name, list(shape), dt, kind="ExternalOutput" if cfg.get('dbg') else "Internal").ap()
    din('xT', [KC, 128, S])
    din('memT', [KC, 128, M])
    din('smalls', [L, 128, NS])
    din('ident', [128, 128])
    din('c_rel', [6, 128, 512])
    din('c_negm', [6, 128, 512])
    din('ffn_w_in', [L * 2, 2 * FC, 128, KC, 128])
    din('ffn_w_out', [L * 2, KC, 128, FC, 128])
    din('mem_w_kv', [L, 2 * MW // 128, 128, KC, 128])
    din('rec_w_in', [NREC, 2 * KC + 2 * MH, 128, KC, 128])
    din('rec_w_out', [NREC, KC, 128, KC + 2 * MH, 128])
    din('gate_w', [NREC, 2, 2, D // 256, 2, 128, 2, 128])
    if NATT:
        din('att_w_in', [NATT, 9 * AH + 2 * MH, 128, KC, 128])
        din('att_w_out', [NATT, KC, 128, AH + 2 * MH, 128])
    dr['out'] = nc.dram_tensor('out', [KC, 128, S], F32, kind="ExternalOutput").ap()
    dsc('KN', [128, 2 * MH, M])
    dsc('VM', [128, M // 128, MW])
    dsc('YM', [2 * MH, 128, S], BF16)
    dsc('GG', [KC, 128, S])
    dsc('RB', [KC, 128, S + 3])
    dsc('HF', [KC, 128, S])
    dsc('YR', [KC, 128, S], BF16)
    dsc('QK', [6 * AH, 128, S], BF16)
    dsc('V', [3 * AH, S, 128], BF16)
    dsc('YA', [AH, 128, S], BF16)
    for name in BIG:
        if name in dr:
            dr[name + '_bf'] = [nc.dram_tensor("%s_bf%d" % (name, i), list(dr[name].shape[1:]), BF16, kind="Internal").ap()
                                for i in range(dr[name].shape[0])]
    P = Prog(nc)
    PC = Precast(P, cfg, dr)
    phase_precast(P, cfg, dr, PC)
    X = dr['out']
    stop = cfg.get('stop')
    for l in range(L):
        phase_ffn(P, cfg, dr, dr['xT'] if l == 0 else X, X, l, 0, PC)
        if stop == 'ffn0':
            break
        phase_memkv(P, cfg, dr, l)
        j = l // 2
        if l % 2 == 0:
            phase_rec_p1(P, cfg, dr, X, l, j)
            phase_rec_p2(P, cfg, dr, l, j, 0)
            phase_rec_p2(P, cfg, dr, l, j, 1)
            phase_wout(P, cfg, dr, X, X, dr['rec_w_out_bf'][j], dr['YR'], KC)
            if stop == 'mix0':
                break
        else:
            phase_att_p1(P, cfg, dr, X, l, j)
            phase_att_p2(P, cfg, dr)
            phase_wout(P, cfg, dr, X, X, dr['att_w_out_bf'][j], dr['YA'], AH)
        phase_ffn(P, cfg, dr, X, X, l, 1, PC)
    return nc


def pieces(W):
    K, Mo = W.shape[-2], W.shape[-1]
    lead = W.shape[:-2]
    Wr = W.reshape(*lead, K // 128, 128, Mo // 128, 128)
    nl = len(lead)
    perm = list(range(nl)) + [nl + 2, nl + 1, nl + 0, nl + 3]
    return np.ascontiguousarray(Wr.transpose(perm))


def cols(v):
    n = v.shape[-1] // 128
    return np.swapaxes(v.reshape(*v.shape[:-1], n, 128), -1, -2)


def const_tables():
    k = np.arange(128)[:, None]
    q = np.arange(512)[None, :]
    rel = np.stack([np.abs(128 * (ci - 1) + k - q) for ci in range(6)]).astype(np.float32)
    rel = np.where(rel <= 64, rel, 1.0e6).astype(np.float32)
    return rel, rel


def prep_inputs(cfg, inp):
    D, S, L = cfg['D'], cfg['S'], cfg['DEPTH']
    KC = D // 128
    SO, NS = smalls_layout(cfg)
    f = lambda a: np.asarray(a, dtype=np.float32)
    sm = np.zeros((L, 128, NS), np.float32)
    for l in range(L):
        j = l // 2
        sm[l, :, SO['ffn_g0']:SO['ffn_g0'] + KC] = cols(f(inp['ffn_norm'])[l, 0])
        sm[l, :, SO['ffn_g1']:SO['ffn_g1'] + KC] = cols(f(inp['ffn_norm'])[l, 1])
        sm[l, :, SO['mix_g']:SO['mix_g'] + KC] = cols(f(inp['mix_norm'])[l])
        sm[l, :, SO['mem_g']:SO['mem_g'] + KC] = cols(f(inp['mem_norm'])[l])
        sm[l, :, SO['memq_g']:SO['memq_g'] + 2] = cols(f(inp['mem_qk_gain'])[l, 0])
        sm[l, :, SO['memk_g']:SO['memk_g'] + 2] = cols(f(inp['mem_qk_gain'])[l, 1])
        if l % 2 == 0:
            cw = f(inp['rec_conv_w'])[j]
            sm[l, :, SO['conv_w']:SO['conv_w'] + 4 * KC] = np.stack([cols(cw[t]) for t in range(4)], -1).reshape(128, 4 * KC)
            sm[l, :, SO['conv_b']:SO['conv_b'] + KC] = cols(f(inp['rec_conv_b'])[j])
            gb = f(inp['rec_gate_b'])[j].reshape(2, 2, D)
            for d in range(2):
                for gt in range(2):
                    o = SO['gb'] + (d * 2 + gt) * KC
                    sm[l, :, o:o + KC] = cols(gb[d, gt])
                o = SO['lam'] + d * KC
                sm[l, :, o:o + KC] = cols(f(inp['rec_lambda'])[j, d])
        else:
            qk = f(inp['att_qk_gain'])[j]
            sm[l, :, SO['qg']:SO['qg'] + 3] = qk[0].T
            sm[l, :, SO['kg']:SO['kg'] + 3] = qk[1].T
    rel, negm = const_tables()
    shared = {
        'smalls': sm, 'ident': np.eye(128, dtype=np.float32), 'c_rel': rel, 'c_negm': negm,
        'ffn_w_in': pieces(f(inp['ffn_w_in'])).reshape(L * 2, -1, 128, KC, 128),
        'ffn_w_out': pieces(f(inp['ffn_w_out'])).reshape(L * 2, KC, 128, -1, 128),
        'mem_w_kv': pieces(f(inp['mem_w_kv'])),
        'rec_w_in': pieces(f(inp['rec_w_in'])),
        'rec_w_out': pieces(f(inp['rec_w_out'])),
        'gate_w': pieces(f(inp['rec_gate_w'])),
    }
    if L // 2:
        shared['att_w_in'] = pieces(f(inp['att_w_in']))
        shared['att_w_out'] = pieces(f(inp['att_w_out']))
    x, mem = f(inp['x']), f(inp['mem'])
    B = x.shape[0]
    maps = []
    for b in range(B):
        m = dict(shared)
        m['xT'] = np.ascontiguousarray(x[b].T).reshape(KC, 128, S)
        m['memT'] = np.ascontiguousarray(mem[b].T).reshape(KC, 128, -1)
        maps.append(m)
    return maps


def run(cfg, inp):
    nc = build(cfg)
    maps = prep_inputs(cfg, inp)
    B = len(maps)
    res = run_bass_kernel_spmd(nc, maps, core_ids=list(range(B)))
    D, S = cfg['D'], cfg['S']
    if cfg.get('dbg'):
        cfg['dbg_out'] = res.results
    return np.stack([np.ascontiguousarray(res.results[b]['out'].reshape(D, S).T) for b in range(B)]).astype(np.float32)


def kernel(**inputs):
    return run(FULL, inputs)
```
